# Optimizing a Trainium2 kernel written in Bass

```python
import math
import jax, jax.numpy as jnp
from jax import lax
import numpy as np

D_MODEL = 1024
BATCH = 8
SEQ = 8192
DEPTH = 2
DEC_BATCH = 16
DEC_SEQ = 32
PAST_LEN = 4096

CHUNK = 64
D_SSM = D_MODEL // 2
SSM_GROUP = 16
N_GROUPS = D_SSM // SSM_GROUP
STATE = 64
N_HEADS = 8
HEAD_DIM = 64
D_ATT = N_HEADS * HEAD_DIM
IDX_HEADS = 8
IDX_DIM = 64
TOPK_MAX = 256
N_BRANCH = 2
EPS = 1e-6
DT_MIN = 1e-3
DT_MAX = 1e-1
SPLITS = (D_SSM, D_SSM, D_ATT, D_ATT, D_ATT, D_ATT, IDX_HEADS * IDX_DIM, IDX_DIM, IDX_HEADS, N_BRANCH * D_MODEL)
D_IN = 4 * D_ATT + 2 * D_SSM + IDX_HEADS * IDX_DIM + IDX_DIM + IDX_HEADS + N_BRANCH * D_MODEL

kernel_name = 'hybrid_s5_dsa_stream_step'

f32 = jnp.float32


def rms_norm(x, g):
    xf = x.astype(f32)
    return xf * lax.rsqrt(jnp.mean(xf * xf, axis=-1, keepdims=True) + EPS) * g.astype(f32)


def ssm_discretize(a_re, a_im, log_dt, b_re, b_im):
    a_re = a_re.astype(f32); a_im = a_im.astype(f32)
    dt = jnp.exp(log_dt.astype(f32))[:, None]
    mag = jnp.exp(a_re * dt); ang = a_im * dt
    ab_re = mag * jnp.cos(ang); ab_im = mag * jnp.sin(ang)
    den = a_re * a_re + a_im * a_im
    nr = ab_re - 1.0; ni = ab_im
    f_re = (nr * a_re + ni * a_im) / den
    f_im = (ni * a_re - nr * a_im) / den
    b_re = b_re.astype(f32); b_im = b_im.astype(f32)
    bb_re = f_re[..., None] * b_re - f_im[..., None] * b_im
    bb_im = f_re[..., None] * b_im + f_im[..., None] * b_re
    return ab_re, ab_im, bb_re, bb_im


def _cplx_combine(e1, e2):
    a1r, a1i, b1r, b1i = e1
    a2r, a2i, b2r, b2i = e2
    ar = a1r * a2r - a1i * a2i
    ai = a1r * a2i + a1i * a2r
    br = a2r * b1r - a2i * b1i + b2r
    bi = a2r * b1i + a2i * b1r + b2i
    return ar, ai, br, bi


def ssm_block(u, h_re, h_im, ab_re, ab_im, bb_re, bb_im, c_re, c_im, d_skip):
    bu_re = jnp.einsum('gpn,btgn->btgp', bb_re, u)
    bu_im = jnp.einsum('gpn,btgn->btgp', bb_im, u)
    bu_re = bu_re.at[:, 0].add(ab_re * h_re - ab_im * h_im)
    bu_im = bu_im.at[:, 0].add(ab_re * h_im + ab_im * h_re)
    a_re = jnp.broadcast_to(ab_re, bu_re.shape)
    a_im = jnp.broadcast_to(ab_im, bu_im.shape)
    _, _, hr, hi = lax.associative_scan(_cplx_combine, (a_re, a_im, bu_re, bu_im), axis=1)
    y = (jnp.einsum('gnp,btgp->btgn', c_re, hr) - jnp.einsum('gnp,btgp->btgn', c_im, hi)
         + d_skip * u)
    return y, hr[:, -1], hi[:, -1]


def ssm_scan(u, h_re, h_im, disc, c_re, c_im, d_skip):
    ab_re, ab_im, bb_re, bb_im = disc
    c_re = c_re.astype(f32); c_im = c_im.astype(f32); d_skip = d_skip.astype(f32)
    B_, T = u.shape[:2]
    if T > CHUNK and T % CHUNK == 0:
        n = T // CHUNK
        uc = jnp.moveaxis(u.reshape(B_, n, CHUNK, N_GROUPS, SSM_GROUP), 1, 0)

        def step(carry, u_c):
            hr, hi = carry
            y, hr, hi = ssm_block(u_c, hr, hi, ab_re, ab_im, bb_re, bb_im, c_re, c_im, d_skip)
            return (hr, hi), y

        (hr, hi), ys = lax.scan(step, (h_re, h_im), uc)
        return jnp.moveaxis(ys, 0, 1).reshape(B_, T, N_GROUPS, SSM_GROUP), hr, hi
    return ssm_block(u, h_re, h_im, ab_re, ab_im, bb_re, bb_im, c_re, c_im, d_skip)


def dsa_attend(q, qi, wi, k_all, v_all, ki_all, n_valid, n_sel):
    logits = jnp.einsum('bthd,bsd->bths', qi.astype(f32), ki_all.astype(f32)) * (IDX_DIM ** -0.5)
    score = jnp.einsum('bth,bths->bts', wi.astype(f32), jax.nn.relu(logits))
    S = k_all.shape[1]
    adm = jnp.arange(S) < n_valid
    score = jnp.where(adm[None, None, :], score, -jnp.inf)
    top_val, top_idx = lax.top_k(score, n_sel)
    valid = jnp.isfinite(top_val)
    gather = jax.vmap(lambda rows, idx: rows[idx])
    kg = gather(k_all, top_idx).astype(f32)
    vg = gather(v_all, top_idx).astype(f32)
    s = jnp.einsum('bthd,btkhd->bthk', q.astype(f32), kg) * (HEAD_DIM ** -0.5)
    s = jnp.where(valid[:, :, None, :], s, -jnp.inf)
    p = jax.nn.softmax(s, axis=-1)
    return jnp.einsum('bthk,btkhd->bthd', p, vg)


def layer(x, c, w_mod, b_mod, g_norm, w_in, a_re, a_im, log_dt, b_re, b_im, c_re, c_im, d_skip,
          w_glu, b_glu, w_ps, w_pa, w_o, h_re, h_im, past_k, past_v, past_ki):
    dtype = x.dtype
    B_, T = x.shape[:2]
    mod = jax.nn.silu(c.astype(f32)) @ w_mod.astype(f32) + b_mod.astype(f32)
    shift, scale, gate = jnp.split(mod, 3, axis=-1)
    h = rms_norm(x, g_norm) * (1.0 + scale[:, None]) + shift[:, None]
    z = jnp.einsum('btd,de->bte', h, w_in.astype(f32))
    offs = list(np.cumsum(SPLITS)[:-1])
    u, zs, q, k, v, za, qi, ki, wi, gm = jnp.split(z, offs, axis=-1)

    disc = ssm_discretize(a_re, a_im, log_dt, b_re, b_im)
    us = u.reshape(B_, T, N_GROUPS, SSM_GROUP)
    ys, hr, hi = ssm_scan(us, h_re.astype(f32), h_im.astype(f32), disc, c_re, c_im, d_skip)
    ys = jax.nn.gelu(ys.reshape(B_, T, D_SSM))
    ys = ys * jax.nn.sigmoid(ys @ w_glu.astype(f32) + b_glu.astype(f32))
    ys = ys * jax.nn.silu(zs)

    q = q.reshape(B_, T, N_HEADS, HEAD_DIM)
    k = k.reshape(B_, T, N_HEADS, HEAD_DIM)
    v = v.reshape(B_, T, N_HEADS, HEAD_DIM)
    qi = qi.reshape(B_, T, IDX_HEADS, IDX_DIM)
    wi = wi * (IDX_HEADS ** -0.5)
    if past_k is None:
        n_sel = min(TOPK_MAX, T // 4)

        def one_chunk(ci):
            q0 = ci * CHUNK
            sl = lambda a: lax.dynamic_slice_in_dim(a, q0, CHUNK, axis=1)
            return dsa_attend(sl(q), sl(qi), sl(wi), k, v, ki, q0 + CHUNK, n_sel)

        o = lax.map(one_chunk, jnp.arange(T // CHUNK))
        o = jnp.moveaxis(o, 0, 1).reshape(B_, T, N_HEADS, HEAD_DIM)
    else:
        k_all = jnp.concatenate([past_k.astype(f32), k], axis=1)
        v_all = jnp.concatenate([past_v.astype(f32), v], axis=1)
        ki_all = jnp.concatenate([past_ki.astype(f32), ki], axis=1)
        L = k_all.shape[1]
        o = dsa_attend(q, qi, wi, k_all, v_all, ki_all, L, min(TOPK_MAX, L // 4))
    ya = o.reshape(B_, T, D_ATT) * jax.nn.silu(za)

    g_s, g_a = jnp.split(gm, N_BRANCH, axis=-1)
    merged = (jax.nn.sigmoid(g_s) * (ys @ w_ps.astype(f32))
              + jax.nn.sigmoid(g_a) * (ya @ w_pa.astype(f32)))
    out = merged @ w_o.astype(f32)
    x_new = (x.astype(f32) + gate[:, None] * out).astype(dtype)
    return x_new, (k.astype(dtype), v.astype(dtype), ki.astype(dtype), hr, hi)


def setup_inputs(seed: int = 0) -> dict:
    key = jax.random.key(seed)
    ks = jax.random.split(key, 32)
    nrm = lambda i, shape, s: jax.random.normal(ks[i], shape, f32) * s
    n_idx = jnp.arange(STATE, dtype=f32)
    return {
        'x_prompt': nrm(0, (BATCH, SEQ, D_MODEL), 1.0),
        'x_sample': nrm(1, (DEC_BATCH, DEC_SEQ, D_MODEL), 1.0),
        'cache_k': nrm(2, (DEPTH, DEC_BATCH, PAST_LEN, N_HEADS, HEAD_DIM), 1.0),
        'cache_v': nrm(3, (DEPTH, DEC_BATCH, PAST_LEN, N_HEADS, HEAD_DIM), 1.0),
        'cache_kidx': nrm(4, (DEPTH, DEC_BATCH, PAST_LEN, IDX_DIM), 1.0),
        'state_ssm_re': nrm(5, (DEPTH, DEC_BATCH, N_GROUPS, STATE), 0.5),
        'state_ssm_im': nrm(6, (DEPTH, DEC_BATCH, N_GROUPS, STATE), 0.5),
        'c_prompt': nrm(7, (BATCH, D_MODEL), 1.0),
        'c_sample': nrm(8, (DEC_BATCH, D_MODEL), 1.0),
        'w_mod': nrm(9, (DEPTH, D_MODEL, 3 * D_MODEL), 0.5 * D_MODEL ** -0.5),
        'b_mod': nrm(10, (DEPTH, 3 * D_MODEL), 0.01),
        'g_norm': 1.0 + nrm(11, (DEPTH, D_MODEL), 0.01),
        'w_in': nrm(12, (DEPTH, D_MODEL, D_IN), D_MODEL ** -0.5),
        'a_re': -0.5 + nrm(13, (DEPTH, N_GROUPS, STATE), 0.01),
        'a_im': math.pi * n_idx + nrm(14, (DEPTH, N_GROUPS, STATE), 0.01),
        'log_dt': jax.random.uniform(ks[15], (DEPTH, N_GROUPS), f32, math.log(DT_MIN), math.log(DT_MAX)),
        'b_re': nrm(16, (DEPTH, N_GROUPS, STATE, SSM_GROUP), (2 * SSM_GROUP) ** -0.5),
        'b_im': nrm(17, (DEPTH, N_GROUPS, STATE, SSM_GROUP), (2 * SSM_GROUP) ** -0.5),
        'c_re': nrm(18, (DEPTH, N_GROUPS, SSM_GROUP, STATE), STATE ** -0.5),
        'c_im': nrm(19, (DEPTH, N_GROUPS, SSM_GROUP, STATE), STATE ** -0.5),
        'd_skip': nrm(20, (DEPTH, N_GROUPS, SSM_GROUP), 1.0),
        'w_glu': nrm(21, (DEPTH, D_SSM, D_SSM), D_SSM ** -0.5),
        'b_glu': nrm(22, (DEPTH, D_SSM), 0.01),
        'w_ps': nrm(23, (DEPTH, D_SSM, D_MODEL), D_SSM ** -0.5),
        'w_pa': nrm(24, (DEPTH, D_ATT, D_MODEL), D_ATT ** -0.5),
        'w_o': nrm(25, (DEPTH, D_MODEL, D_MODEL), D_MODEL ** -0.5),
        'g_final': 1.0 + nrm(26, (D_MODEL,), 0.01),
    }


def reference(x_prompt, x_sample, cache_k, cache_v, cache_kidx, state_ssm_re, state_ssm_im,
              c_prompt, c_sample, w_mod, b_mod, g_norm, w_in, a_re, a_im, log_dt, b_re, b_im,
              c_re, c_im, d_skip, w_glu, b_glu, w_ps, w_pa, w_o, g_final):
    xp, xs = x_prompt, x_sample
    kp, vp, kip, srp, sip = [], [], [], [], []
    kss, vss, kis, srs, sis = [], [], [], [], []
    zeros = jnp.zeros((x_prompt.shape[0], N_GROUPS, STATE), f32)
    for l in range(DEPTH):
        lw = (w_mod[l], b_mod[l], g_norm[l], w_in[l], a_re[l], a_im[l], log_dt[l], b_re[l], b_im[l],
              c_re[l], c_im[l], d_skip[l], w_glu[l], b_glu[l], w_ps[l], w_pa[l], w_o[l])
        xp, (k1, v1, ki1, hr1, hi1) = layer(xp, c_prompt, *lw, zeros, zeros, None, None, None)
        xs, (k2, v2, ki2, hr2, hi2) = layer(xs, c_sample, *lw, state_ssm_re[l], state_ssm_im[l],
                                            cache_k[l], cache_v[l], cache_kidx[l])
        kp.append(k1); vp.append(v1); kip.append(ki1); srp.append(hr1); sip.append(hi1)
        kss.append(k2); vss.append(v2); kis.append(ki2); srs.append(hr2); sis.append(hi2)
    y_prompt = rms_norm(xp, g_final).astype(x_prompt.dtype)
    y_sample = rms_norm(xs, g_final).astype(x_sample.dtype)
    return (y_prompt, y_sample,
            jnp.stack(kp), jnp.stack(vp), jnp.stack(kip), jnp.stack(srp), jnp.stack(sip),
            jnp.stack(kss), jnp.stack(vss), jnp.stack(kis), jnp.stack(srs), jnp.stack(sis))
```

```python
import numpy as np
from contextlib import ExitStack
import concourse.bass as bass
import concourse.mybir as mybir
from concourse.bass_utils import run_bass_kernel_spmd

F32 = mybir.dt.float32
BF16 = mybir.dt.bfloat16
I32 = mybir.dt.int32
AF = mybir.ActivationFunctionType
ALU = mybir.AluOpType

D = 1024
DIN = 5704
NL = 2
EPS = 1e-6
OFF_U, OFF_ZS, OFF_Q, OFF_K, OFF_V, OFF_ZA, OFF_QI, OFF_KI, OFF_WI, OFF_GM = (
    0, 512, 1024, 1536, 2048, 2560, 3072, 3584, 3648, 3656)
NEG = -1.0e30
TOPK = 256


class TT:
    __slots__ = ("t", "name", "w", "r", "dsem", "dtot")

    def __init__(self, t, name):
        self.t = t
        self.name = name
        self.w = None
        self.r = []
        self.dsem = None
        self.dtot = 0

    def __getitem__(self, idx):
        return self.t[idx]


class Phase:
    ENGS = ("sync", "pool", "act", "dve", "pe")

    def __init__(self, nc, name):
        self.nc = nc
        self.name = name
        self.es = ExitStack()
        self.ops = {e: [] for e in self.ENGS}
        self.cnt = {e: 0 for e in self.ENGS}
        self.known = {e: {} for e in self.ENGS}
        self.sems = {}
        self.semobj = {}
        for e in ("pool", "act", "dve", "pe"):
            s = self.es.enter_context(nc.semaphore(f"{name}_{e}"))
            self.sems[e] = s
            self.semobj[("c", e)] = s
        self.ntile = 0
        self.dma_tiles = []
        self.misc = None

    def sb(self, shape, dtype, name=None):
        self.ntile += 1
        nm = f"{self.name}_{name or 't'}{self.ntile}"
        t = self.es.enter_context(self.nc.sbuf_tensor(nm, list(shape), dtype))
        return TT(t, nm)

    def ps(self, shape, dtype, name=None):
        self.ntile += 1
        nm = f"{self.name}_{name or 'p'}{self.ntile}"
        t = self.es.enter_context(self.nc.psum_tensor(nm, list(shape), dtype))
        return TT(t, nm)

    def _dsem(self, tt):
        if tt.dsem is None:
            tt.dsem = self.es.enter_context(self.nc.semaphore(f"d_{tt.name}"))
            self.semobj[("d", tt.name)] = tt.dsem
            self.dma_tiles.append(tt)
        return tt.dsem

    def _need(self, eng, dep, waits):
        if dep is None:
            return
        key, val = dep
        if key == ("c", "pe") and eng == "pe":
            return
        if key == ("c", eng) and eng in ("sync",):
            return
        k = self.known[eng]
        if k.get(key, 0) >= val:
            return
        k[key] = val
        waits.append((self.semobj[key], val))

    def _deps(self, eng, r, w):
        waits = []
        for t in r:
            self._need(eng, t.w, waits)
        for t in w:
            self._need(eng, t.w, waits)
            for d in t.r:
                self._need(eng, d, waits)
        return waits

    def op(self, eng, fn, r=(), w=()):
        waits = self._deps(eng, r, w)
        self.cnt[eng] += 1
        me = (("c", eng), self.cnt[eng])
        for t in r:
            t.r.append(me)
        for t in w:
            t.w = me
            t.r = []
        self.ops[eng].append((waits, fn, ("c", self.sems[eng])))

    def dma(self, fn, r=(), w=(), q="sync"):
        waits = self._deps(q, r, w)
        tiles = list(w) + list(r)
        if tiles:
            tt = tiles[0]
            sem = self._dsem(tt)
            tt.dtot += 16
            me = (("d", tt.name), tt.dtot)
        else:
            if self.misc is None:
                self.misc = TT(None, f"{self.name}_misc")
            tt = self.misc
            sem = self._dsem(tt)
            tt.dtot += 16
            me = (("d", tt.name), tt.dtot)
        for t in r:
            t.r.append(me)
        for t in w:
            t.w = me
            t.r = []
        self.ops[q].append((waits, fn, ("d", sem)))

    def finish(self):
        nc = self.nc
        finals = [(t.dsem, t.dtot) for t in self.dma_tiles if t.dtot > 0]

        def replay(e, name):
            for waits, fn, inc in self.ops[name]:
                for s, v in waits:
                    e.wait_ge(s, v)
                ins = fn(e)
                if inc[0] == "c":
                    ins.then_inc(inc[1], 1)
                else:
                    ins.then_inc(inc[1], 16)
            if name == "sync":
                for s, v in finals:
                    e.wait_ge(s, v)

        with nc.Block() as block:
            @block.sync
            def _(e):
                replay(e, "sync")

            @block.gpsimd
            def _(e):
                replay(e, "pool")

            @block.scalar
            def _(e):
                replay(e, "act")

            @block.vector
            def _(e):
                replay(e, "dve")

            @block.tensor
            def _(e):
                replay(e, "pe")
        allsems = list(self.semobj.values())
        with nc.Block() as block2:
            @block2.sync
            def _(e):
                for sm_ in allsems:
                    e.sem_clear(sm_)
        self.es.close()


def I_ts(out, in0, s1, s2=None, op0=ALU.mult, op1=None, accum_out=None):
    kw = {}
    if op1 is not None:
        kw["op1"] = op1
    if accum_out is not None:
        kw["accum_out"] = accum_out
    return lambda e: e.tensor_scalar(out=out, in0=in0, scalar1=s1, scalar2=s2, op0=op0, **kw)


def I_tt(out, in0, in1, op):
    return lambda e: e.tensor_tensor(out=out, in0=in0, in1=in1, op=op)


def I_stt(out, in0, scalar, in1, op0, op1):
    return lambda e: e.scalar_tensor_tensor(out=out, in0=in0, scalar=scalar, in1=in1, op0=op0, op1=op1)


def I_act(out, in_, func, bias=None, scale=None, accum_out=None):
    kw = {}
    if bias is not None:
        kw["bias"] = bias
    if scale is not None:
        kw["scale"] = scale
    if accum_out is not None:
        kw["accum_out"] = accum_out
    return lambda e: e.activation(out=out, in_=in_, func=func, **kw)


def I_mm(out, lhsT, rhs, start=True, stop=True):
    return lambda e: e.matmul(out, lhsT, rhs, start=start, stop=stop)


def I_tr(out, in_, ident):
    return lambda e: e.transpose(out, in_, ident)


def I_copy(out, in_):
    return lambda e: e.tensor_copy(out=out, in_=in_)


def I_memset(ap, v):
    return lambda e: e.memset(ap, v)


def I_recip(out, in_):
    return lambda e: e.reciprocal(out=out, in_=in_)


def I_scan(out, d0, d1, init):
    return lambda e: e.tensor_tensor_scan(out=out, data0=d0, data1=d1, initial=init, op0=ALU.mult, op1=ALU.add)


def I_dma(out, in_, **kw):
    return lambda e: e.dma_start(out=out, in_=in_, **kw)


def I_iota(out, pattern, base, cm):
    return lambda e: e.iota(out=out, pattern=pattern, base=base, channel_multiplier=cm,
                            allow_small_or_imprecise_dtypes=True)


def mk_ident(ph):
    iot = ph.sb([128, 128], F32, "iot")
    idf = ph.sb([128, 128], F32, "idf")
    idb = ph.sb([128, 128], BF16, "idb")
    ph.op("pool", I_iota(iot.t[:], [[1, 128]], 0, -1), w=[iot])
    ph.op("dve", I_ts(idf.t[:], iot.t[:], 0.0, None, op0=ALU.is_equal), r=[iot], w=[idf])
    ph.op("dve", I_copy(idb.t[:], idf.t[:]), r=[idf], w=[idb])
    return idf, idb


def bcast_rows(ap_row, n):
    return ap_row.to_broadcast([n, ap_row.shape[-1]])


class Seq:
    pass


def build(T=8192, TS=32, PAST=4096, stop_after=None, dbg=False):
    nc = bass.Bass("TRN2", target_bir_lowering=False)
    SS = PAST + TS
    din = lambda n, s, dt=F32: nc.dram_tensor(n, list(s), dt, kind="ExternalInput").ap()
    dout = lambda n, s, dt=F32: nc.dram_tensor(n, list(s), dt, kind="ExternalOutput").ap()
    scr_kind = "ExternalOutput" if dbg else "Internal"
    dscr = lambda n, s, dt=BF16: nc.dram_tensor(n, list(s), dt, kind=scr_kind).ap()

    A = {}
    A["xp"] = din("xp", [T, D]); A["xs"] = din("xs", [2, TS, D])
    A["ck"] = din("ck", [NL, 2, PAST, 512]); A["cv"] = din("cv", [NL, 2, PAST, 512])
    A["cki"] = din("cki", [NL, 2, PAST, 64])
    A["ssr"] = din("ssr", [NL, 2, 2048]); A["ssi"] = din("ssi", [NL, 2, 2048])
    A["call"] = din("call", [3, D])
    A["w_mod"] = din("w_mod", [NL, D, 3 * D]); A["b_mod"] = din("b_mod", [NL, 3 * D])
    A["g_norm"] = din("g_norm", [NL, D]); A["w_in"] = din("w_in", [NL, D, DIN])
    A["a_re"] = din("a_re", [NL, 2048]); A["a_im"] = din("a_im", [NL, 2048])
    A["log_dt"] = din("log_dt", [NL, 32])
    A["b_re"] = din("b_re", [NL, 2048, 16]); A["b_im"] = din("b_im", [NL, 2048, 16])
    A["c_reT"] = din("c_reT", [NL, 2048, 16]); A["c_imT"] = din("c_imT", [NL, 2048, 16])
    A["d_skip"] = din("d_skip", [NL, 512])
    A["w_glu"] = din("w_glu", [NL, 512, 512]); A["b_glu"] = din("b_glu", [NL, 512])
    A["w_ps"] = din("w_ps", [NL, 512, D]); A["w_pa"] = din("w_pa", [NL, 512, D])
    A["w_o"] = din("w_o", [NL, D, D]); A["g_final"] = din("g_final", [1, D])

    O = {}
    O["yp"] = dout("yp", [T, D]); O["ys"] = dout("ys", [2, TS, D])
    O["kp"] = dout("kp", [NL, T, 512]); O["vp"] = dout("vp", [NL, T, 512]); O["kip"] = dout("kip", [NL, T, 64])
    O["srp"] = dout("srp", [NL, 2048]); O["sip"] = dout("sip", [NL, 2048])
    O["ks"] = dout("ks", [NL, 2, TS, 512]); O["vs"] = dout("vs", [NL, 2, TS, 512])
    O["kis"] = dout("kis", [NL, 2, TS, 64])
    O["srs"] = dout("srs", [NL, 2, 2048]); O["sis"] = dout("sis", [NL, 2, 2048])

    modbc = dscr("modbc", [NL, 3, 3, 128, D], F32)

    seqs = []
    for i in range(3):
        s = Seq()
        s.i = i
        s.nm = "p" if i == 0 else f"s{i - 1}"
        s.T = T if i == 0 else TS
        s.S = T if i == 0 else SS
        s.koff = 0 if i == 0 else PAST
        s.causal = (i == 0)
        s.xin = A["xp"] if i == 0 else A["xs"][i - 1]
        s.xres = dscr(f"xres_{s.nm}", [s.T, D], F32)
        for nm, rows, cols in (("uT", 512, s.T), ("zsT", 512, s.T), ("qT", 512, s.T), ("kT", 512, s.S),
                               ("zaT", 512, s.T), ("qiT", 512, s.T), ("kiT", 64, s.S), ("gmT", 2048, s.T),
                               ("vaug", s.S, 520), ("ysT", 512, s.T), ("yaT", 512, s.T)):
            setattr(s, nm, dscr(f"{nm}_{s.nm}", [rows, cols], BF16))
        s.wiS = dscr(f"wiS_{s.nm}", [s.T, 8], F32)
        if i == 0:
            s.kout = [O["kp"][l] for l in range(NL)]; s.vout = [O["vp"][l] for l in range(NL)]
            s.kiout = [O["kip"][l] for l in range(NL)]
            s.srout = [O["srp"][l] for l in range(NL)]; s.siout = [O["sip"][l] for l in range(NL)]
            s.yout = O["yp"]
        else:
            j = i - 1
            s.kout = [O["ks"][l, j] for l in range(NL)]; s.vout = [O["vs"][l, j] for l in range(NL)]
            s.kiout = [O["kis"][l, j] for l in range(NL)]
            s.srout = [O["srs"][l, j] for l in range(NL)]; s.siout = [O["sis"][l, j] for l in range(NL)]
            s.yout = O["ys"][j]
        seqs.append(s)

    ncd = nc.allow_non_contiguous_dma(reason="small strided param loads")
    ncd.__enter__()

    phase_mod(nc, A, modbc)
    if stop_after == "mod":
        return nc
    for l in range(NL):
        phase_proj(nc, A, l, seqs, modbc)
        if stop_after == f"proj{l}":
            return nc
        phase_cache(nc, A, l, seqs, PAST)
        if stop_after == f"cache{l}":
            return nc
        phase_ssm(nc, A, l, seqs)
        if stop_after == f"ssm{l}":
            return nc
        import os as _os2
        for s in seqs:
            if _os2.environ.get("ATTN_SEQ") and s.nm not in _os2.environ.get("ATTN_SEQ").split(","):
                continue
            phase_attn(nc, A, l, s)
        if stop_after == f"attn{l}":
            return nc
        phase_merge(nc, A, l, seqs, modbc)
        if stop_after == f"merge{l}":
            return nc
    return nc


def phase_mod(nc, A, modbc):
    ph = Phase(nc, "M")
    idf, idb = mk_ident(ph)
    cT = ph.sb([128, 8, 3], F32, "cT")
    for s_ in range(3):
        ph.dma(I_dma(cT.t[:, :, s_], A["call"][s_].rearrange("(c p) -> p c", p=128)), w=[cT])
    cTs = ph.sb([128, 8, 3], F32, "cTs")
    ph.op("act", I_act(cTs.t[:], cT.t[:], AF.Silu), r=[cT], w=[cTs])
    iop = ph.sb([3, 128], F32, "iop")
    ph.op("pool", I_iota(iop.t[:], [[0, 128]], 0, 1), w=[iop])
    sel = []
    for s in range(3):
        t = ph.sb([3, 128], F32, f"sel{s}")
        ph.op("dve", I_ts(t.t[:], iop.t[:], float(s), None, op0=ALU.is_equal), r=[iop], w=[t])
        sel.append(t)
    wst = [ph.sb([128, 8, 512], F32, f"wst{i}") for i in range(2)]
    pmm = [ph.ps([128, 512], F32, f"pm{i}") for i in range(2)]
    pbc = [ph.ps([128, 512], F32, f"pb{i}") for i in range(2)]
    stg = [ph.sb([128, D], F32, f"stg{i}") for i in range(2)]
    n = 0
    nb = 0
    for l in range(NL):
        bsb = ph.sb([3, 3 * D], F32, f"bsb{l}")
        ph.dma(I_dma(bsb.t[:], bcast_rows(A["b_mod"][l:l + 1, :], 3)), w=[bsb])
        gbc = ph.sb([128, D], F32, f"gbc{l}")
        ph.dma(I_dma(gbc.t[:], bcast_rows(A["g_norm"][l:l + 1, :], 128)), w=[gbc])
        mod = ph.sb([3, 3 * D], F32, f"mod{l}")
        for ct in range(6):
            w = wst[n % 2]; pm = pmm[n % 2]; n += 1
            ph.dma(I_dma(w.t[:], A["w_mod"][l].rearrange("(c p) n -> p c n", p=128)[:, :, ct * 512:(ct + 1) * 512]), w=[w])
            for c in range(8):
                ph.op("pe", I_mm(pm.t[0:3, :], cTs.t[:, c, :], w.t[:, c, :], start=(c == 0), stop=(c == 7)),
                      r=[cTs, w], w=[pm])
            ph.op("dve", I_tt(mod.t[:, ct * 512:(ct + 1) * 512], pm.t[0:3, :], bsb.t[:, ct * 512:(ct + 1) * 512], ALU.add),
                  r=[pm, bsb], w=[mod])
        ph.op("dve", I_ts(mod.t[:, D:2 * D], mod.t[:, D:2 * D], 1.0, None, op0=ALU.add), r=[mod], w=[mod])
        for s in range(3):
            for kind in range(3):
                st = stg[nb % 2]
                for half in range(2):
                    pb = pbc[nb % 2 if half == 0 else (nb + 1) % 2]
                    c0 = kind * D + half * 512
                    ph.op("pe", I_mm(pb.t[:, :], sel[s].t[:, :], mod.t[:, c0:c0 + 512]), r=[sel[s], mod], w=[pb])
                    if kind == 1:
                        ph.op("dve", I_tt(st.t[:, half * 512:(half + 1) * 512], pb.t[:, :], gbc.t[:, half * 512:(half + 1) * 512], ALU.mult),
                              r=[pb, gbc], w=[st])
                    else:
                        ph.op("act", I_act(st.t[:, half * 512:(half + 1) * 512], pb.t[:, :], AF.Copy), r=[pb], w=[st])
                nb += 1
                ph.dma(I_dma(modbc[l, s, kind], st.t[:]), r=[st])
    ph.finish()


def phase_proj(nc, A, l, seqs, modbc):
    ph = Phase(nc, f"P{l}")
    idf, idb = mk_ident(ph)
    wbf = ph.sb([128, 8, DIN], BF16, "wbf")
    win = A["w_in"][l].rearrange("(c p) n -> p c n", p=128)
    for c in range(8):
        for h0 in range(0, DIN, 1024):
            h1 = min(DIN, h0 + 1024)
            ph.dma(I_dma(wbf.t[:, c, h0:h1], win[:, c, h0:h1]), w=[wbf], q="pool")
    xt = [ph.sb([128, D], F32, f"xt{i}") for i in range(2)]
    junk = ph.sb([128, D], BF16, "junk")
    ss = [ph.sb([128, 1], F32, f"ss{i}") for i in range(2)]
    rstd = [ph.sb([128, 1], F32, f"rstd{i}") for i in range(2)]
    tmp = [ph.sb([128, D], F32, f"tmp{i}") for i in range(2)]
    hb = [ph.sb([128, D], BF16, f"hb{i}") for i in range(2)]
    hT = [ph.sb([128, 8, 512], BF16, f"hT{i}") for i in range(2)]
    gm_bc = ph.sb([128, D], F32, "gmbc")
    sh_bc = ph.sb([128, D], F32, "shbc")
    kvst = [ph.sb([128, 2 * 512], F32, f"kvst{i}") for i in range(2)]
    vst = [ph.sb([128, 8, 65], BF16, f"vst{i}") for i in range(2)]
    for v in vst:
        ph.op("pool", I_memset(v.t[:], 1.0), w=[v])
    kist = [ph.sb([128, 64], F32, f"kist{i}") for i in range(2)]
    wist = [ph.sb([128, 8], F32, f"wist{i}") for i in range(2)]
    fst = [ph.sb([128, 512], BF16, f"fst{i}") for i in range(4)]
    pT = ph.ps([128, 8, 128], BF16, "pT")
    pk = ph.ps([128, 512], F32, "pk")
    pv = ph.ps([128, 512], F32, "pv")
    pki = ph.ps([128, 72], F32, "pki")
    pf = [ph.ps([128, 512], F32, f"pf{i}") for i in range(3)]
    WISC = float(8 ** -0.5 * 64 ** -0.5)

    nt = 0
    nf = 0
    for s in seqs:
        ph.dma(I_dma(gm_bc.t[:], modbc[l, s.i, 1]), w=[gm_bc])
        ph.dma(I_dma(sh_bc.t[:], modbc[l, s.i, 0]), w=[sh_bc])
        xsrc = s.xin if l == 0 else s.xres
        TP = min(128, s.T)
        NW = min(512, s.T)
        for st in range(s.T // NW):
            ht = hT[st % 2]
            for j in range(NW // TP):
                r0 = st * NW + j * TP
                x = xt[nt % 2]; sq = ss[nt % 2]; rs = rstd[nt % 2]; tm = tmp[nt % 2]; h = hb[nt % 2]
                kv = kvst[nt % 2]; vs_ = vst[nt % 2]; kis_ = kist[nt % 2]; wis_ = wist[nt % 2]
                nt += 1
                ph.dma(I_dma(x.t[:TP, :], xsrc[r0:r0 + TP, :]), w=[x])
                ph.op("act", I_act(junk.t[:TP, :], x.t[:TP, :], AF.Square, accum_out=sq.t[:TP, 0:1]), r=[x], w=[junk, sq])
                ph.op("dve", I_ts(sq.t[:TP, :], sq.t[:TP, :], 1.0 / D, EPS, op0=ALU.mult, op1=ALU.add), r=[sq], w=[sq])
                ph.op("act", I_act(sq.t[:TP, :], sq.t[:TP, :], AF.Sqrt), r=[sq], w=[sq])
                ph.op("dve", I_recip(rs.t[:TP, :], sq.t[:TP, :]), r=[sq], w=[rs])
                ph.op("dve", I_stt(tm.t[:TP, :], x.t[:TP, :], rs.t[:TP, 0:1], gm_bc.t[:TP, :], ALU.mult, ALU.mult),
                      r=[x, rs, gm_bc], w=[tm])
                ph.op("pool", I_tt(h.t[:TP, :], tm.t[:TP, :], sh_bc.t[:TP, :], ALU.add), r=[tm, sh_bc], w=[h])
                for c in range(8):
                    ph.op("pe", I_tr(pT.t[:, c, :TP], h.t[:TP, c * 128:(c + 1) * 128], idb.t[:TP, :TP]), r=[h, idb], w=[pT])
                ph.op("act", I_act(ht.t[:, :, j * TP:(j + 1) * TP], pT.t[:, :, :TP], AF.Copy), r=[pT], w=[ht])
                for c in range(8):
                    ph.op("pe", I_mm(pk.t[:TP, :], ht.t[:, c, j * TP:(j + 1) * TP], wbf.t[:, c, OFF_K:OFF_K + 512],
                                     start=(c == 0), stop=(c == 7)), r=[ht, wbf], w=[pk])
                for c in range(8):
                    ph.op("pe", I_mm(pv.t[:TP, :], ht.t[:, c, j * TP:(j + 1) * TP], wbf.t[:, c, OFF_V:OFF_V + 512],
                                     start=(c == 0), stop=(c == 7)), r=[ht, wbf], w=[pv])
                for c in range(8):
                    ph.op("pe", I_mm(pki.t[:TP, :], ht.t[:, c, j * TP:(j + 1) * TP], wbf.t[:, c, OFF_KI:OFF_KI + 72],
                                     start=(c == 0), stop=(c == 7)), r=[ht, wbf], w=[pki])
                ph.op("act", I_act(kv.t[:TP, 0:512], pk.t[:TP, :], AF.Copy), r=[pk], w=[kv])
                ph.op("dve", I_copy(kv.t[:TP, 512:1024], pv.t[:TP, :]), r=[pv], w=[kv])
                ph.op("dve", I_copy(vs_.t[:TP, :, 0:64], pv.t[:TP, :].rearrange("p (h d) -> p h d", d=64)), r=[pv], w=[vs_])
                ph.op("act", I_act(kis_.t[:TP, :], pki.t[:TP, 0:64], AF.Copy), r=[pki], w=[kis_])
                ph.op("dve", I_ts(wis_.t[:TP, :], pki.t[:TP, 64:72], WISC, None, op0=ALU.mult), r=[pki], w=[wis_])
                ph.dma(I_dma(s.kout[l][r0:r0 + TP, :], kv.t[:TP, 0:512]), r=[kv])
                ph.dma(I_dma(s.vout[l][r0:r0 + TP, :], kv.t[:TP, 512:1024]), r=[kv])
                ph.dma(I_dma(s.vaug[s.koff + r0:s.koff + r0 + TP, :], vs_.t[:TP, :, :].rearrange("p h d -> p (h d)")), r=[vs_])
                ph.dma(I_dma(s.kiout[l][r0:r0 + TP, :], kis_.t[:TP, :]), r=[kis_])
                ph.dma(I_dma(s.wiS[r0:r0 + TP, :], wis_.t[:TP, :]), r=[wis_])
            t0 = st * NW
            fm = []
            for i in range(4):
                fm.append((OFF_U + i * 128, 128, "copy", s.uT[i * 128:(i + 1) * 128, t0:t0 + NW]))
                fm.append((OFF_Q + i * 128, 128, "q", s.qT[i * 128:(i + 1) * 128, t0:t0 + NW]))
                fm.append((OFF_K + i * 128, 128, "copy", s.kT[i * 128:(i + 1) * 128, s.koff + t0:s.koff + t0 + NW]))
                fm.append((OFF_QI + i * 128, 128, "copy", s.qiT[i * 128:(i + 1) * 128, t0:t0 + NW]))
            fm.append((OFF_KI, 64, "copy", s.kiT[0:64, s.koff + t0:s.koff + t0 + NW]))
            for i in range(4):
                fm.append((OFF_ZS + i * 128, 128, "silu", s.zsT[i * 128:(i + 1) * 128, t0:t0 + NW]))
            for i in range(4):
                fm.append((OFF_ZA + i * 128, 128, "silu", s.zaT[i * 128:(i + 1) * 128, t0:t0 + NW]))
            for i in range(16):
                fm.append((OFF_GM + i * 128, 128, "sig", s.gmT[i * 128:(i + 1) * 128, t0:t0 + NW]))
            for (off, M, kind, dst) in fm:
                p = pf[nf % 3]; f = fst[nf % 4]; nf += 1
                for c in range(8):
                    ph.op("pe", I_mm(p.t[:M, :NW], wbf.t[:, c, off:off + M], ht.t[:, c, :NW], start=(c == 0), stop=(c == 7)),
                          r=[wbf, ht], w=[p])
                if kind == "copy":
                    ph.op("dve", I_copy(f.t[:M, :NW], p.t[:M, :NW]), r=[p], w=[f])
                elif kind == "q":
                    ph.op("dve", I_ts(f.t[:M, :NW], p.t[:M, :NW], 0.125, None, op0=ALU.mult), r=[p], w=[f])
                elif kind == "silu":
                    ph.op("act", I_act(f.t[:M, :NW], p.t[:M, :NW], AF.Silu), r=[p], w=[f])
                else:
                    ph.op("act", I_act(f.t[:M, :NW], p.t[:M, :NW], AF.Sigmoid), r=[p], w=[f])
                ph.dma(I_dma(dst, f.t[:M, :NW]), r=[f])
    ph.finish()


def make_in_maps(inp, n_cores=8):
    f = lambda a: np.ascontiguousarray(np.asarray(a, dtype=np.float32))
    shared = {
        "w_mod": f(inp["w_mod"]), "b_mod": f(inp["b_mod"]), "g_norm": f(inp["g_norm"]), "w_in": f(inp["w_in"]),
        "a_re": f(inp["a_re"]).reshape(NL, 2048), "a_im": f(inp["a_im"]).reshape(NL, 2048),
        "log_dt": f(inp["log_dt"]),
        "b_re": f(inp["b_re"]).reshape(NL, 2048, 16), "b_im": f(inp["b_im"]).reshape(NL, 2048, 16),
        "c_reT": f(np.transpose(np.asarray(inp["c_re"]), (0, 1, 3, 2))).reshape(NL, 2048, 16),
        "c_imT": f(np.transpose(np.asarray(inp["c_im"]), (0, 1, 3, 2))).reshape(NL, 2048, 16),
        "d_skip": f(inp["d_skip"]).reshape(NL, 512),
        "w_glu": f(inp["w_glu"]), "b_glu": f(inp["b_glu"]), "w_ps": f(inp["w_ps"]), "w_pa": f(inp["w_pa"]),
        "w_o": f(inp["w_o"]), "g_final": f(inp["g_final"]).reshape(1, D),
    }
    xp = np.asarray(inp["x_prompt"]); xs = np.asarray(inp["x_sample"])
    ck = np.asarray(inp["cache_k"]); cv = np.asarray(inp["cache_v"]); cki = np.asarray(inp["cache_kidx"])
    sr = np.asarray(inp["state_ssm_re"]); si = np.asarray(inp["state_ssm_im"])
    cp = np.asarray(inp["c_prompt"]); cs = np.asarray(inp["c_sample"])
    PAST = ck.shape[2]
    maps = []
    for b in range(n_cores):
        m = dict(shared)
        m["xp"] = f(xp[b]); m["xs"] = f(xs[2 * b:2 * b + 2])
        m["ck"] = f(ck[:, 2 * b:2 * b + 2]).reshape(NL, 2, PAST, 512)
        m["cv"] = f(cv[:, 2 * b:2 * b + 2]).reshape(NL, 2, PAST, 512)
        m["cki"] = f(cki[:, 2 * b:2 * b + 2])
        m["ssr"] = f(sr[:, 2 * b:2 * b + 2]).reshape(NL, 2, 2048)
        m["ssi"] = f(si[:, 2 * b:2 * b + 2]).reshape(NL, 2, 2048)
        m["call"] = f(np.concatenate([cp[b:b + 1], cs[2 * b:2 * b + 2]], axis=0))
        maps.append(m)
    return maps


def assemble(results, T, TS, n_cores=8):
    R = results
    cat = lambda k: np.stack([np.asarray(r[k]) for r in R], axis=0)
    yp = cat("yp")
    ys = cat("ys").reshape(2 * n_cores, TS, D)
    kp = np.transpose(cat("kp"), (1, 0, 2, 3)).reshape(NL, n_cores, T, 8, 64)
    vp = np.transpose(cat("vp"), (1, 0, 2, 3)).reshape(NL, n_cores, T, 8, 64)
    kip = np.transpose(cat("kip"), (1, 0, 2, 3))
    srp = np.transpose(cat("srp"), (1, 0, 2)).reshape(NL, n_cores, 32, 64)
    sip = np.transpose(cat("sip"), (1, 0, 2)).reshape(NL, n_cores, 32, 64)
    ks = np.transpose(cat("ks"), (1, 0, 2, 3, 4)).reshape(NL, 2 * n_cores, TS, 8, 64)
    vs = np.transpose(cat("vs"), (1, 0, 2, 3, 4)).reshape(NL, 2 * n_cores, TS, 8, 64)
    kis = np.transpose(cat("kis"), (1, 0, 2, 3, 4)).reshape(NL, 2 * n_cores, TS, 64)
    srs = np.transpose(cat("srs"), (1, 0, 2, 3)).reshape(NL, 2 * n_cores, 32, 64)
    sis = np.transpose(cat("sis"), (1, 0, 2, 3)).reshape(NL, 2 * n_cores, 32, 64)
    outs = (yp, ys, kp, vp, kip, srp, sip, ks, vs, kis, srs, sis)
    return tuple(np.ascontiguousarray(o, dtype=np.float32) for o in outs)


_NC_CACHE = {}


def kernel(**inputs):
    T = int(np.asarray(inputs["x_prompt"]).shape[1])
    TS = int(np.asarray(inputs["x_sample"]).shape[1])
    PAST = int(np.asarray(inputs["cache_k"]).shape[2])
    n_cores = int(np.asarray(inputs["x_prompt"]).shape[0])
    key = (T, TS, PAST)
    nc = build(T=T, TS=TS, PAST=PAST)
    maps = make_in_maps(inputs, n_cores)
    res = run_bass_kernel_spmd(nc, maps, core_ids=list(range(n_cores)))
    return assemble(res.results, T, TS, n_cores)


def phase_cache(nc, A, l, seqs, PAST):
    ph = Phase(nc, f"C{l}")
    idf, idb = mk_ident(ph)
    ckt = [ph.sb([128, 512], F32, f"ckt{i}") for i in range(2)]
    cvt = [ph.sb([128, 512], F32, f"cvt{i}") for i in range(2)]
    cit = [ph.sb([128, 64], F32, f"cit{i}") for i in range(2)]
    ckb = [ph.sb([128, 512], BF16, f"ckb{i}") for i in range(2)]
    cib = [ph.sb([128, 64], BF16, f"cib{i}") for i in range(2)]
    kst = [ph.sb([128, 4, 128], BF16, f"kst{i}") for i in range(2)]
    kis = [ph.sb([64, 128], BF16, f"kis{i}") for i in range(2)]
    vst = [ph.sb([128, 8, 65], BF16, f"vst{i}") for i in range(2)]
    for v in vst:
        ph.op("pool", I_memset(v.t[:], 1.0), w=[v])
    pT = [ph.ps([128, 4, 128], BF16, f"pT{i}") for i in range(2)]
    pI = [ph.ps([64, 128], BF16, f"pI{i}") for i in range(2)]
    n = 0
    for s in seqs[1:]:
        j = s.i - 1
        for kt in range(PAST // 128):
            i = n % 2; n += 1
            r0 = kt * 128
            ph.dma(I_dma(ckt[i].t[:], A["ck"][l, j, r0:r0 + 128, :]), w=[ckt[i]])
            ph.dma(I_dma(cvt[i].t[:], A["cv"][l, j, r0:r0 + 128, :]), w=[cvt[i]])
            ph.dma(I_dma(cit[i].t[:], A["cki"][l, j, r0:r0 + 128, :]), w=[cit[i]])
            ph.op("act", I_act(ckb[i].t[:], ckt[i].t[:], AF.Copy), r=[ckt[i]], w=[ckb[i]])
            ph.op("act", I_act(cib[i].t[:], cit[i].t[:], AF.Copy), r=[cit[i]], w=[cib[i]])
            for hp in range(4):
                ph.op("pe", I_tr(pT[i].t[:, hp, :], ckb[i].t[:, hp * 128:(hp + 1) * 128], idb.t[:, :]), r=[ckb[i], idb], w=[pT[i]])
            ph.op("pe", I_tr(pI[i].t[:, :], cib[i].t[:, :], idb.t[:, :]), r=[cib[i], idb], w=[pI[i]])
            ph.op("dve", I_copy(kst[i].t[:], pT[i].t[:]), r=[pT[i]], w=[kst[i]])
            ph.op("dve", I_copy(kis[i].t[:], pI[i].t[:]), r=[pI[i]], w=[kis[i]])
            ph.op("pool", I_copy(vst[i].t[:, :, 0:64], cvt[i].t[:, :].rearrange("p (h d) -> p h d", d=64)), r=[cvt[i]], w=[vst[i]])
            ph.dma(I_dma(s.kT.rearrange("(hp p) s -> p hp s", p=128)[:, :, r0:r0 + 128], kst[i].t[:]), r=[kst[i]])
            ph.dma(I_dma(s.kiT[:, r0:r0 + 128], kis[i].t[:]), r=[kis[i]])
            ph.dma(I_dma(s.vaug[r0:r0 + 128, :], vst[i].t[:].rearrange("p h d -> p (h d)")), r=[vst[i]])
    ph.finish()


def phase_merge(nc, A, l, seqs, modbc):
    ph = Phase(nc, f"G{l}")
    last = (l == NL - 1)
    wps = ph.sb([128, 4, D], BF16, "wps"); wpa = ph.sb([128, 4, D], BF16, "wpa"); wo = ph.sb([128, 8, D], BF16, "wo")
    for c in range(4):
        ph.dma(I_dma(wps.t[:, c, :], A["w_ps"][l, c * 128:(c + 1) * 128, :]), w=[wps], q="pool")
        ph.dma(I_dma(wpa.t[:, c, :], A["w_pa"][l, c * 128:(c + 1) * 128, :]), w=[wpa], q="pool")
    for c in range(8):
        ph.dma(I_dma(wo.t[:, c, :], A["w_o"][l, c * 128:(c + 1) * 128, :]), w=[wo], q="pool")
    gate = ph.sb([128, D], F32, "gate")
    gfin = ph.sb([128, D], F32, "gfin")
    if last:
        ph.dma(I_dma(gfin.t[:], bcast_rows(A["g_final"][0:1, :], 128)), w=[gfin])
    yst = [ph.sb([128, 4, 512], BF16, f"yst{i}") for i in range(2)]
    yat = [ph.sb([128, 4, 512], BF16, f"yat{i}") for i in range(2)]
    gmt = [ph.sb([128, 16, 512], BF16, f"gmt{i}") for i in range(2)]
    mg = [ph.sb([128, 8, 512], BF16, f"mg{i}") for i in range(2)]
    tA = [ph.sb([128, 512], F32, f"tA{i}") for i in range(2)]
    tB = [ph.sb([128, 512], F32, f"tB{i}") for i in range(2)]
    xt = [ph.sb([128, D], F32, f"xt{i}") for i in range(2)]
    xn = [ph.sb([128, D], F32, f"xn{i}") for i in range(2)]
    yo = [ph.sb([128, D], F32, f"yo{i}") for i in range(2)]
    junk = ph.sb([128, D], BF16, "junk")
    ss = [ph.sb([128, 1], F32, f"ss{i}") for i in range(2)]
    rs = [ph.sb([128, 1], F32, f"rs{i}") for i in range(2)]
    pA = [ph.ps([128, 512], F32, f"pA{i}") for i in range(2)]
    pB = [ph.ps([128, 512], F32, f"pB{i}") for i in range(2)]
    pO = [ph.ps([128, 512], F32, f"pO{i}") for i in range(2)]
    nn = 0
    nt = 0
    for s in seqs:
        ph.dma(I_dma(gate.t[:], modbc[l, s.i, 2]), w=[gate])
        xsrc = s.xin if l == 0 else s.xres
        TP = min(128, s.T); NW = min(512, s.T)
        for st in range(s.T // NW):
            t0 = st * NW
            ys_ = yst[st % 2]; ya_ = yat[st % 2]; gm_ = gmt[st % 2]; m_ = mg[st % 2]
            ph.dma(I_dma(ys_.t[:, :, :NW], s.ysT.rearrange("(c p) t -> p c t", p=128)[:, :, t0:t0 + NW]), w=[ys_])
            ph.dma(I_dma(ya_.t[:, :, :NW], s.yaT.rearrange("(c p) t -> p c t", p=128)[:, :, t0:t0 + NW]), w=[ya_])
            ph.dma(I_dma(gm_.t[:, :, :NW], s.gmT.rearrange("(c p) t -> p c t", p=128)[:, :, t0:t0 + NW]), w=[gm_])
            for ct in range(8):
                a = pA[nn % 2]; b = pB[nn % 2]; ta = tA[nn % 2]; tb = tB[nn % 2]; nn += 1
                for c in range(4):
                    ph.op("pe", I_mm(a.t[:, :NW], wps.t[:, c, ct * 128:(ct + 1) * 128], ys_.t[:, c, :NW], start=(c == 0), stop=(c == 3)),
                          r=[wps, ys_], w=[a])
                for c in range(4):
                    ph.op("pe", I_mm(b.t[:, :NW], wpa.t[:, c, ct * 128:(ct + 1) * 128], ya_.t[:, c, :NW], start=(c == 0), stop=(c == 3)),
                          r=[wpa, ya_], w=[b])
                ph.op("dve", I_tt(ta.t[:, :NW], a.t[:, :NW], gm_.t[:, ct, :NW], ALU.mult), r=[a, gm_], w=[ta])
                ph.op("dve", I_tt(tb.t[:, :NW], b.t[:, :NW], gm_.t[:, 8 + ct, :NW], ALU.mult), r=[b, gm_], w=[tb])
                ph.op("pool", I_tt(m_.t[:, ct, :NW], ta.t[:, :NW], tb.t[:, :NW], ALU.add), r=[ta, tb], w=[m_])
            for j in range(NW // TP):
                r0 = t0 + j * TP
                x = xt[nt % 2]; xo = xn[nt % 2]; y_ = yo[nt % 2]; sq = ss[nt % 2]; r_ = rs[nt % 2]
                ph.dma(I_dma(x.t[:TP, :], xsrc[r0:r0 + TP, :]), w=[x])
                for half in range(2):
                    po = pO[half]
                    hs = slice(half * 512, (half + 1) * 512)
                    for c in range(8):
                        ph.op("pe", I_mm(po.t[:TP, :], m_.t[:, c, j * TP:(j + 1) * TP], wo.t[:, c, hs], start=(c == 0), stop=(c == 7)),
                              r=[m_, wo], w=[po])
                    ph.op("dve", I_tt(xo.t[:TP, hs], po.t[:TP, :], gate.t[:TP, hs], ALU.mult), r=[po, gate], w=[xo])
                ph.op("pool", I_tt(xo.t[:TP, :], xo.t[:TP, :], x.t[:TP, :], ALU.add), r=[xo, x], w=[xo])
                nt += 1
                if not last:
                    ph.dma(I_dma(s.xres[r0:r0 + TP, :], xo.t[:TP, :]), r=[xo])
                else:
                    ph.op("act", I_act(junk.t[:TP, :], xo.t[:TP, :], AF.Square, accum_out=sq.t[:TP, 0:1]), r=[xo], w=[junk, sq])
                    ph.op("dve", I_ts(sq.t[:TP, :], sq.t[:TP, :], 1.0 / D, EPS, op0=ALU.mult, op1=ALU.add), r=[sq], w=[sq])
                    ph.op("act", I_act(sq.t[:TP, :], sq.t[:TP, :], AF.Sqrt), r=[sq], w=[sq])
                    ph.op("dve", I_recip(r_.t[:TP, :], sq.t[:TP, :]), r=[sq], w=[r_])
                    ph.op("dve", I_stt(y_.t[:TP, :], xo.t[:TP, :], r_.t[:TP, 0:1], gfin.t[:TP, :], ALU.mult, ALU.mult),
                          r=[xo, r_, gfin], w=[y_])
                    ph.dma(I_dma(s.yout[r0:r0 + TP, :], y_.t[:TP, :]), r=[y_])
    ph.finish()


NIT = 22
W0 = 64.0


def phase_attn(nc, A, l, s):
    ph = Phase(nc, f"A{l}{s.nm}")
    idf, idb = mk_ident(ph)
    T, S = s.T, s.S
    QP = min(128, T); QB = min(512, T)
    L = T if s.causal else S
    KSEL = float(min(TOPK, L // 4))
    ktiles = [(k0, min(128, S - k0)) for k0 in range(0, S, 128)]
    NKT = len(ktiles)
    kiTa = ph.sb([128, S], BF16, "kiTa"); kiTb = ph.sb([128, S], BF16, "kiTb")
    ph.op("pool", I_memset(kiTa.t[64:128, :], 0.0), w=[kiTa])
    ph.op("pool", I_memset(kiTb.t[0:64, :], 0.0), w=[kiTb])
    ph.dma(I_dma(kiTa.t[0:64, :], s.kiT[:, :]), w=[kiTa])
    ph.dma(I_dma(kiTb.t[64:128, :], s.kiT[:, :]), w=[kiTb])
    kiTz = (kiTa, kiTb)
    sc = ph.sb([128, S], F32, "sc")
    mk = ph.sb([128, S], BF16, "mk")
    maskT = ph.sb([128, NKT, QB], BF16, "maskT")
    qit = [ph.sb([128, 4, 128], BF16, f"qit{i}") for i in range(2)]
    wit = [ph.sb([128, 8], F32, f"wit{i}") for i in range(2)]
    dg = [ph.sb([128, 8, 128], BF16, f"dg{i}") for i in range(2)]
    rl = [ph.sb([128, 512], BF16, f"rl{i}") for i in range(2)]
    mid = ph.sb([128, 1], F32, "mid"); cnt = ph.sb([128, 1], F32, "cnt"); tq = ph.sb([128, 1], F32, "tq")
    thr = ph.sb([128, 1], F32, "thr")
    qta = ph.sb([128, 4, 512], BF16, "qta"); qtb = ph.sb([128, 4, 512], BF16, "qtb")
    ph.op("pool", I_memset(qta.t[64:128, :, :], 0.0), w=[qta])
    ph.op("pool", I_memset(qtb.t[0:64, :, :], 0.0), w=[qtb])
    qtz = (qta, qtb)
    va = [ph.sb([128, 260], BF16, f"va{i}") for i in range(2)]
    ktl = [ph.sb([128, 2, 128], BF16, f"ktl{i}") for i in range(2)]
    pe_ = [ph.sb([128, 512], BF16, f"pe{i}") for i in range(2)]
    pm = [ph.sb([128, 512], BF16, f"pm{i}") for i in range(2)]
    oa = [ph.sb([128, 512], F32, f"oa{i}") for i in range(2)]
    for o__ in oa:
        ph.op("pool", I_memset(o__.t[:], 0.0), w=[o__])
    rden = [ph.sb([64, 512], F32, f"rden{i}") for i in range(2)]
    t1 = [ph.sb([64, 512], F32, f"t1{i}") for i in range(2)]
    zat = [ph.sb([64, 512], BF16, f"zat{i}") for i in range(2)]
    yat = [ph.sb([64, 512], BF16, f"yat{i}") for i in range(2)]
    iop = ph.sb([128, 64], F32, "iop")
    selden = ph.sb([128, 64], F32, "selden")
    ph.op("pool", I_iota(iop.t[:], [[0, 64]], 0, 1), w=[iop])
    ph.op("dve", I_ts(selden.t[:], iop.t[:], 64.0, None, op0=ALU.is_equal), r=[iop], w=[selden])
    lg = [ph.ps([128, 512], F32, f"lg{i}") for i in range(2)]
    acc = ph.ps([128, 512], F32, "acc")
    mtp = ph.ps([128, 8, 128], BF16, "mtp")
    oacc = [ph.ps([128, 512], F32, f"oacc{i}") for i in range(4)]

    nq = 0
    nr = 0
    nev = 0
    nv = 0
    nh = 0
    for jb in range(T // QB):
        qb0 = jb * QB
        Sblk = min(S, (jb + 1) * QB) if s.causal else S
        kts = [(i, k0, ksz) for i, (k0, ksz) in enumerate(ktiles) if k0 < Sblk]
        for jq in range(QB // QP):
            q0 = qb0 + jq * QP
            qoff = jq * QP
            Slim = (q0 + QP) if s.causal else S
            qi_ = qit[nq % 2]; wi_ = wit[nq % 2]; dg_ = dg[nq % 2]; nq += 1
            ph.dma(I_dma(qi_.t[:, :, :QP], s.qiT.rearrange("(hp p) t -> p hp t", p=128)[:, :, q0:q0 + QP]), w=[qi_])
            ph.dma(I_dma(wi_.t[:QP, :], s.wiS[q0:q0 + QP, :]), w=[wi_])
            for h in range(8):
                ph.op("dve", I_ts(dg_.t[:QP, h, :QP], idf.t[:QP, :QP], wi_.t[:QP, h:h + 1], None, op0=ALU.mult),
                      r=[idf, wi_], w=[dg_])
            for c0 in range(0, Slim, 512):
                csz = min(512, Slim - c0)
                for h in range(8):
                    base = 64 * (h % 2)
                    g = lg[nr % 2]; r_ = rl[nr % 2]; nr += 1
                    ph.op("pe", I_mm(g.t[:QP, :csz], qi_.t[:, h // 2, :QP], kiTz[h % 2].t[:, c0:c0 + csz]),
                          r=[qi_, kiTz[h % 2]], w=[g])
                    ph.op("act", I_act(r_.t[:QP, :csz], g.t[:QP, :csz], AF.Relu), r=[g], w=[r_])
                    ph.op("pe", I_mm(acc.t[:QP, :csz], dg_.t[:QP, h, :QP], r_.t[:QP, :csz], start=(h == 0), stop=(h == 7)),
                          r=[dg_, r_], w=[acc])
                ph.op("dve", I_copy(sc.t[:QP, c0:c0 + csz], acc.t[:QP, :csz]), r=[acc], w=[sc])
            if s.causal and QP == 128:
                ph.op("dve", I_memset(sc.t[0:64, Slim - 64:Slim], NEG), w=[sc])
            ph.op("dve", I_memset(mid.t[:QP, :], 0.0), w=[mid])
            w = W0
            for k in range(NIT):
                w = w / 2.0
                ph.op("dve", I_ts(mk.t[:QP, :Slim], sc.t[:QP, :Slim], mid.t[:QP, 0:1], None, op0=ALU.is_ge, op1=ALU.add,
                                  accum_out=cnt.t[:QP, 0:1]), r=[sc, mid], w=[mk, cnt])
                ph.op("dve", I_ts(tq.t[:QP, :], cnt.t[:QP, :], KSEL, 2.0 * w, op0=ALU.is_ge, op1=ALU.mult), r=[cnt], w=[tq])
                ph.op("dve", I_stt(mid.t[:QP, :], mid.t[:QP, :], -w, tq.t[:QP, :], ALU.add, ALU.add), r=[mid, tq], w=[mid])
            ph.op("dve", I_ts(thr.t[:QP, :], mid.t[:QP, :], -w, None, op0=ALU.add), r=[mid], w=[thr])
            ph.op("dve", I_ts(mk.t[:QP, :Slim], sc.t[:QP, :Slim], thr.t[:QP, 0:1], None, op0=ALU.is_ge), r=[sc, thr], w=[mk])
            mine = [(i, k0, ksz) for (i, k0, ksz) in kts if k0 < Slim]
            for g0 in range(0, len(mine), 8):
                grp = mine[g0:g0 + 8]
                for gi, (i, k0, ksz) in enumerate(grp):
                    ph.op("pe", I_tr(mtp.t[:ksz, gi, :QP], mk.t[:QP, k0:k0 + ksz], idb.t[:QP, :QP]), r=[mk, idb], w=[mtp])
                i0 = grp[0][0]
                ng = len(grp)
                eng = "act" if nev % 2 == 0 else "dve"; nev += 1
                if eng == "act":
                    ph.op("act", I_act(maskT.t[:, i0:i0 + ng, qoff:qoff + QP], mtp.t[:, 0:ng, :QP], AF.Copy), r=[mtp], w=[maskT])
                else:
                    ph.op("dve", I_copy(maskT.t[:, i0:i0 + ng, qoff:qoff + QP], mtp.t[:, 0:ng, :QP]), r=[mtp], w=[maskT])
            rest = [(i, k0, ksz) for (i, k0, ksz) in kts if k0 >= Slim]
            if rest:
                i0 = rest[0][0]; i1 = rest[-1][0] + 1
                ph.op("pool", I_memset(maskT.t[:, i0:i1, qoff:qoff + QP], 0.0), w=[maskT])
        import os as _os
        if _os.environ.get("ATTN_STOP") == "idx":
            continue
        qsrc = s.qT.rearrange("(hp p) t -> p hp t", p=128)
        ph.dma(I_dma(qta.t[0:64, :, :QB], qsrc[0:64, :, qb0:qb0 + QB]), w=[qta])
        ph.dma(I_dma(qtb.t[64:128, :, :QB], qsrc[64:128, :, qb0:qb0 + QB]), w=[qtb])
        for hg in range(2):
            for idx, (i, k0, ksz) in enumerate(kts):
                va_ = va[nv % 2]; kt_ = ktl[nv % 2]; nv += 1
                ph.dma(I_dma(va_.t[:ksz, :], s.vaug[k0:k0 + ksz, hg * 260:(hg + 1) * 260]), w=[va_])
                ph.dma(I_dma(kt_.t[:, :, :ksz], s.kT.rearrange("(hp p) s -> p hp s", p=128)[:, 2 * hg:2 * hg + 2, k0:k0 + ksz]), w=[kt_])
                for hh in range(4):
                    h = hg * 4 + hh
                    base = 64 * (h % 2)
                    st_ = lg[nh % 2]; p_ = pe_[nh % 2]; m_ = pm[nh % 2]; nh += 1
                    if _os.environ.get("ATTN_STOP") == "dma":
                        continue
                    ph.op("pe", I_mm(st_.t[:ksz, :QB], kt_.t[:, hh // 2, :ksz], qtz[h % 2].t[:, h // 2, :QB]),
                          r=[kt_, qtz[h % 2]], w=[st_])
                    if _os.environ.get("ATTN_STOP") == "mm":
                        continue
                    ph.op("act", I_act(p_.t[:ksz, :QB], st_.t[:ksz, :QB], AF.Exp), r=[st_], w=[p_])
                    if _os.environ.get("ATTN_STOP") == "exp":
                        continue
                    ph.op("dve", I_tt(m_.t[:ksz, :QB], p_.t[:ksz, :QB], maskT.t[:ksz, i, :QB], ALU.mult), r=[p_, maskT], w=[m_])
                    if _os.environ.get("ATTN_STOP") == "qk":
                        continue
                    ph.op("pe", I_mm(oacc[hh].t[:65, :QB], va_.t[:ksz, hh * 65:(hh + 1) * 65], m_.t[:ksz, :QB], start=(idx == 0), stop=(idx == len(kts) - 1)),
                          r=[va_, m_], w=[oacc[hh]])
            if _os.environ.get("ATTN_STOP") in ("pv", "mm", "exp", "qk", "dma"):
                continue
            for hh in range(4):
                h = hg * 4 + hh
                o_ = oa[hh % 2]; rd = rden[hh % 2]; t_ = t1[hh % 2]; z_ = zat[hh % 2]; y_ = yat[hh % 2]
                ph.op("act", I_act(o_.t[:65, :QB], oacc[hh].t[:65, :QB], AF.Copy), r=[oacc[hh]], w=[o_])
                ph.op("pe", I_mm(acc.t[:64, :QB], selden.t[:, :64], o_.t[:, :QB]), r=[selden, o_], w=[acc])
                ph.op("dve", I_recip(rd.t[:64, :QB], acc.t[:64, :QB]), r=[acc], w=[rd])
                ph.dma(I_dma(z_.t[:64, :QB], s.zaT[64 * h:64 * h + 64, qb0:qb0 + QB]), w=[z_])
                ph.op("dve", I_tt(t_.t[:64, :QB], o_.t[0:64, :QB], rd.t[:64, :QB], ALU.mult), r=[o_, rd], w=[t_])
                ph.op("pool", I_tt(y_.t[:64, :QB], t_.t[:64, :QB], z_.t[:64, :QB], ALU.mult), r=[t_, z_], w=[y_])
                ph.dma(I_dma(s.yaT[64 * h:64 * h + 64, qb0:qb0 + QB], y_.t[:64, :QB]), r=[y_])
    ph.finish()


PI = float(np.pi)


def phase_ssm(nc, A, l, seqs):
    ph = Phase(nc, f"S{l}")
    idf, idb = mk_ident(ph)
    LB = min(512, seqs[0].T)
    sm = lambda nm: ph.sb([128, 16], F32, nm)
    are, aim, ldt, dt, xr, ang, mag = sm("are"), sm("aim"), sm("ldt"), sm("dt"), sm("xr"), sm("ang"), sm("mag")
    kac, angr, angc, angcr, sn1, cs1, abr, abi = sm("kac"), sm("angr"), sm("angc"), sm("angcr"), sm("sn1"), sm("cs1"), sm("abr"), sm("abi")
    t1, t2, den, rdn, nr_, fre, fim, t3, t4 = sm("t1"), sm("t2"), sm("den"), sm("rdn"), sm("nr"), sm("fre"), sm("fim"), sm("t3"), sm("t4")
    wr, wi_, wt = sm("wr"), sm("wi"), sm("wt")
    ph.dma(I_dma(are.t[:], A["a_re"][l].rearrange("(pt q) -> q pt", q=128)), w=[are])
    ph.dma(I_dma(aim.t[:], A["a_im"][l].rearrange("(pt q) -> q pt", q=128)), w=[aim])
    for gl in range(2):
        src = bass.AP(A["log_dt"].tensor, l * 32 + gl, [[0, 64], [2, 16]])
        ph.dma(I_dma(ldt.t[gl * 64:(gl + 1) * 64, :], src), w=[ldt])
    V = "dve"
    pa, pb2, pc = sm("pa"), sm("pb2"), sm("pc")

    def horner(dst, y, coefs):
        ph.op(V, I_memset(dst.t[:], 1.0), w=[dst])
        for c in reversed(coefs):
            ph.op(V, I_tt(dst.t[:], dst.t[:], y.t[:], ALU.mult), r=[dst, y], w=[dst])
            ph.op(V, I_ts(dst.t[:], dst.t[:], float(c), 1.0, op0=ALU.mult, op1=ALU.add), r=[dst], w=[dst])

    def exp_acc(dst, src, nsq, deg):
        ph.op(V, I_ts(pa.t[:], src.t[:], 1.0 / (2 ** nsq), None, op0=ALU.mult), r=[src], w=[pa])
        horner(dst, pa, [1.0 / k for k in range(1, deg + 1)])
        for _ in range(nsq):
            ph.op(V, I_tt(dst.t[:], dst.t[:], dst.t[:], ALU.mult), r=[dst], w=[dst])

    def sincos_acc(sdst, cdst, x):
        ph.op(V, I_ts(pa.t[:], x.t[:], 0.125, None, op0=ALU.mult), r=[x], w=[pa])
        ph.op(V, I_tt(pb2.t[:], pa.t[:], pa.t[:], ALU.mult), r=[pa], w=[pb2])
        horner(sdst, pb2, [-1.0 / 6, -1.0 / 20, -1.0 / 42, -1.0 / 72, -1.0 / 110])
        ph.op(V, I_tt(sdst.t[:], sdst.t[:], pa.t[:], ALU.mult), r=[sdst, pa], w=[sdst])
        horner(cdst, pb2, [-1.0 / 2, -1.0 / 12, -1.0 / 30, -1.0 / 56, -1.0 / 90, -1.0 / 132])
        for _ in range(3):
            ph.op(V, I_tt(pc.t[:], sdst.t[:], sdst.t[:], ALU.mult), r=[sdst], w=[pc])
            ph.op(V, I_stt(sdst.t[:], sdst.t[:], 2.0, cdst.t[:], ALU.mult, ALU.mult), r=[sdst, cdst], w=[sdst])
            ph.op(V, I_ts(cdst.t[:], pc.t[:], -2.0, 1.0, op0=ALU.mult, op1=ALU.add), r=[pc], w=[cdst])

    exp_acc(dt, ldt, 3, 14)
    ph.op(V, I_tt(xr.t[:], are.t[:], dt.t[:], ALU.mult), r=[are, dt], w=[xr])
    ph.op(V, I_tt(ang.t[:], aim.t[:], dt.t[:], ALU.mult), r=[aim, dt], w=[ang])
    exp_acc(mag, xr, 0, 8)

    def reduce_angle(src, dst):
        ph.op(V, I_ts(kac.t[:], src.t[:], PI, None, op0=ALU.is_gt), r=[src], w=[kac])
        for j in range(1, 6):
            ph.op(V, I_stt(kac.t[:], src.t[:], (2 * j + 1) * PI, kac.t[:], ALU.is_gt, ALU.add), r=[src, kac], w=[kac])
        ph.op(V, I_stt(dst.t[:], kac.t[:], -2.0 * PI, src.t[:], ALU.mult, ALU.add), r=[kac, src], w=[dst])
    reduce_angle(ang, angr)
    sincos_acc(sn1, cs1, angr)
    ph.op(V, I_tt(t1.t[:], sn1.t[:], sn1.t[:], ALU.mult), r=[sn1], w=[t1])
    ph.op(V, I_tt(t2.t[:], cs1.t[:], cs1.t[:], ALU.mult), r=[cs1], w=[t2])
    ph.op(V, I_tt(t3.t[:], t1.t[:], t2.t[:], ALU.add), r=[t1, t2], w=[t3])
    ph.op(V, I_ts(t3.t[:], t3.t[:], -0.5, 1.5, op0=ALU.mult, op1=ALU.add), r=[t3], w=[t3])
    ph.op(V, I_tt(sn1.t[:], sn1.t[:], t3.t[:], ALU.mult), r=[sn1, t3], w=[sn1])
    ph.op(V, I_tt(cs1.t[:], cs1.t[:], t3.t[:], ALU.mult), r=[cs1, t3], w=[cs1])
    ph.op(V, I_tt(abr.t[:], mag.t[:], cs1.t[:], ALU.mult), r=[mag, cs1], w=[abr])
    ph.op(V, I_tt(abi.t[:], mag.t[:], sn1.t[:], ALU.mult), r=[mag, sn1], w=[abi])
    ph.op(V, I_tt(t1.t[:], are.t[:], are.t[:], ALU.mult), r=[are], w=[t1])
    ph.op(V, I_tt(t2.t[:], aim.t[:], aim.t[:], ALU.mult), r=[aim], w=[t2])
    ph.op(V, I_tt(den.t[:], t1.t[:], t2.t[:], ALU.add), r=[t1, t2], w=[den])
    ph.op(V, I_recip(rdn.t[:], den.t[:]), r=[den], w=[rdn])
    ph.op(V, I_ts(nr_.t[:], abr.t[:], -1.0, None, op0=ALU.add), r=[abr], w=[nr_])
    ph.op(V, I_tt(t1.t[:], nr_.t[:], are.t[:], ALU.mult), r=[nr_, are], w=[t1])
    ph.op(V, I_tt(t2.t[:], abi.t[:], aim.t[:], ALU.mult), r=[abi, aim], w=[t2])
    ph.op(V, I_tt(t3.t[:], t1.t[:], t2.t[:], ALU.add), r=[t1, t2], w=[t3])
    ph.op(V, I_tt(fre.t[:], t3.t[:], rdn.t[:], ALU.mult), r=[t3, rdn], w=[fre])
    ph.op(V, I_tt(t1.t[:], abi.t[:], are.t[:], ALU.mult), r=[abi, are], w=[t1])
    ph.op(V, I_tt(t2.t[:], nr_.t[:], aim.t[:], ALU.mult), r=[nr_, aim], w=[t2])
    ph.op(V, I_tt(t4.t[:], t1.t[:], t2.t[:], ALU.subtract), r=[t1, t2], w=[t4])
    ph.op(V, I_tt(fim.t[:], t4.t[:], rdn.t[:], ALU.mult), r=[t4, rdn], w=[fim])

    bre_t = ph.sb([128, 16, 16], F32, "bre_t"); bim_t = ph.sb([128, 16, 16], F32, "bim_t")
    cre_t = ph.sb([128, 16, 16], F32, "cre_t"); cim_t = ph.sb([128, 16, 16], F32, "cim_t")
    for t_, nm in ((bre_t, "b_re"), (bim_t, "b_im"), (cre_t, "c_reT"), (cim_t, "c_imT")):
        ph.dma(I_dma(t_.t[:], A[nm][l].rearrange("(pt q) n -> q pt n", q=128)), w=[t_])
    padall = ph.sb([128, 16, 2, 128], F32, "padall")
    Cpad = ph.sb([128, 16, 2, 128], BF16, "Cpad")
    ph.op("pool", I_memset(padall.t[:], 0.0), w=[padall])
    ph.op("pool", I_memset(Cpad.t[:], 0.0), w=[Cpad])
    tb16 = [ph.sb([128, 16], F32, f"tb16{i}") for i in range(2)]
    for pt in range(16):
        qq = pt % 4
        for half in range(2):
            rows = slice(half * 64, (half + 1) * 64)
            cols = slice(32 * qq + 16 * half, 32 * qq + 16 * half + 16)
            ta, tb = tb16
            ph.op(V, I_ts(ta.t[rows, :], bim_t.t[rows, pt, :], fim.t[rows, pt:pt + 1], None, op0=ALU.mult), r=[bim_t, fim], w=[ta])
            ph.op(V, I_stt(padall.t[rows, pt, 0, cols], bre_t.t[rows, pt, :], fre.t[rows, pt:pt + 1], ta.t[rows, :], ALU.mult, ALU.subtract),
                  r=[bre_t, fre, ta], w=[padall])
            ph.op(V, I_ts(tb.t[rows, :], bre_t.t[rows, pt, :], fim.t[rows, pt:pt + 1], None, op0=ALU.mult), r=[bre_t, fim], w=[tb])
            ph.op(V, I_stt(padall.t[rows, pt, 1, cols], bim_t.t[rows, pt, :], fre.t[rows, pt:pt + 1], tb.t[rows, :], ALU.mult, ALU.add),
                  r=[bim_t, fre, tb], w=[padall])
            ph.op("pool", I_copy(Cpad.t[rows, pt, 0, cols], cre_t.t[rows, pt, :]), r=[cre_t], w=[Cpad])
            ph.op("pool", I_ts(Cpad.t[rows, pt, 1, cols], cim_t.t[rows, pt, :], -1.0, None, op0=ALU.mult), r=[cim_t], w=[Cpad])
    pB = [ph.ps([128, 512], F32, f"pB{i}") for i in range(4)]
    pY = [ph.ps([128, 512], F32, f"pY{i}") for i in range(2)]
    pG = ph.ps([128, 512], F32, "pG")
    E = [[ph.sb([128, 128], BF16, f"E{part}_{pt}") for pt in range(16)] for part in range(2)]
    for part in range(2):
        for pt in range(16):
            ph.op("pe", I_mm(pG.t[:, 0:128], padall.t[:, pt, part, :], idf.t[:, :]), r=[padall, idf], w=[pG])
            ph.op("act", I_act(E[part][pt].t[:], pG.t[:, 0:128], AF.Copy), r=[pG], w=[E[part][pt]])
    dsk = ph.sb([128, 4], F32, "dsk")
    ph.dma(I_dma(dsk.t[:], A["d_skip"][l].rearrange("(ct q) -> q ct", q=128)), w=[dsk])
    bglu = ph.sb([128, 4], F32, "bglu")
    ph.dma(I_dma(bglu.t[:], A["b_glu"][l].rearrange("(co q) -> q co", q=128)), w=[bglu])
    wglu = ph.sb([128, 4, 512], BF16, "wglu")
    for c in range(4):
        ph.dma(I_dma(wglu.t[:, c, :], A["w_glu"][l, c * 128:(c + 1) * 128, :]), w=[wglu], q="pool")
    cs = ph.sb([128, 16, LB], F32, "cs"); sn = ph.sb([128, 16, LB], F32, "sn")
    ph.op("pool", I_memset(cs.t[:, :, 0:1], 1.0), w=[cs])
    ph.op("pool", I_memset(sn.t[:, :, 0:1], 0.0), w=[sn])
    ph.op(V, I_copy(wr.t[:], cs1.t[:]), r=[cs1], w=[wr])
    ph.op(V, I_copy(wi_.t[:], sn1.t[:]), r=[sn1], w=[wi_])
    tmpc = ph.sb([128, LB], F32, "tmpc"); tmps = ph.sb([128, LB], F32, "tmps")
    n = 1
    while n < LB:
        for pt in range(16):
            ph.op(V, I_ts(tmpc.t[:, 0:n], sn.t[:, pt, 0:n], wi_.t[:, pt:pt + 1], None, op0=ALU.mult), r=[sn, wi_], w=[tmpc])
            ph.op("pool", I_ts(tmps.t[:, 0:n], cs.t[:, pt, 0:n], wi_.t[:, pt:pt + 1], None, op0=ALU.mult), r=[cs, wi_], w=[tmps])
            ph.op(V, I_stt(cs.t[:, pt, n:2 * n], cs.t[:, pt, 0:n], wr.t[:, pt:pt + 1], tmpc.t[:, 0:n], ALU.mult, ALU.subtract),
                  r=[cs, wr, tmpc], w=[cs])
            ph.op(V, I_stt(sn.t[:, pt, n:2 * n], sn.t[:, pt, 0:n], wr.t[:, pt:pt + 1], tmps.t[:, 0:n], ALU.mult, ALU.add),
                  r=[sn, wr, tmps], w=[sn])
        ph.op(V, I_tt(t1.t[:], wi_.t[:], wi_.t[:], ALU.mult), r=[wi_], w=[t1])
        ph.op(V, I_tt(t2.t[:], wr.t[:], wr.t[:], ALU.mult), r=[wr], w=[t2])
        ph.op(V, I_stt(wt.t[:], wr.t[:], 2.0, wi_.t[:], ALU.mult, ALU.mult), r=[wr, wi_], w=[wt])
        ph.op(V, I_tt(wr.t[:], t2.t[:], t1.t[:], ALU.subtract), r=[t1, t2], w=[wr])
        ph.op(V, I_copy(wi_.t[:], wt.t[:]), r=[wt], w=[wi_])
        n *= 2

    wk = lambda nm: ph.sb([128, LB], F32, nm)
    m1, m2, m3, m4, bre_, bim_, gre, gim, hre, him = [wk(n_) for n_ in ("m1", "m2", "m3", "m4", "bre", "bim", "gre", "gim", "hre", "him")]
    hreb = ph.sb([128, LB], BF16, "hreb"); himb = ph.sb([128, LB], BF16, "himb")
    yv, x2, inn, in2, sg = [wk(n_) for n_ in ("yv", "x2", "inn", "in2", "sg")]
    yg = [wk(f"yg{i}") for i in range(4)]
    ygb = [ph.sb([128, LB], BF16, f"ygb{i}") for i in range(4)]
    ut = [[ph.sb([128, LB], BF16, f"ut{k}{i}") for i in range(4)] for k in range(2)]
    zst = [ph.sb([128, LB], BF16, f"zst{i}") for i in range(2)]
    sgl = wk("sgl"); tg = wk("tg")
    ysf = [ph.sb([128, LB], BF16, f"ysf{i}") for i in range(2)]
    hpr = [sm(f"hpr{i}") for i in range(3)]; hpi = [sm(f"hpi{i}") for i in range(3)]
    a0 = ph.sb([128, 4], F32, "a0")
    nb = 0
    nz = 0
    for s in seqs:
        Lb = min(LB, s.T)
        hr_, hi_ = hpr[s.i], hpi[s.i]
        prev = s.i > 0
        if prev:
            ph.dma(I_dma(hr_.t[:], A["ssr"][l, s.i - 1].rearrange("(pt q) -> q pt", q=128)), w=[hr_])
            ph.dma(I_dma(hi_.t[:], A["ssi"][l, s.i - 1].rearrange("(pt q) -> q pt", q=128)), w=[hi_])
        for blk in range(s.T // Lb):
            t0 = blk * Lb
            u_ = ut[blk % 2]
            for ct in range(4):
                ph.dma(I_dma(u_[ct].t[:, :Lb], s.uT[ct * 128:(ct + 1) * 128, t0:t0 + Lb]), w=[u_[ct]])
            for ct in range(4):
                py = pY[ct % 2]
                for qq in range(4):
                    pt = ct * 4 + qq
                    pr = pB[(nb % 2) * 2]; pi_ = pB[(nb % 2) * 2 + 1]; nb += 1
                    ph.op("pe", I_mm(pr.t[:, :Lb], E[0][pt].t[:, :], u_[ct].t[:, :Lb]), r=[E[0][pt], u_[ct]], w=[pr])
                    ph.op("pe", I_mm(pi_.t[:, :Lb], E[1][pt].t[:, :], u_[ct].t[:, :Lb]), r=[E[1][pt], u_[ct]], w=[pi_])
                    c_ = cs.t[:, pt, :Lb]; s_ = sn.t[:, pt, :Lb]
                    ph.op(V, I_tt(m1.t[:, :Lb], pr.t[:, :Lb], c_, ALU.mult), r=[pr, cs], w=[m1])
                    ph.op(V, I_tt(m2.t[:, :Lb], pi_.t[:, :Lb], s_, ALU.mult), r=[pi_, sn], w=[m2])
                    ph.op(V, I_tt(m3.t[:, :Lb], pi_.t[:, :Lb], c_, ALU.mult), r=[pi_, cs], w=[m3])
                    ph.op(V, I_tt(m4.t[:, :Lb], pr.t[:, :Lb], s_, ALU.mult), r=[pr, sn], w=[m4])
                    ph.op("pool", I_tt(bre_.t[:, :Lb], m1.t[:, :Lb], m2.t[:, :Lb], ALU.add), r=[m1, m2], w=[bre_])
                    ph.op("pool", I_tt(bim_.t[:, :Lb], m3.t[:, :Lb], m4.t[:, :Lb], ALU.subtract), r=[m3, m4], w=[bim_])
                    if prev:
                        ph.op(V, I_tt(a0.t[:, 0:1], abi.t[:, pt:pt + 1], hi_.t[:, pt:pt + 1], ALU.mult), r=[abi, hi_], w=[a0])
                        ph.op(V, I_stt(a0.t[:, 1:2], abr.t[:, pt:pt + 1], hr_.t[:, pt:pt + 1], a0.t[:, 0:1], ALU.mult, ALU.subtract),
                              r=[abr, hr_, a0], w=[a0])
                        ph.op(V, I_tt(a0.t[:, 2:3], abi.t[:, pt:pt + 1], hr_.t[:, pt:pt + 1], ALU.mult), r=[abi, hr_], w=[a0])
                        ph.op(V, I_stt(a0.t[:, 3:4], abr.t[:, pt:pt + 1], hi_.t[:, pt:pt + 1], a0.t[:, 2:3], ALU.mult, ALU.add),
                              r=[abr, hi_, a0], w=[a0])
                        ph.op(V, I_tt(bre_.t[:, 0:1], bre_.t[:, 0:1], a0.t[:, 1:2], ALU.add), r=[bre_, a0], w=[bre_])
                        ph.op(V, I_tt(bim_.t[:, 0:1], bim_.t[:, 0:1], a0.t[:, 3:4], ALU.add), r=[bim_, a0], w=[bim_])
                    rb = mag.t[:, pt:pt + 1].to_broadcast([128, Lb])
                    ph.op(V, I_scan(gre.t[:, :Lb], rb, bre_.t[:, :Lb], 0.0), r=[mag, bre_], w=[gre])
                    ph.op(V, I_scan(gim.t[:, :Lb], rb, bim_.t[:, :Lb], 0.0), r=[mag, bim_], w=[gim])
                    ph.op(V, I_tt(m1.t[:, :Lb], gre.t[:, :Lb], c_, ALU.mult), r=[gre, cs], w=[m1])
                    ph.op(V, I_tt(m2.t[:, :Lb], gim.t[:, :Lb], s_, ALU.mult), r=[gim, sn], w=[m2])
                    ph.op("pool", I_tt(m3.t[:, :Lb], gre.t[:, :Lb], s_, ALU.mult), r=[gre, sn], w=[m3])
                    ph.op("pool", I_tt(m4.t[:, :Lb], gim.t[:, :Lb], c_, ALU.mult), r=[gim, cs], w=[m4])
                    ph.op(V, I_tt(hre.t[:, :Lb], m1.t[:, :Lb], m2.t[:, :Lb], ALU.subtract), r=[m1, m2], w=[hre])
                    ph.op("pool", I_tt(him.t[:, :Lb], m3.t[:, :Lb], m4.t[:, :Lb], ALU.add), r=[m3, m4], w=[him])
                    ph.op("act", I_act(hr_.t[:, pt:pt + 1], hre.t[:, Lb - 1:Lb], AF.Copy), r=[hre], w=[hr_])
                    ph.op("act", I_act(hi_.t[:, pt:pt + 1], him.t[:, Lb - 1:Lb], AF.Copy), r=[him], w=[hi_])
                    ph.op("act", I_act(hreb.t[:, :Lb], hre.t[:, :Lb], AF.Copy), r=[hre], w=[hreb])
                    ph.op("act", I_act(himb.t[:, :Lb], him.t[:, :Lb], AF.Copy), r=[him], w=[himb])
                    ph.op("pe", I_mm(py.t[:, :Lb], Cpad.t[:, pt, 0, :], hreb.t[:, :Lb], start=(qq == 0), stop=False), r=[Cpad, hreb], w=[py])
                    ph.op("pe", I_mm(py.t[:, :Lb], Cpad.t[:, pt, 1, :], himb.t[:, :Lb], start=False, stop=(qq == 3)), r=[Cpad, himb], w=[py])
                ph.op(V, I_stt(yv.t[:, :Lb], u_[ct].t[:, :Lb], dsk.t[:, ct:ct + 1], py.t[:, :Lb], ALU.mult, ALU.add), r=[u_[ct], dsk, py], w=[yv])
                ph.op("pool", I_tt(x2.t[:, :Lb], yv.t[:, :Lb], yv.t[:, :Lb], ALU.mult), r=[yv], w=[x2])
                ph.op("pool", I_ts(inn.t[:, :Lb], x2.t[:, :Lb], 0.044715, 1.0, op0=ALU.mult, op1=ALU.add), r=[x2], w=[inn])
                ph.op("pool", I_tt(in2.t[:, :Lb], inn.t[:, :Lb], yv.t[:, :Lb], ALU.mult), r=[inn, yv], w=[in2])
                ph.op("act", I_act(sg.t[:, :Lb], in2.t[:, :Lb], AF.Sigmoid, scale=1.5957691216057308), r=[in2], w=[sg])
                ph.op(V, I_tt(yg[ct].t[:, :Lb], yv.t[:, :Lb], sg.t[:, :Lb], ALU.mult), r=[yv, sg], w=[yg[ct]])
                ph.op("act", I_act(ygb[ct].t[:, :Lb], yg[ct].t[:, :Lb], AF.Copy), r=[yg[ct]], w=[ygb[ct]])
            prev = True
            for co in range(4):
                z_ = zst[nz % 2]; yf = ysf[nz % 2]; nz += 1
                for ct in range(4):
                    ph.op("pe", I_mm(pG.t[:, :Lb], wglu.t[:, ct, co * 128:(co + 1) * 128], ygb[ct].t[:, :Lb], start=(ct == 0), stop=(ct == 3)),
                          r=[wglu, ygb[ct]], w=[pG])
                ph.op("act", I_act(sgl.t[:, :Lb], pG.t[:, :Lb], AF.Sigmoid, bias=bglu.t[:, co:co + 1]), r=[pG, bglu], w=[sgl])
                ph.dma(I_dma(z_.t[:, :Lb], s.zsT[co * 128:(co + 1) * 128, t0:t0 + Lb]), w=[z_])
                ph.op(V, I_tt(tg.t[:, :Lb], yg[co].t[:, :Lb], sgl.t[:, :Lb], ALU.mult), r=[yg[co], sgl], w=[tg])
                ph.op("pool", I_tt(yf.t[:, :Lb], tg.t[:, :Lb], z_.t[:, :Lb], ALU.mult), r=[tg, z_], w=[yf])
                ph.dma(I_dma(s.ysT[co * 128:(co + 1) * 128, t0:t0 + Lb], yf.t[:, :Lb]), r=[yf])
        ph.dma(I_dma(s.srout[l].rearrange("(pt q) -> q pt", q=128), hr_.t[:]), r=[hr_])
        ph.dma(I_dma(s.siout[l].rearrange("(pt q) -> q pt", q=128), hi_.t[:]), r=[hi_])
    ph.finish()
```

```python
import numpy as np
from contextlib import ExitStack
import concourse.bass as bass
import concourse.mybir as mybir
from concourse.bass_utils import run_bass_kernel_spmd

F32 = mybir.dt.float32
BF16 = mybir.dt.bfloat16
I32 = mybir.dt.int32
AF = mybir.ActivationFunctionType
ALU = mybir.AluOpType

D = 1024
DIN = 5704
NL = 2
EPS = 1e-6
OFF_U, OFF_ZS, OFF_Q, OFF_K, OFF_V, OFF_ZA, OFF_QI, OFF_KI, OFF_WI, OFF_GM = (
    0, 512, 1024, 1536, 2048, 2560, 3072, 3584, 3648, 3656)
NEG = -1.0e30
TOPK = 256


class TT:
    __slots__ = ("t", "name", "w", "r", "dsem", "dtot")

    def __init__(self, t, name):
        self.t = t
        self.name = name
        self.w = None
        self.r = []
        self.dsem = None
        self.dtot = 0

    def __getitem__(self, idx):
        return self.t[idx]


_SEMPOOL = {}
_SEMSTACK = []


class Phase:
    ENGS = ("sync", "pool", "act", "dve", "pe")

    def __init__(self, nc, name):
        self.nc = nc
        self.name = name
        self.es = ExitStack()
        self.ops = {e: [] for e in self.ENGS}
        self.cnt = {e: 0 for e in self.ENGS}
        self.known = {e: {} for e in self.ENGS}
        self.sems = {}
        self.semobj = {}
        self.nsem = 0
        for e in ("pool", "act", "dve", "pe"):
            s = self._newsem(f"{name}_{e}")
            self.sems[e] = s
            self.semobj[("c", e)] = s
        self.ntile = 0
        self.dma_tiles = []
        self.misc = None

    def _newsem(self, nm):
        pool = _SEMPOOL.get(id(self.nc))
        if pool is None:
            return self.es.enter_context(self.nc.semaphore(nm))
        sm_ = pool[self.nsem]
        self.nsem += 1
        return sm_

    def sb(self, shape, dtype, name=None):
        self.ntile += 1
        nm = f"{self.name}_{name or 't'}{self.ntile}"
        t = self.es.enter_context(self.nc.sbuf_tensor(nm, list(shape), dtype))
        return TT(t, nm)

    def ps(self, shape, dtype, name=None):
        self.ntile += 1
        nm = f"{self.name}_{name or 'p'}{self.ntile}"
        t = self.es.enter_context(self.nc.psum_tensor(nm, list(shape), dtype))
        return TT(t, nm)

    def _dsem(self, tt):
        if tt.dsem is None:
            tt.dsem = self._newsem(f"d_{tt.name}")
            self.semobj[("d", tt.name)] = tt.dsem
            self.dma_tiles.append(tt)
        return tt.dsem

    def _need(self, eng, dep, waits):
        if dep is None:
            return
        key, val = dep
        if key == ("c", "pe") and eng == "pe":
            return
        if key == ("c", eng) and eng in ("sync",):
            return
        k = self.known[eng]
        if k.get(key, 0) >= val:
            return
        k[key] = val
        waits.append((self.semobj[key], val))

    def _deps(self, eng, r, w):
        waits = []
        for t in r:
            self._need(eng, t.w, waits)
        for t in w:
            self._need(eng, t.w, waits)
            for d in t.r:
                self._need(eng, d, waits)
        return waits

    def op(self, eng, fn, r=(), w=()):
        waits = self._deps(eng, r, w)
        self.cnt[eng] += 1
        me = (("c", eng), self.cnt[eng])
        for t in r:
            t.r.append(me)
        for t in w:
            t.w = me
            t.r = []
        self.ops[eng].append((waits, fn, ("c", self.sems[eng])))

    def dma(self, fn, r=(), w=(), q="sync"):
        waits = self._deps(q, r, w)
        tiles = list(w) + list(r)
        if tiles:
            tt = tiles[0]
            sem = self._dsem(tt)
            tt.dtot += 16
            me = (("d", tt.name), tt.dtot)
        else:
            if self.misc is None:
                self.misc = TT(None, f"{self.name}_misc")
            tt = self.misc
            sem = self._dsem(tt)
            tt.dtot += 16
            me = (("d", tt.name), tt.dtot)
        for t in r:
            t.r.append(me)
        for t in w:
            t.w = me
            t.r = []
        self.ops[q].append((waits, fn, ("d", sem)))

    def finish(self):
        nc = self.nc
        finals = [(t.dsem, t.dtot) for t in self.dma_tiles if t.dtot > 0]

        def replay(e, name):
            for waits, fn, inc in self.ops[name]:
                for s, v in waits:
                    e.wait_ge(s, v)
                ins = fn(e)
                if inc[0] == "c":
                    ins.then_inc(inc[1], 1)
                else:
                    ins.then_inc(inc[1], 16)
            if name == "sync":
                for s, v in finals:
                    e.wait_ge(s, v)

        with nc.Block() as block:
            @block.sync
            def _(e):
                replay(e, "sync")

            @block.gpsimd
            def _(e):
                replay(e, "pool")

            @block.scalar
            def _(e):
                replay(e, "act")

            @block.vector
            def _(e):
                replay(e, "dve")

            @block.tensor
            def _(e):
                replay(e, "pe")
        allsems = list(self.semobj.values())
        with nc.Block() as block2:
            @block2.sync
            def _(e):
                for sm_ in allsems:
                    e.sem_clear(sm_)
        self.es.close()


def I_ts(out, in0, s1, s2=None, op0=ALU.mult, op1=None, accum_out=None):
    kw = {}
    if op1 is not None:
        kw["op1"] = op1
    if accum_out is not None:
        kw["accum_out"] = accum_out
    return lambda e: e.tensor_scalar(out=out, in0=in0, scalar1=s1, scalar2=s2, op0=op0, **kw)


def I_tt(out, in0, in1, op):
    return lambda e: e.tensor_tensor(out=out, in0=in0, in1=in1, op=op)


def I_stt(out, in0, scalar, in1, op0, op1):
    return lambda e: e.scalar_tensor_tensor(out=out, in0=in0, scalar=scalar, in1=in1, op0=op0, op1=op1)


def I_act(out, in_, func, bias=None, scale=None, accum_out=None):
    kw = {}
    if bias is not None:
        kw["bias"] = bias
    if scale is not None:
        kw["scale"] = scale
    if accum_out is not None:
        kw["accum_out"] = accum_out
    return lambda e: e.activation(out=out, in_=in_, func=func, **kw)


def I_mm(out, lhsT, rhs, start=True, stop=True):
    return lambda e: e.matmul(out, lhsT, rhs, start=start, stop=stop)


def I_tr(out, in_, ident):
    return lambda e: e.transpose(out, in_, ident)


def I_copy(out, in_):
    return lambda e: e.tensor_copy(out=out, in_=in_)


def I_memset(ap, v):
    return lambda e: e.memset(ap, v)


def I_recip(out, in_):
    return lambda e: e.reciprocal(out=out, in_=in_)


def I_scan(out, d0, d1, init):
    return lambda e: e.tensor_tensor_scan(out=out, data0=d0, data1=d1, initial=init, op0=ALU.mult, op1=ALU.add)


def I_dma(out, in_, **kw):
    return lambda e: e.dma_start(out=out, in_=in_, **kw)


def I_iota(out, pattern, base, cm):
    return lambda e: e.iota(out=out, pattern=pattern, base=base, channel_multiplier=cm,
                            allow_small_or_imprecise_dtypes=True)


def mk_ident(ph):
    iot = ph.sb([128, 128], F32, "iot")
    idf = ph.sb([128, 128], F32, "idf")
    idb = ph.sb([128, 128], BF16, "idb")
    ph.op("pool", I_iota(iot.t[:], [[1, 128]], 0, -1), w=[iot])
    ph.op("dve", I_ts(idf.t[:], iot.t[:], 0.0, None, op0=ALU.is_equal), r=[iot], w=[idf])
    ph.op("dve", I_copy(idb.t[:], idf.t[:]), r=[idf], w=[idb])
    return idf, idb


def bcast_rows(ap_row, n):
    return ap_row.to_broadcast([n, ap_row.shape[-1]])


class Seq:
    pass


def build(T=8192, TS=32, PAST=4096, stop_after=None, dbg=False):
    nc = bass.Bass("TRN2", target_bir_lowering=False)
    SS = PAST + TS
    din = lambda n, s, dt=F32: nc.dram_tensor(n, list(s), dt, kind="ExternalInput").ap()
    dout = lambda n, s, dt=F32: nc.dram_tensor(n, list(s), dt, kind="ExternalOutput").ap()
    scr_kind = "ExternalOutput" if dbg else "Internal"
    dscr = lambda n, s, dt=BF16: nc.dram_tensor(n, list(s), dt, kind=scr_kind).ap()

    A = {}
    A["xp"] = din("xp", [T, D]); A["xs"] = din("xs", [2, TS, D])
    A["ck"] = din("ck", [NL, 2, PAST, 512]); A["cv"] = din("cv", [NL, 2, PAST, 512])
    A["cki"] = din("cki", [NL, 2, PAST, 64])
    A["ssr"] = din("ssr", [NL, 2, 2048]); A["ssi"] = din("ssi", [NL, 2, 2048])
    A["call"] = din("call", [3, D])
    A["w_mod"] = din("w_mod", [NL, D, 3 * D]); A["b_mod"] = din("b_mod", [NL, 3 * D])
    A["g_norm"] = din("g_norm", [NL, D]); A["w_in"] = din("w_in", [NL, D, DIN])
    A["a_re"] = din("a_re", [NL, 2048]); A["a_im"] = din("a_im", [NL, 2048])
    A["log_dt"] = din("log_dt", [NL, 32])
    A["b_re"] = din("b_re", [NL, 2048, 16]); A["b_im"] = din("b_im", [NL, 2048, 16])
    A["c_reT"] = din("c_reT", [NL, 2048, 16]); A["c_imT"] = din("c_imT", [NL, 2048, 16])
    A["d_skip"] = din("d_skip", [NL, 512])
    A["w_glu"] = din("w_glu", [NL, 512, 512]); A["b_glu"] = din("b_glu", [NL, 512])
    A["w_ps"] = din("w_ps", [NL, 512, D]); A["w_pa"] = din("w_pa", [NL, 512, D])
    A["w_o"] = din("w_o", [NL, D, D]); A["g_final"] = din("g_final", [1, D])

    O = {}
    O["yp"] = dout("yp", [T, D]); O["ys"] = dout("ys", [2, TS, D])
    O["kp"] = dout("kp", [NL, T, 512]); O["vp"] = dout("vp", [NL, T, 512]); O["kip"] = dout("kip", [NL, T, 64])
    O["srp"] = dout("srp", [NL, 2048]); O["sip"] = dout("sip", [NL, 2048])
    O["ks"] = dout("ks", [NL, 2, TS, 512]); O["vs"] = dout("vs", [NL, 2, TS, 512])
    O["kis"] = dout("kis", [NL, 2, TS, 64])
    O["srs"] = dout("srs", [NL, 2, 2048]); O["sis"] = dout("sis", [NL, 2, 2048])

    modbc = dscr("modbc", [NL, 3, 3, 128, D], F32)

    seqs = []
    for i in range(3):
        s = Seq()
        s.i = i
        s.nm = "p" if i == 0 else f"s{i - 1}"
        s.T = T if i == 0 else TS
        s.S = T if i == 0 else SS
        s.koff = 0 if i == 0 else PAST
        s.causal = (i == 0)
        s.xin = A["xp"] if i == 0 else A["xs"][i - 1]
        s.xres = dscr(f"xres_{s.nm}", [s.T, D], F32)
        for nm, rows, cols in (("uT", 512, s.T), ("zsT", 512, s.T), ("qT", 512, s.T), ("kT", 512, s.S),
                               ("zaT", 512, s.T), ("qiT", 512, s.T), ("kiT", 64, s.S), ("gmT", 2048, s.T),
                               ("vaug", s.S, 520), ("ysT", 512, s.T), ("yaT", 512, s.T)):
            setattr(s, nm, dscr(f"{nm}_{s.nm}", [rows, cols], BF16))
        s.wiS = dscr(f"wiS_{s.nm}", [s.T, 8], F32)
        if i == 0:
            s.kout = [O["kp"][l] for l in range(NL)]; s.vout = [O["vp"][l] for l in range(NL)]
            s.kiout = [O["kip"][l] for l in range(NL)]
            s.srout = [O["srp"][l] for l in range(NL)]; s.siout = [O["sip"][l] for l in range(NL)]
            s.yout = O["yp"]
        else:
            j = i - 1
            s.kout = [O["ks"][l, j] for l in range(NL)]; s.vout = [O["vs"][l, j] for l in range(NL)]
            s.kiout = [O["kis"][l, j] for l in range(NL)]
            s.srout = [O["srs"][l, j] for l in range(NL)]; s.siout = [O["sis"][l, j] for l in range(NL)]
            s.yout = O["ys"][j]
        seqs.append(s)

    ncd = nc.allow_non_contiguous_dma(reason="small strided param loads")
    ncd.__enter__()
    es_ = ExitStack()
    _SEMSTACK.append(es_)

    phase_mod(nc, A, modbc)
    if stop_after == "mod":
        return nc
    for l in range(NL):
        phase_proj(nc, A, l, seqs, modbc)
        if stop_after == f"proj{l}":
            return nc
        phase_cache(nc, A, l, seqs, PAST)
        if stop_after == f"cache{l}":
            return nc
        phase_ssm(nc, A, l, seqs)
        if stop_after == f"ssm{l}":
            return nc
        import os as _os2
        for s in seqs:
            if _os2.environ.get("ATTN_SEQ") and s.nm not in _os2.environ.get("ATTN_SEQ").split(","):
                continue
            (phase_attn2 if _os2.environ.get('ATTN_V','1')=='2' else phase_attn)(nc, A, l, s)
        if stop_after == f"attn{l}":
            return nc
        phase_merge(nc, A, l, seqs, modbc)
        if stop_after == f"merge{l}":
            return nc
    return nc


def phase_mod(nc, A, modbc):
    ph = Phase(nc, "M")
    idf, idb = mk_ident(ph)
    cT = ph.sb([128, 8, 3], F32, "cT")
    for s_ in range(3):
        ph.dma(I_dma(cT.t[:, :, s_], A["call"][s_].rearrange("(c p) -> p c", p=128)), w=[cT])
    cTs = ph.sb([128, 8, 3], F32, "cTs")
    ph.op("act", I_act(cTs.t[:], cT.t[:], AF.Silu), r=[cT], w=[cTs])
    iop = ph.sb([3, 128], F32, "iop")
    ph.op("pool", I_iota(iop.t[:], [[0, 128]], 0, 1), w=[iop])
    sel = []
    for s in range(3):
        t = ph.sb([3, 128], F32, f"sel{s}")
        ph.op("dve", I_ts(t.t[:], iop.t[:], float(s), None, op0=ALU.is_equal), r=[iop], w=[t])
        sel.append(t)
    wst = [ph.sb([128, 8, 512], F32, f"wst{i}") for i in range(2)]
    pmm = [ph.ps([128, 512], F32, f"pm{i}") for i in range(2)]
    pbc = [ph.ps([128, 512], F32, f"pb{i}") for i in range(2)]
    stg = [ph.sb([128, D], F32, f"stg{i}") for i in range(2)]
    n = 0
    nb = 0
    for l in range(NL):
        bsb = ph.sb([3, 3 * D], F32, f"bsb{l}")
        ph.dma(I_dma(bsb.t[:], bcast_rows(A["b_mod"][l:l + 1, :], 3)), w=[bsb])
        gbc = ph.sb([128, D], F32, f"gbc{l}")
        ph.dma(I_dma(gbc.t[:], bcast_rows(A["g_norm"][l:l + 1, :], 128)), w=[gbc])
        mod = ph.sb([3, 3 * D], F32, f"mod{l}")
        for ct in range(6):
            w = wst[n % 2]; pm = pmm[n % 2]; n += 1
            ph.dma(I_dma(w.t[:], A["w_mod"][l].rearrange("(c p) n -> p c n", p=128)[:, :, ct * 512:(ct + 1) * 512]), w=[w])
            for c in range(8):
                ph.op("pe", I_mm(pm.t[0:3, :], cTs.t[:, c, :], w.t[:, c, :], start=(c == 0), stop=(c == 7)),
                      r=[cTs, w], w=[pm])
            ph.op("dve", I_tt(mod.t[:, ct * 512:(ct + 1) * 512], pm.t[0:3, :], bsb.t[:, ct * 512:(ct + 1) * 512], ALU.add),
                  r=[pm, bsb], w=[mod])
        ph.op("dve", I_ts(mod.t[:, D:2 * D], mod.t[:, D:2 * D], 1.0, None, op0=ALU.add), r=[mod], w=[mod])
        for s in range(3):
            for kind in range(3):
                st = stg[nb % 2]
                for half in range(2):
                    pb = pbc[nb % 2 if half == 0 else (nb + 1) % 2]
                    c0 = kind * D + half * 512
                    ph.op("pe", I_mm(pb.t[:, :], sel[s].t[:, :], mod.t[:, c0:c0 + 512]), r=[sel[s], mod], w=[pb])
                    if kind == 1:
                        ph.op("dve", I_tt(st.t[:, half * 512:(half + 1) * 512], pb.t[:, :], gbc.t[:, half * 512:(half + 1) * 512], ALU.mult),
                              r=[pb, gbc], w=[st])
                    else:
                        ph.op("act", I_act(st.t[:, half * 512:(half + 1) * 512], pb.t[:, :], AF.Copy), r=[pb], w=[st])
                nb += 1
                ph.dma(I_dma(modbc[l, s, kind], st.t[:]), r=[st])
    ph.finish()


def phase_proj(nc, A, l, seqs, modbc):
    ph = Phase(nc, f"P{l}")
    idf, idb = mk_ident(ph)
    wbf = ph.sb([128, 8, DIN], BF16, "wbf")
    win = A["w_in"][l].rearrange("(c p) n -> p c n", p=128)
    for c in range(8):
        for h0 in range(0, DIN, 1024):
            h1 = min(DIN, h0 + 1024)
            ph.dma(I_dma(wbf.t[:, c, h0:h1], win[:, c, h0:h1]), w=[wbf], q="pool")
    xt = [ph.sb([128, D], F32, f"xt{i}") for i in range(2)]
    junk = ph.sb([128, D], BF16, "junk")
    ss = [ph.sb([128, 1], F32, f"ss{i}") for i in range(2)]
    rstd = [ph.sb([128, 1], F32, f"rstd{i}") for i in range(2)]
    tmp = [ph.sb([128, D], F32, f"tmp{i}") for i in range(2)]
    hb = [ph.sb([128, D], BF16, f"hb{i}") for i in range(2)]
    hT = [ph.sb([128, 8, 512], BF16, f"hT{i}") for i in range(2)]
    gm_bc = ph.sb([128, D], F32, "gmbc")
    sh_bc = ph.sb([128, D], F32, "shbc")
    kvst = [ph.sb([128, 2 * 512], F32, f"kvst{i}") for i in range(2)]
    vst = [ph.sb([128, 8, 65], BF16, f"vst{i}") for i in range(2)]
    for v in vst:
        ph.op("pool", I_memset(v.t[:], 1.0), w=[v])
    kist = [ph.sb([128, 64], F32, f"kist{i}") for i in range(2)]
    wist = [ph.sb([128, 8], F32, f"wist{i}") for i in range(2)]
    fst = [ph.sb([128, 512], BF16, f"fst{i}") for i in range(4)]
    pT = ph.ps([128, 8, 128], BF16, "pT")
    pk = ph.ps([128, 512], F32, "pk")
    pv = ph.ps([128, 512], F32, "pv")
    pki = ph.ps([128, 72], F32, "pki")
    pf = [ph.ps([128, 512], F32, f"pf{i}") for i in range(3)]
    WISC = float(8 ** -0.5 * 64 ** -0.5)

    nt = 0
    nf = 0
    for s in seqs:
        ph.dma(I_dma(gm_bc.t[:], modbc[l, s.i, 1]), w=[gm_bc])
        ph.dma(I_dma(sh_bc.t[:], modbc[l, s.i, 0]), w=[sh_bc])
        xsrc = s.xin if l == 0 else s.xres
        TP = min(128, s.T)
        NW = min(512, s.T)
        for st in range(s.T // NW):
            ht = hT[st % 2]
            for j in range(NW // TP):
                r0 = st * NW + j * TP
                x = xt[nt % 2]; sq = ss[nt % 2]; rs = rstd[nt % 2]; tm = tmp[nt % 2]; h = hb[nt % 2]
                kv = kvst[nt % 2]; vs_ = vst[nt % 2]; kis_ = kist[nt % 2]; wis_ = wist[nt % 2]
                nt += 1
                ph.dma(I_dma(x.t[:TP, :], xsrc[r0:r0 + TP, :]), w=[x])
                ph.op("act", I_act(junk.t[:TP, :], x.t[:TP, :], AF.Square, accum_out=sq.t[:TP, 0:1]), r=[x], w=[junk, sq])
                ph.op("dve", I_ts(sq.t[:TP, :], sq.t[:TP, :], 1.0 / D, EPS, op0=ALU.mult, op1=ALU.add), r=[sq], w=[sq])
                ph.op("act", I_act(sq.t[:TP, :], sq.t[:TP, :], AF.Sqrt), r=[sq], w=[sq])
                ph.op("dve", I_recip(rs.t[:TP, :], sq.t[:TP, :]), r=[sq], w=[rs])
                ph.op("dve", I_stt(tm.t[:TP, :], x.t[:TP, :], rs.t[:TP, 0:1], gm_bc.t[:TP, :], ALU.mult, ALU.mult),
                      r=[x, rs, gm_bc], w=[tm])
                ph.op("pool", I_tt(h.t[:TP, :], tm.t[:TP, :], sh_bc.t[:TP, :], ALU.add), r=[tm, sh_bc], w=[h])
                for c in range(8):
                    ph.op("pe", I_tr(pT.t[:, c, :TP], h.t[:TP, c * 128:(c + 1) * 128], idb.t[:TP, :TP]), r=[h, idb], w=[pT])
                ph.op("act", I_act(ht.t[:, :, j * TP:(j + 1) * TP], pT.t[:, :, :TP], AF.Copy), r=[pT], w=[ht])
                for c in range(8):
                    ph.op("pe", I_mm(pk.t[:TP, :], ht.t[:, c, j * TP:(j + 1) * TP], wbf.t[:, c, OFF_K:OFF_K + 512],
                                     start=(c == 0), stop=(c == 7)), r=[ht, wbf], w=[pk])
                for c in range(8):
                    ph.op("pe", I_mm(pv.t[:TP, :], ht.t[:, c, j * TP:(j + 1) * TP], wbf.t[:, c, OFF_V:OFF_V + 512],
                                     start=(c == 0), stop=(c == 7)), r=[ht, wbf], w=[pv])
                for c in range(8):
                    ph.op("pe", I_mm(pki.t[:TP, :], ht.t[:, c, j * TP:(j + 1) * TP], wbf.t[:, c, OFF_KI:OFF_KI + 72],
                                     start=(c == 0), stop=(c == 7)), r=[ht, wbf], w=[pki])
                ph.op("act", I_act(kv.t[:TP, 0:512], pk.t[:TP, :], AF.Copy), r=[pk], w=[kv])
                ph.op("dve", I_copy(kv.t[:TP, 512:1024], pv.t[:TP, :]), r=[pv], w=[kv])
                ph.op("dve", I_copy(vs_.t[:TP, :, 0:64], pv.t[:TP, :].rearrange("p (h d) -> p h d", d=64)), r=[pv], w=[vs_])
                ph.op("act", I_act(kis_.t[:TP, :], pki.t[:TP, 0:64], AF.Copy), r=[pki], w=[kis_])
                ph.op("dve", I_ts(wis_.t[:TP, :], pki.t[:TP, 64:72], WISC, None, op0=ALU.mult), r=[pki], w=[wis_])
                ph.dma(I_dma(s.kout[l][r0:r0 + TP, :], kv.t[:TP, 0:512]), r=[kv])
                ph.dma(I_dma(s.vout[l][r0:r0 + TP, :], kv.t[:TP, 512:1024]), r=[kv])
                ph.dma(I_dma(s.vaug[s.koff + r0:s.koff + r0 + TP, :], vs_.t[:TP, :, :].rearrange("p h d -> p (h d)")), r=[vs_])
                ph.dma(I_dma(s.kiout[l][r0:r0 + TP, :], kis_.t[:TP, :]), r=[kis_])
                ph.dma(I_dma(s.wiS[r0:r0 + TP, :], wis_.t[:TP, :]), r=[wis_])
            t0 = st * NW
            fm = []
            for i in range(4):
                fm.append((OFF_U + i * 128, 128, "copy", s.uT[i * 128:(i + 1) * 128, t0:t0 + NW]))
                fm.append((OFF_Q + i * 128, 128, "q", s.qT[i * 128:(i + 1) * 128, t0:t0 + NW]))
                fm.append((OFF_K + i * 128, 128, "copy", s.kT[i * 128:(i + 1) * 128, s.koff + t0:s.koff + t0 + NW]))
                fm.append((OFF_QI + i * 128, 128, "copy", s.qiT[i * 128:(i + 1) * 128, t0:t0 + NW]))
            fm.append((OFF_KI, 64, "copy", s.kiT[0:64, s.koff + t0:s.koff + t0 + NW]))
            for i in range(4):
                fm.append((OFF_ZS + i * 128, 128, "silu", s.zsT[i * 128:(i + 1) * 128, t0:t0 + NW]))
            for i in range(4):
                fm.append((OFF_ZA + i * 128, 128, "silu", s.zaT[i * 128:(i + 1) * 128, t0:t0 + NW]))
            for i in range(16):
                fm.append((OFF_GM + i * 128, 128, "sig", s.gmT[i * 128:(i + 1) * 128, t0:t0 + NW]))
            for (off, M, kind, dst) in fm:
                p = pf[nf % 3]; f = fst[nf % 4]; nf += 1
                for c in range(8):
                    ph.op("pe", I_mm(p.t[:M, :NW], wbf.t[:, c, off:off + M], ht.t[:, c, :NW], start=(c == 0), stop=(c == 7)),
                          r=[wbf, ht], w=[p])
                if kind == "copy":
                    ph.op("dve", I_copy(f.t[:M, :NW], p.t[:M, :NW]), r=[p], w=[f])
                elif kind == "q":
                    ph.op("dve", I_ts(f.t[:M, :NW], p.t[:M, :NW], 0.125, None, op0=ALU.mult), r=[p], w=[f])
                elif kind == "silu":
                    ph.op("act", I_act(f.t[:M, :NW], p.t[:M, :NW], AF.Silu), r=[p], w=[f])
                else:
                    ph.op("act", I_act(f.t[:M, :NW], p.t[:M, :NW], AF.Sigmoid), r=[p], w=[f])
                ph.dma(I_dma(dst, f.t[:M, :NW]), r=[f])
    ph.finish()


def make_in_maps(inp, n_cores=8):
    f = lambda a: np.ascontiguousarray(np.asarray(a, dtype=np.float32))
    shared = {
        "w_mod": f(inp["w_mod"]), "b_mod": f(inp["b_mod"]), "g_norm": f(inp["g_norm"]), "w_in": f(inp["w_in"]),
        "a_re": f(inp["a_re"]).reshape(NL, 2048), "a_im": f(inp["a_im"]).reshape(NL, 2048),
        "log_dt": f(inp["log_dt"]),
        "b_re": f(inp["b_re"]).reshape(NL, 2048, 16), "b_im": f(inp["b_im"]).reshape(NL, 2048, 16),
        "c_reT": f(np.transpose(np.asarray(inp["c_re"]), (0, 1, 3, 2))).reshape(NL, 2048, 16),
        "c_imT": f(np.transpose(np.asarray(inp["c_im"]), (0, 1, 3, 2))).reshape(NL, 2048, 16),
        "d_skip": f(inp["d_skip"]).reshape(NL, 512),
        "w_glu": f(inp["w_glu"]), "b_glu": f(inp["b_glu"]), "w_ps": f(inp["w_ps"]), "w_pa": f(inp["w_pa"]),
        "w_o": f(inp["w_o"]), "g_final": f(inp["g_final"]).reshape(1, D),
    }
    xp = np.asarray(inp["x_prompt"]); xs = np.asarray(inp["x_sample"])
    ck = np.asarray(inp["cache_k"]); cv = np.asarray(inp["cache_v"]); cki = np.asarray(inp["cache_kidx"])
    sr = np.asarray(inp["state_ssm_re"]); si = np.asarray(inp["state_ssm_im"])
    cp = np.asarray(inp["c_prompt"]); cs = np.asarray(inp["c_sample"])
    PAST = ck.shape[2]
    maps = []
    for b in range(n_cores):
        m = dict(shared)
        m["xp"] = f(xp[b]); m["xs"] = f(xs[2 * b:2 * b + 2])
        m["ck"] = f(ck[:, 2 * b:2 * b + 2]).reshape(NL, 2, PAST, 512)
        m["cv"] = f(cv[:, 2 * b:2 * b + 2]).reshape(NL, 2, PAST, 512)
        m["cki"] = f(cki[:, 2 * b:2 * b + 2])
        m["ssr"] = f(sr[:, 2 * b:2 * b + 2]).reshape(NL, 2, 2048)
        m["ssi"] = f(si[:, 2 * b:2 * b + 2]).reshape(NL, 2, 2048)
        m["call"] = f(np.concatenate([cp[b:b + 1], cs[2 * b:2 * b + 2]], axis=0))
        maps.append(m)
    return maps


def assemble(results, T, TS, n_cores=8):
    R = results
    cat = lambda k: np.stack([np.asarray(r[k]) for r in R], axis=0)
    yp = cat("yp")
    ys = cat("ys").reshape(2 * n_cores, TS, D)
    kp = np.transpose(cat("kp"), (1, 0, 2, 3)).reshape(NL, n_cores, T, 8, 64)
    vp = np.transpose(cat("vp"), (1, 0, 2, 3)).reshape(NL, n_cores, T, 8, 64)
    kip = np.transpose(cat("kip"), (1, 0, 2, 3))
    srp = np.transpose(cat("srp"), (1, 0, 2)).reshape(NL, n_cores, 32, 64)
    sip = np.transpose(cat("sip"), (1, 0, 2)).reshape(NL, n_cores, 32, 64)
    ks = np.transpose(cat("ks"), (1, 0, 2, 3, 4)).reshape(NL, 2 * n_cores, TS, 8, 64)
    vs = np.transpose(cat("vs"), (1, 0, 2, 3, 4)).reshape(NL, 2 * n_cores, TS, 8, 64)
    kis = np.transpose(cat("kis"), (1, 0, 2, 3, 4)).reshape(NL, 2 * n_cores, TS, 64)
    srs = np.transpose(cat("srs"), (1, 0, 2, 3)).reshape(NL, 2 * n_cores, 32, 64)
    sis = np.transpose(cat("sis"), (1, 0, 2, 3)).reshape(NL, 2 * n_cores, 32, 64)
    outs = (yp, ys, kp, vp, kip, srp, sip, ks, vs, kis, srs, sis)
    return tuple(np.ascontiguousarray(o, dtype=np.float32) for o in outs)


_NC_CACHE = {}


def kernel(**inputs):
    T = int(np.asarray(inputs["x_prompt"]).shape[1])
    TS = int(np.asarray(inputs["x_sample"]).shape[1])
    PAST = int(np.asarray(inputs["cache_k"]).shape[2])
    n_cores = int(np.asarray(inputs["x_prompt"]).shape[0])
    key = (T, TS, PAST)
    nc = build(T=T, TS=TS, PAST=PAST)
    maps = make_in_maps(inputs, n_cores)
    res = run_bass_kernel_spmd(nc, maps, core_ids=list(range(n_cores)))
    return assemble(res.results, T, TS, n_cores)


def phase_cache(nc, A, l, seqs, PAST):
    ph = Phase(nc, f"C{l}")
    idf, idb = mk_ident(ph)
    ckt = [ph.sb([128, 512], F32, f"ckt{i}") for i in range(2)]
    cvt = [ph.sb([128, 512], F32, f"cvt{i}") for i in range(2)]
    cit = [ph.sb([128, 64], F32, f"cit{i}") for i in range(2)]
    ckb = [ph.sb([128, 512], BF16, f"ckb{i}") for i in range(2)]
    cib = [ph.sb([128, 64], BF16, f"cib{i}") for i in range(2)]
    kst = [ph.sb([128, 4, 128], BF16, f"kst{i}") for i in range(2)]
    kis = [ph.sb([64, 128], BF16, f"kis{i}") for i in range(2)]
    vst = [ph.sb([128, 8, 65], BF16, f"vst{i}") for i in range(2)]
    for v in vst:
        ph.op("pool", I_memset(v.t[:], 1.0), w=[v])
    pT = [ph.ps([128, 4, 128], BF16, f"pT{i}") for i in range(2)]
    pI = [ph.ps([64, 128], BF16, f"pI{i}") for i in range(2)]
    n = 0
    for s in seqs[1:]:
        j = s.i - 1
        for kt in range(PAST // 128):
            i = n % 2; n += 1
            r0 = kt * 128
            ph.dma(I_dma(ckt[i].t[:], A["ck"][l, j, r0:r0 + 128, :]), w=[ckt[i]])
            ph.dma(I_dma(cvt[i].t[:], A["cv"][l, j, r0:r0 + 128, :]), w=[cvt[i]])
            ph.dma(I_dma(cit[i].t[:], A["cki"][l, j, r0:r0 + 128, :]), w=[cit[i]])
            ph.op("act", I_act(ckb[i].t[:], ckt[i].t[:], AF.Copy), r=[ckt[i]], w=[ckb[i]])
            ph.op("act", I_act(cib[i].t[:], cit[i].t[:], AF.Copy), r=[cit[i]], w=[cib[i]])
            for hp in range(4):
                ph.op("pe", I_tr(pT[i].t[:, hp, :], ckb[i].t[:, hp * 128:(hp + 1) * 128], idb.t[:, :]), r=[ckb[i], idb], w=[pT[i]])
            ph.op("pe", I_tr(pI[i].t[:, :], cib[i].t[:, :], idb.t[:, :]), r=[cib[i], idb], w=[pI[i]])
            ph.op("dve", I_copy(kst[i].t[:], pT[i].t[:]), r=[pT[i]], w=[kst[i]])
            ph.op("dve", I_copy(kis[i].t[:], pI[i].t[:]), r=[pI[i]], w=[kis[i]])
            ph.op("pool", I_copy(vst[i].t[:, :, 0:64], cvt[i].t[:, :].rearrange("p (h d) -> p h d", d=64)), r=[cvt[i]], w=[vst[i]])
            ph.dma(I_dma(s.kT.rearrange("(hp p) s -> p hp s", p=128)[:, :, r0:r0 + 128], kst[i].t[:]), r=[kst[i]])
            ph.dma(I_dma(s.kiT[:, r0:r0 + 128], kis[i].t[:]), r=[kis[i]])
            ph.dma(I_dma(s.vaug[r0:r0 + 128, :], vst[i].t[:].rearrange("p h d -> p (h d)")), r=[vst[i]])
    ph.finish()


def phase_merge(nc, A, l, seqs, modbc):
    ph = Phase(nc, f"G{l}")
    last = (l == NL - 1)
    wps = ph.sb([128, 4, D], BF16, "wps"); wpa = ph.sb([128, 4, D], BF16, "wpa"); wo = ph.sb([128, 8, D], BF16, "wo")
    for c in range(4):
        ph.dma(I_dma(wps.t[:, c, :], A["w_ps"][l, c * 128:(c + 1) * 128, :]), w=[wps], q="pool")
        ph.dma(I_dma(wpa.t[:, c, :], A["w_pa"][l, c * 128:(c + 1) * 128, :]), w=[wpa], q="pool")
    for c in range(8):
        ph.dma(I_dma(wo.t[:, c, :], A["w_o"][l, c * 128:(c + 1) * 128, :]), w=[wo], q="pool")
    gate = ph.sb([128, D], F32, "gate")
    gfin = ph.sb([128, D], F32, "gfin")
    if last:
        ph.dma(I_dma(gfin.t[:], bcast_rows(A["g_final"][0:1, :], 128)), w=[gfin])
    yst = [ph.sb([128, 4, 512], BF16, f"yst{i}") for i in range(2)]
    yat = [ph.sb([128, 4, 512], BF16, f"yat{i}") for i in range(2)]
    gmt = [ph.sb([128, 16, 512], BF16, f"gmt{i}") for i in range(2)]
    mg = [ph.sb([128, 8, 512], BF16, f"mg{i}") for i in range(2)]
    tA = [ph.sb([128, 512], F32, f"tA{i}") for i in range(2)]
    tB = [ph.sb([128, 512], F32, f"tB{i}") for i in range(2)]
    xt = [ph.sb([128, D], F32, f"xt{i}") for i in range(2)]
    xn = [ph.sb([128, D], F32, f"xn{i}") for i in range(2)]
    yo = [ph.sb([128, D], F32, f"yo{i}") for i in range(2)]
    junk = ph.sb([128, D], BF16, "junk")
    ss = [ph.sb([128, 1], F32, f"ss{i}") for i in range(2)]
    rs = [ph.sb([128, 1], F32, f"rs{i}") for i in range(2)]
    pA = [ph.ps([128, 512], F32, f"pA{i}") for i in range(2)]
    pB = [ph.ps([128, 512], F32, f"pB{i}") for i in range(2)]
    pO = [ph.ps([128, 512], F32, f"pO{i}") for i in range(2)]
    nn = 0
    nt = 0
    for s in seqs:
        ph.dma(I_dma(gate.t[:], modbc[l, s.i, 2]), w=[gate])
        xsrc = s.xin if l == 0 else s.xres
        TP = min(128, s.T); NW = min(512, s.T)
        for st in range(s.T // NW):
            t0 = st * NW
            ys_ = yst[st % 2]; ya_ = yat[st % 2]; gm_ = gmt[st % 2]; m_ = mg[st % 2]
            ph.dma(I_dma(ys_.t[:, :, :NW], s.ysT.rearrange("(c p) t -> p c t", p=128)[:, :, t0:t0 + NW]), w=[ys_])
            ph.dma(I_dma(ya_.t[:, :, :NW], s.yaT.rearrange("(c p) t -> p c t", p=128)[:, :, t0:t0 + NW]), w=[ya_])
            ph.dma(I_dma(gm_.t[:, :, :NW], s.gmT.rearrange("(c p) t -> p c t", p=128)[:, :, t0:t0 + NW]), w=[gm_])
            for ct in range(8):
                a = pA[nn % 2]; b = pB[nn % 2]; ta = tA[nn % 2]; tb = tB[nn % 2]; nn += 1
                for c in range(4):
                    ph.op("pe", I_mm(a.t[:, :NW], wps.t[:, c, ct * 128:(ct + 1) * 128], ys_.t[:, c, :NW], start=(c == 0), stop=(c == 3)),
                          r=[wps, ys_], w=[a])
                for c in range(4):
                    ph.op("pe", I_mm(b.t[:, :NW], wpa.t[:, c, ct * 128:(ct + 1) * 128], ya_.t[:, c, :NW], start=(c == 0), stop=(c == 3)),
                          r=[wpa, ya_], w=[b])
                ph.op("dve", I_tt(ta.t[:, :NW], a.t[:, :NW], gm_.t[:, ct, :NW], ALU.mult), r=[a, gm_], w=[ta])
                ph.op("dve", I_tt(tb.t[:, :NW], b.t[:, :NW], gm_.t[:, 8 + ct, :NW], ALU.mult), r=[b, gm_], w=[tb])
                ph.op("pool", I_tt(m_.t[:, ct, :NW], ta.t[:, :NW], tb.t[:, :NW], ALU.add), r=[ta, tb], w=[m_])
            for j in range(NW // TP):
                r0 = t0 + j * TP
                x = xt[nt % 2]; xo = xn[nt % 2]; y_ = yo[nt % 2]; sq = ss[nt % 2]; r_ = rs[nt % 2]
                ph.dma(I_dma(x.t[:TP, :], xsrc[r0:r0 + TP, :]), w=[x])
                for half in range(2):
                    po = pO[half]
                    hs = slice(half * 512, (half + 1) * 512)
                    for c in range(8):
                        ph.op("pe", I_mm(po.t[:TP, :], m_.t[:, c, j * TP:(j + 1) * TP], wo.t[:, c, hs], start=(c == 0), stop=(c == 7)),
                              r=[m_, wo], w=[po])
                    ph.op("dve", I_tt(xo.t[:TP, hs], po.t[:TP, :], gate.t[:TP, hs], ALU.mult), r=[po, gate], w=[xo])
                ph.op("pool", I_tt(xo.t[:TP, :], xo.t[:TP, :], x.t[:TP, :], ALU.add), r=[xo, x], w=[xo])
                nt += 1
                if not last:
                    ph.dma(I_dma(s.xres[r0:r0 + TP, :], xo.t[:TP, :]), r=[xo])
                else:
                    ph.op("act", I_act(junk.t[:TP, :], xo.t[:TP, :], AF.Square, accum_out=sq.t[:TP, 0:1]), r=[xo], w=[junk, sq])
                    ph.op("dve", I_ts(sq.t[:TP, :], sq.t[:TP, :], 1.0 / D, EPS, op0=ALU.mult, op1=ALU.add), r=[sq], w=[sq])
                    ph.op("act", I_act(sq.t[:TP, :], sq.t[:TP, :], AF.Sqrt), r=[sq], w=[sq])
                    ph.op("dve", I_recip(r_.t[:TP, :], sq.t[:TP, :]), r=[sq], w=[r_])
                    ph.op("dve", I_stt(y_.t[:TP, :], xo.t[:TP, :], r_.t[:TP, 0:1], gfin.t[:TP, :], ALU.mult, ALU.mult),
                          r=[xo, r_, gfin], w=[y_])
                    ph.dma(I_dma(s.yout[r0:r0 + TP, :], y_.t[:TP, :]), r=[y_])
    ph.finish()


NIT = 22
W0 = 64.0


def phase_attn(nc, A, l, s):
    ph = Phase(nc, f"A{l}{s.nm}")
    idf, idb = mk_ident(ph)
    T, S = s.T, s.S
    QP = min(128, T); QB = min(512, T)
    L = T if s.causal else S
    KSEL = float(min(TOPK, L // 4))
    ktiles = [(k0, min(128, S - k0)) for k0 in range(0, S, 128)]
    NKT = len(ktiles)
    kiTa = ph.sb([128, S], BF16, "kiTa"); kiTb = ph.sb([128, S], BF16, "kiTb")
    ph.op("pool", I_memset(kiTa.t[64:128, :], 0.0), w=[kiTa])
    ph.op("pool", I_memset(kiTb.t[0:64, :], 0.0), w=[kiTb])
    ph.dma(I_dma(kiTa.t[0:64, :], s.kiT[:, :]), w=[kiTa])
    ph.dma(I_dma(kiTb.t[64:128, :], s.kiT[:, :]), w=[kiTb])
    kiTz = (kiTa, kiTb)
    sc = ph.sb([128, S], F32, "sc")
    mk = ph.sb([128, S], BF16, "mk")
    maskT = ph.sb([128, NKT, QB], BF16, "maskT")
    qit = [ph.sb([128, 4, 128], BF16, f"qit{i}") for i in range(2)]
    wit = [ph.sb([128, 8], F32, f"wit{i}") for i in range(2)]
    dg = [ph.sb([128, 8, 128], BF16, f"dg{i}") for i in range(2)]
    rl = [ph.sb([128, 512], BF16, f"rl{i}") for i in range(2)]
    mid = ph.sb([128, 1], F32, "mid"); cnt = ph.sb([128, 1], F32, "cnt"); tq = ph.sb([128, 1], F32, "tq")
    thr = ph.sb([128, 1], F32, "thr")
    negmid = ph.sb([128, 1], F32, "negmid"); sact = ph.sb([128, 1], F32, "sact"); comb = ph.sb([128, 1], F32, "comb")
    junk2 = ph.sb([128, S], BF16, "junk2")
    qta = ph.sb([128, 4, 512], BF16, "qta"); qtb = ph.sb([128, 4, 512], BF16, "qtb")
    ph.op("pool", I_memset(qta.t[64:128, :, :], 0.0), w=[qta])
    ph.op("pool", I_memset(qtb.t[0:64, :, :], 0.0), w=[qtb])
    qtz = (qta, qtb)
    va = [ph.sb([128, 260], BF16, f"va{i}") for i in range(2)]
    ktl = [ph.sb([128, 2, 128], BF16, f"ktl{i}") for i in range(2)]
    pe_ = [ph.sb([128, 512], BF16, f"pe{i}") for i in range(2)]
    pm = [ph.sb([128, 512], BF16, f"pm{i}") for i in range(2)]
    oa = [ph.sb([128, 512], F32, f"oa{i}") for i in range(2)]
    for o__ in oa:
        ph.op("pool", I_memset(o__.t[:], 0.0), w=[o__])
    rden = [ph.sb([64, 512], F32, f"rden{i}") for i in range(2)]
    t1 = [ph.sb([64, 512], F32, f"t1{i}") for i in range(2)]
    zat = [ph.sb([64, 512], BF16, f"zat{i}") for i in range(2)]
    yat = [ph.sb([64, 512], BF16, f"yat{i}") for i in range(2)]
    iop = ph.sb([128, 64], F32, "iop")
    selden = ph.sb([128, 64], F32, "selden")
    ph.op("pool", I_iota(iop.t[:], [[0, 64]], 0, 1), w=[iop])
    ph.op("dve", I_ts(selden.t[:], iop.t[:], 64.0, None, op0=ALU.is_equal), r=[iop], w=[selden])
    lg = [ph.ps([128, 512], F32, f"lg{i}") for i in range(2)]
    acc = ph.ps([128, 512], F32, "acc")
    mtp = ph.ps([128, 8, 128], BF16, "mtp")
    oacc = [ph.ps([128, 512], F32, f"oacc{i}") for i in range(4)]

    nq = 0
    nr = 0
    nev = 0
    nv = 0
    nh = 0
    for jb in range(T // QB):
        qb0 = jb * QB
        Sblk = min(S, (jb + 1) * QB) if s.causal else S
        kts = [(i, k0, ksz) for i, (k0, ksz) in enumerate(ktiles) if k0 < Sblk]
        for jq in range(QB // QP):
            q0 = qb0 + jq * QP
            qoff = jq * QP
            Slim = (q0 + QP) if s.causal else S
            qi_ = qit[nq % 2]; wi_ = wit[nq % 2]; dg_ = dg[nq % 2]; nq += 1
            ph.dma(I_dma(qi_.t[:, :, :QP], s.qiT.rearrange("(hp p) t -> p hp t", p=128)[:, :, q0:q0 + QP]), w=[qi_])
            ph.dma(I_dma(wi_.t[:QP, :], s.wiS[q0:q0 + QP, :]), w=[wi_])
            for h in range(8):
                ph.op("dve", I_ts(dg_.t[:QP, h, :QP], idf.t[:QP, :QP], wi_.t[:QP, h:h + 1], None, op0=ALU.mult),
                      r=[idf, wi_], w=[dg_])
            for c0 in range(0, Slim, 512):
                csz = min(512, Slim - c0)
                for h in range(8):
                    base = 64 * (h % 2)
                    g = lg[nr % 2]; r_ = rl[nr % 2]; nr += 1
                    ph.op("pe", I_mm(g.t[:QP, :csz], qi_.t[:, h // 2, :QP], kiTz[h % 2].t[:, c0:c0 + csz]),
                          r=[qi_, kiTz[h % 2]], w=[g])
                    ph.op("act", I_act(r_.t[:QP, :csz], g.t[:QP, :csz], AF.Relu), r=[g], w=[r_])
                    ph.op("pe", I_mm(acc.t[:QP, :csz], dg_.t[:QP, h, :QP], r_.t[:QP, :csz], start=(h == 0), stop=(h == 7)),
                          r=[dg_, r_], w=[acc])
                ph.op("dve", I_copy(sc.t[:QP, c0:c0 + csz], acc.t[:QP, :csz]), r=[acc], w=[sc])
            if s.causal and QP == 128:
                ph.op("dve", I_memset(sc.t[0:64, Slim - 64:Slim], NEG), w=[sc])
            ph.op("dve", I_memset(mid.t[:QP, :], 0.0), w=[mid])
            ph.op("dve", I_memset(negmid.t[:QP, :], 0.0), w=[negmid])
            S1 = Slim
            if Slim >= 1024:
                S1 = int(Slim * 0.45) // 128 * 128
            w = W02
            for k in range(NIT2):
                w = w / 2.0
                ph.op("dve", I_ts(mk.t[:QP, :S1], sc.t[:QP, :S1], mid.t[:QP, 0:1], None, op0=ALU.is_ge, op1=ALU.add,
                                  accum_out=cnt.t[:QP, 0:1]), r=[sc, mid], w=[mk, cnt])
                if S1 < Slim:
                    ph.op("act", I_act(junk2.t[:QP, S1:Slim], sc.t[:QP, S1:Slim], AF.Sign, bias=negmid.t[:QP, 0:1],
                                       accum_out=sact.t[:QP, 0:1]), r=[sc, negmid], w=[junk2, sact])
                    ph.op("dve", I_stt(comb.t[:QP, :], sact.t[:QP, :], 0.5, cnt.t[:QP, :], ALU.mult, ALU.add), r=[sact, cnt], w=[comb])
                    src_, kadj = comb, KSEL - 0.5 * (Slim - S1)
                else:
                    src_, kadj = cnt, KSEL
                ph.op("dve", I_ts(tq.t[:QP, :], src_.t[:QP, :], kadj, 2.0 * w, op0=ALU.is_ge, op1=ALU.mult), r=[src_], w=[tq])
                ph.op("dve", I_stt(mid.t[:QP, :], mid.t[:QP, :], -w, tq.t[:QP, :], ALU.add, ALU.add), r=[mid, tq], w=[mid])
                if S1 < Slim:
                    ph.op("dve", I_stt(negmid.t[:QP, :], negmid.t[:QP, :], w, tq.t[:QP, :], ALU.add, ALU.subtract), r=[negmid, tq], w=[negmid])
            ph.op("dve", I_ts(thr.t[:QP, :], mid.t[:QP, :], -w, None, op0=ALU.add), r=[mid], w=[thr])
            ph.op("dve", I_ts(mk.t[:QP, :Slim], sc.t[:QP, :Slim], thr.t[:QP, 0:1], None, op0=ALU.is_ge), r=[sc, thr], w=[mk])
            mine = [(i, k0, ksz) for (i, k0, ksz) in kts if k0 < Slim]
            for g0 in range(0, len(mine), 8):
                grp = mine[g0:g0 + 8]
                for gi, (i, k0, ksz) in enumerate(grp):
                    ph.op("pe", I_tr(mtp.t[:ksz, gi, :QP], mk.t[:QP, k0:k0 + ksz], idb.t[:QP, :QP]), r=[mk, idb], w=[mtp])
                i0 = grp[0][0]
                ng = len(grp)
                eng = "act" if nev % 2 == 0 else "dve"; nev += 1
                if eng == "act":
                    ph.op("act", I_act(maskT.t[:, i0:i0 + ng, qoff:qoff + QP], mtp.t[:, 0:ng, :QP], AF.Copy), r=[mtp], w=[maskT])
                else:
                    ph.op("dve", I_copy(maskT.t[:, i0:i0 + ng, qoff:qoff + QP], mtp.t[:, 0:ng, :QP]), r=[mtp], w=[maskT])
            rest = [(i, k0, ksz) for (i, k0, ksz) in kts if k0 >= Slim]
            if rest:
                i0 = rest[0][0]; i1 = rest[-1][0] + 1
                ph.op("pool", I_memset(maskT.t[:, i0:i1, qoff:qoff + QP], 0.0), w=[maskT])
        import os as _os
        if _os.environ.get("ATTN_STOP") == "idx":
            continue
        qsrc = s.qT.rearrange("(hp p) t -> p hp t", p=128)
        ph.dma(I_dma(qta.t[0:64, :, :QB], qsrc[0:64, :, qb0:qb0 + QB]), w=[qta])
        ph.dma(I_dma(qtb.t[64:128, :, :QB], qsrc[64:128, :, qb0:qb0 + QB]), w=[qtb])
        for hg in range(2):
            for idx, (i, k0, ksz) in enumerate(kts):
                va_ = va[nv % 2]; kt_ = ktl[nv % 2]; nv += 1
                ph.dma(I_dma(va_.t[:ksz, :], s.vaug[k0:k0 + ksz, hg * 260:(hg + 1) * 260]), w=[va_])
                ph.dma(I_dma(kt_.t[:, :, :ksz], s.kT.rearrange("(hp p) s -> p hp s", p=128)[:, 2 * hg:2 * hg + 2, k0:k0 + ksz]), w=[kt_])
                for hh in range(4):
                    h = hg * 4 + hh
                    base = 64 * (h % 2)
                    st_ = lg[nh % 2]; p_ = pe_[nh % 2]; m_ = pm[nh % 2]; nh += 1
                    if _os.environ.get("ATTN_STOP") == "dma":
                        continue
                    ph.op("pe", I_mm(st_.t[:ksz, :QB], kt_.t[:, hh // 2, :ksz], qtz[h % 2].t[:, h // 2, :QB]),
                          r=[kt_, qtz[h % 2]], w=[st_])
                    if _os.environ.get("ATTN_STOP") == "mm":
                        continue
                    ph.op("act", I_act(p_.t[:ksz, :QB], st_.t[:ksz, :QB], AF.Exp), r=[st_], w=[p_])
                    if _os.environ.get("ATTN_STOP") == "exp":
                        continue
                    ph.op("dve", I_tt(m_.t[:ksz, :QB], p_.t[:ksz, :QB], maskT.t[:ksz, i, :QB], ALU.mult), r=[p_, maskT], w=[m_])
                    if _os.environ.get("ATTN_STOP") == "qk":
                        continue
                    ph.op("pe", I_mm(oacc[hh].t[:65, :QB], va_.t[:ksz, hh * 65:(hh + 1) * 65], m_.t[:ksz, :QB], start=(idx == 0), stop=(idx == len(kts) - 1)),
                          r=[va_, m_], w=[oacc[hh]])
            if _os.environ.get("ATTN_STOP") in ("pv", "mm", "exp", "qk", "dma"):
                continue
            for hh in range(4):
                h = hg * 4 + hh
                o_ = oa[hh % 2]; rd = rden[hh % 2]; t_ = t1[hh % 2]; z_ = zat[hh % 2]; y_ = yat[hh % 2]
                ph.op("act", I_act(o_.t[:65, :QB], oacc[hh].t[:65, :QB], AF.Copy), r=[oacc[hh]], w=[o_])
                ph.op("pe", I_mm(acc.t[:64, :QB], selden.t[:, :64], o_.t[:, :QB]), r=[selden, o_], w=[acc])
                ph.op("dve", I_recip(rd.t[:64, :QB], acc.t[:64, :QB]), r=[acc], w=[rd])
                ph.dma(I_dma(z_.t[:64, :QB], s.zaT[64 * h:64 * h + 64, qb0:qb0 + QB]), w=[z_])
                ph.op("dve", I_tt(t_.t[:64, :QB], o_.t[0:64, :QB], rd.t[:64, :QB], ALU.mult), r=[o_, rd], w=[t_])
                ph.op("pool", I_tt(y_.t[:64, :QB], t_.t[:64, :QB], z_.t[:64, :QB], ALU.mult), r=[t_, z_], w=[y_])
                ph.dma(I_dma(s.yaT[64 * h:64 * h + 64, qb0:qb0 + QB], y_.t[:64, :QB]), r=[y_])
    ph.finish()


PI = float(np.pi)


def phase_ssm(nc, A, l, seqs):
    ph = Phase(nc, f"S{l}")
    idf, idb = mk_ident(ph)
    LB = min(512, seqs[0].T)
    sm = lambda nm: ph.sb([128, 16], F32, nm)
    are, aim, ldt, dt, xr, ang, mag = sm("are"), sm("aim"), sm("ldt"), sm("dt"), sm("xr"), sm("ang"), sm("mag")
    kac, angr, angc, angcr, sn1, cs1, abr, abi = sm("kac"), sm("angr"), sm("angc"), sm("angcr"), sm("sn1"), sm("cs1"), sm("abr"), sm("abi")
    t1, t2, den, rdn, nr_, fre, fim, t3, t4 = sm("t1"), sm("t2"), sm("den"), sm("rdn"), sm("nr"), sm("fre"), sm("fim"), sm("t3"), sm("t4")
    wr, wi_, wt = sm("wr"), sm("wi"), sm("wt")
    ph.dma(I_dma(are.t[:], A["a_re"][l].rearrange("(pt q) -> q pt", q=128)), w=[are])
    ph.dma(I_dma(aim.t[:], A["a_im"][l].rearrange("(pt q) -> q pt", q=128)), w=[aim])
    for gl in range(2):
        src = bass.AP(A["log_dt"].tensor, l * 32 + gl, [[0, 64], [2, 16]])
        ph.dma(I_dma(ldt.t[gl * 64:(gl + 1) * 64, :], src), w=[ldt])
    V = "dve"
    pa, pb2, pc = sm("pa"), sm("pb2"), sm("pc")

    def horner(dst, y, coefs):
        ph.op(V, I_memset(dst.t[:], 1.0), w=[dst])
        for c in reversed(coefs):
            ph.op(V, I_tt(dst.t[:], dst.t[:], y.t[:], ALU.mult), r=[dst, y], w=[dst])
            ph.op(V, I_ts(dst.t[:], dst.t[:], float(c), 1.0, op0=ALU.mult, op1=ALU.add), r=[dst], w=[dst])

    def exp_acc(dst, src, nsq, deg):
        ph.op(V, I_ts(pa.t[:], src.t[:], 1.0 / (2 ** nsq), None, op0=ALU.mult), r=[src], w=[pa])
        horner(dst, pa, [1.0 / k for k in range(1, deg + 1)])
        for _ in range(nsq):
            ph.op(V, I_tt(dst.t[:], dst.t[:], dst.t[:], ALU.mult), r=[dst], w=[dst])

    def sincos_acc(sdst, cdst, x):
        ph.op(V, I_ts(pa.t[:], x.t[:], 0.125, None, op0=ALU.mult), r=[x], w=[pa])
        ph.op(V, I_tt(pb2.t[:], pa.t[:], pa.t[:], ALU.mult), r=[pa], w=[pb2])
        horner(sdst, pb2, [-1.0 / 6, -1.0 / 20, -1.0 / 42, -1.0 / 72, -1.0 / 110])
        ph.op(V, I_tt(sdst.t[:], sdst.t[:], pa.t[:], ALU.mult), r=[sdst, pa], w=[sdst])
        horner(cdst, pb2, [-1.0 / 2, -1.0 / 12, -1.0 / 30, -1.0 / 56, -1.0 / 90, -1.0 / 132])
        for _ in range(3):
            ph.op(V, I_tt(pc.t[:], sdst.t[:], sdst.t[:], ALU.mult), r=[sdst], w=[pc])
            ph.op(V, I_stt(sdst.t[:], sdst.t[:], 2.0, cdst.t[:], ALU.mult, ALU.mult), r=[sdst, cdst], w=[sdst])
            ph.op(V, I_ts(cdst.t[:], pc.t[:], -2.0, 1.0, op0=ALU.mult, op1=ALU.add), r=[pc], w=[cdst])

    exp_acc(dt, ldt, 3, 14)
    ph.op(V, I_tt(xr.t[:], are.t[:], dt.t[:], ALU.mult), r=[are, dt], w=[xr])
    ph.op(V, I_tt(ang.t[:], aim.t[:], dt.t[:], ALU.mult), r=[aim, dt], w=[ang])
    exp_acc(mag, xr, 0, 8)

    def reduce_angle(src, dst):
        ph.op(V, I_ts(kac.t[:], src.t[:], PI, None, op0=ALU.is_gt), r=[src], w=[kac])
        for j in range(1, 6):
            ph.op(V, I_stt(kac.t[:], src.t[:], (2 * j + 1) * PI, kac.t[:], ALU.is_gt, ALU.add), r=[src, kac], w=[kac])
        ph.op(V, I_stt(dst.t[:], kac.t[:], -2.0 * PI, src.t[:], ALU.mult, ALU.add), r=[kac, src], w=[dst])
    reduce_angle(ang, angr)
    sincos_acc(sn1, cs1, angr)
    ph.op(V, I_tt(t1.t[:], sn1.t[:], sn1.t[:], ALU.mult), r=[sn1], w=[t1])
    ph.op(V, I_tt(t2.t[:], cs1.t[:], cs1.t[:], ALU.mult), r=[cs1], w=[t2])
    ph.op(V, I_tt(t3.t[:], t1.t[:], t2.t[:], ALU.add), r=[t1, t2], w=[t3])
    ph.op(V, I_ts(t3.t[:], t3.t[:], -0.5, 1.5, op0=ALU.mult, op1=ALU.add), r=[t3], w=[t3])
    ph.op(V, I_tt(sn1.t[:], sn1.t[:], t3.t[:], ALU.mult), r=[sn1, t3], w=[sn1])
    ph.op(V, I_tt(cs1.t[:], cs1.t[:], t3.t[:], ALU.mult), r=[cs1, t3], w=[cs1])
    ph.op(V, I_tt(abr.t[:], mag.t[:], cs1.t[:], ALU.mult), r=[mag, cs1], w=[abr])
    ph.op(V, I_tt(abi.t[:], mag.t[:], sn1.t[:], ALU.mult), r=[mag, sn1], w=[abi])
    ph.op(V, I_tt(t1.t[:], are.t[:], are.t[:], ALU.mult), r=[are], w=[t1])
    ph.op(V, I_tt(t2.t[:], aim.t[:], aim.t[:], ALU.mult), r=[aim], w=[t2])
    ph.op(V, I_tt(den.t[:], t1.t[:], t2.t[:], ALU.add), r=[t1, t2], w=[den])
    ph.op(V, I_recip(rdn.t[:], den.t[:]), r=[den], w=[rdn])
    ph.op(V, I_ts(nr_.t[:], abr.t[:], -1.0, None, op0=ALU.add), r=[abr], w=[nr_])
    ph.op(V, I_tt(t1.t[:], nr_.t[:], are.t[:], ALU.mult), r=[nr_, are], w=[t1])
    ph.op(V, I_tt(t2.t[:], abi.t[:], aim.t[:], ALU.mult), r=[abi, aim], w=[t2])
    ph.op(V, I_tt(t3.t[:], t1.t[:], t2.t[:], ALU.add), r=[t1, t2], w=[t3])
    ph.op(V, I_tt(fre.t[:], t3.t[:], rdn.t[:], ALU.mult), r=[t3, rdn], w=[fre])
    ph.op(V, I_tt(t1.t[:], abi.t[:], are.t[:], ALU.mult), r=[abi, are], w=[t1])
    ph.op(V, I_tt(t2.t[:], nr_.t[:], aim.t[:], ALU.mult), r=[nr_, aim], w=[t2])
    ph.op(V, I_tt(t4.t[:], t1.t[:], t2.t[:], ALU.subtract), r=[t1, t2], w=[t4])
    ph.op(V, I_tt(fim.t[:], t4.t[:], rdn.t[:], ALU.mult), r=[t4, rdn], w=[fim])

    bre_t = ph.sb([128, 16, 16], F32, "bre_t"); bim_t = ph.sb([128, 16, 16], F32, "bim_t")
    cre_t = ph.sb([128, 16, 16], F32, "cre_t"); cim_t = ph.sb([128, 16, 16], F32, "cim_t")
    for t_, nm in ((bre_t, "b_re"), (bim_t, "b_im"), (cre_t, "c_reT"), (cim_t, "c_imT")):
        ph.dma(I_dma(t_.t[:], A[nm][l].rearrange("(pt q) n -> q pt n", q=128)), w=[t_])
    padall = ph.sb([128, 16, 2, 128], F32, "padall")
    Cpad = ph.sb([128, 16, 2, 128], BF16, "Cpad")
    ph.op("pool", I_memset(padall.t[:], 0.0), w=[padall])
    ph.op("pool", I_memset(Cpad.t[:], 0.0), w=[Cpad])
    tb16 = [ph.sb([128, 16], F32, f"tb16{i}") for i in range(2)]
    for pt in range(16):
        qq = pt % 4
        for half in range(2):
            rows = slice(half * 64, (half + 1) * 64)
            cols = slice(32 * qq + 16 * half, 32 * qq + 16 * half + 16)
            ta, tb = tb16
            ph.op(V, I_ts(ta.t[rows, :], bim_t.t[rows, pt, :], fim.t[rows, pt:pt + 1], None, op0=ALU.mult), r=[bim_t, fim], w=[ta])
            ph.op(V, I_stt(padall.t[rows, pt, 0, cols], bre_t.t[rows, pt, :], fre.t[rows, pt:pt + 1], ta.t[rows, :], ALU.mult, ALU.subtract),
                  r=[bre_t, fre, ta], w=[padall])
            ph.op(V, I_ts(tb.t[rows, :], bre_t.t[rows, pt, :], fim.t[rows, pt:pt + 1], None, op0=ALU.mult), r=[bre_t, fim], w=[tb])
            ph.op(V, I_stt(padall.t[rows, pt, 1, cols], bim_t.t[rows, pt, :], fre.t[rows, pt:pt + 1], tb.t[rows, :], ALU.mult, ALU.add),
                  r=[bim_t, fre, tb], w=[padall])
            ph.op("pool", I_copy(Cpad.t[rows, pt, 0, cols], cre_t.t[rows, pt, :]), r=[cre_t], w=[Cpad])
            ph.op("pool", I_ts(Cpad.t[rows, pt, 1, cols], cim_t.t[rows, pt, :], -1.0, None, op0=ALU.mult), r=[cim_t], w=[Cpad])
    pB = [ph.ps([128, 512], F32, f"pB{i}") for i in range(4)]
    pY = [ph.ps([128, 512], F32, f"pY{i}") for i in range(2)]
    pG = ph.ps([128, 512], F32, "pG")
    E = [[ph.sb([128, 128], BF16, f"E{part}_{pt}") for pt in range(16)] for part in range(2)]
    for part in range(2):
        for pt in range(16):
            ph.op("pe", I_mm(pG.t[:, 0:128], padall.t[:, pt, part, :], idf.t[:, :]), r=[padall, idf], w=[pG])
            ph.op("act", I_act(E[part][pt].t[:], pG.t[:, 0:128], AF.Copy), r=[pG], w=[E[part][pt]])
    dsk = ph.sb([128, 4], F32, "dsk")
    ph.dma(I_dma(dsk.t[:], A["d_skip"][l].rearrange("(ct q) -> q ct", q=128)), w=[dsk])
    bglu = ph.sb([128, 4], F32, "bglu")
    ph.dma(I_dma(bglu.t[:], A["b_glu"][l].rearrange("(co q) -> q co", q=128)), w=[bglu])
    wglu = ph.sb([128, 4, 512], BF16, "wglu")
    for c in range(4):
        ph.dma(I_dma(wglu.t[:, c, :], A["w_glu"][l, c * 128:(c + 1) * 128, :]), w=[wglu], q="pool")
    cs = ph.sb([128, 16, LB], F32, "cs"); sn = ph.sb([128, 16, LB], F32, "sn")
    ph.op("pool", I_memset(cs.t[:, :, 0:1], 1.0), w=[cs])
    ph.op("pool", I_memset(sn.t[:, :, 0:1], 0.0), w=[sn])
    ph.op(V, I_copy(wr.t[:], cs1.t[:]), r=[cs1], w=[wr])
    ph.op(V, I_copy(wi_.t[:], sn1.t[:]), r=[sn1], w=[wi_])
    tmpc = ph.sb([128, LB], F32, "tmpc"); tmps = ph.sb([128, LB], F32, "tmps")
    n = 1
    while n < LB:
        for pt in range(16):
            ph.op(V, I_ts(tmpc.t[:, 0:n], sn.t[:, pt, 0:n], wi_.t[:, pt:pt + 1], None, op0=ALU.mult), r=[sn, wi_], w=[tmpc])
            ph.op("pool", I_ts(tmps.t[:, 0:n], cs.t[:, pt, 0:n], wi_.t[:, pt:pt + 1], None, op0=ALU.mult), r=[cs, wi_], w=[tmps])
            ph.op(V, I_stt(cs.t[:, pt, n:2 * n], cs.t[:, pt, 0:n], wr.t[:, pt:pt + 1], tmpc.t[:, 0:n], ALU.mult, ALU.subtract),
                  r=[cs, wr, tmpc], w=[cs])
            ph.op(V, I_stt(sn.t[:, pt, n:2 * n], sn.t[:, pt, 0:n], wr.t[:, pt:pt + 1], tmps.t[:, 0:n], ALU.mult, ALU.add),
                  r=[sn, wr, tmps], w=[sn])
        ph.op(V, I_tt(t1.t[:], wi_.t[:], wi_.t[:], ALU.mult), r=[wi_], w=[t1])
        ph.op(V, I_tt(t2.t[:], wr.t[:], wr.t[:], ALU.mult), r=[wr], w=[t2])
        ph.op(V, I_stt(wt.t[:], wr.t[:], 2.0, wi_.t[:], ALU.mult, ALU.mult), r=[wr, wi_], w=[wt])
        ph.op(V, I_tt(wr.t[:], t2.t[:], t1.t[:], ALU.subtract), r=[t1, t2], w=[wr])
        ph.op(V, I_copy(wi_.t[:], wt.t[:]), r=[wt], w=[wi_])
        n *= 2

    wk = lambda nm: ph.sb([128, LB], F32, nm)
    m1, m2, m3, m4, bre_, bim_, gre, gim, hre, him = [wk(n_) for n_ in ("m1", "m2", "m3", "m4", "bre", "bim", "gre", "gim", "hre", "him")]
    hreb = ph.sb([128, LB], BF16, "hreb"); himb = ph.sb([128, LB], BF16, "himb")
    yv, x2, inn, in2, sg = [wk(n_) for n_ in ("yv", "x2", "inn", "in2", "sg")]
    yg = [wk(f"yg{i}") for i in range(4)]
    ygb = [ph.sb([128, LB], BF16, f"ygb{i}") for i in range(4)]
    ut = [[ph.sb([128, LB], BF16, f"ut{k}{i}") for i in range(4)] for k in range(2)]
    zst = [ph.sb([128, LB], BF16, f"zst{i}") for i in range(2)]
    sgl = wk("sgl"); tg = wk("tg")
    ysf = [ph.sb([128, LB], BF16, f"ysf{i}") for i in range(2)]
    hpr = [sm(f"hpr{i}") for i in range(3)]; hpi = [sm(f"hpi{i}") for i in range(3)]
    a0 = ph.sb([128, 4], F32, "a0")
    nb = 0
    nz = 0
    for s in seqs:
        Lb = min(LB, s.T)
        hr_, hi_ = hpr[s.i], hpi[s.i]
        prev = s.i > 0
        if prev:
            ph.dma(I_dma(hr_.t[:], A["ssr"][l, s.i - 1].rearrange("(pt q) -> q pt", q=128)), w=[hr_])
            ph.dma(I_dma(hi_.t[:], A["ssi"][l, s.i - 1].rearrange("(pt q) -> q pt", q=128)), w=[hi_])
        for blk in range(s.T // Lb):
            t0 = blk * Lb
            u_ = ut[blk % 2]
            for ct in range(4):
                ph.dma(I_dma(u_[ct].t[:, :Lb], s.uT[ct * 128:(ct + 1) * 128, t0:t0 + Lb]), w=[u_[ct]])
            for ct in range(4):
                py = pY[ct % 2]
                for qq in range(4):
                    pt = ct * 4 + qq
                    pr = pB[(nb % 2) * 2]; pi_ = pB[(nb % 2) * 2 + 1]; nb += 1
                    ph.op("pe", I_mm(pr.t[:, :Lb], E[0][pt].t[:, :], u_[ct].t[:, :Lb]), r=[E[0][pt], u_[ct]], w=[pr])
                    ph.op("pe", I_mm(pi_.t[:, :Lb], E[1][pt].t[:, :], u_[ct].t[:, :Lb]), r=[E[1][pt], u_[ct]], w=[pi_])
                    c_ = cs.t[:, pt, :Lb]; s_ = sn.t[:, pt, :Lb]
                    ph.op(V, I_tt(m1.t[:, :Lb], pr.t[:, :Lb], c_, ALU.mult), r=[pr, cs], w=[m1])
                    ph.op(V, I_tt(m2.t[:, :Lb], pi_.t[:, :Lb], s_, ALU.mult), r=[pi_, sn], w=[m2])
                    ph.op(V, I_tt(m3.t[:, :Lb], pi_.t[:, :Lb], c_, ALU.mult), r=[pi_, cs], w=[m3])
                    ph.op(V, I_tt(m4.t[:, :Lb], pr.t[:, :Lb], s_, ALU.mult), r=[pr, sn], w=[m4])
                    ph.op("pool", I_tt(bre_.t[:, :Lb], m1.t[:, :Lb], m2.t[:, :Lb], ALU.add), r=[m1, m2], w=[bre_])
                    ph.op("pool", I_tt(bim_.t[:, :Lb], m3.t[:, :Lb], m4.t[:, :Lb], ALU.subtract), r=[m3, m4], w=[bim_])
                    if prev:
                        ph.op(V, I_tt(a0.t[:, 0:1], abi.t[:, pt:pt + 1], hi_.t[:, pt:pt + 1], ALU.mult), r=[abi, hi_], w=[a0])
                        ph.op(V, I_stt(a0.t[:, 1:2], abr.t[:, pt:pt + 1], hr_.t[:, pt:pt + 1], a0.t[:, 0:1], ALU.mult, ALU.subtract),
                              r=[abr, hr_, a0], w=[a0])
                        ph.op(V, I_tt(a0.t[:, 2:3], abi.t[:, pt:pt + 1], hr_.t[:, pt:pt + 1], ALU.mult), r=[abi, hr_], w=[a0])
                        ph.op(V, I_stt(a0.t[:, 3:4], abr.t[:, pt:pt + 1], hi_.t[:, pt:pt + 1], a0.t[:, 2:3], ALU.mult, ALU.add),
                              r=[abr, hi_, a0], w=[a0])
                        ph.op(V, I_tt(bre_.t[:, 0:1], bre_.t[:, 0:1], a0.t[:, 1:2], ALU.add), r=[bre_, a0], w=[bre_])
                        ph.op(V, I_tt(bim_.t[:, 0:1], bim_.t[:, 0:1], a0.t[:, 3:4], ALU.add), r=[bim_, a0], w=[bim_])
                    rb = mag.t[:, pt:pt + 1].to_broadcast([128, Lb])
                    ph.op(V, I_scan(gre.t[:, :Lb], rb, bre_.t[:, :Lb], 0.0), r=[mag, bre_], w=[gre])
                    ph.op(V, I_scan(gim.t[:, :Lb], rb, bim_.t[:, :Lb], 0.0), r=[mag, bim_], w=[gim])
                    ph.op(V, I_tt(m1.t[:, :Lb], gre.t[:, :Lb], c_, ALU.mult), r=[gre, cs], w=[m1])
                    ph.op(V, I_tt(m2.t[:, :Lb], gim.t[:, :Lb], s_, ALU.mult), r=[gim, sn], w=[m2])
                    ph.op("pool", I_tt(m3.t[:, :Lb], gre.t[:, :Lb], s_, ALU.mult), r=[gre, sn], w=[m3])
                    ph.op("pool", I_tt(m4.t[:, :Lb], gim.t[:, :Lb], c_, ALU.mult), r=[gim, cs], w=[m4])
                    ph.op(V, I_tt(hre.t[:, :Lb], m1.t[:, :Lb], m2.t[:, :Lb], ALU.subtract), r=[m1, m2], w=[hre])
                    ph.op("pool", I_tt(him.t[:, :Lb], m3.t[:, :Lb], m4.t[:, :Lb], ALU.add), r=[m3, m4], w=[him])
                    ph.op("act", I_act(hr_.t[:, pt:pt + 1], hre.t[:, Lb - 1:Lb], AF.Copy), r=[hre], w=[hr_])
                    ph.op("act", I_act(hi_.t[:, pt:pt + 1], him.t[:, Lb - 1:Lb], AF.Copy), r=[him], w=[hi_])
                    ph.op("act", I_act(hreb.t[:, :Lb], hre.t[:, :Lb], AF.Copy), r=[hre], w=[hreb])
                    ph.op("act", I_act(himb.t[:, :Lb], him.t[:, :Lb], AF.Copy), r=[him], w=[himb])
                    ph.op("pe", I_mm(py.t[:, :Lb], Cpad.t[:, pt, 0, :], hreb.t[:, :Lb], start=(qq == 0), stop=False), r=[Cpad, hreb], w=[py])
                    ph.op("pe", I_mm(py.t[:, :Lb], Cpad.t[:, pt, 1, :], himb.t[:, :Lb], start=False, stop=(qq == 3)), r=[Cpad, himb], w=[py])
                ph.op(V, I_stt(yv.t[:, :Lb], u_[ct].t[:, :Lb], dsk.t[:, ct:ct + 1], py.t[:, :Lb], ALU.mult, ALU.add), r=[u_[ct], dsk, py], w=[yv])
                ph.op("pool", I_tt(x2.t[:, :Lb], yv.t[:, :Lb], yv.t[:, :Lb], ALU.mult), r=[yv], w=[x2])
                ph.op("pool", I_ts(inn.t[:, :Lb], x2.t[:, :Lb], 0.044715, 1.0, op0=ALU.mult, op1=ALU.add), r=[x2], w=[inn])
                ph.op("pool", I_tt(in2.t[:, :Lb], inn.t[:, :Lb], yv.t[:, :Lb], ALU.mult), r=[inn, yv], w=[in2])
                ph.op("act", I_act(sg.t[:, :Lb], in2.t[:, :Lb], AF.Sigmoid, scale=1.5957691216057308), r=[in2], w=[sg])
                ph.op(V, I_tt(yg[ct].t[:, :Lb], yv.t[:, :Lb], sg.t[:, :Lb], ALU.mult), r=[yv, sg], w=[yg[ct]])
                ph.op("act", I_act(ygb[ct].t[:, :Lb], yg[ct].t[:, :Lb], AF.Copy), r=[yg[ct]], w=[ygb[ct]])
            prev = True
            for co in range(4):
                z_ = zst[nz % 2]; yf = ysf[nz % 2]; nz += 1
                for ct in range(4):
                    ph.op("pe", I_mm(pG.t[:, :Lb], wglu.t[:, ct, co * 128:(co + 1) * 128], ygb[ct].t[:, :Lb], start=(ct == 0), stop=(ct == 3)),
                          r=[wglu, ygb[ct]], w=[pG])
                ph.op("act", I_act(sgl.t[:, :Lb], pG.t[:, :Lb], AF.Sigmoid, bias=bglu.t[:, co:co + 1]), r=[pG, bglu], w=[sgl])
                ph.dma(I_dma(z_.t[:, :Lb], s.zsT[co * 128:(co + 1) * 128, t0:t0 + Lb]), w=[z_])
                ph.op(V, I_tt(tg.t[:, :Lb], yg[co].t[:, :Lb], sgl.t[:, :Lb], ALU.mult), r=[yg[co], sgl], w=[tg])
                ph.op("pool", I_tt(yf.t[:, :Lb], tg.t[:, :Lb], z_.t[:, :Lb], ALU.mult), r=[tg, z_], w=[yf])
                ph.dma(I_dma(s.ysT[co * 128:(co + 1) * 128, t0:t0 + Lb], yf.t[:, :Lb]), r=[yf])
        ph.dma(I_dma(s.srout[l].rearrange("(pt q) -> q pt", q=128), hr_.t[:]), r=[hr_])
        ph.dma(I_dma(s.siout[l].rearrange("(pt q) -> q pt", q=128), hi_.t[:]), r=[hi_])
    ph.finish()


def bc_mid(ap, n):
    return bass.AP(ap.tensor, ap.offset, [list(ap.ap[0]), [0, n]] + [list(x) for x in ap.ap[1:]])


NIT2 = 20
W02 = 16.0


def phase_attn2(nc, A, l, s):
    ph = Phase(nc, f"B{l}{s.nm}")
    idf, idb = mk_ident(ph)
    T, S = s.T, s.S
    QP = min(128, T); QB = min(256, T)
    TPB = QB // QP
    ntile = T // QP; nblk = T // QB
    L = T if s.causal else S
    KSEL = float(min(TOPK, L // 4))
    ktiles = [(k0, min(128, S - k0)) for k0 in range(0, S, 128)]
    NKT = len(ktiles)
    kiT2 = ph.sb([128, S], BF16, "kiT2")
    ph.dma(I_dma(kiT2.t[0:64, :], s.kiT[:, :]), w=[kiT2])
    ph.dma(I_dma(kiT2.t[64:128, :], s.kiT[:, :]), w=[kiT2])
    sc = [ph.sb([128, S], F32, f"sc{i}") for i in range(2)]
    junk = ph.sb([128, S], BF16, "junk")
    maskT = [ph.sb([128, NKT, QB], BF16, f"maskT{i}") for i in range(2)]
    qia = [ph.sb([128, 4, 128], BF16, f"qia{i}") for i in range(2)]
    qib = [ph.sb([128, 4, 128], BF16, f"qib{i}") for i in range(2)]
    qta = [ph.sb([128, 4, 256], BF16, f"qta{i}") for i in range(2)]
    qtb = [ph.sb([128, 4, 256], BF16, f"qtb{i}") for i in range(2)]
    for i in range(2):
        ph.op("pool", I_memset(qia[i].t[64:128, :, :], 0.0), w=[qia[i]])
        ph.op("pool", I_memset(qib[i].t[0:64, :, :], 0.0), w=[qib[i]])
        ph.op("pool", I_memset(qta[i].t[64:128, :, :], 0.0), w=[qta[i]])
        ph.op("pool", I_memset(qtb[i].t[0:64, :, :], 0.0), w=[qtb[i]])
    wit = [ph.sb([128, 8], F32, f"wit{i}") for i in range(2)]
    dg = [ph.sb([128, 8, 128], BF16, f"dg{i}") for i in range(2)]
    rl = [ph.sb([128, 512], BF16, f"rl{i}") for i in range(3)]
    sm1 = lambda nm: ph.sb([128, 1], F32, nm)
    mid, negmid, cnt, sact, comb, tq, thr = [sm1(n_) for n_ in ("mid", "negmid", "cnt", "sact", "comb", "tq", "thr")]
    dgthr = ph.sb([128, 128], F32, "dgthr")
    thrbc = ph.sb([128, 128], F32, "thrbc")
    onesf = ph.sb([128, 128], F32, "onesf")
    ph.op("pool", I_memset(onesf.t[:], 1.0), w=[onesf])
    va = [ph.sb([128, 130], BF16, f"va{i}") for i in range(3)]
    ktl = [ph.sb([128, 128], BF16, f"ktl{i}") for i in range(3)]
    pe_ = [ph.sb([128, 2, 256], BF16, f"pe{i}") for i in range(2)]
    pm = [ph.sb([128, 2, 256], BF16, f"pm{i}") for i in range(2)]
    oa = [ph.sb([128, 256], F32, f"oa{i}") for i in range(2)]
    for o__ in oa:
        ph.op("pool", I_memset(o__.t[:], 0.0), w=[o__])
    rden = [ph.sb([64, 256], F32, f"rden{i}") for i in range(2)]
    t1 = [ph.sb([64, 256], F32, f"t1{i}") for i in range(2)]
    zat = [ph.sb([64, 256], BF16, f"zat{i}") for i in range(2)]
    yat = [ph.sb([64, 256], BF16, f"yat{i}") for i in range(2)]
    iop = ph.sb([128, 64], F32, "iop")
    selden = ph.sb([128, 64], F32, "selden")
    ph.op("pool", I_iota(iop.t[:], [[0, 64]], 0, 1), w=[iop])
    ph.op("dve", I_ts(selden.t[:], iop.t[:], 64.0, None, op0=ALU.is_equal), r=[iop], w=[selden])
    lg = [ph.ps([128, 512], F32, f"lg{i}") for i in range(2)]
    acc = ph.ps([128, 512], F32, "acc")
    st = ph.ps([128, 2, 256], F32, "st")
    oacc = [ph.ps([128, 512], F32, f"oacc{i}") for i in range(2)]
    scTp = ph.ps([128, 4, 128], F32, "scTp")
    misc = ph.ps([128, 512], F32, "misc")
    cnts = {"r": 0, "v": 0, "h": 0, "e": 0}

    def tile_info(i):
        q0 = i * QP
        Slim = (q0 + QP) if s.causal else S
        return q0, Slim

    def blk_kts(b):
        Sblk = min(S, (b + 1) * QB) if s.causal else S
        return [(i, k0, ksz) for i, (k0, ksz) in enumerate(ktiles) if k0 < Sblk]

    def gen_index(i):
        q0, Slim = tile_info(i)
        sl = i % 2
        qa, qb_, wi_, dg_, sc_ = qia[sl], qib[sl], wit[sl], dg[sl], sc[sl]
        qsrc = s.qiT.rearrange("(hp p) t -> p hp t", p=128)
        ph.dma(I_dma(qa.t[0:64, :, :QP], qsrc[0:64, :, q0:q0 + QP]), w=[qa])
        ph.dma(I_dma(qb_.t[64:128, :, :QP], qsrc[64:128, :, q0:q0 + QP]), w=[qb_])
        ph.dma(I_dma(wi_.t[:QP, :], s.wiS[q0:q0 + QP, :]), w=[wi_])
        for h in range(8):
            ph.op("dve", I_ts(dg_.t[:QP, h, :QP], idf.t[:QP, :QP], wi_.t[:QP, h:h + 1], None, op0=ALU.mult), r=[idf, wi_], w=[dg_])
        yield
        for c0 in range(0, Slim, 512):
            csz = min(512, Slim - c0)
            for h in range(8):
                g = lg[cnts["r"] % 2]; r_ = rl[cnts["r"] % 3]; cnts["r"] += 1
                qz = qa if h % 2 == 0 else qb_
                ph.op("pe", I_mm(g.t[:QP, :csz], qz.t[:, h // 2, :QP], kiT2.t[:, c0:c0 + csz]), r=[qz, kiT2], w=[g])
                if h % 4 == 3:
                    ph.op("dve", I_ts(r_.t[:QP, :csz], g.t[:QP, :csz], 0.0, None, op0=ALU.max), r=[g], w=[r_])
                else:
                    ph.op("act", I_act(r_.t[:QP, :csz], g.t[:QP, :csz], AF.Relu), r=[g], w=[r_])
                ph.op("pe", I_mm(acc.t[:QP, :csz], dg_.t[:QP, h, :QP], r_.t[:QP, :csz], start=(h == 0), stop=(h == 7)), r=[dg_, r_], w=[acc])
            ph.op("dve", I_copy(sc_.t[:QP, c0:c0 + csz], acc.t[:QP, :csz]), r=[acc], w=[sc_])
            yield
        if s.causal and QP == 128:
            ph.op("pool", I_memset(sc_.t[0:64, Slim - 64:Slim], NEG), w=[sc_])

    def n_index(i):
        q0, Slim = tile_info(i)
        return 1 + (Slim + 511) // 512

    def gen_bisect(i):
        q0, Slim = tile_info(i)
        sc_ = sc[i % 2]
        b = i // TPB
        qoff = (i % TPB) * QP
        mT = maskT[b % 2]
        S1 = Slim
        if Slim >= 1024:
            S1 = int(Slim * 0.45) // 128 * 128
        ph.op("dve", I_memset(mid.t[:QP, :], 0.0), w=[mid])
        ph.op("dve", I_memset(negmid.t[:QP, :], 0.0), w=[negmid])
        w = W02
        for k in range(NIT2):
            w = w / 2.0
            ph.op("dve", I_ts(junk.t[:QP, :S1], sc_.t[:QP, :S1], mid.t[:QP, 0:1], None, op0=ALU.is_ge, op1=ALU.add,
                              accum_out=cnt.t[:QP, 0:1]), r=[sc_, mid], w=[cnt])
            if S1 < Slim:
                ph.op("act", I_act(junk.t[:QP, S1:Slim], sc_.t[:QP, S1:Slim], AF.Sign, bias=negmid.t[:QP, 0:1],
                                   accum_out=sact.t[:QP, 0:1]), r=[sc_, negmid], w=[sact])
                ph.op("dve", I_stt(comb.t[:QP, :], sact.t[:QP, :], 0.5, cnt.t[:QP, :], ALU.mult, ALU.add), r=[sact, cnt], w=[comb])
                src, kadj = comb, KSEL - 0.5 * (Slim - S1)
            else:
                src, kadj = cnt, KSEL
            ph.op("dve", I_ts(tq.t[:QP, :], src.t[:QP, :], kadj, 2.0 * w, op0=ALU.is_ge, op1=ALU.mult), r=[src], w=[tq])
            ph.op("dve", I_stt(mid.t[:QP, :], mid.t[:QP, :], -w, tq.t[:QP, :], ALU.add, ALU.add), r=[mid, tq], w=[mid])
            if S1 < Slim:
                ph.op("dve", I_stt(negmid.t[:QP, :], negmid.t[:QP, :], w, tq.t[:QP, :], ALU.add, ALU.subtract), r=[negmid, tq], w=[negmid])
            yield
        ph.op("dve", I_ts(thr.t[:QP, :], mid.t[:QP, :], -w, None, op0=ALU.add), r=[mid], w=[thr])
        ph.op("dve", I_ts(dgthr.t[:QP, :QP], idf.t[:QP, :QP], thr.t[:QP, 0:1], None, op0=ALU.mult), r=[idf, thr], w=[dgthr])
        ph.op("pe", I_mm(misc.t[:, 0:QP], onesf.t[:QP, :], dgthr.t[:QP, :QP]), r=[onesf, dgthr], w=[misc])
        ph.op("act", I_act(thrbc.t[:, :QP], misc.t[:, 0:QP], AF.Copy), r=[misc], w=[thrbc])
        yield
        kts = blk_kts(b)
        mine = [(i_, k0, ksz) for (i_, k0, ksz) in kts if k0 < Slim]
        for g0 in range(0, len(mine), 4):
            grp = mine[g0:g0 + 4]
            for gi, (i_, k0, ksz) in enumerate(grp):
                ph.op("pe", I_tr(scTp.t[:ksz, gi, :QP], sc_.t[:QP, k0:k0 + ksz], idf.t[:QP, :QP]), r=[sc_, idf], w=[scTp])
            i0 = grp[0][0]; ng = len(grp)
            ph.op("dve", I_tt(mT.t[:, i0:i0 + ng, qoff:qoff + QP], scTp.t[:, 0:ng, :QP], bc_mid(thrbc.t[:, 0:QP], ng), ALU.is_ge),
                  r=[scTp, thrbc], w=[mT])
            yield
        rest = [(i_, k0, ksz) for (i_, k0, ksz) in kts if k0 >= Slim]
        if rest:
            i0 = rest[0][0]; i1 = rest[-1][0] + 1
            ph.op("pool", I_memset(mT.t[:, i0:i1, qoff:qoff + QP], 0.0), w=[mT])

    def n_bisect(i):
        q0, Slim = tile_info(i)
        nm = len([1 for (k0, ksz) in ktiles if k0 < Slim])
        return NIT2 + 1 + (nm + 3) // 4

    import os as _os4
    _stop = _os4.environ.get("ATTN2_STOP", "")

    def gen_attn(b):
        qb0 = b * QB
        kts = blk_kts(b)
        mT = maskT[b % 2]
        qa, qb_ = qta[b % 2], qtb[b % 2]
        qsrc = s.qT.rearrange("(hp p) t -> p hp t", p=128)
        ph.dma(I_dma(qa.t[0:64, :, :QB], qsrc[0:64, :, qb0:qb0 + QB]), w=[qa])
        ph.dma(I_dma(qb_.t[64:128, :, :QB], qsrc[64:128, :, qb0:qb0 + QB]), w=[qb_])
        for hg in range(4):
            for idx, (i_, k0, ksz) in enumerate(kts):
                va_ = va[cnts["v"] % 3]; kt_ = ktl[cnts["v"] % 3]; cnts["v"] += 1
                p_ = pe_[cnts["h"] % 2]; m_ = pm[cnts["h"] % 2]; cnts["h"] += 1
                ph.dma(I_dma(va_.t[:ksz, :], s.vaug[k0:k0 + ksz, hg * 130:(hg + 1) * 130]), w=[va_])
                ph.dma(I_dma(kt_.t[:, :ksz], s.kT[hg * 128:(hg + 1) * 128, k0:k0 + ksz]), w=[kt_])
                ph.op("pe", I_mm(st.t[:ksz, 0, :QB], kt_.t[:, :ksz], qa.t[:, hg, :QB]), r=[kt_, qa], w=[st])
                ph.op("pe", I_mm(st.t[:ksz, 1, :QB], kt_.t[:, :ksz], qb_.t[:, hg, :QB]), r=[kt_, qb_], w=[st])
                if _stop == "st":
                    yield
                    continue
                ph.op("act", I_act(p_.t[:ksz, :, :QB], st.t[:ksz, :, :QB], AF.Exp), r=[st], w=[p_])
                if _stop == "exp":
                    yield
                    continue
                ph.op("dve", I_tt(m_.t[:ksz, :, :QB], p_.t[:ksz, :, :QB], bc_mid(mT.t[:ksz, i_, :QB], 2), ALU.mult), r=[p_, mT], w=[m_])
                if _stop == "mul":
                    yield
                    continue
                for hh in range(2):
                    ph.op("pe", I_mm(oacc[hh].t[:65, :QB], va_.t[:ksz, hh * 65:(hh + 1) * 65], m_.t[:ksz, hh, :QB],
                                     start=(idx == 0), stop=(idx == len(kts) - 1)), r=[va_, m_], w=[oacc[hh]])
                yield
            if _stop in ("st", "exp", "mul", "pv"):
                yield
                continue
            for hh in range(2):
                h = hg * 2 + hh
                e = cnts["e"] % 2; cnts["e"] += 1
                o_ = oa[e]; rd = rden[e]; t_ = t1[e]; z_ = zat[e]; y_ = yat[e]
                ph.op("act", I_act(o_.t[:65, :QB], oacc[hh].t[:65, :QB], AF.Copy), r=[oacc[hh]], w=[o_])
                if _stop == "ep1":
                    continue
                ph.op("pe", I_mm(acc.t[:64, 0:QB], selden.t[:, :64], o_.t[:, :QB]), r=[selden, o_], w=[acc])
                if _stop == "ep2":
                    continue
                ph.op("dve", I_recip(rd.t[:64, :QB], acc.t[:64, 0:QB]), r=[acc], w=[rd])
                if _stop == "ep3":
                    continue
                ph.dma(I_dma(z_.t[:64, :QB], s.zaT[64 * h:64 * h + 64, qb0:qb0 + QB]), w=[z_])
                ph.op("dve", I_tt(t_.t[:64, :QB], o_.t[0:64, :QB], rd.t[:64, :QB], ALU.mult), r=[o_, rd], w=[t_])
                if _stop == "ep4":
                    continue
                ph.op("pool", I_tt(y_.t[:64, :QB], t_.t[:64, :QB], z_.t[:64, :QB], ALU.mult), r=[t_, z_], w=[y_])
                if _stop == "ep5":
                    continue
                ph.dma(I_dma(s.yaT[64 * h:64 * h + 64, qb0:qb0 + QB], y_.t[:64, :QB]), r=[y_], q="act")
            yield

    def n_attn(b):
        return 4 * (len(blk_kts(b)) + 1)

    attn_state = {}
    nslot = ntile + 1 + TPB + 1
    for slot in range(nslot + 2 * TPB):
        work = []
        if slot < ntile:
            work.append([gen_index(slot), n_index(slot)])
        import os as _os3
        _stop = _os3.environ.get("ATTN2_STOP", "")
        if 1 <= slot <= ntile and _stop != "index":
            work.append([gen_bisect(slot - 1), n_bisect(slot - 1)])
        for b in range(nblk):
            ready = (b + 1) * TPB + 1
            if slot == ready and _stop not in ("index", "bisect"):
                attn_state[b] = [gen_attn(b), n_attn(b), 0]
        for b, stt_ in list(attn_state.items()):
            g, tot, done = stt_
            share = (tot + TPB - 1) // TPB
            work.append([g, min(share, tot - done) + 1])
            stt_[2] = done + share
            if stt_[2] >= tot:
                del attn_state[b]
        if not work:
            continue
        maxc = max(c for _, c in work)
        done_c = [0] * len(work)
        alive = [True] * len(work)
        for k in range(maxc):
            for wi_x, (g, c) in enumerate(work):
                while alive[wi_x] and done_c[wi_x] < c and (done_c[wi_x] + 1) * maxc <= (k + 1) * c:
                    try:
                        next(g)
                    except StopIteration:
                        alive[wi_x] = False
                    done_c[wi_x] += 1
        for wi_x, (g, c) in enumerate(work[: (1 if slot < ntile else 0) + (1 if (1 <= slot <= ntile and _stop != "index") else 0)]):
            if alive[wi_x]:
                for _ in g:
                    pass
    for b, stt_ in list(attn_state.items()):
        for _ in stt_[0]:
            pass
    ph.finish()
```

```python
import numpy as np
from contextlib import ExitStack
import concourse.bass as bass
import concourse.mybir as mybir
from concourse.bass_utils import run_bass_kernel_spmd

F32 = mybir.dt.float32
BF16 = mybir.dt.bfloat16
I32 = mybir.dt.int32
AF = mybir.ActivationFunctionType
ALU = mybir.AluOpType

D = 1024
DIN = 5704
NL = 2
EPS = 1e-6
OFF_U, OFF_ZS, OFF_Q, OFF_K, OFF_V, OFF_ZA, OFF_QI, OFF_KI, OFF_WI, OFF_GM = (
    0, 512, 1024, 1536, 2048, 2560, 3072, 3584, 3648, 3656)
NEG = -1.0e30
TOPK = 256


class TT:
    __slots__ = ("t", "name", "w", "r", "dsem", "dtot")

    def __init__(self, t, name):
        self.t = t
        self.name = name
        self.w = None
        self.r = []
        self.dsem = None
        self.dtot = 0

    def __getitem__(self, idx):
        return self.t[idx]


_SEMPOOL = {}
_SEMSTACK = []


class Phase:
    ENGS = ("sync", "pool", "act", "dve", "pe")

    def __init__(self, nc, name):
        self.nc = nc
        self.name = name
        self.es = ExitStack()
        self.ops = {e: [] for e in self.ENGS}
        self.cnt = {e: 0 for e in self.ENGS}
        self.known = {e: {} for e in self.ENGS}
        self.sems = {}
        self.semobj = {}
        self.nsem = 0
        for e in ("pool", "act", "dve", "pe"):
            s = self._newsem(f"{name}_{e}")
            self.sems[e] = s
            self.semobj[("c", e)] = s
        self.ntile = 0
        self.dma_tiles = []
        self.misc = None

    def _newsem(self, nm):
        pool = _SEMPOOL.get(id(self.nc))
        if pool is None:
            return self.es.enter_context(self.nc.semaphore(nm))
        sm_ = pool[self.nsem]
        self.nsem += 1
        return sm_

    def sb(self, shape, dtype, name=None):
        self.ntile += 1
        nm = f"{self.name}_{name or 't'}{self.ntile}"
        t = self.es.enter_context(self.nc.sbuf_tensor(nm, list(shape), dtype))
        return TT(t, nm)

    def ps(self, shape, dtype, name=None):
        self.ntile += 1
        nm = f"{self.name}_{name or 'p'}{self.ntile}"
        t = self.es.enter_context(self.nc.psum_tensor(nm, list(shape), dtype))
        return TT(t, nm)

    def _dsem(self, tt):
        if tt.dsem is None:
            tt.dsem = self._newsem(f"d_{tt.name}")
            self.semobj[("d", tt.name)] = tt.dsem
            self.dma_tiles.append(tt)
        return tt.dsem

    def _need(self, eng, dep, waits):
        if dep is None:
            return
        key, val = dep
        if key == ("c", "pe") and eng == "pe":
            return
        if key == ("c", eng) and eng in ("sync",):
            return
        k = self.known[eng]
        if k.get(key, 0) >= val:
            return
        k[key] = val
        waits.append((self.semobj[key], val))

    def _deps(self, eng, r, w):
        waits = []
        for t in r:
            self._need(eng, t.w, waits)
        for t in w:
            self._need(eng, t.w, waits)
            for d in t.r:
                self._need(eng, d, waits)
        return waits

    def op(self, eng, fn, r=(), w=()):
        waits = self._deps(eng, r, w)
        self.cnt[eng] += 1
        me = (("c", eng), self.cnt[eng])
        for t in r:
            t.r.append(me)
        for t in w:
            t.w = me
            t.r = []
        self.ops[eng].append((waits, fn, ("c", self.sems[eng])))

    def dma(self, fn, r=(), w=(), q="sync"):
        waits = self._deps(q, r, w)
        tiles = list(w) + list(r)
        if tiles:
            tt = tiles[0]
            sem = self._dsem(tt)
            tt.dtot += 16
            me = (("d", tt.name), tt.dtot)
        else:
            if self.misc is None:
                self.misc = TT(None, f"{self.name}_misc")
            tt = self.misc
            sem = self._dsem(tt)
            tt.dtot += 16
            me = (("d", tt.name), tt.dtot)
        for t in r:
            t.r.append(me)
        for t in w:
            t.w = me
            t.r = []
        self.ops[q].append((waits, fn, ("d", sem)))

    def finish(self):
        nc = self.nc
        finals = [(t.dsem, t.dtot) for t in self.dma_tiles if t.dtot > 0]

        def replay(e, name):
            for waits, fn, inc in self.ops[name]:
                for s, v in waits:
                    e.wait_ge(s, v)
                ins = fn(e)
                if inc[0] == "c":
                    ins.then_inc(inc[1], 1)
                else:
                    ins.then_inc(inc[1], 16)
            if name == "sync":
                for s, v in finals:
                    e.wait_ge(s, v)

        with nc.Block() as block:
            @block.sync
            def _(e):
                replay(e, "sync")

            @block.gpsimd
            def _(e):
                replay(e, "pool")

            @block.scalar
            def _(e):
                replay(e, "act")

            @block.vector
            def _(e):
                replay(e, "dve")

            @block.tensor
            def _(e):
                replay(e, "pe")
        allsems = list(self.semobj.values())
        with nc.Block() as block2:
            @block2.sync
            def _(e):
                for sm_ in allsems:
                    e.sem_clear(sm_)
        self.es.close()


def I_ts(out, in0, s1, s2=None, op0=ALU.mult, op1=None, accum_out=None):
    kw = {}
    if op1 is not None:
        kw["op1"] = op1
    if accum_out is not None:
        kw["accum_out"] = accum_out
    return lambda e: e.tensor_scalar(out=out, in0=in0, scalar1=s1, scalar2=s2, op0=op0, **kw)


def I_tt(out, in0, in1, op):
    return lambda e: e.tensor_tensor(out=out, in0=in0, in1=in1, op=op)


def I_stt(out, in0, scalar, in1, op0, op1):
    return lambda e: e.scalar_tensor_tensor(out=out, in0=in0, scalar=scalar, in1=in1, op0=op0, op1=op1)


def I_act(out, in_, func, bias=None, scale=None, accum_out=None):
    kw = {}
    if bias is not None:
        kw["bias"] = bias
    if scale is not None:
        kw["scale"] = scale
    if accum_out is not None:
        kw["accum_out"] = accum_out
    return lambda e: e.activation(out=out, in_=in_, func=func, **kw)


def I_mm(out, lhsT, rhs, start=True, stop=True):
    return lambda e: e.matmul(out, lhsT, rhs, start=start, stop=stop)


def I_tr(out, in_, ident):
    return lambda e: e.transpose(out, in_, ident)


def I_copy(out, in_):
    return lambda e: e.tensor_copy(out=out, in_=in_)


def I_memset(ap, v):
    return lambda e: e.memset(ap, v)


def I_recip(out, in_):
    return lambda e: e.reciprocal(out=out, in_=in_)


def I_scan(out, d0, d1, init):
    return lambda e: e.tensor_tensor_scan(out=out, data0=d0, data1=d1, initial=init, op0=ALU.mult, op1=ALU.add)


def I_dma(out, in_, **kw):
    return lambda e: e.dma_start(out=out, in_=in_, **kw)


def I_iota(out, pattern, base, cm):
    return lambda e: e.iota(out=out, pattern=pattern, base=base, channel_multiplier=cm,
                            allow_small_or_imprecise_dtypes=True)


def mk_ident(ph):
    iot = ph.sb([128, 128], F32, "iot")
    idf = ph.sb([128, 128], F32, "idf")
    idb = ph.sb([128, 128], BF16, "idb")
    ph.op("pool", I_iota(iot.t[:], [[1, 128]], 0, -1), w=[iot])
    ph.op("dve", I_ts(idf.t[:], iot.t[:], 0.0, None, op0=ALU.is_equal), r=[iot], w=[idf])
    ph.op("dve", I_copy(idb.t[:], idf.t[:]), r=[idf], w=[idb])
    return idf, idb


def bcast_rows(ap_row, n):
    return ap_row.to_broadcast([n, ap_row.shape[-1]])


class Seq:
    pass


def build(T=8192, TS=32, PAST=4096, stop_after=None, dbg=False):
    nc = bass.Bass("TRN2", target_bir_lowering=False)
    SS = PAST + TS
    din = lambda n, s, dt=F32: nc.dram_tensor(n, list(s), dt, kind="ExternalInput").ap()
    dout = lambda n, s, dt=F32: nc.dram_tensor(n, list(s), dt, kind="ExternalOutput").ap()
    scr_kind = "ExternalOutput" if dbg else "Internal"
    dscr = lambda n, s, dt=BF16: nc.dram_tensor(n, list(s), dt, kind=scr_kind).ap()

    A = {}
    A["xp"] = din("xp", [T, D]); A["xs"] = din("xs", [2, TS, D])
    A["ck"] = din("ck", [NL, 2, PAST, 512]); A["cv"] = din("cv", [NL, 2, PAST, 512])
    A["cki"] = din("cki", [NL, 2, PAST, 64])
    A["ssr"] = din("ssr", [NL, 2, 2048]); A["ssi"] = din("ssi", [NL, 2, 2048])
    A["call"] = din("call", [3, D])
    A["w_mod"] = din("w_mod", [NL, D, 3 * D]); A["b_mod"] = din("b_mod", [NL, 3 * D])
    A["g_norm"] = din("g_norm", [NL, D]); A["w_in"] = din("w_in", [NL, D, DIN])
    A["a_re"] = din("a_re", [NL, 2048]); A["a_im"] = din("a_im", [NL, 2048])
    A["log_dt"] = din("log_dt", [NL, 32])
    A["b_re"] = din("b_re", [NL, 2048, 16]); A["b_im"] = din("b_im", [NL, 2048, 16])
    A["c_reT"] = din("c_reT", [NL, 2048, 16]); A["c_imT"] = din("c_imT", [NL, 2048, 16])
    A["d_skip"] = din("d_skip", [NL, 512])
    A["w_glu"] = din("w_glu", [NL, 512, 512]); A["b_glu"] = din("b_glu", [NL, 512])
    A["w_ps"] = din("w_ps", [NL, 512, D]); A["w_pa"] = din("w_pa", [NL, 512, D])
    A["w_o"] = din("w_o", [NL, D, D]); A["g_final"] = din("g_final", [1, D])

    O = {}
    O["yp"] = dout("yp", [T, D]); O["ys"] = dout("ys", [2, TS, D])
    O["kp"] = dout("kp", [NL, T, 512]); O["vp"] = dout("vp", [NL, T, 512]); O["kip"] = dout("kip", [NL, T, 64])
    O["srp"] = dout("srp", [NL, 2048]); O["sip"] = dout("sip", [NL, 2048])
    O["ks"] = dout("ks", [NL, 2, TS, 512]); O["vs"] = dout("vs", [NL, 2, TS, 512])
    O["kis"] = dout("kis", [NL, 2, TS, 64])
    O["srs"] = dout("srs", [NL, 2, 2048]); O["sis"] = dout("sis", [NL, 2, 2048])

    modbc = dscr("modbc", [NL, 3, 3, 128, D], F32)

    seqs = []
    for i in range(3):
        s = Seq()
        s.i = i
        s.nm = "p" if i == 0 else f"s{i - 1}"
        s.T = T if i == 0 else TS
        s.S = T if i == 0 else SS
        s.koff = 0 if i == 0 else PAST
        s.causal = (i == 0)
        s.xin = A["xp"] if i == 0 else A["xs"][i - 1]
        s.xres = dscr(f"xres_{s.nm}", [s.T, D], F32)
        for nm, rows, cols in (("uT", 512, s.T), ("zsT", 512, s.T), ("qT", 512, s.T), ("kT", 512, s.S),
                               ("zaT", 512, s.T), ("qiT", 512, s.T), ("kiT", 64, s.S), ("gmT", 2048, s.T),
                               ("vaug", s.S, 520), ("ysT", 512, s.T), ("yaT", 512, s.T)):
            setattr(s, nm, dscr(f"{nm}_{s.nm}", [rows, cols], BF16))
        s.wiS = dscr(f"wiS_{s.nm}", [s.T, 8], F32)
        if i == 0:
            s.kout = [O["kp"][l] for l in range(NL)]; s.vout = [O["vp"][l] for l in range(NL)]
            s.kiout = [O["kip"][l] for l in range(NL)]
            s.srout = [O["srp"][l] for l in range(NL)]; s.siout = [O["sip"][l] for l in range(NL)]
            s.yout = O["yp"]
        else:
            j = i - 1
            s.kout = [O["ks"][l, j] for l in range(NL)]; s.vout = [O["vs"][l, j] for l in range(NL)]
            s.kiout = [O["kis"][l, j] for l in range(NL)]
            s.srout = [O["srs"][l, j] for l in range(NL)]; s.siout = [O["sis"][l, j] for l in range(NL)]
            s.yout = O["ys"][j]
        seqs.append(s)

    ncd = nc.allow_non_contiguous_dma(reason="small strided param loads")
    ncd.__enter__()
    es_ = ExitStack()
    _SEMSTACK.append(es_)

    phase_mod(nc, A, modbc)
    if stop_after == "mod":
        return nc
    for l in range(NL):
        phase_proj(nc, A, l, seqs, modbc)
        if stop_after == f"proj{l}":
            return nc
        phase_cache(nc, A, l, seqs, PAST)
        if stop_after == f"cache{l}":
            return nc
        phase_ssm(nc, A, l, seqs)
        if stop_after == f"ssm{l}":
            return nc
        import os as _os2
        for s in seqs:
            if _os2.environ.get("ATTN_SEQ") and s.nm not in _os2.environ.get("ATTN_SEQ").split(","):
                continue
            (phase_attn2 if (_os2.environ.get('ATTN_V','h')=='2' or (_os2.environ.get('ATTN_V','h')=='h' and s.i == 0)) else phase_attn)(nc, A, l, s)
        if stop_after == f"attn{l}":
            return nc
        phase_merge(nc, A, l, seqs, modbc)
        if stop_after == f"merge{l}":
            return nc
    return nc


def phase_mod(nc, A, modbc):
    ph = Phase(nc, "M")
    idf, idb = mk_ident(ph)
    cT = ph.sb([128, 8, 3], F32, "cT")
    for s_ in range(3):
        ph.dma(I_dma(cT.t[:, :, s_], A["call"][s_].rearrange("(c p) -> p c", p=128)), w=[cT])
    cTs = ph.sb([128, 8, 3], F32, "cTs")
    ph.op("act", I_act(cTs.t[:], cT.t[:], AF.Silu), r=[cT], w=[cTs])
    iop = ph.sb([3, 128], F32, "iop")
    ph.op("pool", I_iota(iop.t[:], [[0, 128]], 0, 1), w=[iop])
    sel = []
    for s in range(3):
        t = ph.sb([3, 128], F32, f"sel{s}")
        ph.op("dve", I_ts(t.t[:], iop.t[:], float(s), None, op0=ALU.is_equal), r=[iop], w=[t])
        sel.append(t)
    wst = [ph.sb([128, 8, 512], F32, f"wst{i}") for i in range(2)]
    pmm = [ph.ps([128, 512], F32, f"pm{i}") for i in range(2)]
    pbc = [ph.ps([128, 512], F32, f"pb{i}") for i in range(2)]
    stg = [ph.sb([128, D], F32, f"stg{i}") for i in range(2)]
    n = 0
    nb = 0
    for l in range(NL):
        bsb = ph.sb([3, 3 * D], F32, f"bsb{l}")
        ph.dma(I_dma(bsb.t[:], bcast_rows(A["b_mod"][l:l + 1, :], 3)), w=[bsb])
        gbc = ph.sb([128, D], F32, f"gbc{l}")
        ph.dma(I_dma(gbc.t[:], bcast_rows(A["g_norm"][l:l + 1, :], 128)), w=[gbc])
        mod = ph.sb([3, 3 * D], F32, f"mod{l}")
        for ct in range(6):
            w = wst[n % 2]; pm = pmm[n % 2]; n += 1
            ph.dma(I_dma(w.t[:], A["w_mod"][l].rearrange("(c p) n -> p c n", p=128)[:, :, ct * 512:(ct + 1) * 512]), w=[w])
            for c in range(8):
                ph.op("pe", I_mm(pm.t[0:3, :], cTs.t[:, c, :], w.t[:, c, :], start=(c == 0), stop=(c == 7)),
                      r=[cTs, w], w=[pm])
            ph.op("dve", I_tt(mod.t[:, ct * 512:(ct + 1) * 512], pm.t[0:3, :], bsb.t[:, ct * 512:(ct + 1) * 512], ALU.add),
                  r=[pm, bsb], w=[mod])
        ph.op("dve", I_ts(mod.t[:, D:2 * D], mod.t[:, D:2 * D], 1.0, None, op0=ALU.add), r=[mod], w=[mod])
        for s in range(3):
            for kind in range(3):
                st = stg[nb % 2]
                for half in range(2):
                    pb = pbc[nb % 2 if half == 0 else (nb + 1) % 2]
                    c0 = kind * D + half * 512
                    ph.op("pe", I_mm(pb.t[:, :], sel[s].t[:, :], mod.t[:, c0:c0 + 512]), r=[sel[s], mod], w=[pb])
                    if kind == 1:
                        ph.op("dve", I_tt(st.t[:, half * 512:(half + 1) * 512], pb.t[:, :], gbc.t[:, half * 512:(half + 1) * 512], ALU.mult),
                              r=[pb, gbc], w=[st])
                    else:
                        ph.op("act", I_act(st.t[:, half * 512:(half + 1) * 512], pb.t[:, :], AF.Copy), r=[pb], w=[st])
                nb += 1
                ph.dma(I_dma(modbc[l, s, kind], st.t[:]), r=[st])
    ph.finish()


def phase_proj(nc, A, l, seqs, modbc):
    ph = Phase(nc, f"P{l}")
    idf, idb = mk_ident(ph)
    wbf = ph.sb([128, 8, DIN], BF16, "wbf")
    win = A["w_in"][l].rearrange("(c p) n -> p c n", p=128)
    for c in range(8):
        for h0 in range(0, DIN, 1024):
            h1 = min(DIN, h0 + 1024)
            ph.dma(I_dma(wbf.t[:, c, h0:h1], win[:, c, h0:h1]), w=[wbf], q="pool")
    xt = [ph.sb([128, D], F32, f"xt{i}") for i in range(2)]
    junk = ph.sb([128, D], BF16, "junk")
    ss = [ph.sb([128, 1], F32, f"ss{i}") for i in range(2)]
    rstd = [ph.sb([128, 1], F32, f"rstd{i}") for i in range(2)]
    tmp = [ph.sb([128, D], F32, f"tmp{i}") for i in range(2)]
    hb = [ph.sb([128, D], BF16, f"hb{i}") for i in range(2)]
    hT = [ph.sb([128, 8, 512], BF16, f"hT{i}") for i in range(2)]
    gm_bc = ph.sb([128, D], F32, "gmbc")
    sh_bc = ph.sb([128, D], F32, "shbc")
    kvst = [ph.sb([128, 2 * 512], F32, f"kvst{i}") for i in range(2)]
    vst = [ph.sb([128, 8, 65], BF16, f"vst{i}") for i in range(2)]
    for v in vst:
        ph.op("pool", I_memset(v.t[:], 1.0), w=[v])
    kist = [ph.sb([128, 64], F32, f"kist{i}") for i in range(2)]
    wist = [ph.sb([128, 8], F32, f"wist{i}") for i in range(2)]
    fst = [ph.sb([128, 512], BF16, f"fst{i}") for i in range(4)]
    pT = ph.ps([128, 8, 128], BF16, "pT")
    pk = ph.ps([128, 512], F32, "pk")
    pv = ph.ps([128, 512], F32, "pv")
    pki = ph.ps([128, 72], F32, "pki")
    pf = [ph.ps([128, 512], F32, f"pf{i}") for i in range(3)]
    WISC = float(8 ** -0.5 * 64 ** -0.5)

    nt = 0
    nf = 0
    for s in seqs:
        ph.dma(I_dma(gm_bc.t[:], modbc[l, s.i, 1]), w=[gm_bc])
        ph.dma(I_dma(sh_bc.t[:], modbc[l, s.i, 0]), w=[sh_bc])
        xsrc = s.xin if l == 0 else s.xres
        TP = min(128, s.T)
        NW = min(512, s.T)
        for st in range(s.T // NW):
            ht = hT[st % 2]
            for j in range(NW // TP):
                r0 = st * NW + j * TP
                x = xt[nt % 2]; sq = ss[nt % 2]; rs = rstd[nt % 2]; tm = tmp[nt % 2]; h = hb[nt % 2]
                kv = kvst[nt % 2]; vs_ = vst[nt % 2]; kis_ = kist[nt % 2]; wis_ = wist[nt % 2]
                nt += 1
                ph.dma(I_dma(x.t[:TP, :], xsrc[r0:r0 + TP, :]), w=[x])
                ph.op("act", I_act(junk.t[:TP, :], x.t[:TP, :], AF.Square, accum_out=sq.t[:TP, 0:1]), r=[x], w=[junk, sq])
                ph.op("dve", I_ts(sq.t[:TP, :], sq.t[:TP, :], 1.0 / D, EPS, op0=ALU.mult, op1=ALU.add), r=[sq], w=[sq])
                ph.op("act", I_act(sq.t[:TP, :], sq.t[:TP, :], AF.Sqrt), r=[sq], w=[sq])
                ph.op("dve", I_recip(rs.t[:TP, :], sq.t[:TP, :]), r=[sq], w=[rs])
                ph.op("dve", I_stt(tm.t[:TP, :], x.t[:TP, :], rs.t[:TP, 0:1], gm_bc.t[:TP, :], ALU.mult, ALU.mult),
                      r=[x, rs, gm_bc], w=[tm])
                ph.op("pool", I_tt(h.t[:TP, :], tm.t[:TP, :], sh_bc.t[:TP, :], ALU.add), r=[tm, sh_bc], w=[h])
                for c in range(8):
                    ph.op("pe", I_tr(pT.t[:, c, :TP], h.t[:TP, c * 128:(c + 1) * 128], idb.t[:TP, :TP]), r=[h, idb], w=[pT])
                ph.op("act", I_act(ht.t[:, :, j * TP:(j + 1) * TP], pT.t[:, :, :TP], AF.Copy), r=[pT], w=[ht])
                for c in range(8):
                    ph.op("pe", I_mm(pk.t[:TP, :], ht.t[:, c, j * TP:(j + 1) * TP], wbf.t[:, c, OFF_K:OFF_K + 512],
                                     start=(c == 0), stop=(c == 7)), r=[ht, wbf], w=[pk])
                for c in range(8):
                    ph.op("pe", I_mm(pv.t[:TP, :], ht.t[:, c, j * TP:(j + 1) * TP], wbf.t[:, c, OFF_V:OFF_V + 512],
                                     start=(c == 0), stop=(c == 7)), r=[ht, wbf], w=[pv])
                for c in range(8):
                    ph.op("pe", I_mm(pki.t[:TP, :], ht.t[:, c, j * TP:(j + 1) * TP], wbf.t[:, c, OFF_KI:OFF_KI + 72],
                                     start=(c == 0), stop=(c == 7)), r=[ht, wbf], w=[pki])
                ph.op("act", I_act(kv.t[:TP, 0:512], pk.t[:TP, :], AF.Copy), r=[pk], w=[kv])
                ph.op("dve", I_copy(kv.t[:TP, 512:1024], pv.t[:TP, :]), r=[pv], w=[kv])
                ph.op("dve", I_copy(vs_.t[:TP, :, 0:64], pv.t[:TP, :].rearrange("p (h d) -> p h d", d=64)), r=[pv], w=[vs_])
                ph.op("act", I_act(kis_.t[:TP, :], pki.t[:TP, 0:64], AF.Copy), r=[pki], w=[kis_])
                ph.op("dve", I_ts(wis_.t[:TP, :], pki.t[:TP, 64:72], WISC, None, op0=ALU.mult), r=[pki], w=[wis_])
                ph.dma(I_dma(s.kout[l][r0:r0 + TP, :], kv.t[:TP, 0:512]), r=[kv])
                ph.dma(I_dma(s.vout[l][r0:r0 + TP, :], kv.t[:TP, 512:1024]), r=[kv])
                ph.dma(I_dma(s.vaug[s.koff + r0:s.koff + r0 + TP, :], vs_.t[:TP, :, :].rearrange("p h d -> p (h d)")), r=[vs_])
                ph.dma(I_dma(s.kiout[l][r0:r0 + TP, :], kis_.t[:TP, :]), r=[kis_])
                ph.dma(I_dma(s.wiS[r0:r0 + TP, :], wis_.t[:TP, :]), r=[wis_])
            t0 = st * NW
            fm = []
            for i in range(4):
                fm.append((OFF_U + i * 128, 128, "copy", s.uT[i * 128:(i + 1) * 128, t0:t0 + NW]))
                fm.append((OFF_Q + i * 128, 128, "q", s.qT[i * 128:(i + 1) * 128, t0:t0 + NW]))
                fm.append((OFF_K + i * 128, 128, "copy", s.kT[i * 128:(i + 1) * 128, s.koff + t0:s.koff + t0 + NW]))
                fm.append((OFF_QI + i * 128, 128, "copy", s.qiT[i * 128:(i + 1) * 128, t0:t0 + NW]))
            fm.append((OFF_KI, 64, "copy", s.kiT[0:64, s.koff + t0:s.koff + t0 + NW]))
            for i in range(4):
                fm.append((OFF_ZS + i * 128, 128, "silu", s.zsT[i * 128:(i + 1) * 128, t0:t0 + NW]))
            for i in range(4):
                fm.append((OFF_ZA + i * 128, 128, "silu", s.zaT[i * 128:(i + 1) * 128, t0:t0 + NW]))
            for i in range(16):
                fm.append((OFF_GM + i * 128, 128, "sig", s.gmT[i * 128:(i + 1) * 128, t0:t0 + NW]))
            for (off, M, kind, dst) in fm:
                p = pf[nf % 3]; f = fst[nf % 4]; nf += 1
                for c in range(8):
                    ph.op("pe", I_mm(p.t[:M, :NW], wbf.t[:, c, off:off + M], ht.t[:, c, :NW], start=(c == 0), stop=(c == 7)),
                          r=[wbf, ht], w=[p])
                if kind == "copy":
                    ph.op("dve", I_copy(f.t[:M, :NW], p.t[:M, :NW]), r=[p], w=[f])
                elif kind == "q":
                    ph.op("dve", I_ts(f.t[:M, :NW], p.t[:M, :NW], 0.125, None, op0=ALU.mult), r=[p], w=[f])
                elif kind == "silu":
                    ph.op("act", I_act(f.t[:M, :NW], p.t[:M, :NW], AF.Silu), r=[p], w=[f])
                else:
                    ph.op("act", I_act(f.t[:M, :NW], p.t[:M, :NW], AF.Sigmoid), r=[p], w=[f])
                ph.dma(I_dma(dst, f.t[:M, :NW]), r=[f])
    ph.finish()


def make_in_maps(inp, n_cores=8):
    f = lambda a: np.ascontiguousarray(np.asarray(a, dtype=np.float32))
    shared = {
        "w_mod": f(inp["w_mod"]), "b_mod": f(inp["b_mod"]), "g_norm": f(inp["g_norm"]), "w_in": f(inp["w_in"]),
        "a_re": f(inp["a_re"]).reshape(NL, 2048), "a_im": f(inp["a_im"]).reshape(NL, 2048),
        "log_dt": f(inp["log_dt"]),
        "b_re": f(inp["b_re"]).reshape(NL, 2048, 16), "b_im": f(inp["b_im"]).reshape(NL, 2048, 16),
        "c_reT": f(np.transpose(np.asarray(inp["c_re"]), (0, 1, 3, 2))).reshape(NL, 2048, 16),
        "c_imT": f(np.transpose(np.asarray(inp["c_im"]), (0, 1, 3, 2))).reshape(NL, 2048, 16),
        "d_skip": f(inp["d_skip"]).reshape(NL, 512),
        "w_glu": f(inp["w_glu"]), "b_glu": f(inp["b_glu"]), "w_ps": f(inp["w_ps"]), "w_pa": f(inp["w_pa"]),
        "w_o": f(inp["w_o"]), "g_final": f(inp["g_final"]).reshape(1, D),
    }
    xp = np.asarray(inp["x_prompt"]); xs = np.asarray(inp["x_sample"])
    ck = np.asarray(inp["cache_k"]); cv = np.asarray(inp["cache_v"]); cki = np.asarray(inp["cache_kidx"])
    sr = np.asarray(inp["state_ssm_re"]); si = np.asarray(inp["state_ssm_im"])
    cp = np.asarray(inp["c_prompt"]); cs = np.asarray(inp["c_sample"])
    PAST = ck.shape[2]
    maps = []
    for b in range(n_cores):
        m = dict(shared)
        m["xp"] = f(xp[b]); m["xs"] = f(xs[2 * b:2 * b + 2])
        m["ck"] = f(ck[:, 2 * b:2 * b + 2]).reshape(NL, 2, PAST, 512)
        m["cv"] = f(cv[:, 2 * b:2 * b + 2]).reshape(NL, 2, PAST, 512)
        m["cki"] = f(cki[:, 2 * b:2 * b + 2])
        m["ssr"] = f(sr[:, 2 * b:2 * b + 2]).reshape(NL, 2, 2048)
        m["ssi"] = f(si[:, 2 * b:2 * b + 2]).reshape(NL, 2, 2048)
        m["call"] = f(np.concatenate([cp[b:b + 1], cs[2 * b:2 * b + 2]], axis=0))
        maps.append(m)
    return maps


def assemble(results, T, TS, n_cores=8):
    R = results
    cat = lambda k: np.stack([np.asarray(r[k]) for r in R], axis=0)
    yp = cat("yp")
    ys = cat("ys").reshape(2 * n_cores, TS, D)
    kp = np.transpose(cat("kp"), (1, 0, 2, 3)).reshape(NL, n_cores, T, 8, 64)
    vp = np.transpose(cat("vp"), (1, 0, 2, 3)).reshape(NL, n_cores, T, 8, 64)
    kip = np.transpose(cat("kip"), (1, 0, 2, 3))
    srp = np.transpose(cat("srp"), (1, 0, 2)).reshape(NL, n_cores, 32, 64)
    sip = np.transpose(cat("sip"), (1, 0, 2)).reshape(NL, n_cores, 32, 64)
    ks = np.transpose(cat("ks"), (1, 0, 2, 3, 4)).reshape(NL, 2 * n_cores, TS, 8, 64)
    vs = np.transpose(cat("vs"), (1, 0, 2, 3, 4)).reshape(NL, 2 * n_cores, TS, 8, 64)
    kis = np.transpose(cat("kis"), (1, 0, 2, 3, 4)).reshape(NL, 2 * n_cores, TS, 64)
    srs = np.transpose(cat("srs"), (1, 0, 2, 3)).reshape(NL, 2 * n_cores, 32, 64)
    sis = np.transpose(cat("sis"), (1, 0, 2, 3)).reshape(NL, 2 * n_cores, 32, 64)
    outs = (yp, ys, kp, vp, kip, srp, sip, ks, vs, kis, srs, sis)
    return tuple(np.ascontiguousarray(o, dtype=np.float32) for o in outs)


_NC_CACHE = {}


def kernel(**inputs):
    T = int(np.asarray(inputs["x_prompt"]).shape[1])
    TS = int(np.asarray(inputs["x_sample"]).shape[1])
    PAST = int(np.asarray(inputs["cache_k"]).shape[2])
    n_cores = int(np.asarray(inputs["x_prompt"]).shape[0])
    key = (T, TS, PAST)
    nc = build(T=T, TS=TS, PAST=PAST)
    maps = make_in_maps(inputs, n_cores)
    res = run_bass_kernel_spmd(nc, maps, core_ids=list(range(n_cores)))
    return assemble(res.results, T, TS, n_cores)


def phase_cache(nc, A, l, seqs, PAST):
    ph = Phase(nc, f"C{l}")
    idf, idb = mk_ident(ph)
    ckt = [ph.sb([128, 512], F32, f"ckt{i}") for i in range(2)]
    cvt = [ph.sb([128, 512], F32, f"cvt{i}") for i in range(2)]
    cit = [ph.sb([128, 64], F32, f"cit{i}") for i in range(2)]
    ckb = [ph.sb([128, 512], BF16, f"ckb{i}") for i in range(2)]
    cib = [ph.sb([128, 64], BF16, f"cib{i}") for i in range(2)]
    kst = [ph.sb([128, 4, 128], BF16, f"kst{i}") for i in range(2)]
    kis = [ph.sb([64, 128], BF16, f"kis{i}") for i in range(2)]
    vst = [ph.sb([128, 8, 65], BF16, f"vst{i}") for i in range(2)]
    for v in vst:
        ph.op("pool", I_memset(v.t[:], 1.0), w=[v])
    pT = [ph.ps([128, 4, 128], BF16, f"pT{i}") for i in range(2)]
    pI = [ph.ps([64, 128], BF16, f"pI{i}") for i in range(2)]
    n = 0
    for s in seqs[1:]:
        j = s.i - 1
        for kt in range(PAST // 128):
            i = n % 2; n += 1
            r0 = kt * 128
            ph.dma(I_dma(ckt[i].t[:], A["ck"][l, j, r0:r0 + 128, :]), w=[ckt[i]])
            ph.dma(I_dma(cvt[i].t[:], A["cv"][l, j, r0:r0 + 128, :]), w=[cvt[i]])
            ph.dma(I_dma(cit[i].t[:], A["cki"][l, j, r0:r0 + 128, :]), w=[cit[i]])
            ph.op("act", I_act(ckb[i].t[:], ckt[i].t[:], AF.Copy), r=[ckt[i]], w=[ckb[i]])
            ph.op("act", I_act(cib[i].t[:], cit[i].t[:], AF.Copy), r=[cit[i]], w=[cib[i]])
            for hp in range(4):
                ph.op("pe", I_tr(pT[i].t[:, hp, :], ckb[i].t[:, hp * 128:(hp + 1) * 128], idb.t[:, :]), r=[ckb[i], idb], w=[pT[i]])
            ph.op("pe", I_tr(pI[i].t[:, :], cib[i].t[:, :], idb.t[:, :]), r=[cib[i], idb], w=[pI[i]])
            ph.op("dve", I_copy(kst[i].t[:], pT[i].t[:]), r=[pT[i]], w=[kst[i]])
            ph.op("dve", I_copy(kis[i].t[:], pI[i].t[:]), r=[pI[i]], w=[kis[i]])
            ph.op("pool", I_copy(vst[i].t[:, :, 0:64], cvt[i].t[:, :].rearrange("p (h d) -> p h d", d=64)), r=[cvt[i]], w=[vst[i]])
            ph.dma(I_dma(s.kT.rearrange("(hp p) s -> p hp s", p=128)[:, :, r0:r0 + 128], kst[i].t[:]), r=[kst[i]])
            ph.dma(I_dma(s.kiT[:, r0:r0 + 128], kis[i].t[:]), r=[kis[i]])
            ph.dma(I_dma(s.vaug[r0:r0 + 128, :], vst[i].t[:].rearrange("p h d -> p (h d)")), r=[vst[i]])
    ph.finish()


def phase_merge(nc, A, l, seqs, modbc):
    ph = Phase(nc, f"G{l}")
    last = (l == NL - 1)
    wps = ph.sb([128, 4, D], BF16, "wps"); wpa = ph.sb([128, 4, D], BF16, "wpa"); wo = ph.sb([128, 8, D], BF16, "wo")
    for c in range(4):
        ph.dma(I_dma(wps.t[:, c, :], A["w_ps"][l, c * 128:(c + 1) * 128, :]), w=[wps], q="pool")
        ph.dma(I_dma(wpa.t[:, c, :], A["w_pa"][l, c * 128:(c + 1) * 128, :]), w=[wpa], q="pool")
    for c in range(8):
        ph.dma(I_dma(wo.t[:, c, :], A["w_o"][l, c * 128:(c + 1) * 128, :]), w=[wo], q="pool")
    gate = ph.sb([128, D], F32, "gate")
    gfin = ph.sb([128, D], F32, "gfin")
    if last:
        ph.dma(I_dma(gfin.t[:], bcast_rows(A["g_final"][0:1, :], 128)), w=[gfin])
    yst = [ph.sb([128, 4, 512], BF16, f"yst{i}") for i in range(2)]
    yat = [ph.sb([128, 4, 512], BF16, f"yat{i}") for i in range(2)]
    gmt = [ph.sb([128, 16, 512], BF16, f"gmt{i}") for i in range(2)]
    mg = [ph.sb([128, 8, 512], BF16, f"mg{i}") for i in range(2)]
    tA = [ph.sb([128, 512], F32, f"tA{i}") for i in range(2)]
    tB = [ph.sb([128, 512], F32, f"tB{i}") for i in range(2)]
    xt = [ph.sb([128, D], F32, f"xt{i}") for i in range(2)]
    xn = [ph.sb([128, D], F32, f"xn{i}") for i in range(2)]
    yo = [ph.sb([128, D], F32, f"yo{i}") for i in range(2)]
    junk = ph.sb([128, D], BF16, "junk")
    ss = [ph.sb([128, 1], F32, f"ss{i}") for i in range(2)]
    rs = [ph.sb([128, 1], F32, f"rs{i}") for i in range(2)]
    pA = [ph.ps([128, 512], F32, f"pA{i}") for i in range(2)]
    pB = [ph.ps([128, 512], F32, f"pB{i}") for i in range(2)]
    pO = [ph.ps([128, 512], F32, f"pO{i}") for i in range(2)]
    nn = 0
    nt = 0
    for s in seqs:
        ph.dma(I_dma(gate.t[:], modbc[l, s.i, 2]), w=[gate])
        xsrc = s.xin if l == 0 else s.xres
        TP = min(128, s.T); NW = min(512, s.T)
        for st in range(s.T // NW):
            t0 = st * NW
            ys_ = yst[st % 2]; ya_ = yat[st % 2]; gm_ = gmt[st % 2]; m_ = mg[st % 2]
            ph.dma(I_dma(ys_.t[:, :, :NW], s.ysT.rearrange("(c p) t -> p c t", p=128)[:, :, t0:t0 + NW]), w=[ys_])
            ph.dma(I_dma(ya_.t[:, :, :NW], s.yaT.rearrange("(c p) t -> p c t", p=128)[:, :, t0:t0 + NW]), w=[ya_])
            ph.dma(I_dma(gm_.t[:, :, :NW], s.gmT.rearrange("(c p) t -> p c t", p=128)[:, :, t0:t0 + NW]), w=[gm_])
            for ct in range(8):
                a = pA[nn % 2]; b = pB[nn % 2]; ta = tA[nn % 2]; tb = tB[nn % 2]; nn += 1
                for c in range(4):
                    ph.op("pe", I_mm(a.t[:, :NW], wps.t[:, c, ct * 128:(ct + 1) * 128], ys_.t[:, c, :NW], start=(c == 0), stop=(c == 3)),
                          r=[wps, ys_], w=[a])
                for c in range(4):
                    ph.op("pe", I_mm(b.t[:, :NW], wpa.t[:, c, ct * 128:(ct + 1) * 128], ya_.t[:, c, :NW], start=(c == 0), stop=(c == 3)),
                          r=[wpa, ya_], w=[b])
                ph.op("dve", I_tt(ta.t[:, :NW], a.t[:, :NW], gm_.t[:, ct, :NW], ALU.mult), r=[a, gm_], w=[ta])
                ph.op("dve", I_tt(tb.t[:, :NW], b.t[:, :NW], gm_.t[:, 8 + ct, :NW], ALU.mult), r=[b, gm_], w=[tb])
                ph.op("pool", I_tt(m_.t[:, ct, :NW], ta.t[:, :NW], tb.t[:, :NW], ALU.add), r=[ta, tb], w=[m_])
            for j in range(NW // TP):
                r0 = t0 + j * TP
                x = xt[nt % 2]; xo = xn[nt % 2]; y_ = yo[nt % 2]; sq = ss[nt % 2]; r_ = rs[nt % 2]
                ph.dma(I_dma(x.t[:TP, :], xsrc[r0:r0 + TP, :]), w=[x])
                for half in range(2):
                    po = pO[half]
                    hs = slice(half * 512, (half + 1) * 512)
                    for c in range(8):
                        ph.op("pe", I_mm(po.t[:TP, :], m_.t[:, c, j * TP:(j + 1) * TP], wo.t[:, c, hs], start=(c == 0), stop=(c == 7)),
                              r=[m_, wo], w=[po])
                    ph.op("dve", I_tt(xo.t[:TP, hs], po.t[:TP, :], gate.t[:TP, hs], ALU.mult), r=[po, gate], w=[xo])
                ph.op("pool", I_tt(xo.t[:TP, :], xo.t[:TP, :], x.t[:TP, :], ALU.add), r=[xo, x], w=[xo])
                nt += 1
                if not last:
                    ph.dma(I_dma(s.xres[r0:r0 + TP, :], xo.t[:TP, :]), r=[xo])
                else:
                    ph.op("act", I_act(junk.t[:TP, :], xo.t[:TP, :], AF.Square, accum_out=sq.t[:TP, 0:1]), r=[xo], w=[junk, sq])
                    ph.op("dve", I_ts(sq.t[:TP, :], sq.t[:TP, :], 1.0 / D, EPS, op0=ALU.mult, op1=ALU.add), r=[sq], w=[sq])
                    ph.op("act", I_act(sq.t[:TP, :], sq.t[:TP, :], AF.Sqrt), r=[sq], w=[sq])
                    ph.op("dve", I_recip(r_.t[:TP, :], sq.t[:TP, :]), r=[sq], w=[r_])
                    ph.op("dve", I_stt(y_.t[:TP, :], xo.t[:TP, :], r_.t[:TP, 0:1], gfin.t[:TP, :], ALU.mult, ALU.mult),
                          r=[xo, r_, gfin], w=[y_])
                    ph.dma(I_dma(s.yout[r0:r0 + TP, :], y_.t[:TP, :]), r=[y_])
    ph.finish()


NIT = 22
W0 = 64.0


def phase_attn(nc, A, l, s):
    ph = Phase(nc, f"A{l}{s.nm}")
    idf, idb = mk_ident(ph)
    T, S = s.T, s.S
    QP = min(128, T); QB = min(512, T)
    L = T if s.causal else S
    KSEL = float(min(TOPK, L // 4))
    ktiles = [(k0, min(128, S - k0)) for k0 in range(0, S, 128)]
    NKT = len(ktiles)
    kiTa = ph.sb([128, S], BF16, "kiTa"); kiTb = ph.sb([128, S], BF16, "kiTb")
    ph.op("pool", I_memset(kiTa.t[64:128, :], 0.0), w=[kiTa])
    ph.op("pool", I_memset(kiTb.t[0:64, :], 0.0), w=[kiTb])
    ph.dma(I_dma(kiTa.t[0:64, :], s.kiT[:, :]), w=[kiTa])
    ph.dma(I_dma(kiTb.t[64:128, :], s.kiT[:, :]), w=[kiTb])
    kiTz = (kiTa, kiTb)
    sc = ph.sb([128, S], F32, "sc")
    mk = ph.sb([128, S], BF16, "mk")
    maskT = ph.sb([128, NKT, QB], BF16, "maskT")
    qit = [ph.sb([128, 4, 128], BF16, f"qit{i}") for i in range(2)]
    wit = [ph.sb([128, 8], F32, f"wit{i}") for i in range(2)]
    dg = [ph.sb([128, 8, 128], BF16, f"dg{i}") for i in range(2)]
    rl = [ph.sb([128, 512], BF16, f"rl{i}") for i in range(2)]
    mid = ph.sb([128, 1], F32, "mid"); cnt = ph.sb([128, 1], F32, "cnt"); tq = ph.sb([128, 1], F32, "tq")
    thr = ph.sb([128, 1], F32, "thr")
    negmid = ph.sb([128, 1], F32, "negmid"); sact = ph.sb([128, 1], F32, "sact"); comb = ph.sb([128, 1], F32, "comb")
    junk2 = ph.sb([128, S], BF16, "junk2")
    qta = ph.sb([128, 4, 512], BF16, "qta"); qtb = ph.sb([128, 4, 512], BF16, "qtb")
    ph.op("pool", I_memset(qta.t[64:128, :, :], 0.0), w=[qta])
    ph.op("pool", I_memset(qtb.t[0:64, :, :], 0.0), w=[qtb])
    qtz = (qta, qtb)
    va = [ph.sb([128, 260], BF16, f"va{i}") for i in range(2)]
    ktl = [ph.sb([128, 2, 128], BF16, f"ktl{i}") for i in range(2)]
    pe_ = [ph.sb([128, 512], BF16, f"pe{i}") for i in range(2)]
    pm = [ph.sb([128, 512], BF16, f"pm{i}") for i in range(2)]
    oa = [ph.sb([128, 512], F32, f"oa{i}") for i in range(2)]
    for o__ in oa:
        ph.op("pool", I_memset(o__.t[:], 0.0), w=[o__])
    rden = [ph.sb([64, 512], F32, f"rden{i}") for i in range(2)]
    t1 = [ph.sb([64, 512], F32, f"t1{i}") for i in range(2)]
    zat = [ph.sb([64, 512], BF16, f"zat{i}") for i in range(2)]
    yat = [ph.sb([64, 512], BF16, f"yat{i}") for i in range(2)]
    iop = ph.sb([128, 64], F32, "iop")
    selden = ph.sb([128, 64], F32, "selden")
    ph.op("pool", I_iota(iop.t[:], [[0, 64]], 0, 1), w=[iop])
    ph.op("dve", I_ts(selden.t[:], iop.t[:], 64.0, None, op0=ALU.is_equal), r=[iop], w=[selden])
    lg = [ph.ps([128, 512], F32, f"lg{i}") for i in range(2)]
    acc = ph.ps([128, 512], F32, "acc")
    mtp = ph.ps([128, 8, 128], BF16, "mtp")
    oacc = [ph.ps([128, 512], F32, f"oacc{i}") for i in range(4)]

    nq = 0
    nr = 0
    nev = 0
    nv = 0
    nh = 0
    for jb in range(T // QB):
        qb0 = jb * QB
        Sblk = min(S, (jb + 1) * QB) if s.causal else S
        kts = [(i, k0, ksz) for i, (k0, ksz) in enumerate(ktiles) if k0 < Sblk]
        for jq in range(QB // QP):
            q0 = qb0 + jq * QP
            qoff = jq * QP
            Slim = (q0 + QP) if s.causal else S
            qi_ = qit[nq % 2]; wi_ = wit[nq % 2]; dg_ = dg[nq % 2]; nq += 1
            ph.dma(I_dma(qi_.t[:, :, :QP], s.qiT.rearrange("(hp p) t -> p hp t", p=128)[:, :, q0:q0 + QP]), w=[qi_])
            ph.dma(I_dma(wi_.t[:QP, :], s.wiS[q0:q0 + QP, :]), w=[wi_])
            for h in range(8):
                ph.op("dve", I_ts(dg_.t[:QP, h, :QP], idf.t[:QP, :QP], wi_.t[:QP, h:h + 1], None, op0=ALU.mult),
                      r=[idf, wi_], w=[dg_])
            for c0 in range(0, Slim, 512):
                csz = min(512, Slim - c0)
                for h in range(8):
                    base = 64 * (h % 2)
                    g = lg[nr % 2]; r_ = rl[nr % 2]; nr += 1
                    ph.op("pe", I_mm(g.t[:QP, :csz], qi_.t[:, h // 2, :QP], kiTz[h % 2].t[:, c0:c0 + csz]),
                          r=[qi_, kiTz[h % 2]], w=[g])
                    ph.op("act", I_act(r_.t[:QP, :csz], g.t[:QP, :csz], AF.Relu), r=[g], w=[r_])
                    ph.op("pe", I_mm(acc.t[:QP, :csz], dg_.t[:QP, h, :QP], r_.t[:QP, :csz], start=(h == 0), stop=(h == 7)),
                          r=[dg_, r_], w=[acc])
                ph.op("dve", I_copy(sc.t[:QP, c0:c0 + csz], acc.t[:QP, :csz]), r=[acc], w=[sc])
            if s.causal and QP == 128:
                ph.op("dve", I_memset(sc.t[0:64, Slim - 64:Slim], NEG), w=[sc])
            ph.op("dve", I_memset(mid.t[:QP, :], 0.0), w=[mid])
            ph.op("dve", I_memset(negmid.t[:QP, :], 0.0), w=[negmid])
            S1 = Slim
            if Slim >= 1024:
                S1 = int(Slim * 0.45) // 128 * 128
            w = W02
            for k in range(NIT2):
                w = w / 2.0
                ph.op("dve", I_ts(mk.t[:QP, :S1], sc.t[:QP, :S1], mid.t[:QP, 0:1], None, op0=ALU.is_ge, op1=ALU.add,
                                  accum_out=cnt.t[:QP, 0:1]), r=[sc, mid], w=[mk, cnt])
                if S1 < Slim:
                    ph.op("act", I_act(junk2.t[:QP, S1:Slim], sc.t[:QP, S1:Slim], AF.Sign, bias=negmid.t[:QP, 0:1],
                                       accum_out=sact.t[:QP, 0:1]), r=[sc, negmid], w=[junk2, sact])
                    ph.op("dve", I_stt(comb.t[:QP, :], sact.t[:QP, :], 0.5, cnt.t[:QP, :], ALU.mult, ALU.add), r=[sact, cnt], w=[comb])
                    src_, kadj = comb, KSEL - 0.5 * (Slim - S1)
                else:
                    src_, kadj = cnt, KSEL
                ph.op("dve", I_ts(tq.t[:QP, :], src_.t[:QP, :], kadj, 2.0 * w, op0=ALU.is_ge, op1=ALU.mult), r=[src_], w=[tq])
                ph.op("dve", I_stt(mid.t[:QP, :], mid.t[:QP, :], -w, tq.t[:QP, :], ALU.add, ALU.add), r=[mid, tq], w=[mid])
                if S1 < Slim:
                    ph.op("dve", I_stt(negmid.t[:QP, :], negmid.t[:QP, :], w, tq.t[:QP, :], ALU.add, ALU.subtract), r=[negmid, tq], w=[negmid])
            ph.op("dve", I_ts(thr.t[:QP, :], mid.t[:QP, :], -w, None, op0=ALU.add), r=[mid], w=[thr])
            ph.op("dve", I_ts(mk.t[:QP, :Slim], sc.t[:QP, :Slim], thr.t[:QP, 0:1], None, op0=ALU.is_ge), r=[sc, thr], w=[mk])
            mine = [(i, k0, ksz) for (i, k0, ksz) in kts if k0 < Slim]
            for g0 in range(0, len(mine), 8):
                grp = mine[g0:g0 + 8]
                for gi, (i, k0, ksz) in enumerate(grp):
                    ph.op("pe", I_tr(mtp.t[:ksz, gi, :QP], mk.t[:QP, k0:k0 + ksz], idb.t[:QP, :QP]), r=[mk, idb], w=[mtp])
                i0 = grp[0][0]
                ng = len(grp)
                eng = "act" if nev % 2 == 0 else "dve"; nev += 1
                if eng == "act":
                    ph.op("act", I_act(maskT.t[:, i0:i0 + ng, qoff:qoff + QP], mtp.t[:, 0:ng, :QP], AF.Copy), r=[mtp], w=[maskT])
                else:
                    ph.op("dve", I_copy(maskT.t[:, i0:i0 + ng, qoff:qoff + QP], mtp.t[:, 0:ng, :QP]), r=[mtp], w=[maskT])
            rest = [(i, k0, ksz) for (i, k0, ksz) in kts if k0 >= Slim]
            if rest:
                i0 = rest[0][0]; i1 = rest[-1][0] + 1
                ph.op("pool", I_memset(maskT.t[:, i0:i1, qoff:qoff + QP], 0.0), w=[maskT])
        import os as _os
        if _os.environ.get("ATTN_STOP") == "idx":
            continue
        qsrc = s.qT.rearrange("(hp p) t -> p hp t", p=128)
        ph.dma(I_dma(qta.t[0:64, :, :QB], qsrc[0:64, :, qb0:qb0 + QB]), w=[qta])
        ph.dma(I_dma(qtb.t[64:128, :, :QB], qsrc[64:128, :, qb0:qb0 + QB]), w=[qtb])
        for hg in range(2):
            for idx, (i, k0, ksz) in enumerate(kts):
                va_ = va[nv % 2]; kt_ = ktl[nv % 2]; nv += 1
                ph.dma(I_dma(va_.t[:ksz, :], s.vaug[k0:k0 + ksz, hg * 260:(hg + 1) * 260]), w=[va_])
                ph.dma(I_dma(kt_.t[:, :, :ksz], s.kT.rearrange("(hp p) s -> p hp s", p=128)[:, 2 * hg:2 * hg + 2, k0:k0 + ksz]), w=[kt_])
                for hh in range(4):
                    h = hg * 4 + hh
                    base = 64 * (h % 2)
                    st_ = lg[nh % 2]; p_ = pe_[nh % 2]; m_ = pm[nh % 2]; nh += 1
                    if _os.environ.get("ATTN_STOP") == "dma":
                        continue
                    ph.op("pe", I_mm(st_.t[:ksz, :QB], kt_.t[:, hh // 2, :ksz], qtz[h % 2].t[:, h // 2, :QB]),
                          r=[kt_, qtz[h % 2]], w=[st_])
                    if _os.environ.get("ATTN_STOP") == "mm":
                        continue
                    ph.op("act", I_act(p_.t[:ksz, :QB], st_.t[:ksz, :QB], AF.Exp), r=[st_], w=[p_])
                    if _os.environ.get("ATTN_STOP") == "exp":
                        continue
                    ph.op("dve", I_tt(m_.t[:ksz, :QB], p_.t[:ksz, :QB], maskT.t[:ksz, i, :QB], ALU.mult), r=[p_, maskT], w=[m_])
                    if _os.environ.get("ATTN_STOP") == "qk":
                        continue
                    ph.op("pe", I_mm(oacc[hh].t[:65, :QB], va_.t[:ksz, hh * 65:(hh + 1) * 65], m_.t[:ksz, :QB], start=(idx == 0), stop=(idx == len(kts) - 1)),
                          r=[va_, m_], w=[oacc[hh]])
            if _os.environ.get("ATTN_STOP") in ("pv", "mm", "exp", "qk", "dma"):
                continue
            for hh in range(4):
                h = hg * 4 + hh
                o_ = oa[hh % 2]; rd = rden[hh % 2]; t_ = t1[hh % 2]; z_ = zat[hh % 2]; y_ = yat[hh % 2]
                ph.op("act", I_act(o_.t[:65, :QB], oacc[hh].t[:65, :QB], AF.Copy), r=[oacc[hh]], w=[o_])
                ph.op("pe", I_mm(acc.t[:64, :QB], selden.t[:, :64], o_.t[:, :QB]), r=[selden, o_], w=[acc])
                ph.op("dve", I_recip(rd.t[:64, :QB], acc.t[:64, :QB]), r=[acc], w=[rd])
                ph.dma(I_dma(z_.t[:64, :QB], s.zaT[64 * h:64 * h + 64, qb0:qb0 + QB]), w=[z_])
                ph.op("dve", I_tt(t_.t[:64, :QB], o_.t[0:64, :QB], rd.t[:64, :QB], ALU.mult), r=[o_, rd], w=[t_])
                ph.op("pool", I_tt(y_.t[:64, :QB], t_.t[:64, :QB], z_.t[:64, :QB], ALU.mult), r=[t_, z_], w=[y_])
                ph.dma(I_dma(s.yaT[64 * h:64 * h + 64, qb0:qb0 + QB], y_.t[:64, :QB]), r=[y_])
    ph.finish()


PI = float(np.pi)


def phase_ssm(nc, A, l, seqs):
    ph = Phase(nc, f"S{l}")
    idf, idb = mk_ident(ph)
    LB = min(512, seqs[0].T)
    sm = lambda nm: ph.sb([128, 16], F32, nm)
    are, aim, ldt, dt, xr, ang, mag = sm("are"), sm("aim"), sm("ldt"), sm("dt"), sm("xr"), sm("ang"), sm("mag")
    kac, angr, angc, angcr, sn1, cs1, abr, abi = sm("kac"), sm("angr"), sm("angc"), sm("angcr"), sm("sn1"), sm("cs1"), sm("abr"), sm("abi")
    t1, t2, den, rdn, nr_, fre, fim, t3, t4 = sm("t1"), sm("t2"), sm("den"), sm("rdn"), sm("nr"), sm("fre"), sm("fim"), sm("t3"), sm("t4")
    wr, wi_, wt = sm("wr"), sm("wi"), sm("wt")
    ph.dma(I_dma(are.t[:], A["a_re"][l].rearrange("(pt q) -> q pt", q=128)), w=[are])
    ph.dma(I_dma(aim.t[:], A["a_im"][l].rearrange("(pt q) -> q pt", q=128)), w=[aim])
    for gl in range(2):
        src = bass.AP(A["log_dt"].tensor, l * 32 + gl, [[0, 64], [2, 16]])
        ph.dma(I_dma(ldt.t[gl * 64:(gl + 1) * 64, :], src), w=[ldt])
    V = "dve"
    pa, pb2, pc = sm("pa"), sm("pb2"), sm("pc")

    def horner(dst, y, coefs):
        ph.op(V, I_memset(dst.t[:], 1.0), w=[dst])
        for c in reversed(coefs):
            ph.op(V, I_tt(dst.t[:], dst.t[:], y.t[:], ALU.mult), r=[dst, y], w=[dst])
            ph.op(V, I_ts(dst.t[:], dst.t[:], float(c), 1.0, op0=ALU.mult, op1=ALU.add), r=[dst], w=[dst])

    def exp_acc(dst, src, nsq, deg):
        ph.op(V, I_ts(pa.t[:], src.t[:], 1.0 / (2 ** nsq), None, op0=ALU.mult), r=[src], w=[pa])
        horner(dst, pa, [1.0 / k for k in range(1, deg + 1)])
        for _ in range(nsq):
            ph.op(V, I_tt(dst.t[:], dst.t[:], dst.t[:], ALU.mult), r=[dst], w=[dst])

    def sincos_acc(sdst, cdst, x):
        ph.op(V, I_ts(pa.t[:], x.t[:], 0.125, None, op0=ALU.mult), r=[x], w=[pa])
        ph.op(V, I_tt(pb2.t[:], pa.t[:], pa.t[:], ALU.mult), r=[pa], w=[pb2])
        horner(sdst, pb2, [-1.0 / 6, -1.0 / 20, -1.0 / 42, -1.0 / 72, -1.0 / 110])
        ph.op(V, I_tt(sdst.t[:], sdst.t[:], pa.t[:], ALU.mult), r=[sdst, pa], w=[sdst])
        horner(cdst, pb2, [-1.0 / 2, -1.0 / 12, -1.0 / 30, -1.0 / 56, -1.0 / 90, -1.0 / 132])
        for _ in range(3):
            ph.op(V, I_tt(pc.t[:], sdst.t[:], sdst.t[:], ALU.mult), r=[sdst], w=[pc])
            ph.op(V, I_stt(sdst.t[:], sdst.t[:], 2.0, cdst.t[:], ALU.mult, ALU.mult), r=[sdst, cdst], w=[sdst])
            ph.op(V, I_ts(cdst.t[:], pc.t[:], -2.0, 1.0, op0=ALU.mult, op1=ALU.add), r=[pc], w=[cdst])

    exp_acc(dt, ldt, 3, 14)
    ph.op(V, I_tt(xr.t[:], are.t[:], dt.t[:], ALU.mult), r=[are, dt], w=[xr])
    ph.op(V, I_tt(ang.t[:], aim.t[:], dt.t[:], ALU.mult), r=[aim, dt], w=[ang])
    exp_acc(mag, xr, 0, 8)

    def reduce_angle(src, dst):
        ph.op(V, I_ts(kac.t[:], src.t[:], PI, None, op0=ALU.is_gt), r=[src], w=[kac])
        for j in range(1, 6):
            ph.op(V, I_stt(kac.t[:], src.t[:], (2 * j + 1) * PI, kac.t[:], ALU.is_gt, ALU.add), r=[src, kac], w=[kac])
        ph.op(V, I_stt(dst.t[:], kac.t[:], -2.0 * PI, src.t[:], ALU.mult, ALU.add), r=[kac, src], w=[dst])
    reduce_angle(ang, angr)
    sincos_acc(sn1, cs1, angr)
    ph.op(V, I_tt(t1.t[:], sn1.t[:], sn1.t[:], ALU.mult), r=[sn1], w=[t1])
    ph.op(V, I_tt(t2.t[:], cs1.t[:], cs1.t[:], ALU.mult), r=[cs1], w=[t2])
    ph.op(V, I_tt(t3.t[:], t1.t[:], t2.t[:], ALU.add), r=[t1, t2], w=[t3])
    ph.op(V, I_ts(t3.t[:], t3.t[:], -0.5, 1.5, op0=ALU.mult, op1=ALU.add), r=[t3], w=[t3])
    ph.op(V, I_tt(sn1.t[:], sn1.t[:], t3.t[:], ALU.mult), r=[sn1, t3], w=[sn1])
    ph.op(V, I_tt(cs1.t[:], cs1.t[:], t3.t[:], ALU.mult), r=[cs1, t3], w=[cs1])
    ph.op(V, I_tt(abr.t[:], mag.t[:], cs1.t[:], ALU.mult), r=[mag, cs1], w=[abr])
    ph.op(V, I_tt(abi.t[:], mag.t[:], sn1.t[:], ALU.mult), r=[mag, sn1], w=[abi])
    ph.op(V, I_tt(t1.t[:], are.t[:], are.t[:], ALU.mult), r=[are], w=[t1])
    ph.op(V, I_tt(t2.t[:], aim.t[:], aim.t[:], ALU.mult), r=[aim], w=[t2])
    ph.op(V, I_tt(den.t[:], t1.t[:], t2.t[:], ALU.add), r=[t1, t2], w=[den])
    ph.op(V, I_recip(rdn.t[:], den.t[:]), r=[den], w=[rdn])
    ph.op(V, I_ts(nr_.t[:], abr.t[:], -1.0, None, op0=ALU.add), r=[abr], w=[nr_])
    ph.op(V, I_tt(t1.t[:], nr_.t[:], are.t[:], ALU.mult), r=[nr_, are], w=[t1])
    ph.op(V, I_tt(t2.t[:], abi.t[:], aim.t[:], ALU.mult), r=[abi, aim], w=[t2])
    ph.op(V, I_tt(t3.t[:], t1.t[:], t2.t[:], ALU.add), r=[t1, t2], w=[t3])
    ph.op(V, I_tt(fre.t[:], t3.t[:], rdn.t[:], ALU.mult), r=[t3, rdn], w=[fre])
    ph.op(V, I_tt(t1.t[:], abi.t[:], are.t[:], ALU.mult), r=[abi, are], w=[t1])
    ph.op(V, I_tt(t2.t[:], nr_.t[:], aim.t[:], ALU.mult), r=[nr_, aim], w=[t2])
    ph.op(V, I_tt(t4.t[:], t1.t[:], t2.t[:], ALU.subtract), r=[t1, t2], w=[t4])
    ph.op(V, I_tt(fim.t[:], t4.t[:], rdn.t[:], ALU.mult), r=[t4, rdn], w=[fim])

    bre_t = ph.sb([128, 16, 16], F32, "bre_t"); bim_t = ph.sb([128, 16, 16], F32, "bim_t")
    cre_t = ph.sb([128, 16, 16], F32, "cre_t"); cim_t = ph.sb([128, 16, 16], F32, "cim_t")
    for t_, nm in ((bre_t, "b_re"), (bim_t, "b_im"), (cre_t, "c_reT"), (cim_t, "c_imT")):
        ph.dma(I_dma(t_.t[:], A[nm][l].rearrange("(pt q) n -> q pt n", q=128)), w=[t_])
    padall = ph.sb([128, 16, 2, 128], F32, "padall")
    Cpad = ph.sb([128, 16, 2, 128], BF16, "Cpad")
    ph.op("pool", I_memset(padall.t[:], 0.0), w=[padall])
    ph.op("pool", I_memset(Cpad.t[:], 0.0), w=[Cpad])
    tb16 = [ph.sb([128, 16], F32, f"tb16{i}") for i in range(2)]
    for pt in range(16):
        qq = pt % 4
        for half in range(2):
            rows = slice(half * 64, (half + 1) * 64)
            cols = slice(32 * qq + 16 * half, 32 * qq + 16 * half + 16)
            ta, tb = tb16
            ph.op(V, I_ts(ta.t[rows, :], bim_t.t[rows, pt, :], fim.t[rows, pt:pt + 1], None, op0=ALU.mult), r=[bim_t, fim], w=[ta])
            ph.op(V, I_stt(padall.t[rows, pt, 0, cols], bre_t.t[rows, pt, :], fre.t[rows, pt:pt + 1], ta.t[rows, :], ALU.mult, ALU.subtract),
                  r=[bre_t, fre, ta], w=[padall])
            ph.op(V, I_ts(tb.t[rows, :], bre_t.t[rows, pt, :], fim.t[rows, pt:pt + 1], None, op0=ALU.mult), r=[bre_t, fim], w=[tb])
            ph.op(V, I_stt(padall.t[rows, pt, 1, cols], bim_t.t[rows, pt, :], fre.t[rows, pt:pt + 1], tb.t[rows, :], ALU.mult, ALU.add),
                  r=[bim_t, fre, tb], w=[padall])
            ph.op("pool", I_copy(Cpad.t[rows, pt, 0, cols], cre_t.t[rows, pt, :]), r=[cre_t], w=[Cpad])
            ph.op("pool", I_ts(Cpad.t[rows, pt, 1, cols], cim_t.t[rows, pt, :], -1.0, None, op0=ALU.mult), r=[cim_t], w=[Cpad])
    pB = [ph.ps([128, 512], F32, f"pB{i}") for i in range(4)]
    pY = [ph.ps([128, 512], F32, f"pY{i}") for i in range(2)]
    pG = ph.ps([128, 512], F32, "pG")
    E = [[ph.sb([128, 128], BF16, f"E{part}_{pt}") for pt in range(16)] for part in range(2)]
    for part in range(2):
        for pt in range(16):
            ph.op("pe", I_mm(pG.t[:, 0:128], padall.t[:, pt, part, :], idf.t[:, :]), r=[padall, idf], w=[pG])
            ph.op("act", I_act(E[part][pt].t[:], pG.t[:, 0:128], AF.Copy), r=[pG], w=[E[part][pt]])
    dsk = ph.sb([128, 4], F32, "dsk")
    ph.dma(I_dma(dsk.t[:], A["d_skip"][l].rearrange("(ct q) -> q ct", q=128)), w=[dsk])
    bglu = ph.sb([128, 4], F32, "bglu")
    ph.dma(I_dma(bglu.t[:], A["b_glu"][l].rearrange("(co q) -> q co", q=128)), w=[bglu])
    wglu = ph.sb([128, 4, 512], BF16, "wglu")
    for c in range(4):
        ph.dma(I_dma(wglu.t[:, c, :], A["w_glu"][l, c * 128:(c + 1) * 128, :]), w=[wglu], q="pool")
    cs = ph.sb([128, 16, LB], F32, "cs"); sn = ph.sb([128, 16, LB], F32, "sn")
    ph.op("pool", I_memset(cs.t[:, :, 0:1], 1.0), w=[cs])
    ph.op("pool", I_memset(sn.t[:, :, 0:1], 0.0), w=[sn])
    ph.op(V, I_copy(wr.t[:], cs1.t[:]), r=[cs1], w=[wr])
    ph.op(V, I_copy(wi_.t[:], sn1.t[:]), r=[sn1], w=[wi_])
    tmpc = ph.sb([128, LB], F32, "tmpc"); tmps = ph.sb([128, LB], F32, "tmps")
    n = 1
    while n < LB:
        for pt in range(16):
            ph.op(V, I_ts(tmpc.t[:, 0:n], sn.t[:, pt, 0:n], wi_.t[:, pt:pt + 1], None, op0=ALU.mult), r=[sn, wi_], w=[tmpc])
            ph.op("pool", I_ts(tmps.t[:, 0:n], cs.t[:, pt, 0:n], wi_.t[:, pt:pt + 1], None, op0=ALU.mult), r=[cs, wi_], w=[tmps])
            ph.op(V, I_stt(cs.t[:, pt, n:2 * n], cs.t[:, pt, 0:n], wr.t[:, pt:pt + 1], tmpc.t[:, 0:n], ALU.mult, ALU.subtract),
                  r=[cs, wr, tmpc], w=[cs])
            ph.op(V, I_stt(sn.t[:, pt, n:2 * n], sn.t[:, pt, 0:n], wr.t[:, pt:pt + 1], tmps.t[:, 0:n], ALU.mult, ALU.add),
                  r=[sn, wr, tmps], w=[sn])
        ph.op(V, I_tt(t1.t[:], wi_.t[:], wi_.t[:], ALU.mult), r=[wi_], w=[t1])
        ph.op(V, I_tt(t2.t[:], wr.t[:], wr.t[:], ALU.mult), r=[wr], w=[t2])
        ph.op(V, I_stt(wt.t[:], wr.t[:], 2.0, wi_.t[:], ALU.mult, ALU.mult), r=[wr, wi_], w=[wt])
        ph.op(V, I_tt(wr.t[:], t2.t[:], t1.t[:], ALU.subtract), r=[t1, t2], w=[wr])
        ph.op(V, I_copy(wi_.t[:], wt.t[:]), r=[wt], w=[wi_])
        n *= 2

    wk = lambda nm: ph.sb([128, LB], F32, nm)
    m1, m2, m3, m4, bre_, bim_, gre, gim, hre, him = [wk(n_) for n_ in ("m1", "m2", "m3", "m4", "bre", "bim", "gre", "gim", "hre", "him")]
    hreb = ph.sb([128, LB], BF16, "hreb"); himb = ph.sb([128, LB], BF16, "himb")
    yv, x2, inn, in2, sg = [wk(n_) for n_ in ("yv", "x2", "inn", "in2", "sg")]
    yg = [wk(f"yg{i}") for i in range(4)]
    ygb = [ph.sb([128, LB], BF16, f"ygb{i}") for i in range(4)]
    ut = [[ph.sb([128, LB], BF16, f"ut{k}{i}") for i in range(4)] for k in range(2)]
    zst = [ph.sb([128, LB], BF16, f"zst{i}") for i in range(2)]
    sgl = wk("sgl"); tg = wk("tg")
    ysf = [ph.sb([128, LB], BF16, f"ysf{i}") for i in range(2)]
    hpr = [sm(f"hpr{i}") for i in range(3)]; hpi = [sm(f"hpi{i}") for i in range(3)]
    a0 = ph.sb([128, 4], F32, "a0")
    nb = 0
    nz = 0
    for s in seqs:
        Lb = min(LB, s.T)
        hr_, hi_ = hpr[s.i], hpi[s.i]
        prev = s.i > 0
        if prev:
            ph.dma(I_dma(hr_.t[:], A["ssr"][l, s.i - 1].rearrange("(pt q) -> q pt", q=128)), w=[hr_])
            ph.dma(I_dma(hi_.t[:], A["ssi"][l, s.i - 1].rearrange("(pt q) -> q pt", q=128)), w=[hi_])
        for blk in range(s.T // Lb):
            t0 = blk * Lb
            u_ = ut[blk % 2]
            for ct in range(4):
                ph.dma(I_dma(u_[ct].t[:, :Lb], s.uT[ct * 128:(ct + 1) * 128, t0:t0 + Lb]), w=[u_[ct]])
            for ct in range(4):
                py = pY[ct % 2]
                for qq in range(4):
                    pt = ct * 4 + qq
                    pr = pB[(nb % 2) * 2]; pi_ = pB[(nb % 2) * 2 + 1]; nb += 1
                    ph.op("pe", I_mm(pr.t[:, :Lb], E[0][pt].t[:, :], u_[ct].t[:, :Lb]), r=[E[0][pt], u_[ct]], w=[pr])
                    ph.op("pe", I_mm(pi_.t[:, :Lb], E[1][pt].t[:, :], u_[ct].t[:, :Lb]), r=[E[1][pt], u_[ct]], w=[pi_])
                    c_ = cs.t[:, pt, :Lb]; s_ = sn.t[:, pt, :Lb]
                    ph.op(V, I_tt(m1.t[:, :Lb], pr.t[:, :Lb], c_, ALU.mult), r=[pr, cs], w=[m1])
                    ph.op(V, I_tt(m2.t[:, :Lb], pi_.t[:, :Lb], s_, ALU.mult), r=[pi_, sn], w=[m2])
                    ph.op(V, I_tt(m3.t[:, :Lb], pi_.t[:, :Lb], c_, ALU.mult), r=[pi_, cs], w=[m3])
                    ph.op(V, I_tt(m4.t[:, :Lb], pr.t[:, :Lb], s_, ALU.mult), r=[pr, sn], w=[m4])
                    ph.op("pool", I_tt(bre_.t[:, :Lb], m1.t[:, :Lb], m2.t[:, :Lb], ALU.add), r=[m1, m2], w=[bre_])
                    ph.op("pool", I_tt(bim_.t[:, :Lb], m3.t[:, :Lb], m4.t[:, :Lb], ALU.subtract), r=[m3, m4], w=[bim_])
                    if prev:
                        ph.op(V, I_tt(a0.t[:, 0:1], abi.t[:, pt:pt + 1], hi_.t[:, pt:pt + 1], ALU.mult), r=[abi, hi_], w=[a0])
                        ph.op(V, I_stt(a0.t[:, 1:2], abr.t[:, pt:pt + 1], hr_.t[:, pt:pt + 1], a0.t[:, 0:1], ALU.mult, ALU.subtract),
                              r=[abr, hr_, a0], w=[a0])
                        ph.op(V, I_tt(a0.t[:, 2:3], abi.t[:, pt:pt + 1], hr_.t[:, pt:pt + 1], ALU.mult), r=[abi, hr_], w=[a0])
                        ph.op(V, I_stt(a0.t[:, 3:4], abr.t[:, pt:pt + 1], hi_.t[:, pt:pt + 1], a0.t[:, 2:3], ALU.mult, ALU.add),
                              r=[abr, hi_, a0], w=[a0])
                        ph.op(V, I_tt(bre_.t[:, 0:1], bre_.t[:, 0:1], a0.t[:, 1:2], ALU.add), r=[bre_, a0], w=[bre_])
                        ph.op(V, I_tt(bim_.t[:, 0:1], bim_.t[:, 0:1], a0.t[:, 3:4], ALU.add), r=[bim_, a0], w=[bim_])
                    rb = mag.t[:, pt:pt + 1].to_broadcast([128, Lb])
                    ph.op(V, I_scan(gre.t[:, :Lb], rb, bre_.t[:, :Lb], 0.0), r=[mag, bre_], w=[gre])
                    ph.op(V, I_scan(gim.t[:, :Lb], rb, bim_.t[:, :Lb], 0.0), r=[mag, bim_], w=[gim])
                    ph.op(V, I_tt(m1.t[:, :Lb], gre.t[:, :Lb], c_, ALU.mult), r=[gre, cs], w=[m1])
                    ph.op(V, I_tt(m2.t[:, :Lb], gim.t[:, :Lb], s_, ALU.mult), r=[gim, sn], w=[m2])
                    ph.op("pool", I_tt(m3.t[:, :Lb], gre.t[:, :Lb], s_, ALU.mult), r=[gre, sn], w=[m3])
                    ph.op("pool", I_tt(m4.t[:, :Lb], gim.t[:, :Lb], c_, ALU.mult), r=[gim, cs], w=[m4])
                    ph.op(V, I_tt(hre.t[:, :Lb], m1.t[:, :Lb], m2.t[:, :Lb], ALU.subtract), r=[m1, m2], w=[hre])
                    ph.op("pool", I_tt(him.t[:, :Lb], m3.t[:, :Lb], m4.t[:, :Lb], ALU.add), r=[m3, m4], w=[him])
                    ph.op("act", I_act(hr_.t[:, pt:pt + 1], hre.t[:, Lb - 1:Lb], AF.Copy), r=[hre], w=[hr_])
                    ph.op("act", I_act(hi_.t[:, pt:pt + 1], him.t[:, Lb - 1:Lb], AF.Copy), r=[him], w=[hi_])
                    ph.op("act", I_act(hreb.t[:, :Lb], hre.t[:, :Lb], AF.Copy), r=[hre], w=[hreb])
                    ph.op("act", I_act(himb.t[:, :Lb], him.t[:, :Lb], AF.Copy), r=[him], w=[himb])
                    ph.op("pe", I_mm(py.t[:, :Lb], Cpad.t[:, pt, 0, :], hreb.t[:, :Lb], start=(qq == 0), stop=False), r=[Cpad, hreb], w=[py])
                    ph.op("pe", I_mm(py.t[:, :Lb], Cpad.t[:, pt, 1, :], himb.t[:, :Lb], start=False, stop=(qq == 3)), r=[Cpad, himb], w=[py])
                ph.op(V, I_stt(yv.t[:, :Lb], u_[ct].t[:, :Lb], dsk.t[:, ct:ct + 1], py.t[:, :Lb], ALU.mult, ALU.add), r=[u_[ct], dsk, py], w=[yv])
                ph.op("pool", I_tt(x2.t[:, :Lb], yv.t[:, :Lb], yv.t[:, :Lb], ALU.mult), r=[yv], w=[x2])
                ph.op("pool", I_ts(inn.t[:, :Lb], x2.t[:, :Lb], 0.044715, 1.0, op0=ALU.mult, op1=ALU.add), r=[x2], w=[inn])
                ph.op("pool", I_tt(in2.t[:, :Lb], inn.t[:, :Lb], yv.t[:, :Lb], ALU.mult), r=[inn, yv], w=[in2])
                ph.op("act", I_act(sg.t[:, :Lb], in2.t[:, :Lb], AF.Sigmoid, scale=1.5957691216057308), r=[in2], w=[sg])
                ph.op(V, I_tt(yg[ct].t[:, :Lb], yv.t[:, :Lb], sg.t[:, :Lb], ALU.mult), r=[yv, sg], w=[yg[ct]])
                ph.op("act", I_act(ygb[ct].t[:, :Lb], yg[ct].t[:, :Lb], AF.Copy), r=[yg[ct]], w=[ygb[ct]])
            prev = True
            for co in range(4):
                z_ = zst[nz % 2]; yf = ysf[nz % 2]; nz += 1
                for ct in range(4):
                    ph.op("pe", I_mm(pG.t[:, :Lb], wglu.t[:, ct, co * 128:(co + 1) * 128], ygb[ct].t[:, :Lb], start=(ct == 0), stop=(ct == 3)),
                          r=[wglu, ygb[ct]], w=[pG])
                ph.op("act", I_act(sgl.t[:, :Lb], pG.t[:, :Lb], AF.Sigmoid, bias=bglu.t[:, co:co + 1]), r=[pG, bglu], w=[sgl])
                ph.dma(I_dma(z_.t[:, :Lb], s.zsT[co * 128:(co + 1) * 128, t0:t0 + Lb]), w=[z_])
                ph.op(V, I_tt(tg.t[:, :Lb], yg[co].t[:, :Lb], sgl.t[:, :Lb], ALU.mult), r=[yg[co], sgl], w=[tg])
                ph.op("pool", I_tt(yf.t[:, :Lb], tg.t[:, :Lb], z_.t[:, :Lb], ALU.mult), r=[tg, z_], w=[yf])
                ph.dma(I_dma(s.ysT[co * 128:(co + 1) * 128, t0:t0 + Lb], yf.t[:, :Lb]), r=[yf])
        ph.dma(I_dma(s.srout[l].rearrange("(pt q) -> q pt", q=128), hr_.t[:]), r=[hr_])
        ph.dma(I_dma(s.siout[l].rearrange("(pt q) -> q pt", q=128), hi_.t[:]), r=[hi_])
    ph.finish()


def bc_mid(ap, n):
    return bass.AP(ap.tensor, ap.offset, [list(ap.ap[0]), [0, n]] + [list(x) for x in ap.ap[1:]])


NIT2 = 20
W02 = 16.0


def phase_attn2(nc, A, l, s):
    ph = Phase(nc, f"B{l}{s.nm}")
    idf, idb = mk_ident(ph)
    T, S = s.T, s.S
    QP = min(128, T); QB = min(256, T)
    TPB = QB // QP
    ntile = T // QP; nblk = T // QB
    L = T if s.causal else S
    KSEL = float(min(TOPK, L // 4))
    ktiles = [(k0, min(128, S - k0)) for k0 in range(0, S, 128)]
    NKT = len(ktiles)
    kiT2 = ph.sb([128, S], BF16, "kiT2")
    ph.dma(I_dma(kiT2.t[0:64, :], s.kiT[:, :]), w=[kiT2])
    ph.dma(I_dma(kiT2.t[64:128, :], s.kiT[:, :]), w=[kiT2])
    sc = [ph.sb([128, S], F32, f"sc{i}") for i in range(2)]
    junk = ph.sb([128, S], BF16, "junk")
    maskT = [ph.sb([128, NKT, QB], BF16, f"maskT{i}") for i in range(2)]
    qia = [ph.sb([128, 4, 128], BF16, f"qia{i}") for i in range(2)]
    qib = [ph.sb([128, 4, 128], BF16, f"qib{i}") for i in range(2)]
    qta = [ph.sb([128, 4, 256], BF16, f"qta{i}") for i in range(2)]
    qtb = [ph.sb([128, 4, 256], BF16, f"qtb{i}") for i in range(2)]
    for i in range(2):
        ph.op("pool", I_memset(qia[i].t[64:128, :, :], 0.0), w=[qia[i]])
        ph.op("pool", I_memset(qib[i].t[0:64, :, :], 0.0), w=[qib[i]])
        ph.op("pool", I_memset(qta[i].t[64:128, :, :], 0.0), w=[qta[i]])
        ph.op("pool", I_memset(qtb[i].t[0:64, :, :], 0.0), w=[qtb[i]])
    wit = [ph.sb([128, 8], F32, f"wit{i}") for i in range(2)]
    dg = [ph.sb([128, 8, 128], BF16, f"dg{i}") for i in range(2)]
    rl = [ph.sb([128, 512], BF16, f"rl{i}") for i in range(3)]
    sm1 = lambda nm: ph.sb([128, 1], F32, nm)
    mid, negmid, cnt, sact, comb, tq, thr = [sm1(n_) for n_ in ("mid", "negmid", "cnt", "sact", "comb", "tq", "thr")]
    dgthr = ph.sb([128, 128], F32, "dgthr")
    thrbc = ph.sb([128, 128], F32, "thrbc")
    onesf = ph.sb([128, 128], F32, "onesf")
    ph.op("pool", I_memset(onesf.t[:], 1.0), w=[onesf])
    va = [ph.sb([128, 130], BF16, f"va{i}") for i in range(3)]
    ktl = [ph.sb([128, 128], BF16, f"ktl{i}") for i in range(3)]
    pe_ = [ph.sb([128, 2, 256], BF16, f"pe{i}") for i in range(2)]
    pm = [ph.sb([128, 2, 256], BF16, f"pm{i}") for i in range(2)]
    oa = [ph.sb([128, 256], F32, f"oa{i}") for i in range(2)]
    for o__ in oa:
        ph.op("pool", I_memset(o__.t[:], 0.0), w=[o__])
    rden = [ph.sb([64, 256], F32, f"rden{i}") for i in range(2)]
    t1 = [ph.sb([64, 256], F32, f"t1{i}") for i in range(2)]
    zat = [ph.sb([64, 256], BF16, f"zat{i}") for i in range(2)]
    yat = [ph.sb([64, 256], BF16, f"yat{i}") for i in range(2)]
    iop = ph.sb([128, 64], F32, "iop")
    selden = ph.sb([128, 64], F32, "selden")
    ph.op("pool", I_iota(iop.t[:], [[0, 64]], 0, 1), w=[iop])
    ph.op("dve", I_ts(selden.t[:], iop.t[:], 64.0, None, op0=ALU.is_equal), r=[iop], w=[selden])
    lg = [ph.ps([128, 512], F32, f"lg{i}") for i in range(2)]
    acc = ph.ps([128, 512], F32, "acc")
    st = ph.ps([128, 2, 256], F32, "st")
    oacc = [ph.ps([128, 512], F32, f"oacc{i}") for i in range(2)]
    scTp = ph.ps([128, 4, 128], F32, "scTp")
    misc = ph.ps([128, 512], F32, "misc")
    cnts = {"r": 0, "v": 0, "h": 0, "e": 0}

    def tile_info(i):
        q0 = i * QP
        Slim = (q0 + QP) if s.causal else S
        return q0, Slim

    def blk_kts(b):
        Sblk = min(S, (b + 1) * QB) if s.causal else S
        return [(i, k0, ksz) for i, (k0, ksz) in enumerate(ktiles) if k0 < Sblk]

    def gen_index(i):
        q0, Slim = tile_info(i)
        sl = i % 2
        qa, qb_, wi_, dg_, sc_ = qia[sl], qib[sl], wit[sl], dg[sl], sc[sl]
        qsrc = s.qiT.rearrange("(hp p) t -> p hp t", p=128)
        ph.dma(I_dma(qa.t[0:64, :, :QP], qsrc[0:64, :, q0:q0 + QP]), w=[qa])
        ph.dma(I_dma(qb_.t[64:128, :, :QP], qsrc[64:128, :, q0:q0 + QP]), w=[qb_])
        ph.dma(I_dma(wi_.t[:QP, :], s.wiS[q0:q0 + QP, :]), w=[wi_])
        for h in range(8):
            ph.op("dve", I_ts(dg_.t[:QP, h, :QP], idf.t[:QP, :QP], wi_.t[:QP, h:h + 1], None, op0=ALU.mult), r=[idf, wi_], w=[dg_])
        yield
        for c0 in range(0, Slim, 512):
            csz = min(512, Slim - c0)
            for h in range(8):
                g = lg[cnts["r"] % 2]; r_ = rl[cnts["r"] % 3]; cnts["r"] += 1
                qz = qa if h % 2 == 0 else qb_
                ph.op("pe", I_mm(g.t[:QP, :csz], qz.t[:, h // 2, :QP], kiT2.t[:, c0:c0 + csz]), r=[qz, kiT2], w=[g])
                if h % 4 == 3:
                    ph.op("dve", I_ts(r_.t[:QP, :csz], g.t[:QP, :csz], 0.0, None, op0=ALU.max), r=[g], w=[r_])
                else:
                    ph.op("act", I_act(r_.t[:QP, :csz], g.t[:QP, :csz], AF.Relu), r=[g], w=[r_])
                ph.op("pe", I_mm(acc.t[:QP, :csz], dg_.t[:QP, h, :QP], r_.t[:QP, :csz], start=(h == 0), stop=(h == 7)), r=[dg_, r_], w=[acc])
            ph.op("dve", I_copy(sc_.t[:QP, c0:c0 + csz], acc.t[:QP, :csz]), r=[acc], w=[sc_])
            yield
        if s.causal and QP == 128:
            ph.op("pool", I_memset(sc_.t[0:64, Slim - 64:Slim], NEG), w=[sc_])

    def n_index(i):
        q0, Slim = tile_info(i)
        return 1 + (Slim + 511) // 512

    def gen_bisect(i):
        q0, Slim = tile_info(i)
        sc_ = sc[i % 2]
        b = i // TPB
        qoff = (i % TPB) * QP
        mT = maskT[b % 2]
        S1 = Slim
        if Slim >= 1024:
            S1 = int(Slim * 0.45) // 128 * 128
        ph.op("dve", I_memset(mid.t[:QP, :], 0.0), w=[mid])
        ph.op("dve", I_memset(negmid.t[:QP, :], 0.0), w=[negmid])
        w = W02
        for k in range(NIT2):
            w = w / 2.0
            ph.op("dve", I_ts(junk.t[:QP, :S1], sc_.t[:QP, :S1], mid.t[:QP, 0:1], None, op0=ALU.is_ge, op1=ALU.add,
                              accum_out=cnt.t[:QP, 0:1]), r=[sc_, mid], w=[cnt])
            if S1 < Slim:
                ph.op("act", I_act(junk.t[:QP, S1:Slim], sc_.t[:QP, S1:Slim], AF.Sign, bias=negmid.t[:QP, 0:1],
                                   accum_out=sact.t[:QP, 0:1]), r=[sc_, negmid], w=[sact])
                ph.op("dve", I_stt(comb.t[:QP, :], sact.t[:QP, :], 0.5, cnt.t[:QP, :], ALU.mult, ALU.add), r=[sact, cnt], w=[comb])
                src, kadj = comb, KSEL - 0.5 * (Slim - S1)
            else:
                src, kadj = cnt, KSEL
            ph.op("dve", I_ts(tq.t[:QP, :], src.t[:QP, :], kadj, 2.0 * w, op0=ALU.is_ge, op1=ALU.mult), r=[src], w=[tq])
            ph.op("dve", I_stt(mid.t[:QP, :], mid.t[:QP, :], -w, tq.t[:QP, :], ALU.add, ALU.add), r=[mid, tq], w=[mid])
            if S1 < Slim:
                ph.op("dve", I_stt(negmid.t[:QP, :], negmid.t[:QP, :], w, tq.t[:QP, :], ALU.add, ALU.subtract), r=[negmid, tq], w=[negmid])
            yield
        ph.op("dve", I_ts(thr.t[:QP, :], mid.t[:QP, :], -w, None, op0=ALU.add), r=[mid], w=[thr])
        ph.op("dve", I_ts(dgthr.t[:QP, :QP], idf.t[:QP, :QP], thr.t[:QP, 0:1], None, op0=ALU.mult), r=[idf, thr], w=[dgthr])
        ph.op("pe", I_mm(misc.t[:, 0:QP], onesf.t[:QP, :], dgthr.t[:QP, :QP]), r=[onesf, dgthr], w=[misc])
        ph.op("act", I_act(thrbc.t[:, :QP], misc.t[:, 0:QP], AF.Copy), r=[misc], w=[thrbc])
        yield
        kts = blk_kts(b)
        mine = [(i_, k0, ksz) for (i_, k0, ksz) in kts if k0 < Slim]
        for g0 in range(0, len(mine), 4):
            grp = mine[g0:g0 + 4]
            for gi, (i_, k0, ksz) in enumerate(grp):
                ph.op("pe", I_tr(scTp.t[:ksz, gi, :QP], sc_.t[:QP, k0:k0 + ksz], idf.t[:QP, :QP]), r=[sc_, idf], w=[scTp])
            i0 = grp[0][0]; ng = len(grp)
            ph.op("dve", I_tt(mT.t[:, i0:i0 + ng, qoff:qoff + QP], scTp.t[:, 0:ng, :QP], bc_mid(thrbc.t[:, 0:QP], ng), ALU.is_ge),
                  r=[scTp, thrbc], w=[mT])
            yield
        rest = [(i_, k0, ksz) for (i_, k0, ksz) in kts if k0 >= Slim]
        if rest:
            i0 = rest[0][0]; i1 = rest[-1][0] + 1
            ph.op("pool", I_memset(mT.t[:, i0:i1, qoff:qoff + QP], 0.0), w=[mT])

    def n_bisect(i):
        q0, Slim = tile_info(i)
        nm = len([1 for (k0, ksz) in ktiles if k0 < Slim])
        return NIT2 + 1 + (nm + 3) // 4

    import os as _os4
    _stop = _os4.environ.get("ATTN2_STOP", "")

    def gen_attn(b):
        qb0 = b * QB
        kts = blk_kts(b)
        mT = maskT[b % 2]
        qa, qb_ = qta[b % 2], qtb[b % 2]
        qsrc = s.qT.rearrange("(hp p) t -> p hp t", p=128)
        ph.dma(I_dma(qa.t[0:64, :, :QB], qsrc[0:64, :, qb0:qb0 + QB]), w=[qa])
        ph.dma(I_dma(qb_.t[64:128, :, :QB], qsrc[64:128, :, qb0:qb0 + QB]), w=[qb_])
        for hg in range(4):
            for idx, (i_, k0, ksz) in enumerate(kts):
                va_ = va[cnts["v"] % 3]; kt_ = ktl[cnts["v"] % 3]; cnts["v"] += 1
                p_ = pe_[cnts["h"] % 2]; m_ = pm[cnts["h"] % 2]; cnts["h"] += 1
                ph.dma(I_dma(va_.t[:ksz, :], s.vaug[k0:k0 + ksz, hg * 130:(hg + 1) * 130]), w=[va_])
                ph.dma(I_dma(kt_.t[:, :ksz], s.kT[hg * 128:(hg + 1) * 128, k0:k0 + ksz]), w=[kt_])
                ph.op("pe", I_mm(st.t[:ksz, 0, :QB], kt_.t[:, :ksz], qa.t[:, hg, :QB]), r=[kt_, qa], w=[st])
                ph.op("pe", I_mm(st.t[:ksz, 1, :QB], kt_.t[:, :ksz], qb_.t[:, hg, :QB]), r=[kt_, qb_], w=[st])
                if _stop == "st":
                    yield
                    continue
                ph.op("act", I_act(p_.t[:ksz, :, :QB], st.t[:ksz, :, :QB], AF.Exp), r=[st], w=[p_])
                if _stop == "exp":
                    yield
                    continue
                ph.op("dve", I_tt(m_.t[:ksz, :, :QB], p_.t[:ksz, :, :QB], bc_mid(mT.t[:ksz, i_, :QB], 2), ALU.mult), r=[p_, mT], w=[m_])
                if _stop == "mul":
                    yield
                    continue
                for hh in range(2):
                    ph.op("pe", I_mm(oacc[hh].t[:65, :QB], va_.t[:ksz, hh * 65:(hh + 1) * 65], m_.t[:ksz, hh, :QB],
                                     start=(idx == 0), stop=(idx == len(kts) - 1)), r=[va_, m_], w=[oacc[hh]])
                yield
            if _stop in ("st", "exp", "mul", "pv"):
                yield
                continue
            for hh in range(2):
                h = hg * 2 + hh
                e = cnts["e"] % 2; cnts["e"] += 1
                o_ = oa[e]; rd = rden[e]; t_ = t1[e]; z_ = zat[e]; y_ = yat[e]
                ph.op("act", I_act(o_.t[:65, :QB], oacc[hh].t[:65, :QB], AF.Copy), r=[oacc[hh]], w=[o_])
                if _stop == "ep1":
                    continue
                ph.op("pe", I_mm(acc.t[:64, 0:QB], selden.t[:, :64], o_.t[:, :QB]), r=[selden, o_], w=[acc])
                if _stop == "ep2":
                    continue
                ph.op("dve", I_recip(rd.t[:64, :QB], acc.t[:64, 0:QB]), r=[acc], w=[rd])
                if _stop == "ep3":
                    continue
                ph.dma(I_dma(z_.t[:64, :QB], s.zaT[64 * h:64 * h + 64, qb0:qb0 + QB]), w=[z_])
                ph.op("dve", I_tt(t_.t[:64, :QB], o_.t[0:64, :QB], rd.t[:64, :QB], ALU.mult), r=[o_, rd], w=[t_])
                if _stop == "ep4":
                    continue
                ph.op("pool", I_tt(y_.t[:64, :QB], t_.t[:64, :QB], z_.t[:64, :QB], ALU.mult), r=[t_, z_], w=[y_])
                if _stop == "ep5":
                    continue
                ph.dma(I_dma(s.yaT[64 * h:64 * h + 64, qb0:qb0 + QB], y_.t[:64, :QB]), r=[y_], q="act")
            yield

    def n_attn(b):
        return 4 * (len(blk_kts(b)) + 1)

    attn_state = {}
    nslot = ntile + 1 + TPB + 1
    for slot in range(nslot + 2 * TPB):
        work = []
        if slot < ntile:
            work.append([gen_index(slot), n_index(slot)])
        import os as _os3
        _stop = _os3.environ.get("ATTN2_STOP", "")
        if 1 <= slot <= ntile and _stop != "index":
            work.append([gen_bisect(slot - 1), n_bisect(slot - 1)])
        for b in range(nblk):
            ready = (b + 1) * TPB + 1
            if slot == ready and _stop not in ("index", "bisect"):
                attn_state[b] = [gen_attn(b), n_attn(b), 0]
        for b, stt_ in list(attn_state.items()):
            g, tot, done = stt_
            share = (tot + TPB - 1) // TPB
            work.append([g, min(share, tot - done) + 1])
            stt_[2] = done + share
            if stt_[2] >= tot:
                del attn_state[b]
        if not work:
            continue
        maxc = max(c for _, c in work)
        done_c = [0] * len(work)
        alive = [True] * len(work)
        for k in range(maxc):
            for wi_x, (g, c) in enumerate(work):
                while alive[wi_x] and done_c[wi_x] < c and (done_c[wi_x] + 1) * maxc <= (k + 1) * c:
                    try:
                        next(g)
                    except StopIteration:
                        alive[wi_x] = False
                    done_c[wi_x] += 1
        for wi_x, (g, c) in enumerate(work[: (1 if slot < ntile else 0) + (1 if (1 <= slot <= ntile and _stop != "index") else 0)]):
            if alive[wi_x]:
                for _ in g:
                    pass
    for b, stt_ in list(attn_state.items()):
        for _ in stt_[0]:
            pass
    ph.finish()
```

```python
import numpy as np
from contextlib import ExitStack
import concourse.bass as bass
import concourse.mybir as mybir
from concourse.bass_utils import run_bass_kernel_spmd

F32 = mybir.dt.float32
BF16 = mybir.dt.bfloat16
I32 = mybir.dt.int32
AF = mybir.ActivationFunctionType
ALU = mybir.AluOpType

D = 1024
DIN = 5704
NL = 2
EPS = 1e-6
OFF_U, OFF_ZS, OFF_Q, OFF_K, OFF_V, OFF_ZA, OFF_QI, OFF_KI, OFF_WI, OFF_GM = (
    0, 512, 1024, 1536, 2048, 2560, 3072, 3584, 3648, 3656)
NEG = -1.0e30
TOPK = 256


class TT:
    __slots__ = ("t", "name", "w", "r", "dsem", "dtot")

    def __init__(self, t, name):
        self.t = t
        self.name = name
        self.w = None
        self.r = []
        self.dsem = None
        self.dtot = 0

    def __getitem__(self, idx):
        return self.t[idx]


_SEMPOOL = {}
_SEMSTACK = []


class Phase:
    ENGS = ("sync", "pool", "act", "dve", "pe")

    def __init__(self, nc, name):
        self.nc = nc
        self.name = name
        self.es = ExitStack()
        self.ops = {e: [] for e in self.ENGS}
        self.cnt = {e: 0 for e in self.ENGS}
        self.known = {e: {} for e in self.ENGS}
        self.sems = {}
        self.semobj = {}
        self.nsem = 0
        for e in ("pool", "act", "dve", "pe"):
            s = self._newsem(f"{name}_{e}")
            self.sems[e] = s
            self.semobj[("c", e)] = s
        self.ntile = 0
        self.dma_tiles = []
        self.misc = None

    def _newsem(self, nm):
        pool = _SEMPOOL.get(id(self.nc))
        if pool is None:
            return self.es.enter_context(self.nc.semaphore(nm))
        sm_ = pool[self.nsem]
        self.nsem += 1
        return sm_

    def sb(self, shape, dtype, name=None):
        self.ntile += 1
        nm = f"{self.name}_{name or 't'}{self.ntile}"
        t = self.es.enter_context(self.nc.sbuf_tensor(nm, list(shape), dtype))
        return TT(t, nm)

    def ps(self, shape, dtype, name=None):
        self.ntile += 1
        nm = f"{self.name}_{name or 'p'}{self.ntile}"
        t = self.es.enter_context(self.nc.psum_tensor(nm, list(shape), dtype))
        return TT(t, nm)

    def _dsem(self, tt):
        if tt.dsem is None:
            tt.dsem = self._newsem(f"d_{tt.name}")
            self.semobj[("d", tt.name)] = tt.dsem
            self.dma_tiles.append(tt)
        return tt.dsem

    def _need(self, eng, dep, waits):
        if dep is None:
            return
        key, val = dep
        if key == ("c", "pe") and eng == "pe":
            return
        if key == ("c", eng) and eng in ("sync",):
            return
        k = self.known[eng]
        if k.get(key, 0) >= val:
            return
        k[key] = val
        waits.append((self.semobj[key], val))

    def _deps(self, eng, r, w):
        waits = []
        for t in r:
            self._need(eng, t.w, waits)
        for t in w:
            self._need(eng, t.w, waits)
            for d in t.r:
                self._need(eng, d, waits)
        return waits

    def op(self, eng, fn, r=(), w=()):
        waits = self._deps(eng, r, w)
        self.cnt[eng] += 1
        me = (("c", eng), self.cnt[eng])
        for t in r:
            t.r.append(me)
        for t in w:
            t.w = me
            t.r = []
        self.ops[eng].append((waits, fn, ("c", self.sems[eng])))

    def dma(self, fn, r=(), w=(), q="sync"):
        waits = self._deps(q, r, w)
        tiles = list(w) + list(r)
        if tiles:
            tt = tiles[0]
            sem = self._dsem(tt)
            tt.dtot += 16
            me = (("d", tt.name), tt.dtot)
        else:
            if self.misc is None:
                self.misc = TT(None, f"{self.name}_misc")
            tt = self.misc
            sem = self._dsem(tt)
            tt.dtot += 16
            me = (("d", tt.name), tt.dtot)
        for t in r:
            t.r.append(me)
        for t in w:
            t.w = me
            t.r = []
        self.ops[q].append((waits, fn, ("d", sem)))

    def finish(self):
        nc = self.nc
        finals = [(t.dsem, t.dtot) for t in self.dma_tiles if t.dtot > 0]

        def replay(e, name):
            for waits, fn, inc in self.ops[name]:
                for s, v in waits:
                    e.wait_ge(s, v)
                ins = fn(e)
                if inc[0] == "c":
                    ins.then_inc(inc[1], 1)
                else:
                    ins.then_inc(inc[1], 16)
            if name == "sync":
                for s, v in finals:
                    e.wait_ge(s, v)

        with nc.Block() as block:
            @block.sync
            def _(e):
                replay(e, "sync")

            @block.gpsimd
            def _(e):
                replay(e, "pool")

            @block.scalar
            def _(e):
                replay(e, "act")

            @block.vector
            def _(e):
                replay(e, "dve")

            @block.tensor
            def _(e):
                replay(e, "pe")
        allsems = list(self.semobj.values())
        with nc.Block() as block2:
            @block2.sync
            def _(e):
                for sm_ in allsems:
                    e.sem_clear(sm_)
        self.es.close()


def I_ts(out, in0, s1, s2=None, op0=ALU.mult, op1=None, accum_out=None):
    kw = {}
    if op1 is not None:
        kw["op1"] = op1
    if accum_out is not None:
        kw["accum_out"] = accum_out
    return lambda e: e.tensor_scalar(out=out, in0=in0, scalar1=s1, scalar2=s2, op0=op0, **kw)


def I_tt(out, in0, in1, op):
    return lambda e: e.tensor_tensor(out=out, in0=in0, in1=in1, op=op)


def I_stt(out, in0, scalar, in1, op0, op1):
    return lambda e: e.scalar_tensor_tensor(out=out, in0=in0, scalar=scalar, in1=in1, op0=op0, op1=op1)


def I_act(out, in_, func, bias=None, scale=None, accum_out=None):
    kw = {}
    if bias is not None:
        kw["bias"] = bias
    if scale is not None:
        kw["scale"] = scale
    if accum_out is not None:
        kw["accum_out"] = accum_out
    return lambda e: e.activation(out=out, in_=in_, func=func, **kw)


def I_mm(out, lhsT, rhs, start=True, stop=True):
    return lambda e: e.matmul(out, lhsT, rhs, start=start, stop=stop)


def I_tr(out, in_, ident):
    return lambda e: e.transpose(out, in_, ident)


def I_copy(out, in_):
    return lambda e: e.tensor_copy(out=out, in_=in_)


def I_memset(ap, v):
    return lambda e: e.memset(ap, v)


def I_recip(out, in_):
    return lambda e: e.reciprocal(out=out, in_=in_)


def I_scan(out, d0, d1, init):
    return lambda e: e.tensor_tensor_scan(out=out, data0=d0, data1=d1, initial=init, op0=ALU.mult, op1=ALU.add)


def I_dma(out, in_, **kw):
    return lambda e: e.dma_start(out=out, in_=in_, **kw)


def I_iota(out, pattern, base, cm):
    return lambda e: e.iota(out=out, pattern=pattern, base=base, channel_multiplier=cm,
                            allow_small_or_imprecise_dtypes=True)


def mk_ident(ph):
    iot = ph.sb([128, 128], F32, "iot")
    idf = ph.sb([128, 128], F32, "idf")
    idb = ph.sb([128, 128], BF16, "idb")
    ph.op("pool", I_iota(iot.t[:], [[1, 128]], 0, -1), w=[iot])
    ph.op("dve", I_ts(idf.t[:], iot.t[:], 0.0, None, op0=ALU.is_equal), r=[iot], w=[idf])
    ph.op("dve", I_copy(idb.t[:], idf.t[:]), r=[idf], w=[idb])
    return idf, idb


def bcast_rows(ap_row, n):
    return ap_row.to_broadcast([n, ap_row.shape[-1]])


class Seq:
    pass


def build(T=8192, TS=32, PAST=4096, stop_after=None, dbg=False):
    nc = bass.Bass("TRN2", target_bir_lowering=False)
    SS = PAST + TS
    din = lambda n, s, dt=F32: nc.dram_tensor(n, list(s), dt, kind="ExternalInput").ap()
    dout = lambda n, s, dt=F32: nc.dram_tensor(n, list(s), dt, kind="ExternalOutput").ap()
    scr_kind = "ExternalOutput" if dbg else "Internal"
    dscr = lambda n, s, dt=BF16: nc.dram_tensor(n, list(s), dt, kind=scr_kind).ap()

    A = {}
    A["xp"] = din("xp", [T, D]); A["xs"] = din("xs", [2, TS, D])
    A["ck"] = din("ck", [NL, 2, PAST, 512]); A["cv"] = din("cv", [NL, 2, PAST, 512])
    A["cki"] = din("cki", [NL, 2, PAST, 64])
    A["ssr"] = din("ssr", [NL, 2, 2048]); A["ssi"] = din("ssi", [NL, 2, 2048])
    A["call"] = din("call", [3, D])
    A["w_mod"] = din("w_mod", [NL, D, 3 * D]); A["b_mod"] = din("b_mod", [NL, 3 * D])
    A["g_norm"] = din("g_norm", [NL, D]); A["w_in"] = din("w_in", [NL, D, DIN])
    A["a_re"] = din("a_re", [NL, 2048]); A["a_im"] = din("a_im", [NL, 2048])
    A["log_dt"] = din("log_dt", [NL, 32])
    A["b_re"] = din("b_re", [NL, 2048, 16]); A["b_im"] = din("b_im", [NL, 2048, 16])
    A["c_reT"] = din("c_reT", [NL, 2048, 16]); A["c_imT"] = din("c_imT", [NL, 2048, 16])
    A["d_skip"] = din("d_skip", [NL, 512])
    A["w_glu"] = din("w_glu", [NL, 512, 512]); A["b_glu"] = din("b_glu", [NL, 512])
    A["w_ps"] = din("w_ps", [NL, 512, D]); A["w_pa"] = din("w_pa", [NL, 512, D])
    A["w_o"] = din("w_o", [NL, D, D]); A["g_final"] = din("g_final", [1, D])

    O = {}
    O["yp"] = dout("yp", [T, D]); O["ys"] = dout("ys", [2, TS, D])
    O["kp"] = dout("kp", [NL, T, 512]); O["vp"] = dout("vp", [NL, T, 512]); O["kip"] = dout("kip", [NL, T, 64])
    O["srp"] = dout("srp", [NL, 2048]); O["sip"] = dout("sip", [NL, 2048])
    O["ks"] = dout("ks", [NL, 2, TS, 512]); O["vs"] = dout("vs", [NL, 2, TS, 512])
    O["kis"] = dout("kis", [NL, 2, TS, 64])
    O["srs"] = dout("srs", [NL, 2, 2048]); O["sis"] = dout("sis", [NL, 2, 2048])

    modbc = dscr("modbc", [NL, 3, 3, 128, D], F32)

    seqs = []
    for i in range(3):
        s = Seq()
        s.i = i
        s.nm = "p" if i == 0 else f"s{i - 1}"
        s.T = T if i == 0 else TS
        s.S = T if i == 0 else SS
        s.koff = 0 if i == 0 else PAST
        s.causal = (i == 0)
        s.xin = A["xp"] if i == 0 else A["xs"][i - 1]
        s.xres = dscr(f"xres_{s.nm}", [s.T, D], F32)
        for nm, rows, cols in (("uT", 512, s.T), ("zsT", 512, s.T), ("qT", 512, s.T), ("kT", 512, s.S),
                               ("zaT", 512, s.T), ("qiT", 512, s.T), ("kiT", 64, s.S), ("gmT", 2048, s.T),
                               ("vaug", s.S, 520), ("ysT", 512, s.T), ("yaT", 512, s.T)):
            setattr(s, nm, dscr(f"{nm}_{s.nm}", [rows, cols], BF16))
        s.wiS = dscr(f"wiS_{s.nm}", [s.T, 8], F32)
        if i == 0:
            s.kout = [O["kp"][l] for l in range(NL)]; s.vout = [O["vp"][l] for l in range(NL)]
            s.kiout = [O["kip"][l] for l in range(NL)]
            s.srout = [O["srp"][l] for l in range(NL)]; s.siout = [O["sip"][l] for l in range(NL)]
            s.yout = O["yp"]
        else:
            j = i - 1
            s.kout = [O["ks"][l, j] for l in range(NL)]; s.vout = [O["vs"][l, j] for l in range(NL)]
            s.kiout = [O["kis"][l, j] for l in range(NL)]
            s.srout = [O["srs"][l, j] for l in range(NL)]; s.siout = [O["sis"][l, j] for l in range(NL)]
            s.yout = O["ys"][j]
        seqs.append(s)

    ncd = nc.allow_non_contiguous_dma(reason="small strided param loads")
    ncd.__enter__()
    es_ = ExitStack()
    _SEMSTACK.append(es_)

    phase_mod(nc, A, modbc)
    if stop_after == "mod":
        return nc
    for l in range(NL):
        phase_proj(nc, A, l, seqs, modbc)
        if stop_after == f"proj{l}":
            return nc
        phase_cache(nc, A, l, seqs, PAST)
        if stop_after == f"cache{l}":
            return nc
        phase_ssm(nc, A, l, seqs)
        if stop_after == f"ssm{l}":
            return nc
        import os as _os2
        for s in seqs:
            if _os2.environ.get("ATTN_SEQ") and s.nm not in _os2.environ.get("ATTN_SEQ").split(","):
                continue
            (phase_attn2 if (_os2.environ.get('ATTN_V','h')=='2' or (_os2.environ.get('ATTN_V','h')=='h' and s.i == 0)) else phase_attn)(nc, A, l, s)
        if stop_after == f"attn{l}":
            return nc
        phase_merge(nc, A, l, seqs, modbc)
        if stop_after == f"merge{l}":
            return nc
    return nc


def phase_mod(nc, A, modbc):
    ph = Phase(nc, "M")
    idf, idb = mk_ident(ph)
    cT = ph.sb([128, 8, 3], F32, "cT")
    for s_ in range(3):
        ph.dma(I_dma(cT.t[:, :, s_], A["call"][s_].rearrange("(c p) -> p c", p=128)), w=[cT])
    cTs = ph.sb([128, 8, 3], F32, "cTs")
    ph.op("act", I_act(cTs.t[:], cT.t[:], AF.Silu), r=[cT], w=[cTs])
    iop = ph.sb([3, 128], F32, "iop")
    ph.op("pool", I_iota(iop.t[:], [[0, 128]], 0, 1), w=[iop])
    sel = []
    for s in range(3):
        t = ph.sb([3, 128], F32, f"sel{s}")
        ph.op("dve", I_ts(t.t[:], iop.t[:], float(s), None, op0=ALU.is_equal), r=[iop], w=[t])
        sel.append(t)
    wst = [ph.sb([128, 8, 512], F32, f"wst{i}") for i in range(2)]
    pmm = [ph.ps([128, 512], F32, f"pm{i}") for i in range(2)]
    pbc = [ph.ps([128, 512], F32, f"pb{i}") for i in range(2)]
    stg = [ph.sb([128, D], F32, f"stg{i}") for i in range(2)]
    n = 0
    nb = 0
    for l in range(NL):
        bsb = ph.sb([3, 3 * D], F32, f"bsb{l}")
        ph.dma(I_dma(bsb.t[:], bcast_rows(A["b_mod"][l:l + 1, :], 3)), w=[bsb])
        gbc = ph.sb([128, D], F32, f"gbc{l}")
        ph.dma(I_dma(gbc.t[:], bcast_rows(A["g_norm"][l:l + 1, :], 128)), w=[gbc])
        mod = ph.sb([3, 3 * D], F32, f"mod{l}")
        for ct in range(6):
            w = wst[n % 2]; pm = pmm[n % 2]; n += 1
            ph.dma(I_dma(w.t[:], A["w_mod"][l].rearrange("(c p) n -> p c n", p=128)[:, :, ct * 512:(ct + 1) * 512]), w=[w])
            for c in range(8):
                ph.op("pe", I_mm(pm.t[0:3, :], cTs.t[:, c, :], w.t[:, c, :], start=(c == 0), stop=(c == 7)),
                      r=[cTs, w], w=[pm])
            ph.op("dve", I_tt(mod.t[:, ct * 512:(ct + 1) * 512], pm.t[0:3, :], bsb.t[:, ct * 512:(ct + 1) * 512], ALU.add),
                  r=[pm, bsb], w=[mod])
        ph.op("dve", I_ts(mod.t[:, D:2 * D], mod.t[:, D:2 * D], 1.0, None, op0=ALU.add), r=[mod], w=[mod])
        for s in range(3):
            for kind in range(3):
                st = stg[nb % 2]
                for half in range(2):
                    pb = pbc[nb % 2 if half == 0 else (nb + 1) % 2]
                    c0 = kind * D + half * 512
                    ph.op("pe", I_mm(pb.t[:, :], sel[s].t[:, :], mod.t[:, c0:c0 + 512]), r=[sel[s], mod], w=[pb])
                    if kind == 1:
                        ph.op("dve", I_tt(st.t[:, half * 512:(half + 1) * 512], pb.t[:, :], gbc.t[:, half * 512:(half + 1) * 512], ALU.mult),
                              r=[pb, gbc], w=[st])
                    else:
                        ph.op("act", I_act(st.t[:, half * 512:(half + 1) * 512], pb.t[:, :], AF.Copy), r=[pb], w=[st])
                nb += 1
                ph.dma(I_dma(modbc[l, s, kind], st.t[:]), r=[st])
    ph.finish()


def phase_proj(nc, A, l, seqs, modbc):
    ph = Phase(nc, f"P{l}")
    idf, idb = mk_ident(ph)
    wbf = ph.sb([128, 8, DIN], BF16, "wbf")
    win = A["w_in"][l].rearrange("(c p) n -> p c n", p=128)
    for c in range(8):
        for h0 in range(0, DIN, 1024):
            h1 = min(DIN, h0 + 1024)
            ph.dma(I_dma(wbf.t[:, c, h0:h1], win[:, c, h0:h1]), w=[wbf], q="pool")
    xt = [ph.sb([128, D], F32, f"xt{i}") for i in range(2)]
    junk = ph.sb([128, D], BF16, "junk")
    ss = [ph.sb([128, 1], F32, f"ss{i}") for i in range(2)]
    rstd = [ph.sb([128, 1], F32, f"rstd{i}") for i in range(2)]
    tmp = [ph.sb([128, D], F32, f"tmp{i}") for i in range(2)]
    hb = [ph.sb([128, D], BF16, f"hb{i}") for i in range(2)]
    hT = [ph.sb([128, 8, 512], BF16, f"hT{i}") for i in range(2)]
    gm_bc = ph.sb([128, D], F32, "gmbc")
    sh_bc = ph.sb([128, D], F32, "shbc")
    kvst = [ph.sb([128, 2 * 512], F32, f"kvst{i}") for i in range(2)]
    vst = [ph.sb([128, 8, 65], BF16, f"vst{i}") for i in range(2)]
    for v in vst:
        ph.op("pool", I_memset(v.t[:], 1.0), w=[v])
    kist = [ph.sb([128, 64], F32, f"kist{i}") for i in range(2)]
    wist = [ph.sb([128, 8], F32, f"wist{i}") for i in range(2)]
    fst = [ph.sb([128, 512], BF16, f"fst{i}") for i in range(4)]
    pT = ph.ps([128, 8, 128], BF16, "pT")
    pk = ph.ps([128, 512], F32, "pk")
    pv = ph.ps([128, 512], F32, "pv")
    pki = ph.ps([128, 72], F32, "pki")
    pf = [ph.ps([128, 512], F32, f"pf{i}") for i in range(3)]
    WISC = float(8 ** -0.5 * 64 ** -0.5)

    nt = 0
    nf = 0
    for s in seqs:
        ph.dma(I_dma(gm_bc.t[:], modbc[l, s.i, 1]), w=[gm_bc])
        ph.dma(I_dma(sh_bc.t[:], modbc[l, s.i, 0]), w=[sh_bc])
        xsrc = s.xin if l == 0 else s.xres
        TP = min(128, s.T)
        NW = min(512, s.T)
        for st in range(s.T // NW):
            ht = hT[st % 2]
            for j in range(NW // TP):
                r0 = st * NW + j * TP
                x = xt[nt % 2]; sq = ss[nt % 2]; rs = rstd[nt % 2]; tm = tmp[nt % 2]; h = hb[nt % 2]
                kv = kvst[nt % 2]; vs_ = vst[nt % 2]; kis_ = kist[nt % 2]; wis_ = wist[nt % 2]
                nt += 1
                ph.dma(I_dma(x.t[:TP, :], xsrc[r0:r0 + TP, :]), w=[x])
                ph.op("act", I_act(junk.t[:TP, :], x.t[:TP, :], AF.Square, accum_out=sq.t[:TP, 0:1]), r=[x], w=[junk, sq])
                ph.op("dve", I_ts(sq.t[:TP, :], sq.t[:TP, :], 1.0 / D, EPS, op0=ALU.mult, op1=ALU.add), r=[sq], w=[sq])
                ph.op("act", I_act(sq.t[:TP, :], sq.t[:TP, :], AF.Sqrt), r=[sq], w=[sq])
                ph.op("dve", I_recip(rs.t[:TP, :], sq.t[:TP, :]), r=[sq], w=[rs])
                ph.op("dve", I_stt(tm.t[:TP, :], x.t[:TP, :], rs.t[:TP, 0:1], gm_bc.t[:TP, :], ALU.mult, ALU.mult),
                      r=[x, rs, gm_bc], w=[tm])
                ph.op("pool", I_tt(h.t[:TP, :], tm.t[:TP, :], sh_bc.t[:TP, :], ALU.add), r=[tm, sh_bc], w=[h])
                for c in range(8):
                    ph.op("pe", I_tr(pT.t[:, c, :TP], h.t[:TP, c * 128:(c + 1) * 128], idb.t[:TP, :TP]), r=[h, idb], w=[pT])
                ph.op("act", I_act(ht.t[:, :, j * TP:(j + 1) * TP], pT.t[:, :, :TP], AF.Copy), r=[pT], w=[ht])
                for c in range(8):
                    ph.op("pe", I_mm(pk.t[:TP, :], ht.t[:, c, j * TP:(j + 1) * TP], wbf.t[:, c, OFF_K:OFF_K + 512],
                                     start=(c == 0), stop=(c == 7)), r=[ht, wbf], w=[pk])
                for c in range(8):
                    ph.op("pe", I_mm(pv.t[:TP, :], ht.t[:, c, j * TP:(j + 1) * TP], wbf.t[:, c, OFF_V:OFF_V + 512],
                                     start=(c == 0), stop=(c == 7)), r=[ht, wbf], w=[pv])
                for c in range(8):
                    ph.op("pe", I_mm(pki.t[:TP, :], ht.t[:, c, j * TP:(j + 1) * TP], wbf.t[:, c, OFF_KI:OFF_KI + 72],
                                     start=(c == 0), stop=(c == 7)), r=[ht, wbf], w=[pki])
                ph.op("act", I_act(kv.t[:TP, 0:512], pk.t[:TP, :], AF.Copy), r=[pk], w=[kv])
                ph.op("dve", I_copy(kv.t[:TP, 512:1024], pv.t[:TP, :]), r=[pv], w=[kv])
                ph.op("dve", I_copy(vs_.t[:TP, :, 0:64], pv.t[:TP, :].rearrange("p (h d) -> p h d", d=64)), r=[pv], w=[vs_])
                ph.op("act", I_act(kis_.t[:TP, :], pki.t[:TP, 0:64], AF.Copy), r=[pki], w=[kis_])
                ph.op("dve", I_ts(wis_.t[:TP, :], pki.t[:TP, 64:72], WISC, None, op0=ALU.mult), r=[pki], w=[wis_])
                ph.dma(I_dma(s.kout[l][r0:r0 + TP, :], kv.t[:TP, 0:512]), r=[kv])
                ph.dma(I_dma(s.vout[l][r0:r0 + TP, :], kv.t[:TP, 512:1024]), r=[kv])
                ph.dma(I_dma(s.vaug[s.koff + r0:s.koff + r0 + TP, :], vs_.t[:TP, :, :].rearrange("p h d -> p (h d)")), r=[vs_])
                ph.dma(I_dma(s.kiout[l][r0:r0 + TP, :], kis_.t[:TP, :]), r=[kis_])
                ph.dma(I_dma(s.wiS[r0:r0 + TP, :], wis_.t[:TP, :]), r=[wis_])
            t0 = st * NW
            fm = []
            for i in range(4):
                fm.append((OFF_U + i * 128, 128, "copy", s.uT[i * 128:(i + 1) * 128, t0:t0 + NW]))
                fm.append((OFF_Q + i * 128, 128, "q", s.qT[i * 128:(i + 1) * 128, t0:t0 + NW]))
                fm.append((OFF_K + i * 128, 128, "copy", s.kT[i * 128:(i + 1) * 128, s.koff + t0:s.koff + t0 + NW]))
                fm.append((OFF_QI + i * 128, 128, "copy", s.qiT[i * 128:(i + 1) * 128, t0:t0 + NW]))
            fm.append((OFF_KI, 64, "copy", s.kiT[0:64, s.koff + t0:s.koff + t0 + NW]))
            for i in range(4):
                fm.append((OFF_ZS + i * 128, 128, "silu", s.zsT[i * 128:(i + 1) * 128, t0:t0 + NW]))
            for i in range(4):
                fm.append((OFF_ZA + i * 128, 128, "silu", s.zaT[i * 128:(i + 1) * 128, t0:t0 + NW]))
            for i in range(16):
                fm.append((OFF_GM + i * 128, 128, "sig", s.gmT[i * 128:(i + 1) * 128, t0:t0 + NW]))
            for (off, M, kind, dst) in fm:
                p = pf[nf % 3]; f = fst[nf % 4]; nf += 1
                for c in range(8):
                    ph.op("pe", I_mm(p.t[:M, :NW], wbf.t[:, c, off:off + M], ht.t[:, c, :NW], start=(c == 0), stop=(c == 7)),
                          r=[wbf, ht], w=[p])
                if kind == "copy":
                    ph.op("dve", I_copy(f.t[:M, :NW], p.t[:M, :NW]), r=[p], w=[f])
                elif kind == "q":
                    ph.op("dve", I_ts(f.t[:M, :NW], p.t[:M, :NW], 0.125, None, op0=ALU.mult), r=[p], w=[f])
                elif kind == "silu":
                    ph.op("act", I_act(f.t[:M, :NW], p.t[:M, :NW], AF.Silu), r=[p], w=[f])
                else:
                    ph.op("act", I_act(f.t[:M, :NW], p.t[:M, :NW], AF.Sigmoid), r=[p], w=[f])
                ph.dma(I_dma(dst, f.t[:M, :NW]), r=[f])
    ph.finish()


def make_in_maps(inp, n_cores=8):
    f = lambda a: np.ascontiguousarray(np.asarray(a, dtype=np.float32))
    shared = {
        "w_mod": f(inp["w_mod"]), "b_mod": f(inp["b_mod"]), "g_norm": f(inp["g_norm"]), "w_in": f(inp["w_in"]),
        "a_re": f(inp["a_re"]).reshape(NL, 2048), "a_im": f(inp["a_im"]).reshape(NL, 2048),
        "log_dt": f(inp["log_dt"]),
        "b_re": f(inp["b_re"]).reshape(NL, 2048, 16), "b_im": f(inp["b_im"]).reshape(NL, 2048, 16),
        "c_reT": f(np.transpose(np.asarray(inp["c_re"]), (0, 1, 3, 2))).reshape(NL, 2048, 16),
        "c_imT": f(np.transpose(np.asarray(inp["c_im"]), (0, 1, 3, 2))).reshape(NL, 2048, 16),
        "d_skip": f(inp["d_skip"]).reshape(NL, 512),
        "w_glu": f(inp["w_glu"]), "b_glu": f(inp["b_glu"]), "w_ps": f(inp["w_ps"]), "w_pa": f(inp["w_pa"]),
        "w_o": f(inp["w_o"]), "g_final": f(inp["g_final"]).reshape(1, D),
    }
    xp = np.asarray(inp["x_prompt"]); xs = np.asarray(inp["x_sample"])
    ck = np.asarray(inp["cache_k"]); cv = np.asarray(inp["cache_v"]); cki = np.asarray(inp["cache_kidx"])
    sr = np.asarray(inp["state_ssm_re"]); si = np.asarray(inp["state_ssm_im"])
    cp = np.asarray(inp["c_prompt"]); cs = np.asarray(inp["c_sample"])
    PAST = ck.shape[2]
    maps = []
    for b in range(n_cores):
        m = dict(shared)
        m["xp"] = f(xp[b]); m["xs"] = f(xs[2 * b:2 * b + 2])
        m["ck"] = f(ck[:, 2 * b:2 * b + 2]).reshape(NL, 2, PAST, 512)
        m["cv"] = f(cv[:, 2 * b:2 * b + 2]).reshape(NL, 2, PAST, 512)
        m["cki"] = f(cki[:, 2 * b:2 * b + 2])
        m["ssr"] = f(sr[:, 2 * b:2 * b + 2]).reshape(NL, 2, 2048)
        m["ssi"] = f(si[:, 2 * b:2 * b + 2]).reshape(NL, 2, 2048)
        m["call"] = f(np.concatenate([cp[b:b + 1], cs[2 * b:2 * b + 2]], axis=0))
        maps.append(m)
    return maps


def assemble(results, T, TS, n_cores=8):
    R = results
    cat = lambda k: np.stack([np.asarray(r[k]) for r in R], axis=0)
    yp = cat("yp")
    ys = cat("ys").reshape(2 * n_cores, TS, D)
    kp = np.transpose(cat("kp"), (1, 0, 2, 3)).reshape(NL, n_cores, T, 8, 64)
    vp = np.transpose(cat("vp"), (1, 0, 2, 3)).reshape(NL, n_cores, T, 8, 64)
    kip = np.transpose(cat("kip"), (1, 0, 2, 3))
    srp = np.transpose(cat("srp"), (1, 0, 2)).reshape(NL, n_cores, 32, 64)
    sip = np.transpose(cat("sip"), (1, 0, 2)).reshape(NL, n_cores, 32, 64)
    ks = np.transpose(cat("ks"), (1, 0, 2, 3, 4)).reshape(NL, 2 * n_cores, TS, 8, 64)
    vs = np.transpose(cat("vs"), (1, 0, 2, 3, 4)).reshape(NL, 2 * n_cores, TS, 8, 64)
    kis = np.transpose(cat("kis"), (1, 0, 2, 3, 4)).reshape(NL, 2 * n_cores, TS, 64)
    srs = np.transpose(cat("srs"), (1, 0, 2, 3)).reshape(NL, 2 * n_cores, 32, 64)
    sis = np.transpose(cat("sis"), (1, 0, 2, 3)).reshape(NL, 2 * n_cores, 32, 64)
    outs = (yp, ys, kp, vp, kip, srp, sip, ks, vs, kis, srs, sis)
    return tuple(np.ascontiguousarray(o, dtype=np.float32) for o in outs)


_NC_CACHE = {}


def kernel(**inputs):
    T = int(np.asarray(inputs["x_prompt"]).shape[1])
    TS = int(np.asarray(inputs["x_sample"]).shape[1])
    PAST = int(np.asarray(inputs["cache_k"]).shape[2])
    n_cores = int(np.asarray(inputs["x_prompt"]).shape[0])
    key = (T, TS, PAST)
    nc = build(T=T, TS=TS, PAST=PAST)
    maps = make_in_maps(inputs, n_cores)
    res = run_bass_kernel_spmd(nc, maps, core_ids=list(range(n_cores)))
    return assemble(res.results, T, TS, n_cores)


def phase_cache(nc, A, l, seqs, PAST):
    ph = Phase(nc, f"C{l}")
    idf, idb = mk_ident(ph)
    ckt = [ph.sb([128, 512], F32, f"ckt{i}") for i in range(2)]
    cvt = [ph.sb([128, 512], F32, f"cvt{i}") for i in range(2)]
    cit = [ph.sb([128, 64], F32, f"cit{i}") for i in range(2)]
    ckb = [ph.sb([128, 512], BF16, f"ckb{i}") for i in range(2)]
    cib = [ph.sb([128, 64], BF16, f"cib{i}") for i in range(2)]
    kst = [ph.sb([128, 4, 128], BF16, f"kst{i}") for i in range(2)]
    kis = [ph.sb([64, 128], BF16, f"kis{i}") for i in range(2)]
    vst = [ph.sb([128, 8, 65], BF16, f"vst{i}") for i in range(2)]
    for v in vst:
        ph.op("pool", I_memset(v.t[:], 1.0), w=[v])
    pT = [ph.ps([128, 4, 128], BF16, f"pT{i}") for i in range(2)]
    pI = [ph.ps([64, 128], BF16, f"pI{i}") for i in range(2)]
    n = 0
    for s in seqs[1:]:
        j = s.i - 1
        for kt in range(PAST // 128):
            i = n % 2; n += 1
            r0 = kt * 128
            ph.dma(I_dma(ckt[i].t[:], A["ck"][l, j, r0:r0 + 128, :]), w=[ckt[i]])
            ph.dma(I_dma(cvt[i].t[:], A["cv"][l, j, r0:r0 + 128, :]), w=[cvt[i]])
            ph.dma(I_dma(cit[i].t[:], A["cki"][l, j, r0:r0 + 128, :]), w=[cit[i]])
            ph.op("act", I_act(ckb[i].t[:], ckt[i].t[:], AF.Copy), r=[ckt[i]], w=[ckb[i]])
            ph.op("act", I_act(cib[i].t[:], cit[i].t[:], AF.Copy), r=[cit[i]], w=[cib[i]])
            for hp in range(4):
                ph.op("pe", I_tr(pT[i].t[:, hp, :], ckb[i].t[:, hp * 128:(hp + 1) * 128], idb.t[:, :]), r=[ckb[i], idb], w=[pT[i]])
            ph.op("pe", I_tr(pI[i].t[:, :], cib[i].t[:, :], idb.t[:, :]), r=[cib[i], idb], w=[pI[i]])
            ph.op("dve", I_copy(kst[i].t[:], pT[i].t[:]), r=[pT[i]], w=[kst[i]])
            ph.op("dve", I_copy(kis[i].t[:], pI[i].t[:]), r=[pI[i]], w=[kis[i]])
            ph.op("pool", I_copy(vst[i].t[:, :, 0:64], cvt[i].t[:, :].rearrange("p (h d) -> p h d", d=64)), r=[cvt[i]], w=[vst[i]])
            ph.dma(I_dma(s.kT.rearrange("(hp p) s -> p hp s", p=128)[:, :, r0:r0 + 128], kst[i].t[:]), r=[kst[i]])
            ph.dma(I_dma(s.kiT[:, r0:r0 + 128], kis[i].t[:]), r=[kis[i]])
            ph.dma(I_dma(s.vaug[r0:r0 + 128, :], vst[i].t[:].rearrange("p h d -> p (h d)")), r=[vst[i]])
    ph.finish()


def phase_merge(nc, A, l, seqs, modbc):
    ph = Phase(nc, f"G{l}")
    last = (l == NL - 1)
    wps = ph.sb([128, 4, D], BF16, "wps"); wpa = ph.sb([128, 4, D], BF16, "wpa"); wo = ph.sb([128, 8, D], BF16, "wo")
    for c in range(4):
        ph.dma(I_dma(wps.t[:, c, :], A["w_ps"][l, c * 128:(c + 1) * 128, :]), w=[wps], q="pool")
        ph.dma(I_dma(wpa.t[:, c, :], A["w_pa"][l, c * 128:(c + 1) * 128, :]), w=[wpa], q="pool")
    for c in range(8):
        ph.dma(I_dma(wo.t[:, c, :], A["w_o"][l, c * 128:(c + 1) * 128, :]), w=[wo], q="pool")
    gate = ph.sb([128, D], F32, "gate")
    gfin = ph.sb([128, D], F32, "gfin")
    if last:
        ph.dma(I_dma(gfin.t[:], bcast_rows(A["g_final"][0:1, :], 128)), w=[gfin])
    yst = [ph.sb([128, 4, 512], BF16, f"yst{i}") for i in range(2)]
    yat = [ph.sb([128, 4, 512], BF16, f"yat{i}") for i in range(2)]
    gmt = [ph.sb([128, 16, 512], BF16, f"gmt{i}") for i in range(2)]
    mg = [ph.sb([128, 8, 512], BF16, f"mg{i}") for i in range(2)]
    tA = [ph.sb([128, 512], F32, f"tA{i}") for i in range(2)]
    tB = [ph.sb([128, 512], F32, f"tB{i}") for i in range(2)]
    xt = [ph.sb([128, D], F32, f"xt{i}") for i in range(2)]
    xn = [ph.sb([128, D], F32, f"xn{i}") for i in range(2)]
    yo = [ph.sb([128, D], F32, f"yo{i}") for i in range(2)]
    junk = ph.sb([128, D], BF16, "junk")
    ss = [ph.sb([128, 1], F32, f"ss{i}") for i in range(2)]
    rs = [ph.sb([128, 1], F32, f"rs{i}") for i in range(2)]
    pA = [ph.ps([128, 512], F32, f"pA{i}") for i in range(2)]
    pB = [ph.ps([128, 512], F32, f"pB{i}") for i in range(2)]
    pO = [ph.ps([128, 512], F32, f"pO{i}") for i in range(2)]
    nn = 0
    nt = 0
    for s in seqs:
        ph.dma(I_dma(gate.t[:], modbc[l, s.i, 2]), w=[gate])
        xsrc = s.xin if l == 0 else s.xres
        TP = min(128, s.T); NW = min(512, s.T)
        for st in range(s.T // NW):
            t0 = st * NW
            ys_ = yst[st % 2]; ya_ = yat[st % 2]; gm_ = gmt[st % 2]; m_ = mg[st % 2]
            ph.dma(I_dma(ys_.t[:, :, :NW], s.ysT.rearrange("(c p) t -> p c t", p=128)[:, :, t0:t0 + NW]), w=[ys_])
            ph.dma(I_dma(ya_.t[:, :, :NW], s.yaT.rearrange("(c p) t -> p c t", p=128)[:, :, t0:t0 + NW]), w=[ya_])
            ph.dma(I_dma(gm_.t[:, :, :NW], s.gmT.rearrange("(c p) t -> p c t", p=128)[:, :, t0:t0 + NW]), w=[gm_])
            for ct in range(8):
                a = pA[nn % 2]; b = pB[nn % 2]; ta = tA[nn % 2]; tb = tB[nn % 2]; nn += 1
                for c in range(4):
                    ph.op("pe", I_mm(a.t[:, :NW], wps.t[:, c, ct * 128:(ct + 1) * 128], ys_.t[:, c, :NW], start=(c == 0), stop=(c == 3)),
                          r=[wps, ys_], w=[a])
                for c in range(4):
                    ph.op("pe", I_mm(b.t[:, :NW], wpa.t[:, c, ct * 128:(ct + 1) * 128], ya_.t[:, c, :NW], start=(c == 0), stop=(c == 3)),
                          r=[wpa, ya_], w=[b])
                ph.op("dve", I_tt(ta.t[:, :NW], a.t[:, :NW], gm_.t[:, ct, :NW], ALU.mult), r=[a, gm_], w=[ta])
                ph.op("dve", I_tt(tb.t[:, :NW], b.t[:, :NW], gm_.t[:, 8 + ct, :NW], ALU.mult), r=[b, gm_], w=[tb])
                ph.op("pool", I_tt(m_.t[:, ct, :NW], ta.t[:, :NW], tb.t[:, :NW], ALU.add), r=[ta, tb], w=[m_])
            for j in range(NW // TP):
                r0 = t0 + j * TP
                x = xt[nt % 2]; xo = xn[nt % 2]; y_ = yo[nt % 2]; sq = ss[nt % 2]; r_ = rs[nt % 2]
                ph.dma(I_dma(x.t[:TP, :], xsrc[r0:r0 + TP, :]), w=[x])
                for half in range(2):
                    po = pO[half]
                    hs = slice(half * 512, (half + 1) * 512)
                    for c in range(8):
                        ph.op("pe", I_mm(po.t[:TP, :], m_.t[:, c, j * TP:(j + 1) * TP], wo.t[:, c, hs], start=(c == 0), stop=(c == 7)),
                              r=[m_, wo], w=[po])
                    ph.op("dve", I_tt(xo.t[:TP, hs], po.t[:TP, :], gate.t[:TP, hs], ALU.mult), r=[po, gate], w=[xo])
                ph.op("pool", I_tt(xo.t[:TP, :], xo.t[:TP, :], x.t[:TP, :], ALU.add), r=[xo, x], w=[xo])
                nt += 1
                if not last:
                    ph.dma(I_dma(s.xres[r0:r0 + TP, :], xo.t[:TP, :]), r=[xo])
                else:
                    ph.op("act", I_act(junk.t[:TP, :], xo.t[:TP, :], AF.Square, accum_out=sq.t[:TP, 0:1]), r=[xo], w=[junk, sq])
                    ph.op("dve", I_ts(sq.t[:TP, :], sq.t[:TP, :], 1.0 / D, EPS, op0=ALU.mult, op1=ALU.add), r=[sq], w=[sq])
                    ph.op("act", I_act(sq.t[:TP, :], sq.t[:TP, :], AF.Sqrt), r=[sq], w=[sq])
                    ph.op("dve", I_recip(r_.t[:TP, :], sq.t[:TP, :]), r=[sq], w=[r_])
                    ph.op("dve", I_stt(y_.t[:TP, :], xo.t[:TP, :], r_.t[:TP, 0:1], gfin.t[:TP, :], ALU.mult, ALU.mult),
                          r=[xo, r_, gfin], w=[y_])
                    ph.dma(I_dma(s.yout[r0:r0 + TP, :], y_.t[:TP, :]), r=[y_])
    ph.finish()


NIT = 22
W0 = 64.0


def phase_attn(nc, A, l, s):
    ph = Phase(nc, f"A{l}{s.nm}")
    idf, idb = mk_ident(ph)
    T, S = s.T, s.S
    QP = min(128, T); QB = min(512, T)
    L = T if s.causal else S
    KSEL = float(min(TOPK, L // 4))
    ktiles = [(k0, min(128, S - k0)) for k0 in range(0, S, 128)]
    NKT = len(ktiles)
    kiTa = ph.sb([128, S], BF16, "kiTa"); kiTb = ph.sb([128, S], BF16, "kiTb")
    ph.op("pool", I_memset(kiTa.t[64:128, :], 0.0), w=[kiTa])
    ph.op("pool", I_memset(kiTb.t[0:64, :], 0.0), w=[kiTb])
    ph.dma(I_dma(kiTa.t[0:64, :], s.kiT[:, :]), w=[kiTa])
    ph.dma(I_dma(kiTb.t[64:128, :], s.kiT[:, :]), w=[kiTb])
    kiTz = (kiTa, kiTb)
    sc = ph.sb([128, S], F32, "sc")
    mk = ph.sb([128, S], BF16, "mk")
    maskT = ph.sb([128, NKT, QB], BF16, "maskT")
    qit = [ph.sb([128, 4, 128], BF16, f"qit{i}") for i in range(2)]
    wit = [ph.sb([128, 8], F32, f"wit{i}") for i in range(2)]
    dg = [ph.sb([128, 8, 128], BF16, f"dg{i}") for i in range(2)]
    rl = [ph.sb([128, 512], BF16, f"rl{i}") for i in range(2)]
    mid = ph.sb([128, 1], F32, "mid"); cnt = ph.sb([128, 1], F32, "cnt"); tq = ph.sb([128, 1], F32, "tq")
    thr = ph.sb([128, 1], F32, "thr")
    negmid = ph.sb([128, 1], F32, "negmid"); sact = ph.sb([128, 1], F32, "sact"); comb = ph.sb([128, 1], F32, "comb")
    junk2 = ph.sb([128, S], BF16, "junk2")
    qta = ph.sb([128, 4, 512], BF16, "qta"); qtb = ph.sb([128, 4, 512], BF16, "qtb")
    ph.op("pool", I_memset(qta.t[64:128, :, :], 0.0), w=[qta])
    ph.op("pool", I_memset(qtb.t[0:64, :, :], 0.0), w=[qtb])
    qtz = (qta, qtb)
    va = [ph.sb([128, 260], BF16, f"va{i}") for i in range(2)]
    ktl = [ph.sb([128, 2, 128], BF16, f"ktl{i}") for i in range(2)]
    pe_ = [ph.sb([128, 512], BF16, f"pe{i}") for i in range(2)]
    pm = [ph.sb([128, 512], BF16, f"pm{i}") for i in range(2)]
    oa = [ph.sb([128, 512], F32, f"oa{i}") for i in range(2)]
    for o__ in oa:
        ph.op("pool", I_memset(o__.t[:], 0.0), w=[o__])
    rden = [ph.sb([64, 512], F32, f"rden{i}") for i in range(2)]
    t1 = [ph.sb([64, 512], F32, f"t1{i}") for i in range(2)]
    zat = [ph.sb([64, 512], BF16, f"zat{i}") for i in range(2)]
    yat = [ph.sb([64, 512], BF16, f"yat{i}") for i in range(2)]
    iop = ph.sb([128, 64], F32, "iop")
    selden = ph.sb([128, 64], F32, "selden")
    ph.op("pool", I_iota(iop.t[:], [[0, 64]], 0, 1), w=[iop])
    ph.op("dve", I_ts(selden.t[:], iop.t[:], 64.0, None, op0=ALU.is_equal), r=[iop], w=[selden])
    lg = [ph.ps([128, 512], F32, f"lg{i}") for i in range(2)]
    acc = ph.ps([128, 512], F32, "acc")
    mtp = ph.ps([128, 8, 128], BF16, "mtp")
    oacc = [ph.ps([128, 512], F32, f"oacc{i}") for i in range(4)]

    nq = 0
    nr = 0
    nev = 0
    nv = 0
    nh = 0
    for jb in range(T // QB):
        qb0 = jb * QB
        Sblk = min(S, (jb + 1) * QB) if s.causal else S
        kts = [(i, k0, ksz) for i, (k0, ksz) in enumerate(ktiles) if k0 < Sblk]
        for jq in range(QB // QP):
            q0 = qb0 + jq * QP
            qoff = jq * QP
            Slim = (q0 + QP) if s.causal else S
            qi_ = qit[nq % 2]; wi_ = wit[nq % 2]; dg_ = dg[nq % 2]; nq += 1
            ph.dma(I_dma(qi_.t[:, :, :QP], s.qiT.rearrange("(hp p) t -> p hp t", p=128)[:, :, q0:q0 + QP]), w=[qi_])
            ph.dma(I_dma(wi_.t[:QP, :], s.wiS[q0:q0 + QP, :]), w=[wi_])
            for h in range(8):
                ph.op("dve", I_ts(dg_.t[:QP, h, :QP], idf.t[:QP, :QP], wi_.t[:QP, h:h + 1], None, op0=ALU.mult),
                      r=[idf, wi_], w=[dg_])
            for c0 in range(0, Slim, 512):
                csz = min(512, Slim - c0)
                for h in range(8):
                    base = 64 * (h % 2)
                    g = lg[nr % 2]; r_ = rl[nr % 2]; nr += 1
                    ph.op("pe", I_mm(g.t[:QP, :csz], qi_.t[:, h // 2, :QP], kiTz[h % 2].t[:, c0:c0 + csz]),
                          r=[qi_, kiTz[h % 2]], w=[g])
                    ph.op("act", I_act(r_.t[:QP, :csz], g.t[:QP, :csz], AF.Relu), r=[g], w=[r_])
                    ph.op("pe", I_mm(acc.t[:QP, :csz], dg_.t[:QP, h, :QP], r_.t[:QP, :csz], start=(h == 0), stop=(h == 7)),
                          r=[dg_, r_], w=[acc])
                ph.op("dve", I_copy(sc.t[:QP, c0:c0 + csz], acc.t[:QP, :csz]), r=[acc], w=[sc])
            if s.causal and QP == 128:
                ph.op("dve", I_memset(sc.t[0:64, Slim - 64:Slim], NEG), w=[sc])
            ph.op("dve", I_memset(mid.t[:QP, :], 0.0), w=[mid])
            ph.op("dve", I_memset(negmid.t[:QP, :], 0.0), w=[negmid])
            S1 = Slim
            if Slim >= 1024:
                S1 = int(Slim * 0.45) // 128 * 128
            w = W02
            for k in range(NIT2):
                w = w / 2.0
                ph.op("dve", I_ts(mk.t[:QP, :S1], sc.t[:QP, :S1], mid.t[:QP, 0:1], None, op0=ALU.is_ge, op1=ALU.add,
                                  accum_out=cnt.t[:QP, 0:1]), r=[sc, mid], w=[mk, cnt])
                if S1 < Slim:
                    ph.op("act", I_act(junk2.t[:QP, S1:Slim], sc.t[:QP, S1:Slim], AF.Sign, bias=negmid.t[:QP, 0:1],
                                       accum_out=sact.t[:QP, 0:1]), r=[sc, negmid], w=[junk2, sact])
                    ph.op("dve", I_stt(comb.t[:QP, :], sact.t[:QP, :], 0.5, cnt.t[:QP, :], ALU.mult, ALU.add), r=[sact, cnt], w=[comb])
                    src_, kadj = comb, KSEL - 0.5 * (Slim - S1)
                else:
                    src_, kadj = cnt, KSEL
                ph.op("dve", I_ts(tq.t[:QP, :], src_.t[:QP, :], kadj, 2.0 * w, op0=ALU.is_ge, op1=ALU.mult), r=[src_], w=[tq])
                ph.op("dve", I_stt(mid.t[:QP, :], mid.t[:QP, :], -w, tq.t[:QP, :], ALU.add, ALU.add), r=[mid, tq], w=[mid])
                if S1 < Slim:
                    ph.op("dve", I_stt(negmid.t[:QP, :], negmid.t[:QP, :], w, tq.t[:QP, :], ALU.add, ALU.subtract), r=[negmid, tq], w=[negmid])
            ph.op("dve", I_ts(thr.t[:QP, :], mid.t[:QP, :], -w, None, op0=ALU.add), r=[mid], w=[thr])
            ph.op("dve", I_ts(mk.t[:QP, :Slim], sc.t[:QP, :Slim], thr.t[:QP, 0:1], None, op0=ALU.is_ge), r=[sc, thr], w=[mk])
            mine = [(i, k0, ksz) for (i, k0, ksz) in kts if k0 < Slim]
            for g0 in range(0, len(mine), 8):
                grp = mine[g0:g0 + 8]
                for gi, (i, k0, ksz) in enumerate(grp):
                    ph.op("pe", I_tr(mtp.t[:ksz, gi, :QP], mk.t[:QP, k0:k0 + ksz], idb.t[:QP, :QP]), r=[mk, idb], w=[mtp])
                i0 = grp[0][0]
                ng = len(grp)
                eng = "act" if nev % 2 == 0 else "dve"; nev += 1
                if eng == "act":
                    ph.op("act", I_act(maskT.t[:, i0:i0 + ng, qoff:qoff + QP], mtp.t[:, 0:ng, :QP], AF.Copy), r=[mtp], w=[maskT])
                else:
                    ph.op("dve", I_copy(maskT.t[:, i0:i0 + ng, qoff:qoff + QP], mtp.t[:, 0:ng, :QP]), r=[mtp], w=[maskT])
            rest = [(i, k0, ksz) for (i, k0, ksz) in kts if k0 >= Slim]
            if rest:
                i0 = rest[0][0]; i1 = rest[-1][0] + 1
                ph.op("pool", I_memset(maskT.t[:, i0:i1, qoff:qoff + QP], 0.0), w=[maskT])
        import os as _os
        if _os.environ.get("ATTN_STOP") == "idx":
            continue
        qsrc = s.qT.rearrange("(hp p) t -> p hp t", p=128)
        ph.dma(I_dma(qta.t[0:64, :, :QB], qsrc[0:64, :, qb0:qb0 + QB]), w=[qta])
        ph.dma(I_dma(qtb.t[64:128, :, :QB], qsrc[64:128, :, qb0:qb0 + QB]), w=[qtb])
        for hg in range(2):
            for idx, (i, k0, ksz) in enumerate(kts):
                va_ = va[nv % 2]; kt_ = ktl[nv % 2]; nv += 1
                ph.dma(I_dma(va_.t[:ksz, :], s.vaug[k0:k0 + ksz, hg * 260:(hg + 1) * 260]), w=[va_])
                ph.dma(I_dma(kt_.t[:, :, :ksz], s.kT.rearrange("(hp p) s -> p hp s", p=128)[:, 2 * hg:2 * hg + 2, k0:k0 + ksz]), w=[kt_])
                for hh in range(4):
                    h = hg * 4 + hh
                    base = 64 * (h % 2)
                    st_ = lg[nh % 2]; p_ = pe_[nh % 2]; m_ = pm[nh % 2]; nh += 1
                    if _os.environ.get("ATTN_STOP") == "dma":
                        continue
                    ph.op("pe", I_mm(st_.t[:ksz, :QB], kt_.t[:, hh // 2, :ksz], qtz[h % 2].t[:, h // 2, :QB]),
                          r=[kt_, qtz[h % 2]], w=[st_])
                    if _os.environ.get("ATTN_STOP") == "mm":
                        continue
                    ph.op("act", I_act(p_.t[:ksz, :QB], st_.t[:ksz, :QB], AF.Exp), r=[st_], w=[p_])
                    if _os.environ.get("ATTN_STOP") == "exp":
                        continue
                    ph.op("dve", I_tt(m_.t[:ksz, :QB], p_.t[:ksz, :QB], maskT.t[:ksz, i, :QB], ALU.mult), r=[p_, maskT], w=[m_])
                    if _os.environ.get("ATTN_STOP") == "qk":
                        continue
                    ph.op("pe", I_mm(oacc[hh].t[:65, :QB], va_.t[:ksz, hh * 65:(hh + 1) * 65], m_.t[:ksz, :QB], start=(idx == 0), stop=(idx == len(kts) - 1)),
                          r=[va_, m_], w=[oacc[hh]])
            if _os.environ.get("ATTN_STOP") in ("pv", "mm", "exp", "qk", "dma"):
                continue
            for hh in range(4):
                h = hg * 4 + hh
                o_ = oa[hh % 2]; rd = rden[hh % 2]; t_ = t1[hh % 2]; z_ = zat[hh % 2]; y_ = yat[hh % 2]
                ph.op("act", I_act(o_.t[:65, :QB], oacc[hh].t[:65, :QB], AF.Copy), r=[oacc[hh]], w=[o_])
                ph.op("pe", I_mm(acc.t[:64, :QB], selden.t[:, :64], o_.t[:, :QB]), r=[selden, o_], w=[acc])
                ph.op("dve", I_recip(rd.t[:64, :QB], acc.t[:64, :QB]), r=[acc], w=[rd])
                ph.dma(I_dma(z_.t[:64, :QB], s.zaT[64 * h:64 * h + 64, qb0:qb0 + QB]), w=[z_])
                ph.op("dve", I_tt(t_.t[:64, :QB], o_.t[0:64, :QB], rd.t[:64, :QB], ALU.mult), r=[o_, rd], w=[t_])
                ph.op("pool", I_tt(y_.t[:64, :QB], t_.t[:64, :QB], z_.t[:64, :QB], ALU.mult), r=[t_, z_], w=[y_])
                ph.dma(I_dma(s.yaT[64 * h:64 * h + 64, qb0:qb0 + QB], y_.t[:64, :QB]), r=[y_])
    ph.finish()


PI = float(np.pi)


def phase_ssm(nc, A, l, seqs):
    ph = Phase(nc, f"S{l}")
    idf, idb = mk_ident(ph)
    LB = min(512, seqs[0].T)
    sm = lambda nm: ph.sb([128, 16], F32, nm)
    are, aim, ldt, dt, xr, ang, mag = sm("are"), sm("aim"), sm("ldt"), sm("dt"), sm("xr"), sm("ang"), sm("mag")
    kac, angr, angc, angcr, sn1, cs1, abr, abi = sm("kac"), sm("angr"), sm("angc"), sm("angcr"), sm("sn1"), sm("cs1"), sm("abr"), sm("abi")
    t1, t2, den, rdn, nr_, fre, fim, t3, t4 = sm("t1"), sm("t2"), sm("den"), sm("rdn"), sm("nr"), sm("fre"), sm("fim"), sm("t3"), sm("t4")
    wr, wi_, wt = sm("wr"), sm("wi"), sm("wt")
    ph.dma(I_dma(are.t[:], A["a_re"][l].rearrange("(pt q) -> q pt", q=128)), w=[are])
    ph.dma(I_dma(aim.t[:], A["a_im"][l].rearrange("(pt q) -> q pt", q=128)), w=[aim])
    for gl in range(2):
        src = bass.AP(A["log_dt"].tensor, l * 32 + gl, [[0, 64], [2, 16]])
        ph.dma(I_dma(ldt.t[gl * 64:(gl + 1) * 64, :], src), w=[ldt])
    V = "dve"
    pa, pb2, pc = sm("pa"), sm("pb2"), sm("pc")

    def horner(dst, y, coefs):
        ph.op(V, I_memset(dst.t[:], 1.0), w=[dst])
        for c in reversed(coefs):
            ph.op(V, I_tt(dst.t[:], dst.t[:], y.t[:], ALU.mult), r=[dst, y], w=[dst])
            ph.op(V, I_ts(dst.t[:], dst.t[:], float(c), 1.0, op0=ALU.mult, op1=ALU.add), r=[dst], w=[dst])

    def exp_acc(dst, src, nsq, deg):
        ph.op(V, I_ts(pa.t[:], src.t[:], 1.0 / (2 ** nsq), None, op0=ALU.mult), r=[src], w=[pa])
        horner(dst, pa, [1.0 / k for k in range(1, deg + 1)])
        for _ in range(nsq):
            ph.op(V, I_tt(dst.t[:], dst.t[:], dst.t[:], ALU.mult), r=[dst], w=[dst])

    def sincos_acc(sdst, cdst, x):
        ph.op(V, I_ts(pa.t[:], x.t[:], 0.125, None, op0=ALU.mult), r=[x], w=[pa])
        ph.op(V, I_tt(pb2.t[:], pa.t[:], pa.t[:], ALU.mult), r=[pa], w=[pb2])
        horner(sdst, pb2, [-1.0 / 6, -1.0 / 20, -1.0 / 42, -1.0 / 72, -1.0 / 110])
        ph.op(V, I_tt(sdst.t[:], sdst.t[:], pa.t[:], ALU.mult), r=[sdst, pa], w=[sdst])
        horner(cdst, pb2, [-1.0 / 2, -1.0 / 12, -1.0 / 30, -1.0 / 56, -1.0 / 90, -1.0 / 132])
        for _ in range(3):
            ph.op(V, I_tt(pc.t[:], sdst.t[:], sdst.t[:], ALU.mult), r=[sdst], w=[pc])
            ph.op(V, I_stt(sdst.t[:], sdst.t[:], 2.0, cdst.t[:], ALU.mult, ALU.mult), r=[sdst, cdst], w=[sdst])
            ph.op(V, I_ts(cdst.t[:], pc.t[:], -2.0, 1.0, op0=ALU.mult, op1=ALU.add), r=[pc], w=[cdst])

    exp_acc(dt, ldt, 3, 14)
    ph.op(V, I_tt(xr.t[:], are.t[:], dt.t[:], ALU.mult), r=[are, dt], w=[xr])
    ph.op(V, I_tt(ang.t[:], aim.t[:], dt.t[:], ALU.mult), r=[aim, dt], w=[ang])
    exp_acc(mag, xr, 0, 8)

    def reduce_angle(src, dst):
        ph.op(V, I_ts(kac.t[:], src.t[:], PI, None, op0=ALU.is_gt), r=[src], w=[kac])
        for j in range(1, 6):
            ph.op(V, I_stt(kac.t[:], src.t[:], (2 * j + 1) * PI, kac.t[:], ALU.is_gt, ALU.add), r=[src, kac], w=[kac])
        ph.op(V, I_stt(dst.t[:], kac.t[:], -2.0 * PI, src.t[:], ALU.mult, ALU.add), r=[kac, src], w=[dst])
    reduce_angle(ang, angr)
    sincos_acc(sn1, cs1, angr)
    ph.op(V, I_tt(t1.t[:], sn1.t[:], sn1.t[:], ALU.mult), r=[sn1], w=[t1])
    ph.op(V, I_tt(t2.t[:], cs1.t[:], cs1.t[:], ALU.mult), r=[cs1], w=[t2])
    ph.op(V, I_tt(t3.t[:], t1.t[:], t2.t[:], ALU.add), r=[t1, t2], w=[t3])
    ph.op(V, I_ts(t3.t[:], t3.t[:], -0.5, 1.5, op0=ALU.mult, op1=ALU.add), r=[t3], w=[t3])
    ph.op(V, I_tt(sn1.t[:], sn1.t[:], t3.t[:], ALU.mult), r=[sn1, t3], w=[sn1])
    ph.op(V, I_tt(cs1.t[:], cs1.t[:], t3.t[:], ALU.mult), r=[cs1, t3], w=[cs1])
    ph.op(V, I_tt(abr.t[:], mag.t[:], cs1.t[:], ALU.mult), r=[mag, cs1], w=[abr])
    ph.op(V, I_tt(abi.t[:], mag.t[:], sn1.t[:], ALU.mult), r=[mag, sn1], w=[abi])
    ph.op(V, I_tt(t1.t[:], are.t[:], are.t[:], ALU.mult), r=[are], w=[t1])
    ph.op(V, I_tt(t2.t[:], aim.t[:], aim.t[:], ALU.mult), r=[aim], w=[t2])
    ph.op(V, I_tt(den.t[:], t1.t[:], t2.t[:], ALU.add), r=[t1, t2], w=[den])
    ph.op(V, I_recip(rdn.t[:], den.t[:]), r=[den], w=[rdn])
    ph.op(V, I_ts(nr_.t[:], abr.t[:], -1.0, None, op0=ALU.add), r=[abr], w=[nr_])
    ph.op(V, I_tt(t1.t[:], nr_.t[:], are.t[:], ALU.mult), r=[nr_, are], w=[t1])
    ph.op(V, I_tt(t2.t[:], abi.t[:], aim.t[:], ALU.mult), r=[abi, aim], w=[t2])
    ph.op(V, I_tt(t3.t[:], t1.t[:], t2.t[:], ALU.add), r=[t1, t2], w=[t3])
    ph.op(V, I_tt(fre.t[:], t3.t[:], rdn.t[:], ALU.mult), r=[t3, rdn], w=[fre])
    ph.op(V, I_tt(t1.t[:], abi.t[:], are.t[:], ALU.mult), r=[abi, are], w=[t1])
    ph.op(V, I_tt(t2.t[:], nr_.t[:], aim.t[:], ALU.mult), r=[nr_, aim], w=[t2])
    ph.op(V, I_tt(t4.t[:], t1.t[:], t2.t[:], ALU.subtract), r=[t1, t2], w=[t4])
    ph.op(V, I_tt(fim.t[:], t4.t[:], rdn.t[:], ALU.mult), r=[t4, rdn], w=[fim])

    bre_t = ph.sb([128, 16, 16], F32, "bre_t"); bim_t = ph.sb([128, 16, 16], F32, "bim_t")
    cre_t = ph.sb([128, 16, 16], F32, "cre_t"); cim_t = ph.sb([128, 16, 16], F32, "cim_t")
    for t_, nm in ((bre_t, "b_re"), (bim_t, "b_im"), (cre_t, "c_reT"), (cim_t, "c_imT")):
        ph.dma(I_dma(t_.t[:], A[nm][l].rearrange("(pt q) n -> q pt n", q=128)), w=[t_])
    padall = ph.sb([128, 16, 2, 128], F32, "padall")
    Cpad = ph.sb([128, 16, 2, 128], BF16, "Cpad")
    ph.op("pool", I_memset(padall.t[:], 0.0), w=[padall])
    ph.op("pool", I_memset(Cpad.t[:], 0.0), w=[Cpad])
    tb16 = [ph.sb([128, 16], F32, f"tb16{i}") for i in range(2)]
    for pt in range(16):
        qq = pt % 4
        for half in range(2):
            rows = slice(half * 64, (half + 1) * 64)
            cols = slice(32 * qq + 16 * half, 32 * qq + 16 * half + 16)
            ta, tb = tb16
            ph.op(V, I_ts(ta.t[rows, :], bim_t.t[rows, pt, :], fim.t[rows, pt:pt + 1], None, op0=ALU.mult), r=[bim_t, fim], w=[ta])
            ph.op(V, I_stt(padall.t[rows, pt, 0, cols], bre_t.t[rows, pt, :], fre.t[rows, pt:pt + 1], ta.t[rows, :], ALU.mult, ALU.subtract),
                  r=[bre_t, fre, ta], w=[padall])
            ph.op(V, I_ts(tb.t[rows, :], bre_t.t[rows, pt, :], fim.t[rows, pt:pt + 1], None, op0=ALU.mult), r=[bre_t, fim], w=[tb])
            ph.op(V, I_stt(padall.t[rows, pt, 1, cols], bim_t.t[rows, pt, :], fre.t[rows, pt:pt + 1], tb.t[rows, :], ALU.mult, ALU.add),
                  r=[bim_t, fre, tb], w=[padall])
            ph.op("pool", I_copy(Cpad.t[rows, pt, 0, cols], cre_t.t[rows, pt, :]), r=[cre_t], w=[Cpad])
            ph.op("pool", I_ts(Cpad.t[rows, pt, 1, cols], cim_t.t[rows, pt, :], -1.0, None, op0=ALU.mult), r=[cim_t], w=[Cpad])
    pB = [ph.ps([128, 512], F32, f"pB{i}") for i in range(4)]
    pY = [ph.ps([128, 512], F32, f"pY{i}") for i in range(2)]
    pG = ph.ps([128, 512], F32, "pG")
    E = [[ph.sb([128, 128], BF16, f"E{part}_{pt}") for pt in range(16)] for part in range(2)]
    for part in range(2):
        for pt in range(16):
            ph.op("pe", I_mm(pG.t[:, 0:128], padall.t[:, pt, part, :], idf.t[:, :]), r=[padall, idf], w=[pG])
            ph.op("act", I_act(E[part][pt].t[:], pG.t[:, 0:128], AF.Copy), r=[pG], w=[E[part][pt]])
    dsk = ph.sb([128, 4], F32, "dsk")
    ph.dma(I_dma(dsk.t[:], A["d_skip"][l].rearrange("(ct q) -> q ct", q=128)), w=[dsk])
    bglu = ph.sb([128, 4], F32, "bglu")
    ph.dma(I_dma(bglu.t[:], A["b_glu"][l].rearrange("(co q) -> q co", q=128)), w=[bglu])
    wglu = ph.sb([128, 4, 512], BF16, "wglu")
    for c in range(4):
        ph.dma(I_dma(wglu.t[:, c, :], A["w_glu"][l, c * 128:(c + 1) * 128, :]), w=[wglu], q="pool")
    cs = ph.sb([128, 16, LB], F32, "cs"); sn = ph.sb([128, 16, LB], F32, "sn")
    ph.op("pool", I_memset(cs.t[:, :, 0:1], 1.0), w=[cs])
    ph.op("pool", I_memset(sn.t[:, :, 0:1], 0.0), w=[sn])
    ph.op(V, I_copy(wr.t[:], cs1.t[:]), r=[cs1], w=[wr])
    ph.op(V, I_copy(wi_.t[:], sn1.t[:]), r=[sn1], w=[wi_])
    tmpc = ph.sb([128, LB], F32, "tmpc"); tmps = ph.sb([128, LB], F32, "tmps")
    n = 1
    while n < LB:
        for pt in range(16):
            ph.op(V, I_ts(tmpc.t[:, 0:n], sn.t[:, pt, 0:n], wi_.t[:, pt:pt + 1], None, op0=ALU.mult), r=[sn, wi_], w=[tmpc])
            ph.op("pool", I_ts(tmps.t[:, 0:n], cs.t[:, pt, 0:n], wi_.t[:, pt:pt + 1], None, op0=ALU.mult), r=[cs, wi_], w=[tmps])
            ph.op(V, I_stt(cs.t[:, pt, n:2 * n], cs.t[:, pt, 0:n], wr.t[:, pt:pt + 1], tmpc.t[:, 0:n], ALU.mult, ALU.subtract),
                  r=[cs, wr, tmpc], w=[cs])
            ph.op(V, I_stt(sn.t[:, pt, n:2 * n], sn.t[:, pt, 0:n], wr.t[:, pt:pt + 1], tmps.t[:, 0:n], ALU.mult, ALU.add),
                  r=[sn, wr, tmps], w=[sn])
        ph.op(V, I_tt(t1.t[:], wi_.t[:], wi_.t[:], ALU.mult), r=[wi_], w=[t1])
        ph.op(V, I_tt(t2.t[:], wr.t[:], wr.t[:], ALU.mult), r=[wr], w=[t2])
        ph.op(V, I_stt(wt.t[:], wr.t[:], 2.0, wi_.t[:], ALU.mult, ALU.mult), r=[wr, wi_], w=[wt])
        ph.op(V, I_tt(wr.t[:], t2.t[:], t1.t[:], ALU.subtract), r=[t1, t2], w=[wr])
        ph.op(V, I_copy(wi_.t[:], wt.t[:]), r=[wt], w=[wi_])
        n *= 2

    wk = lambda nm: ph.sb([128, LB], F32, nm)
    m1, m2, m3, m4, bre_, bim_, gre, gim, hre, him = [wk(n_) for n_ in ("m1", "m2", "m3", "m4", "bre", "bim", "gre", "gim", "hre", "him")]
    hreb = ph.sb([128, LB], BF16, "hreb"); himb = ph.sb([128, LB], BF16, "himb")
    yv, x2, inn, in2, sg = [wk(n_) for n_ in ("yv", "x2", "inn", "in2", "sg")]
    yg = [wk(f"yg{i}") for i in range(4)]
    ygb = [ph.sb([128, LB], BF16, f"ygb{i}") for i in range(4)]
    ut = [[ph.sb([128, LB], BF16, f"ut{k}{i}") for i in range(4)] for k in range(2)]
    zst = [ph.sb([128, LB], BF16, f"zst{i}") for i in range(2)]
    sgl = wk("sgl"); tg = wk("tg")
    ysf = [ph.sb([128, LB], BF16, f"ysf{i}") for i in range(2)]
    hpr = [sm(f"hpr{i}") for i in range(3)]; hpi = [sm(f"hpi{i}") for i in range(3)]
    a0 = ph.sb([128, 4], F32, "a0")
    nb = 0
    nz = 0
    for s in seqs:
        Lb = min(LB, s.T)
        hr_, hi_ = hpr[s.i], hpi[s.i]
        prev = s.i > 0
        if prev:
            ph.dma(I_dma(hr_.t[:], A["ssr"][l, s.i - 1].rearrange("(pt q) -> q pt", q=128)), w=[hr_])
            ph.dma(I_dma(hi_.t[:], A["ssi"][l, s.i - 1].rearrange("(pt q) -> q pt", q=128)), w=[hi_])
        for blk in range(s.T // Lb):
            t0 = blk * Lb
            u_ = ut[blk % 2]
            for ct in range(4):
                ph.dma(I_dma(u_[ct].t[:, :Lb], s.uT[ct * 128:(ct + 1) * 128, t0:t0 + Lb]), w=[u_[ct]])
            for ct in range(4):
                py = pY[ct % 2]
                for qq in range(4):
                    pt = ct * 4 + qq
                    pr = pB[(nb % 2) * 2]; pi_ = pB[(nb % 2) * 2 + 1]; nb += 1
                    ph.op("pe", I_mm(pr.t[:, :Lb], E[0][pt].t[:, :], u_[ct].t[:, :Lb]), r=[E[0][pt], u_[ct]], w=[pr])
                    ph.op("pe", I_mm(pi_.t[:, :Lb], E[1][pt].t[:, :], u_[ct].t[:, :Lb]), r=[E[1][pt], u_[ct]], w=[pi_])
                    c_ = cs.t[:, pt, :Lb]; s_ = sn.t[:, pt, :Lb]
                    ph.op(V, I_tt(m1.t[:, :Lb], pr.t[:, :Lb], c_, ALU.mult), r=[pr, cs], w=[m1])
                    ph.op(V, I_tt(m2.t[:, :Lb], pi_.t[:, :Lb], s_, ALU.mult), r=[pi_, sn], w=[m2])
                    ph.op(V, I_tt(m3.t[:, :Lb], pi_.t[:, :Lb], c_, ALU.mult), r=[pi_, cs], w=[m3])
                    ph.op(V, I_tt(m4.t[:, :Lb], pr.t[:, :Lb], s_, ALU.mult), r=[pr, sn], w=[m4])
                    ph.op("pool", I_tt(bre_.t[:, :Lb], m1.t[:, :Lb], m2.t[:, :Lb], ALU.add), r=[m1, m2], w=[bre_])
                    ph.op("pool", I_tt(bim_.t[:, :Lb], m3.t[:, :Lb], m4.t[:, :Lb], ALU.subtract), r=[m3, m4], w=[bim_])
                    if prev:
                        ph.op(V, I_tt(a0.t[:, 0:1], abi.t[:, pt:pt + 1], hi_.t[:, pt:pt + 1], ALU.mult), r=[abi, hi_], w=[a0])
                        ph.op(V, I_stt(a0.t[:, 1:2], abr.t[:, pt:pt + 1], hr_.t[:, pt:pt + 1], a0.t[:, 0:1], ALU.mult, ALU.subtract),
                              r=[abr, hr_, a0], w=[a0])
                        ph.op(V, I_tt(a0.t[:, 2:3], abi.t[:, pt:pt + 1], hr_.t[:, pt:pt + 1], ALU.mult), r=[abi, hr_], w=[a0])
                        ph.op(V, I_stt(a0.t[:, 3:4], abr.t[:, pt:pt + 1], hi_.t[:, pt:pt + 1], a0.t[:, 2:3], ALU.mult, ALU.add),
                              r=[abr, hi_, a0], w=[a0])
                        ph.op(V, I_tt(bre_.t[:, 0:1], bre_.t[:, 0:1], a0.t[:, 1:2], ALU.add), r=[bre_, a0], w=[bre_])
                        ph.op(V, I_tt(bim_.t[:, 0:1], bim_.t[:, 0:1], a0.t[:, 3:4], ALU.add), r=[bim_, a0], w=[bim_])
                    rb = mag.t[:, pt:pt + 1].to_broadcast([128, Lb])
                    ph.op(V, I_scan(gre.t[:, :Lb], rb, bre_.t[:, :Lb], 0.0), r=[mag, bre_], w=[gre])
                    ph.op(V, I_scan(gim.t[:, :Lb], rb, bim_.t[:, :Lb], 0.0), r=[mag, bim_], w=[gim])
                    ph.op(V, I_tt(m1.t[:, :Lb], gre.t[:, :Lb], c_, ALU.mult), r=[gre, cs], w=[m1])
                    ph.op(V, I_tt(m2.t[:, :Lb], gim.t[:, :Lb], s_, ALU.mult), r=[gim, sn], w=[m2])
                    ph.op("pool", I_tt(m3.t[:, :Lb], gre.t[:, :Lb], s_, ALU.mult), r=[gre, sn], w=[m3])
                    ph.op("pool", I_tt(m4.t[:, :Lb], gim.t[:, :Lb], c_, ALU.mult), r=[gim, cs], w=[m4])
                    ph.op(V, I_tt(hre.t[:, :Lb], m1.t[:, :Lb], m2.t[:, :Lb], ALU.subtract), r=[m1, m2], w=[hre])
                    ph.op("pool", I_tt(him.t[:, :Lb], m3.t[:, :Lb], m4.t[:, :Lb], ALU.add), r=[m3, m4], w=[him])
                    ph.op("act", I_act(hr_.t[:, pt:pt + 1], hre.t[:, Lb - 1:Lb], AF.Copy), r=[hre], w=[hr_])
                    ph.op("act", I_act(hi_.t[:, pt:pt + 1], him.t[:, Lb - 1:Lb], AF.Copy), r=[him], w=[hi_])
                    ph.op("act", I_act(hreb.t[:, :Lb], hre.t[:, :Lb], AF.Copy), r=[hre], w=[hreb])
                    ph.op("act", I_act(himb.t[:, :Lb], him.t[:, :Lb], AF.Copy), r=[him], w=[himb])
                    ph.op("pe", I_mm(py.t[:, :Lb], Cpad.t[:, pt, 0, :], hreb.t[:, :Lb], start=(qq == 0), stop=False), r=[Cpad, hreb], w=[py])
                    ph.op("pe", I_mm(py.t[:, :Lb], Cpad.t[:, pt, 1, :], himb.t[:, :Lb], start=False, stop=(qq == 3)), r=[Cpad, himb], w=[py])
                ph.op(V, I_stt(yv.t[:, :Lb], u_[ct].t[:, :Lb], dsk.t[:, ct:ct + 1], py.t[:, :Lb], ALU.mult, ALU.add), r=[u_[ct], dsk, py], w=[yv])
                ph.op("pool", I_tt(x2.t[:, :Lb], yv.t[:, :Lb], yv.t[:, :Lb], ALU.mult), r=[yv], w=[x2])
                ph.op("pool", I_ts(inn.t[:, :Lb], x2.t[:, :Lb], 0.044715, 1.0, op0=ALU.mult, op1=ALU.add), r=[x2], w=[inn])
                ph.op("pool", I_tt(in2.t[:, :Lb], inn.t[:, :Lb], yv.t[:, :Lb], ALU.mult), r=[inn, yv], w=[in2])
                ph.op("act", I_act(sg.t[:, :Lb], in2.t[:, :Lb], AF.Sigmoid, scale=1.5957691216057308), r=[in2], w=[sg])
                ph.op(V, I_tt(yg[ct].t[:, :Lb], yv.t[:, :Lb], sg.t[:, :Lb], ALU.mult), r=[yv, sg], w=[yg[ct]])
                ph.op("act", I_act(ygb[ct].t[:, :Lb], yg[ct].t[:, :Lb], AF.Copy), r=[yg[ct]], w=[ygb[ct]])
            prev = True
            for co in range(4):
                z_ = zst[nz % 2]; yf = ysf[nz % 2]; nz += 1
                for ct in range(4):
                    ph.op("pe", I_mm(pG.t[:, :Lb], wglu.t[:, ct, co * 128:(co + 1) * 128], ygb[ct].t[:, :Lb], start=(ct == 0), stop=(ct == 3)),
                          r=[wglu, ygb[ct]], w=[pG])
                ph.op("act", I_act(sgl.t[:, :Lb], pG.t[:, :Lb], AF.Sigmoid, bias=bglu.t[:, co:co + 1]), r=[pG, bglu], w=[sgl])
                ph.dma(I_dma(z_.t[:, :Lb], s.zsT[co * 128:(co + 1) * 128, t0:t0 + Lb]), w=[z_])
                ph.op(V, I_tt(tg.t[:, :Lb], yg[co].t[:, :Lb], sgl.t[:, :Lb], ALU.mult), r=[yg[co], sgl], w=[tg])
                ph.op("pool", I_tt(yf.t[:, :Lb], tg.t[:, :Lb], z_.t[:, :Lb], ALU.mult), r=[tg, z_], w=[yf])
                ph.dma(I_dma(s.ysT[co * 128:(co + 1) * 128, t0:t0 + Lb], yf.t[:, :Lb]), r=[yf])
        ph.dma(I_dma(s.srout[l].rearrange("(pt q) -> q pt", q=128), hr_.t[:]), r=[hr_])
        ph.dma(I_dma(s.siout[l].rearrange("(pt q) -> q pt", q=128), hi_.t[:]), r=[hi_])
    ph.finish()


def bc_mid(ap, n):
    return bass.AP(ap.tensor, ap.offset, [list(ap.ap[0]), [0, n]] + [list(x) for x in ap.ap[1:]])


NIT2 = 20
W02 = 16.0


def phase_attn2(nc, A, l, s):
    ph = Phase(nc, f"B{l}{s.nm}")
    idf, idb = mk_ident(ph)
    T, S = s.T, s.S
    QP = min(128, T); QB = min(256, T)
    TPB = QB // QP
    ntile = T // QP; nblk = T // QB
    L = T if s.causal else S
    KSEL = float(min(TOPK, L // 4))
    ktiles = [(k0, min(128, S - k0)) for k0 in range(0, S, 128)]
    NKT = len(ktiles)
    kiT2 = ph.sb([128, S], BF16, "kiT2")
    ph.dma(I_dma(kiT2.t[0:64, :], s.kiT[:, :]), w=[kiT2])
    ph.dma(I_dma(kiT2.t[64:128, :], s.kiT[:, :]), w=[kiT2])
    sc = [ph.sb([128, S], F32, f"sc{i}") for i in range(2)]
    junk = ph.sb([128, S], BF16, "junk")
    maskT = [ph.sb([128, NKT, QB], BF16, f"maskT{i}") for i in range(2)]
    qia = [ph.sb([128, 4, 128], BF16, f"qia{i}") for i in range(2)]
    qib = [ph.sb([128, 4, 128], BF16, f"qib{i}") for i in range(2)]
    qta = [ph.sb([128, 4, 256], BF16, f"qta{i}") for i in range(2)]
    qtb = [ph.sb([128, 4, 256], BF16, f"qtb{i}") for i in range(2)]
    for i in range(2):
        ph.op("pool", I_memset(qia[i].t[64:128, :, :], 0.0), w=[qia[i]])
        ph.op("pool", I_memset(qib[i].t[0:64, :, :], 0.0), w=[qib[i]])
        ph.op("pool", I_memset(qta[i].t[64:128, :, :], 0.0), w=[qta[i]])
        ph.op("pool", I_memset(qtb[i].t[0:64, :, :], 0.0), w=[qtb[i]])
    wit = [ph.sb([128, 8], F32, f"wit{i}") for i in range(2)]
    dg = [ph.sb([128, 8, 128], BF16, f"dg{i}") for i in range(2)]
    rl = [ph.sb([128, 512], BF16, f"rl{i}") for i in range(3)]
    sm1 = lambda nm: ph.sb([128, 1], F32, nm)
    mid, negmid, cnt, sact, comb, tq, thr = [sm1(n_) for n_ in ("mid", "negmid", "cnt", "sact", "comb", "tq", "thr")]
    dgthr = ph.sb([128, 128], F32, "dgthr")
    thrbc = ph.sb([128, 128], F32, "thrbc")
    onesf = ph.sb([128, 128], F32, "onesf")
    ph.op("pool", I_memset(onesf.t[:], 1.0), w=[onesf])
    va = [ph.sb([128, 130], BF16, f"va{i}") for i in range(3)]
    ktl = [ph.sb([128, 128], BF16, f"ktl{i}") for i in range(3)]
    pe_ = [ph.sb([128, 2, 256], BF16, f"pe{i}") for i in range(2)]
    pm = [ph.sb([128, 2, 256], BF16, f"pm{i}") for i in range(2)]
    oa = [ph.sb([128, 256], F32, f"oa{i}") for i in range(2)]
    for o__ in oa:
        ph.op("pool", I_memset(o__.t[:], 0.0), w=[o__])
    rden = [ph.sb([64, 256], F32, f"rden{i}") for i in range(2)]
    t1 = [ph.sb([64, 256], F32, f"t1{i}") for i in range(2)]
    zat = [ph.sb([64, 256], BF16, f"zat{i}") for i in range(2)]
    yat = [ph.sb([64, 256], BF16, f"yat{i}") for i in range(2)]
    iop = ph.sb([128, 64], F32, "iop")
    selden = ph.sb([128, 64], F32, "selden")
    ph.op("pool", I_iota(iop.t[:], [[0, 64]], 0, 1), w=[iop])
    ph.op("dve", I_ts(selden.t[:], iop.t[:], 64.0, None, op0=ALU.is_equal), r=[iop], w=[selden])
    lg = [ph.ps([128, 512], F32, f"lg{i}") for i in range(2)]
    acc = ph.ps([128, 512], F32, "acc")
    st = ph.ps([128, 2, 256], F32, "st")
    oacc = [ph.ps([128, 512], F32, f"oacc{i}") for i in range(2)]
    scTp = ph.ps([128, 4, 128], F32, "scTp")
    misc = ph.ps([128, 512], F32, "misc")
    cnts = {"r": 0, "v": 0, "h": 0, "e": 0}

    def tile_info(i):
        q0 = i * QP
        Slim = (q0 + QP) if s.causal else S
        return q0, Slim

    def blk_kts(b):
        Sblk = min(S, (b + 1) * QB) if s.causal else S
        return [(i, k0, ksz) for i, (k0, ksz) in enumerate(ktiles) if k0 < Sblk]

    def gen_index(i):
        q0, Slim = tile_info(i)
        sl = i % 2
        qa, qb_, wi_, dg_, sc_ = qia[sl], qib[sl], wit[sl], dg[sl], sc[sl]
        qsrc = s.qiT.rearrange("(hp p) t -> p hp t", p=128)
        ph.dma(I_dma(qa.t[0:64, :, :QP], qsrc[0:64, :, q0:q0 + QP]), w=[qa])
        ph.dma(I_dma(qb_.t[64:128, :, :QP], qsrc[64:128, :, q0:q0 + QP]), w=[qb_])
        ph.dma(I_dma(wi_.t[:QP, :], s.wiS[q0:q0 + QP, :]), w=[wi_])
        for h in range(8):
            ph.op("dve", I_ts(dg_.t[:QP, h, :QP], idf.t[:QP, :QP], wi_.t[:QP, h:h + 1], None, op0=ALU.mult), r=[idf, wi_], w=[dg_])
        yield
        for c0 in range(0, Slim, 512):
            csz = min(512, Slim - c0)
            bufs = []
            for h in range(8):
                bufs.append((lg[cnts["r"] % 2], rl[cnts["r"] % 3])); cnts["r"] += 1

            def emit_lg(h):
                g, r_ = bufs[h]
                qz = qa if h % 2 == 0 else qb_
                ph.op("pe", I_mm(g.t[:QP, :csz], qz.t[:, h // 2, :QP], kiT2.t[:, c0:c0 + csz]), r=[qz, kiT2], w=[g])
            emit_lg(0)
            for h in range(8):
                g, r_ = bufs[h]
                if h < 7:
                    emit_lg(h + 1)
                if h % 4 == 3:
                    ph.op("dve", I_ts(r_.t[:QP, :csz], g.t[:QP, :csz], 0.0, None, op0=ALU.max), r=[g], w=[r_])
                else:
                    ph.op("act", I_act(r_.t[:QP, :csz], g.t[:QP, :csz], AF.Relu), r=[g], w=[r_])
                ph.op("pe", I_mm(acc.t[:QP, :csz], dg_.t[:QP, h, :QP], r_.t[:QP, :csz], start=(h == 0), stop=(h == 7)), r=[dg_, r_], w=[acc])
            ph.op("dve", I_copy(sc_.t[:QP, c0:c0 + csz], acc.t[:QP, :csz]), r=[acc], w=[sc_])
            yield
        if s.causal and QP == 128:
            ph.op("pool", I_memset(sc_.t[0:64, Slim - 64:Slim], NEG), w=[sc_])

    def n_index(i):
        q0, Slim = tile_info(i)
        return 1 + (Slim + 511) // 512

    def gen_bisect(i):
        q0, Slim = tile_info(i)
        sc_ = sc[i % 2]
        b = i // TPB
        qoff = (i % TPB) * QP
        mT = maskT[b % 2]
        S1 = Slim
        if Slim >= 1024:
            S1 = int(Slim * 0.45) // 128 * 128
        ph.op("dve", I_memset(mid.t[:QP, :], 0.0), w=[mid])
        w = W02
        kadj = KSEL - 0.5 * (Slim - S1)
        for k in range(NIT2):
            w = w / 2.0
            ph.op("pool", I_ts(negmid.t[:QP, :], mid.t[:QP, :], -w, None, op0=ALU.add), r=[mid], w=[negmid])
            ph.op("dve", I_ts(junk.t[:QP, :S1], sc_.t[:QP, :S1], mid.t[:QP, 0:1], -kadj, op0=ALU.is_ge, op1=ALU.add,
                              accum_out=cnt.t[:QP, 0:1]), r=[sc_, mid], w=[cnt])
            if S1 < Slim:
                ph.op("act", I_act(junk.t[:QP, S1:Slim], sc_.t[:QP, S1:Slim], AF.Sign, bias=mid.t[:QP, 0:1], scale=-1.0,
                                   accum_out=sact.t[:QP, 0:1]), r=[sc_, mid], w=[sact])
                ph.op("dve", I_stt(tq.t[:QP, :], sact.t[:QP, :], 0.5, cnt.t[:QP, :], ALU.mult, ALU.is_le), r=[sact, cnt], w=[tq])
            else:
                ph.op("dve", I_ts(tq.t[:QP, :], cnt.t[:QP, :], 0.0, None, op0=ALU.is_ge), r=[cnt], w=[tq])
            ph.op("dve", I_stt(mid.t[:QP, :], tq.t[:QP, :], 2.0 * w, negmid.t[:QP, :], ALU.mult, ALU.add), r=[tq, negmid], w=[mid])
            yield
        ph.op("dve", I_ts(thr.t[:QP, :], mid.t[:QP, :], -w, None, op0=ALU.add), r=[mid], w=[thr])
        ph.op("dve", I_ts(dgthr.t[:QP, :QP], idf.t[:QP, :QP], thr.t[:QP, 0:1], None, op0=ALU.mult), r=[idf, thr], w=[dgthr])
        ph.op("pe", I_mm(misc.t[:, 0:QP], onesf.t[:QP, :], dgthr.t[:QP, :QP]), r=[onesf, dgthr], w=[misc])
        ph.op("act", I_act(thrbc.t[:, :QP], misc.t[:, 0:QP], AF.Copy), r=[misc], w=[thrbc])
        yield
        kts = blk_kts(b)
        mine = [(i_, k0, ksz) for (i_, k0, ksz) in kts if k0 < Slim]
        for g0 in range(0, len(mine), 4):
            grp = mine[g0:g0 + 4]
            for gi, (i_, k0, ksz) in enumerate(grp):
                ph.op("pe", I_tr(scTp.t[:ksz, gi, :QP], sc_.t[:QP, k0:k0 + ksz], idf.t[:QP, :QP]), r=[sc_, idf], w=[scTp])
            i0 = grp[0][0]; ng = len(grp)
            ph.op("dve", I_tt(mT.t[:, i0:i0 + ng, qoff:qoff + QP], scTp.t[:, 0:ng, :QP], bc_mid(thrbc.t[:, 0:QP], ng), ALU.is_ge),
                  r=[scTp, thrbc], w=[mT])
            yield
        rest = [(i_, k0, ksz) for (i_, k0, ksz) in kts if k0 >= Slim]
        if rest:
            i0 = rest[0][0]; i1 = rest[-1][0] + 1
            ph.op("pool", I_memset(mT.t[:, i0:i1, qoff:qoff + QP], 0.0), w=[mT])

    def n_bisect(i):
        q0, Slim = tile_info(i)
        nm = len([1 for (k0, ksz) in ktiles if k0 < Slim])
        return NIT2 + 1 + (nm + 3) // 4

    import os as _os4
    _stop = _os4.environ.get("ATTN2_STOP", "")

    def gen_attn(b):
        qb0 = b * QB
        kts = blk_kts(b)
        mT = maskT[b % 2]
        qa, qb_ = qta[b % 2], qtb[b % 2]
        qsrc = s.qT.rearrange("(hp p) t -> p hp t", p=128)
        ph.dma(I_dma(qa.t[0:64, :, :QB], qsrc[0:64, :, qb0:qb0 + QB]), w=[qa])
        ph.dma(I_dma(qb_.t[64:128, :, :QB], qsrc[64:128, :, qb0:qb0 + QB]), w=[qb_])
        for hg in range(4):
            ubuf = []
            for idx in range(len(kts)):
                ubuf.append((va[cnts["v"] % 3], ktl[cnts["v"] % 3])); cnts["v"] += 1

            def emit_st(idx):
                i_, k0, ksz = kts[idx]
                va_, kt_ = ubuf[idx]
                ph.dma(I_dma(va_.t[:ksz, :], s.vaug[k0:k0 + ksz, hg * 130:(hg + 1) * 130]), w=[va_])
                ph.dma(I_dma(kt_.t[:, :ksz], s.kT[hg * 128:(hg + 1) * 128, k0:k0 + ksz]), w=[kt_])
                ph.op("pe", I_mm(st.t[:ksz, 0, :QB], kt_.t[:, :ksz], qa.t[:, hg, :QB]), r=[kt_, qa], w=[st])
                ph.op("pe", I_mm(st.t[:ksz, 1, :QB], kt_.t[:, :ksz], qb_.t[:, hg, :QB]), r=[kt_, qb_], w=[st])
            emit_st(0)
            for idx, (i_, k0, ksz) in enumerate(kts):
                va_, kt_ = ubuf[idx]
                p_ = pe_[cnts["h"] % 2]; m_ = pm[cnts["h"] % 2]; cnts["h"] += 1
                ph.op("act", I_act(p_.t[:ksz, :, :QB], st.t[:ksz, :, :QB], AF.Exp), r=[st], w=[p_])
                if idx + 1 < len(kts):
                    emit_st(idx + 1)
                ph.op("dve", I_tt(m_.t[:ksz, :, :QB], p_.t[:ksz, :, :QB], bc_mid(mT.t[:ksz, i_, :QB], 2), ALU.mult), r=[p_, mT], w=[m_])
                for hh in range(2):
                    ph.op("pe", I_mm(oacc[hh].t[:65, :QB], va_.t[:ksz, hh * 65:(hh + 1) * 65], m_.t[:ksz, hh, :QB],
                                     start=(idx == 0), stop=(idx == len(kts) - 1)), r=[va_, m_], w=[oacc[hh]])
                yield
            if _stop in ("st", "exp", "mul", "pv"):
                yield
                continue
            for hh in range(2):
                h = hg * 2 + hh
                e = cnts["e"] % 2; cnts["e"] += 1
                o_ = oa[e]; rd = rden[e]; t_ = t1[e]; z_ = zat[e]; y_ = yat[e]
                ph.op("act", I_act(o_.t[:65, :QB], oacc[hh].t[:65, :QB], AF.Copy), r=[oacc[hh]], w=[o_])
                if _stop == "ep1":
                    continue
                ph.op("pe", I_mm(acc.t[:64, 0:QB], selden.t[:, :64], o_.t[:, :QB]), r=[selden, o_], w=[acc])
                if _stop == "ep2":
                    continue
                ph.op("dve", I_recip(rd.t[:64, :QB], acc.t[:64, 0:QB]), r=[acc], w=[rd])
                if _stop == "ep3":
                    continue
                ph.dma(I_dma(z_.t[:64, :QB], s.zaT[64 * h:64 * h + 64, qb0:qb0 + QB]), w=[z_])
                ph.op("dve", I_tt(t_.t[:64, :QB], o_.t[0:64, :QB], rd.t[:64, :QB], ALU.mult), r=[o_, rd], w=[t_])
                if _stop == "ep4":
                    continue
                ph.op("pool", I_tt(y_.t[:64, :QB], t_.t[:64, :QB], z_.t[:64, :QB], ALU.mult), r=[t_, z_], w=[y_])
                if _stop == "ep5":
                    continue
                ph.dma(I_dma(s.yaT[64 * h:64 * h + 64, qb0:qb0 + QB], y_.t[:64, :QB]), r=[y_], q="act")
            yield

    def n_attn(b):
        return 4 * (len(blk_kts(b)) + 1)

    attn_state = {}
    nslot = ntile + 1 + TPB + 1
    for slot in range(nslot + 2 * TPB):
        work = []
        if slot < ntile:
            work.append([gen_index(slot), n_index(slot)])
        import os as _os3
        _stop = _os3.environ.get("ATTN2_STOP", "")
        if 1 <= slot <= ntile and _stop != "index":
            work.append([gen_bisect(slot - 1), n_bisect(slot - 1)])
        for b in range(nblk):
            ready = (b + 1) * TPB + 1
            if slot == ready and _stop not in ("index", "bisect"):
                attn_state[b] = [gen_attn(b), n_attn(b), 0]
        for b, stt_ in list(attn_state.items()):
            g, tot, done = stt_
            share = (tot + TPB - 1) // TPB
            work.append([g, min(share, tot - done) + 1])
            stt_[2] = done + share
            if stt_[2] >= tot:
                del attn_state[b]
        if not work:
            continue
        maxc = max(c for _, c in work)
        done_c = [0] * len(work)
        alive = [True] * len(work)
        for k in range(maxc):
            for wi_x, (g, c) in enumerate(work):
                while alive[wi_x] and done_c[wi_x] < c and (done_c[wi_x] + 1) * maxc <= (k + 1) * c:
                    try:
                        next(g)
                    except StopIteration:
                        alive[wi_x] = False
                    done_c[wi_x] += 1
        for wi_x, (g, c) in enumerate(work[: (1 if slot < ntile else 0) + (1 if (1 <= slot <= ntile and _stop != "index") else 0)]):
            if alive[wi_x]:
                for _ in g:
                    pass
    for b, stt_ in list(attn_state.items()):
        for _ in stt_[0]:
            pass
    ph.finish()
```

```python
import numpy as np
from contextlib import ExitStack
import concourse.bass as bass
import concourse.mybir as mybir
from concourse.bass_utils import run_bass_kernel_spmd

F32 = mybir.dt.float32
BF16 = mybir.dt.bfloat16
I32 = mybir.dt.int32
AF = mybir.ActivationFunctionType
ALU = mybir.AluOpType

D = 1024
DIN = 5704
NL = 2
EPS = 1e-6
OFF_U, OFF_ZS, OFF_Q, OFF_K, OFF_V, OFF_ZA, OFF_QI, OFF_KI, OFF_WI, OFF_GM = (
    0, 512, 1024, 1536, 2048, 2560, 3072, 3584, 3648, 3656)
NEG = -1.0e30
TOPK = 256


class TT:
    __slots__ = ("t", "name", "w", "r", "dsem", "dtot")

    def __init__(self, t, name):
        self.t = t
        self.name = name
        self.w = None
        self.r = []
        self.dsem = None
        self.dtot = 0

    def __getitem__(self, idx):
        return self.t[idx]


_SEMPOOL = {}
_SEMSTACK = []


class Phase:
    ENGS = ("sync", "pool", "act", "dve", "pe")

    def __init__(self, nc, name):
        self.nc = nc
        self.name = name
        self.es = ExitStack()
        self.ops = {e: [] for e in self.ENGS}
        self.cnt = {e: 0 for e in self.ENGS}
        self.known = {e: {} for e in self.ENGS}
        self.sems = {}
        self.semobj = {}
        self.nsem = 0
        for e in ("pool", "act", "dve", "pe"):
            s = self._newsem(f"{name}_{e}")
            self.sems[e] = s
            self.semobj[("c", e)] = s
        self.ntile = 0
        self.dma_tiles = []
        self.misc = None

    def _newsem(self, nm):
        pool = _SEMPOOL.get(id(self.nc))
        if pool is None:
            return self.es.enter_context(self.nc.semaphore(nm))
        sm_ = pool[self.nsem]
        self.nsem += 1
        return sm_

    def sb(self, shape, dtype, name=None):
        self.ntile += 1
        nm = f"{self.name}_{name or 't'}{self.ntile}"
        t = self.es.enter_context(self.nc.sbuf_tensor(nm, list(shape), dtype))
        return TT(t, nm)

    def ps(self, shape, dtype, name=None):
        self.ntile += 1
        nm = f"{self.name}_{name or 'p'}{self.ntile}"
        t = self.es.enter_context(self.nc.psum_tensor(nm, list(shape), dtype))
        return TT(t, nm)

    def _dsem(self, tt):
        if tt.dsem is None:
            tt.dsem = self._newsem(f"d_{tt.name}")
            self.semobj[("d", tt.name)] = tt.dsem
            self.dma_tiles.append(tt)
        return tt.dsem

    def _need(self, eng, dep, waits):
        if dep is None:
            return
        key, val = dep
        if key == ("c", "pe") and eng == "pe":
            return
        if key == ("c", eng) and eng in ("sync",):
            return
        k = self.known[eng]
        if k.get(key, 0) >= val:
            return
        k[key] = val
        waits.append((self.semobj[key], val))

    def _deps(self, eng, r, w):
        waits = []
        for t in r:
            self._need(eng, t.w, waits)
        for t in w:
            self._need(eng, t.w, waits)
            for d in t.r:
                self._need(eng, d, waits)
        return waits

    def op(self, eng, fn, r=(), w=()):
        waits = self._deps(eng, r, w)
        self.cnt[eng] += 1
        me = (("c", eng), self.cnt[eng])
        for t in r:
            t.r.append(me)
        for t in w:
            t.w = me
            t.r = []
        self.ops[eng].append((waits, fn, ("c", self.sems[eng])))

    def dma(self, fn, r=(), w=(), q="sync"):
        waits = self._deps(q, r, w)
        tiles = list(w) + list(r)
        if tiles:
            tt = tiles[0]
            sem = self._dsem(tt)
            tt.dtot += 16
            me = (("d", tt.name), tt.dtot)
        else:
            if self.misc is None:
                self.misc = TT(None, f"{self.name}_misc")
            tt = self.misc
            sem = self._dsem(tt)
            tt.dtot += 16
            me = (("d", tt.name), tt.dtot)
        for t in r:
            t.r.append(me)
        for t in w:
            t.w = me
            t.r = []
        self.ops[q].append((waits, fn, ("d", sem)))

    def finish(self):
        nc = self.nc
        finals = [(t.dsem, t.dtot) for t in self.dma_tiles if t.dtot > 0]

        def replay(e, name):
            for waits, fn, inc in self.ops[name]:
                for s, v in waits:
                    e.wait_ge(s, v)
                ins = fn(e)
                if inc[0] == "c":
                    ins.then_inc(inc[1], 1)
                else:
                    ins.then_inc(inc[1], 16)
            if name == "sync":
                for s, v in finals:
                    e.wait_ge(s, v)

        with nc.Block() as block:
            @block.sync
            def _(e):
                replay(e, "sync")

            @block.gpsimd
            def _(e):
                replay(e, "pool")

            @block.scalar
            def _(e):
                replay(e, "act")

            @block.vector
            def _(e):
                replay(e, "dve")

            @block.tensor
            def _(e):
                replay(e, "pe")
        allsems = list(self.semobj.values())
        with nc.Block() as block2:
            @block2.sync
            def _(e):
                for sm_ in allsems:
                    e.sem_clear(sm_)
        self.es.close()


def I_ts(out, in0, s1, s2=None, op0=ALU.mult, op1=None, accum_out=None):
    kw = {}
    if op1 is not None:
        kw["op1"] = op1
    if accum_out is not None:
        kw["accum_out"] = accum_out
    return lambda e: e.tensor_scalar(out=out, in0=in0, scalar1=s1, scalar2=s2, op0=op0, **kw)


def I_tt(out, in0, in1, op):
    return lambda e: e.tensor_tensor(out=out, in0=in0, in1=in1, op=op)


def I_stt(out, in0, scalar, in1, op0, op1):
    return lambda e: e.scalar_tensor_tensor(out=out, in0=in0, scalar=scalar, in1=in1, op0=op0, op1=op1)


def I_act(out, in_, func, bias=None, scale=None, accum_out=None):
    kw = {}
    if bias is not None:
        kw["bias"] = bias
    if scale is not None:
        kw["scale"] = scale
    if accum_out is not None:
        kw["accum_out"] = accum_out
    return lambda e: e.activation(out=out, in_=in_, func=func, **kw)


def I_mm(out, lhsT, rhs, start=True, stop=True):
    return lambda e: e.matmul(out, lhsT, rhs, start=start, stop=stop)


def I_tr(out, in_, ident):
    return lambda e: e.transpose(out, in_, ident)


def I_copy(out, in_):
    return lambda e: e.tensor_copy(out=out, in_=in_)


def I_memset(ap, v):
    return lambda e: e.memset(ap, v)


def I_recip(out, in_):
    return lambda e: e.reciprocal(out=out, in_=in_)


def I_scan(out, d0, d1, init):
    return lambda e: e.tensor_tensor_scan(out=out, data0=d0, data1=d1, initial=init, op0=ALU.mult, op1=ALU.add)


def I_dma(out, in_, **kw):
    return lambda e: e.dma_start(out=out, in_=in_, **kw)


def I_iota(out, pattern, base, cm):
    return lambda e: e.iota(out=out, pattern=pattern, base=base, channel_multiplier=cm,
                            allow_small_or_imprecise_dtypes=True)


def mk_ident(ph):
    iot = ph.sb([128, 128], F32, "iot")
    idf = ph.sb([128, 128], F32, "idf")
    idb = ph.sb([128, 128], BF16, "idb")
    ph.op("pool", I_iota(iot.t[:], [[1, 128]], 0, -1), w=[iot])
    ph.op("dve", I_ts(idf.t[:], iot.t[:], 0.0, None, op0=ALU.is_equal), r=[iot], w=[idf])
    ph.op("dve", I_copy(idb.t[:], idf.t[:]), r=[idf], w=[idb])
    return idf, idb


def bcast_rows(ap_row, n):
    return ap_row.to_broadcast([n, ap_row.shape[-1]])


class Seq:
    pass


def build(T=8192, TS=32, PAST=4096, stop_after=None, dbg=False):
    nc = bass.Bass("TRN2", target_bir_lowering=False)
    SS = PAST + TS
    din = lambda n, s, dt=F32: nc.dram_tensor(n, list(s), dt, kind="ExternalInput").ap()
    dout = lambda n, s, dt=F32: nc.dram_tensor(n, list(s), dt, kind="ExternalOutput").ap()
    scr_kind = "ExternalOutput" if dbg else "Internal"
    dscr = lambda n, s, dt=BF16: nc.dram_tensor(n, list(s), dt, kind=scr_kind).ap()

    A = {}
    A["xp"] = din("xp", [T, D]); A["xs"] = din("xs", [2, TS, D])
    A["ck"] = din("ck", [NL, 2, PAST, 512]); A["cv"] = din("cv", [NL, 2, PAST, 512])
    A["cki"] = din("cki", [NL, 2, PAST, 64])
    A["ssr"] = din("ssr", [NL, 2, 2048]); A["ssi"] = din("ssi", [NL, 2, 2048])
    A["call"] = din("call", [3, D])
    A["w_mod"] = din("w_mod", [NL, D, 3 * D]); A["b_mod"] = din("b_mod", [NL, 3 * D])
    A["g_norm"] = din("g_norm", [NL, D]); A["w_in"] = din("w_in", [NL, D, DIN])
    A["a_re"] = din("a_re", [NL, 2048]); A["a_im"] = din("a_im", [NL, 2048])
    A["log_dt"] = din("log_dt", [NL, 32])
    A["b_re"] = din("b_re", [NL, 2048, 16]); A["b_im"] = din("b_im", [NL, 2048, 16])
    A["c_reT"] = din("c_reT", [NL, 2048, 16]); A["c_imT"] = din("c_imT", [NL, 2048, 16])
    A["d_skip"] = din("d_skip", [NL, 512])
    A["w_glu"] = din("w_glu", [NL, 512, 512]); A["b_glu"] = din("b_glu", [NL, 512])
    A["w_ps"] = din("w_ps", [NL, 512, D]); A["w_pa"] = din("w_pa", [NL, 512, D])
    A["w_o"] = din("w_o", [NL, D, D]); A["g_final"] = din("g_final", [1, D])

    O = {}
    O["yp"] = dout("yp", [T, D]); O["ys"] = dout("ys", [2, TS, D])
    O["kp"] = dout("kp", [NL, T, 512]); O["vp"] = dout("vp", [NL, T, 512]); O["kip"] = dout("kip", [NL, T, 64])
    O["srp"] = dout("srp", [NL, 2048]); O["sip"] = dout("sip", [NL, 2048])
    O["ks"] = dout("ks", [NL, 2, TS, 512]); O["vs"] = dout("vs", [NL, 2, TS, 512])
    O["kis"] = dout("kis", [NL, 2, TS, 64])
    O["srs"] = dout("srs", [NL, 2, 2048]); O["sis"] = dout("sis", [NL, 2, 2048])

    modbc = dscr("modbc", [NL, 3, 3, 128, D], F32)

    seqs = []
    for i in range(3):
        s = Seq()
        s.i = i
        s.nm = "p" if i == 0 else f"s{i - 1}"
        s.T = T if i == 0 else TS
        s.S = T if i == 0 else SS
        s.koff = 0 if i == 0 else PAST
        s.causal = (i == 0)
        s.xin = A["xp"] if i == 0 else A["xs"][i - 1]
        s.xres = dscr(f"xres_{s.nm}", [s.T, D], F32)
        for nm, rows, cols in (("uT", 512, s.T), ("zsT", 512, s.T), ("qT", 512, s.T), ("kT", 512, s.S),
                               ("zaT", 512, s.T), ("qiT", 512, s.T), ("kiT", 64, s.S), ("gmT", 2048, s.T),
                               ("vaug", s.S, 520), ("ysT", 512, s.T), ("yaT", 512, s.T)):
            setattr(s, nm, dscr(f"{nm}_{s.nm}", [rows, cols], BF16))
        s.wiS = dscr(f"wiS_{s.nm}", [s.T, 8], F32)
        if i == 0:
            s.kout = [O["kp"][l] for l in range(NL)]; s.vout = [O["vp"][l] for l in range(NL)]
            s.kiout = [O["kip"][l] for l in range(NL)]
            s.srout = [O["srp"][l] for l in range(NL)]; s.siout = [O["sip"][l] for l in range(NL)]
            s.yout = O["yp"]
        else:
            j = i - 1
            s.kout = [O["ks"][l, j] for l in range(NL)]; s.vout = [O["vs"][l, j] for l in range(NL)]
            s.kiout = [O["kis"][l, j] for l in range(NL)]
            s.srout = [O["srs"][l, j] for l in range(NL)]; s.siout = [O["sis"][l, j] for l in range(NL)]
            s.yout = O["ys"][j]
        seqs.append(s)

    ncd = nc.allow_non_contiguous_dma(reason="small strided param loads")
    ncd.__enter__()
    es_ = ExitStack()
    _SEMSTACK.append(es_)

    phase_mod(nc, A, modbc)
    if stop_after == "mod":
        return nc
    for l in range(NL):
        phase_proj(nc, A, l, seqs, modbc)
        if stop_after == f"proj{l}":
            return nc
        phase_cache(nc, A, l, seqs, PAST)
        if stop_after == f"cache{l}":
            return nc
        phase_ssm(nc, A, l, seqs)
        if stop_after == f"ssm{l}":
            return nc
        import os as _os2
        for s in seqs:
            if _os2.environ.get("ATTN_SEQ") and s.nm not in _os2.environ.get("ATTN_SEQ").split(","):
                continue
            (phase_attn2 if (_os2.environ.get('ATTN_V','h')=='2' or (_os2.environ.get('ATTN_V','h')=='h' and s.i == 0)) else phase_attn)(nc, A, l, s)
        if stop_after == f"attn{l}":
            return nc
        phase_merge(nc, A, l, seqs, modbc)
        if stop_after == f"merge{l}":
            return nc
    return nc


def phase_mod(nc, A, modbc):
    ph = Phase(nc, "M")
    idf, idb = mk_ident(ph)
    cT = ph.sb([128, 8, 3], F32, "cT")
    for s_ in range(3):
        ph.dma(I_dma(cT.t[:, :, s_], A["call"][s_].rearrange("(c p) -> p c", p=128)), w=[cT])
    cTs = ph.sb([128, 8, 3], F32, "cTs")
    ph.op("act", I_act(cTs.t[:], cT.t[:], AF.Silu), r=[cT], w=[cTs])
    iop = ph.sb([3, 128], F32, "iop")
    ph.op("pool", I_iota(iop.t[:], [[0, 128]], 0, 1), w=[iop])
    sel = []
    for s in range(3):
        t = ph.sb([3, 128], F32, f"sel{s}")
        ph.op("dve", I_ts(t.t[:], iop.t[:], float(s), None, op0=ALU.is_equal), r=[iop], w=[t])
        sel.append(t)
    wst = [ph.sb([128, 8, 512], F32, f"wst{i}") for i in range(2)]
    pmm = [ph.ps([128, 512], F32, f"pm{i}") for i in range(2)]
    pbc = [ph.ps([128, 512], F32, f"pb{i}") for i in range(2)]
    stg = [ph.sb([128, D], F32, f"stg{i}") for i in range(2)]
    n = 0
    nb = 0
    for l in range(NL):
        bsb = ph.sb([3, 3 * D], F32, f"bsb{l}")
        ph.dma(I_dma(bsb.t[:], bcast_rows(A["b_mod"][l:l + 1, :], 3)), w=[bsb])
        gbc = ph.sb([128, D], F32, f"gbc{l}")
        ph.dma(I_dma(gbc.t[:], bcast_rows(A["g_norm"][l:l + 1, :], 128)), w=[gbc])
        mod = ph.sb([3, 3 * D], F32, f"mod{l}")
        for ct in range(6):
            w = wst[n % 2]; pm = pmm[n % 2]; n += 1
            ph.dma(I_dma(w.t[:], A["w_mod"][l].rearrange("(c p) n -> p c n", p=128)[:, :, ct * 512:(ct + 1) * 512]), w=[w])
            for c in range(8):
                ph.op("pe", I_mm(pm.t[0:3, :], cTs.t[:, c, :], w.t[:, c, :], start=(c == 0), stop=(c == 7)),
                      r=[cTs, w], w=[pm])
            ph.op("dve", I_tt(mod.t[:, ct * 512:(ct + 1) * 512], pm.t[0:3, :], bsb.t[:, ct * 512:(ct + 1) * 512], ALU.add),
                  r=[pm, bsb], w=[mod])
        ph.op("dve", I_ts(mod.t[:, D:2 * D], mod.t[:, D:2 * D], 1.0, None, op0=ALU.add), r=[mod], w=[mod])
        for s in range(3):
            for kind in range(3):
                st = stg[nb % 2]
                for half in range(2):
                    pb = pbc[nb % 2 if half == 0 else (nb + 1) % 2]
                    c0 = kind * D + half * 512
                    ph.op("pe", I_mm(pb.t[:, :], sel[s].t[:, :], mod.t[:, c0:c0 + 512]), r=[sel[s], mod], w=[pb])
                    if kind == 1:
                        ph.op("dve", I_tt(st.t[:, half * 512:(half + 1) * 512], pb.t[:, :], gbc.t[:, half * 512:(half + 1) * 512], ALU.mult),
                              r=[pb, gbc], w=[st])
                    else:
                        ph.op("act", I_act(st.t[:, half * 512:(half + 1) * 512], pb.t[:, :], AF.Copy), r=[pb], w=[st])
                nb += 1
                ph.dma(I_dma(modbc[l, s, kind], st.t[:]), r=[st])
    ph.finish()


def phase_proj(nc, A, l, seqs, modbc):
    ph = Phase(nc, f"P{l}")
    idf, idb = mk_ident(ph)
    wbf = ph.sb([128, 8, DIN], BF16, "wbf")
    win = A["w_in"][l].rearrange("(c p) n -> p c n", p=128)
    for c in range(8):
        for h0 in range(0, DIN, 1024):
            h1 = min(DIN, h0 + 1024)
            ph.dma(I_dma(wbf.t[:, c, h0:h1], win[:, c, h0:h1]), w=[wbf], q="pool")
    xt = [ph.sb([128, D], F32, f"xt{i}") for i in range(2)]
    junk = ph.sb([128, D], BF16, "junk")
    ss = [ph.sb([128, 1], F32, f"ss{i}") for i in range(2)]
    rstd = [ph.sb([128, 1], F32, f"rstd{i}") for i in range(2)]
    tmp = [ph.sb([128, D], F32, f"tmp{i}") for i in range(2)]
    hb = [ph.sb([128, D], BF16, f"hb{i}") for i in range(2)]
    hT = [ph.sb([128, 8, 512], BF16, f"hT{i}") for i in range(2)]
    gm_bc = ph.sb([128, D], F32, "gmbc")
    sh_bc = ph.sb([128, D], F32, "shbc")
    kvst = [ph.sb([128, 2 * 512], F32, f"kvst{i}") for i in range(2)]
    vst = [ph.sb([128, 8, 65], BF16, f"vst{i}") for i in range(2)]
    for v in vst:
        ph.op("pool", I_memset(v.t[:], 1.0), w=[v])
    kist = [ph.sb([128, 64], F32, f"kist{i}") for i in range(2)]
    wist = [ph.sb([128, 8], F32, f"wist{i}") for i in range(2)]
    fst = [ph.sb([128, 512], BF16, f"fst{i}") for i in range(4)]
    pT = ph.ps([128, 8, 128], BF16, "pT")
    pk = ph.ps([128, 512], F32, "pk")
    pv = ph.ps([128, 512], F32, "pv")
    pki = ph.ps([128, 72], F32, "pki")
    pf = [ph.ps([128, 512], F32, f"pf{i}") for i in range(3)]
    WISC = float(8 ** -0.5 * 64 ** -0.5)

    nt = 0
    nf = 0
    for s in seqs:
        ph.dma(I_dma(gm_bc.t[:], modbc[l, s.i, 1]), w=[gm_bc])
        ph.dma(I_dma(sh_bc.t[:], modbc[l, s.i, 0]), w=[sh_bc])
        xsrc = s.xin if l == 0 else s.xres
        TP = min(128, s.T)
        NW = min(512, s.T)
        for st in range(s.T // NW):
            ht = hT[st % 2]
            for j in range(NW // TP):
                r0 = st * NW + j * TP
                x = xt[nt % 2]; sq = ss[nt % 2]; rs = rstd[nt % 2]; tm = tmp[nt % 2]; h = hb[nt % 2]
                kv = kvst[nt % 2]; vs_ = vst[nt % 2]; kis_ = kist[nt % 2]; wis_ = wist[nt % 2]
                nt += 1
                ph.dma(I_dma(x.t[:TP, :], xsrc[r0:r0 + TP, :]), w=[x])
                ph.op("act", I_act(junk.t[:TP, :], x.t[:TP, :], AF.Square, accum_out=sq.t[:TP, 0:1]), r=[x], w=[junk, sq])
                ph.op("dve", I_ts(sq.t[:TP, :], sq.t[:TP, :], 1.0 / D, EPS, op0=ALU.mult, op1=ALU.add), r=[sq], w=[sq])
                ph.op("act", I_act(sq.t[:TP, :], sq.t[:TP, :], AF.Sqrt), r=[sq], w=[sq])
                ph.op("dve", I_recip(rs.t[:TP, :], sq.t[:TP, :]), r=[sq], w=[rs])
                ph.op("dve", I_stt(tm.t[:TP, :], x.t[:TP, :], rs.t[:TP, 0:1], gm_bc.t[:TP, :], ALU.mult, ALU.mult),
                      r=[x, rs, gm_bc], w=[tm])
                ph.op("pool", I_tt(h.t[:TP, :], tm.t[:TP, :], sh_bc.t[:TP, :], ALU.add), r=[tm, sh_bc], w=[h])
                for c in range(8):
                    ph.op("pe", I_tr(pT.t[:, c, :TP], h.t[:TP, c * 128:(c + 1) * 128], idb.t[:TP, :TP]), r=[h, idb], w=[pT])
                ph.op("act", I_act(ht.t[:, :, j * TP:(j + 1) * TP], pT.t[:, :, :TP], AF.Copy), r=[pT], w=[ht])
                for c in range(8):
                    ph.op("pe", I_mm(pk.t[:TP, :], ht.t[:, c, j * TP:(j + 1) * TP], wbf.t[:, c, OFF_K:OFF_K + 512],
                                     start=(c == 0), stop=(c == 7)), r=[ht, wbf], w=[pk])
                for c in range(8):
                    ph.op("pe", I_mm(pv.t[:TP, :], ht.t[:, c, j * TP:(j + 1) * TP], wbf.t[:, c, OFF_V:OFF_V + 512],
                                     start=(c == 0), stop=(c == 7)), r=[ht, wbf], w=[pv])
                for c in range(8):
                    ph.op("pe", I_mm(pki.t[:TP, :], ht.t[:, c, j * TP:(j + 1) * TP], wbf.t[:, c, OFF_KI:OFF_KI + 72],
                                     start=(c == 0), stop=(c == 7)), r=[ht, wbf], w=[pki])
                ph.op("act", I_act(kv.t[:TP, 0:512], pk.t[:TP, :], AF.Copy), r=[pk], w=[kv])
                ph.op("dve", I_copy(kv.t[:TP, 512:1024], pv.t[:TP, :]), r=[pv], w=[kv])
                ph.op("dve", I_copy(vs_.t[:TP, :, 0:64], pv.t[:TP, :].rearrange("p (h d) -> p h d", d=64)), r=[pv], w=[vs_])
                ph.op("act", I_act(kis_.t[:TP, :], pki.t[:TP, 0:64], AF.Copy), r=[pki], w=[kis_])
                ph.op("dve", I_ts(wis_.t[:TP, :], pki.t[:TP, 64:72], WISC, None, op0=ALU.mult), r=[pki], w=[wis_])
                ph.dma(I_dma(s.kout[l][r0:r0 + TP, :], kv.t[:TP, 0:512]), r=[kv])
                ph.dma(I_dma(s.vout[l][r0:r0 + TP, :], kv.t[:TP, 512:1024]), r=[kv])
                ph.dma(I_dma(s.vaug[s.koff + r0:s.koff + r0 + TP, :], vs_.t[:TP, :, :].rearrange("p h d -> p (h d)")), r=[vs_])
                ph.dma(I_dma(s.kiout[l][r0:r0 + TP, :], kis_.t[:TP, :]), r=[kis_])
                ph.dma(I_dma(s.wiS[r0:r0 + TP, :], wis_.t[:TP, :]), r=[wis_])
            t0 = st * NW
            fm = []
            for i in range(4):
                fm.append((OFF_U + i * 128, 128, "copy", s.uT[i * 128:(i + 1) * 128, t0:t0 + NW]))
                fm.append((OFF_Q + i * 128, 128, "q", s.qT[i * 128:(i + 1) * 128, t0:t0 + NW]))
                fm.append((OFF_K + i * 128, 128, "copy", s.kT[i * 128:(i + 1) * 128, s.koff + t0:s.koff + t0 + NW]))
                fm.append((OFF_QI + i * 128, 128, "copy", s.qiT[i * 128:(i + 1) * 128, t0:t0 + NW]))
            fm.append((OFF_KI, 64, "copy", s.kiT[0:64, s.koff + t0:s.koff + t0 + NW]))
            for i in range(4):
                fm.append((OFF_ZS + i * 128, 128, "silu", s.zsT[i * 128:(i + 1) * 128, t0:t0 + NW]))
            for i in range(4):
                fm.append((OFF_ZA + i * 128, 128, "silu", s.zaT[i * 128:(i + 1) * 128, t0:t0 + NW]))
            for i in range(16):
                fm.append((OFF_GM + i * 128, 128, "sig", s.gmT[i * 128:(i + 1) * 128, t0:t0 + NW]))
            for (off, M, kind, dst) in fm:
                p = pf[nf % 3]; f = fst[nf % 4]; nf += 1
                for c in range(8):
                    ph.op("pe", I_mm(p.t[:M, :NW], wbf.t[:, c, off:off + M], ht.t[:, c, :NW], start=(c == 0), stop=(c == 7)),
                          r=[wbf, ht], w=[p])
                if kind == "copy":
                    ph.op("dve", I_copy(f.t[:M, :NW], p.t[:M, :NW]), r=[p], w=[f])
                elif kind == "q":
                    ph.op("dve", I_ts(f.t[:M, :NW], p.t[:M, :NW], 0.125, None, op0=ALU.mult), r=[p], w=[f])
                elif kind == "silu":
                    ph.op("act", I_act(f.t[:M, :NW], p.t[:M, :NW], AF.Silu), r=[p], w=[f])
                else:
                    ph.op("act", I_act(f.t[:M, :NW], p.t[:M, :NW], AF.Sigmoid), r=[p], w=[f])
                ph.dma(I_dma(dst, f.t[:M, :NW]), r=[f])
    ph.finish()


def make_in_maps(inp, n_cores=8):
    f = lambda a: np.ascontiguousarray(np.asarray(a, dtype=np.float32))
    shared = {
        "w_mod": f(inp["w_mod"]), "b_mod": f(inp["b_mod"]), "g_norm": f(inp["g_norm"]), "w_in": f(inp["w_in"]),
        "a_re": f(inp["a_re"]).reshape(NL, 2048), "a_im": f(inp["a_im"]).reshape(NL, 2048),
        "log_dt": f(inp["log_dt"]),
        "b_re": f(inp["b_re"]).reshape(NL, 2048, 16), "b_im": f(inp["b_im"]).reshape(NL, 2048, 16),
        "c_reT": f(np.transpose(np.asarray(inp["c_re"]), (0, 1, 3, 2))).reshape(NL, 2048, 16),
        "c_imT": f(np.transpose(np.asarray(inp["c_im"]), (0, 1, 3, 2))).reshape(NL, 2048, 16),
        "d_skip": f(inp["d_skip"]).reshape(NL, 512),
        "w_glu": f(inp["w_glu"]), "b_glu": f(inp["b_glu"]), "w_ps": f(inp["w_ps"]), "w_pa": f(inp["w_pa"]),
        "w_o": f(inp["w_o"]), "g_final": f(inp["g_final"]).reshape(1, D),
    }
    xp = np.asarray(inp["x_prompt"]); xs = np.asarray(inp["x_sample"])
    ck = np.asarray(inp["cache_k"]); cv = np.asarray(inp["cache_v"]); cki = np.asarray(inp["cache_kidx"])
    sr = np.asarray(inp["state_ssm_re"]); si = np.asarray(inp["state_ssm_im"])
    cp = np.asarray(inp["c_prompt"]); cs = np.asarray(inp["c_sample"])
    PAST = ck.shape[2]
    maps = []
    for b in range(n_cores):
        m = dict(shared)
        m["xp"] = f(xp[b]); m["xs"] = f(xs[2 * b:2 * b + 2])
        m["ck"] = f(ck[:, 2 * b:2 * b + 2]).reshape(NL, 2, PAST, 512)
        m["cv"] = f(cv[:, 2 * b:2 * b + 2]).reshape(NL, 2, PAST, 512)
        m["cki"] = f(cki[:, 2 * b:2 * b + 2])
        m["ssr"] = f(sr[:, 2 * b:2 * b + 2]).reshape(NL, 2, 2048)
        m["ssi"] = f(si[:, 2 * b:2 * b + 2]).reshape(NL, 2, 2048)
        m["call"] = f(np.concatenate([cp[b:b + 1], cs[2 * b:2 * b + 2]], axis=0))
        maps.append(m)
    return maps


def assemble(results, T, TS, n_cores=8):
    R = results
    cat = lambda k: np.stack([np.asarray(r[k]) for r in R], axis=0)
    yp = cat("yp")
    ys = cat("ys").reshape(2 * n_cores, TS, D)
    kp = np.transpose(cat("kp"), (1, 0, 2, 3)).reshape(NL, n_cores, T, 8, 64)
    vp = np.transpose(cat("vp"), (1, 0, 2, 3)).reshape(NL, n_cores, T, 8, 64)
    kip = np.transpose(cat("kip"), (1, 0, 2, 3))
    srp = np.transpose(cat("srp"), (1, 0, 2)).reshape(NL, n_cores, 32, 64)
    sip = np.transpose(cat("sip"), (1, 0, 2)).reshape(NL, n_cores, 32, 64)
    ks = np.transpose(cat("ks"), (1, 0, 2, 3, 4)).reshape(NL, 2 * n_cores, TS, 8, 64)
    vs = np.transpose(cat("vs"), (1, 0, 2, 3, 4)).reshape(NL, 2 * n_cores, TS, 8, 64)
    kis = np.transpose(cat("kis"), (1, 0, 2, 3, 4)).reshape(NL, 2 * n_cores, TS, 64)
    srs = np.transpose(cat("srs"), (1, 0, 2, 3)).reshape(NL, 2 * n_cores, 32, 64)
    sis = np.transpose(cat("sis"), (1, 0, 2, 3)).reshape(NL, 2 * n_cores, 32, 64)
    outs = (yp, ys, kp, vp, kip, srp, sip, ks, vs, kis, srs, sis)
    return tuple(np.ascontiguousarray(o, dtype=np.float32) for o in outs)


_NC_CACHE = {}


def kernel(**inputs):
    T = int(np.asarray(inputs["x_prompt"]).shape[1])
    TS = int(np.asarray(inputs["x_sample"]).shape[1])
    PAST = int(np.asarray(inputs["cache_k"]).shape[2])
    n_cores = int(np.asarray(inputs["x_prompt"]).shape[0])
    key = (T, TS, PAST)
    nc = build(T=T, TS=TS, PAST=PAST)
    maps = make_in_maps(inputs, n_cores)
    res = run_bass_kernel_spmd(nc, maps, core_ids=list(range(n_cores)))
    return assemble(res.results, T, TS, n_cores)


def phase_cache(nc, A, l, seqs, PAST):
    ph = Phase(nc, f"C{l}")
    idf, idb = mk_ident(ph)
    ckt = [ph.sb([128, 512], F32, f"ckt{i}") for i in range(2)]
    cvt = [ph.sb([128, 512], F32, f"cvt{i}") for i in range(2)]
    cit = [ph.sb([128, 64], F32, f"cit{i}") for i in range(2)]
    ckb = [ph.sb([128, 512], BF16, f"ckb{i}") for i in range(2)]
    cib = [ph.sb([128, 64], BF16, f"cib{i}") for i in range(2)]
    kst = [ph.sb([128, 4, 128], BF16, f"kst{i}") for i in range(2)]
    kis = [ph.sb([64, 128], BF16, f"kis{i}") for i in range(2)]
    vst = [ph.sb([128, 8, 65], BF16, f"vst{i}") for i in range(2)]
    for v in vst:
        ph.op("pool", I_memset(v.t[:], 1.0), w=[v])
    pT = [ph.ps([128, 4, 128], BF16, f"pT{i}") for i in range(2)]
    pI = [ph.ps([64, 128], BF16, f"pI{i}") for i in range(2)]
    n = 0
    for s in seqs[1:]:
        j = s.i - 1
        for kt in range(PAST // 128):
            i = n % 2; n += 1
            r0 = kt * 128
            ph.dma(I_dma(ckt[i].t[:], A["ck"][l, j, r0:r0 + 128, :]), w=[ckt[i]])
            ph.dma(I_dma(cvt[i].t[:], A["cv"][l, j, r0:r0 + 128, :]), w=[cvt[i]])
            ph.dma(I_dma(cit[i].t[:], A["cki"][l, j, r0:r0 + 128, :]), w=[cit[i]])
            ph.op("act", I_act(ckb[i].t[:], ckt[i].t[:], AF.Copy), r=[ckt[i]], w=[ckb[i]])
            ph.op("act", I_act(cib[i].t[:], cit[i].t[:], AF.Copy), r=[cit[i]], w=[cib[i]])
            for hp in range(4):
                ph.op("pe", I_tr(pT[i].t[:, hp, :], ckb[i].t[:, hp * 128:(hp + 1) * 128], idb.t[:, :]), r=[ckb[i], idb], w=[pT[i]])
            ph.op("pe", I_tr(pI[i].t[:, :], cib[i].t[:, :], idb.t[:, :]), r=[cib[i], idb], w=[pI[i]])
            ph.op("dve", I_copy(kst[i].t[:], pT[i].t[:]), r=[pT[i]], w=[kst[i]])
            ph.op("dve", I_copy(kis[i].t[:], pI[i].t[:]), r=[pI[i]], w=[kis[i]])
            ph.op("pool", I_copy(vst[i].t[:, :, 0:64], cvt[i].t[:, :].rearrange("p (h d) -> p h d", d=64)), r=[cvt[i]], w=[vst[i]])
            ph.dma(I_dma(s.kT.rearrange("(hp p) s -> p hp s", p=128)[:, :, r0:r0 + 128], kst[i].t[:]), r=[kst[i]])
            ph.dma(I_dma(s.kiT[:, r0:r0 + 128], kis[i].t[:]), r=[kis[i]])
            ph.dma(I_dma(s.vaug[r0:r0 + 128, :], vst[i].t[:].rearrange("p h d -> p (h d)")), r=[vst[i]])
    ph.finish()


def phase_merge(nc, A, l, seqs, modbc):
    ph = Phase(nc, f"G{l}")
    last = (l == NL - 1)
    wps = ph.sb([128, 4, D], BF16, "wps"); wpa = ph.sb([128, 4, D], BF16, "wpa"); wo = ph.sb([128, 8, D], BF16, "wo")
    for c in range(4):
        ph.dma(I_dma(wps.t[:, c, :], A["w_ps"][l, c * 128:(c + 1) * 128, :]), w=[wps], q="pool")
        ph.dma(I_dma(wpa.t[:, c, :], A["w_pa"][l, c * 128:(c + 1) * 128, :]), w=[wpa], q="pool")
    for c in range(8):
        ph.dma(I_dma(wo.t[:, c, :], A["w_o"][l, c * 128:(c + 1) * 128, :]), w=[wo], q="pool")
    gate = ph.sb([128, D], F32, "gate")
    gfin = ph.sb([128, D], F32, "gfin")
    if last:
        ph.dma(I_dma(gfin.t[:], bcast_rows(A["g_final"][0:1, :], 128)), w=[gfin])
    yst = [ph.sb([128, 4, 512], BF16, f"yst{i}") for i in range(2)]
    yat = [ph.sb([128, 4, 512], BF16, f"yat{i}") for i in range(2)]
    gmt = [ph.sb([128, 16, 512], BF16, f"gmt{i}") for i in range(2)]
    mg = [ph.sb([128, 8, 512], BF16, f"mg{i}") for i in range(2)]
    tA = [ph.sb([128, 512], F32, f"tA{i}") for i in range(2)]
    tB = [ph.sb([128, 512], F32, f"tB{i}") for i in range(2)]
    xt = [ph.sb([128, D], F32, f"xt{i}") for i in range(2)]
    xn = [ph.sb([128, D], F32, f"xn{i}") for i in range(2)]
    yo = [ph.sb([128, D], F32, f"yo{i}") for i in range(2)]
    junk = ph.sb([128, D], BF16, "junk")
    ss = [ph.sb([128, 1], F32, f"ss{i}") for i in range(2)]
    rs = [ph.sb([128, 1], F32, f"rs{i}") for i in range(2)]
    pA = [ph.ps([128, 512], F32, f"pA{i}") for i in range(2)]
    pB = [ph.ps([128, 512], F32, f"pB{i}") for i in range(2)]
    pO = [ph.ps([128, 512], F32, f"pO{i}") for i in range(2)]
    nn = 0
    nt = 0
    for s in seqs:
        ph.dma(I_dma(gate.t[:], modbc[l, s.i, 2]), w=[gate])
        xsrc = s.xin if l == 0 else s.xres
        TP = min(128, s.T); NW = min(512, s.T)
        for st in range(s.T // NW):
            t0 = st * NW
            ys_ = yst[st % 2]; ya_ = yat[st % 2]; gm_ = gmt[st % 2]; m_ = mg[st % 2]
            ph.dma(I_dma(ys_.t[:, :, :NW], s.ysT.rearrange("(c p) t -> p c t", p=128)[:, :, t0:t0 + NW]), w=[ys_])
            ph.dma(I_dma(ya_.t[:, :, :NW], s.yaT.rearrange("(c p) t -> p c t", p=128)[:, :, t0:t0 + NW]), w=[ya_])
            ph.dma(I_dma(gm_.t[:, :, :NW], s.gmT.rearrange("(c p) t -> p c t", p=128)[:, :, t0:t0 + NW]), w=[gm_])
            for ct in range(8):
                a = pA[nn % 2]; b = pB[nn % 2]; ta = tA[nn % 2]; tb = tB[nn % 2]; nn += 1
                for c in range(4):
                    ph.op("pe", I_mm(a.t[:, :NW], wps.t[:, c, ct * 128:(ct + 1) * 128], ys_.t[:, c, :NW], start=(c == 0), stop=(c == 3)),
                          r=[wps, ys_], w=[a])
                for c in range(4):
                    ph.op("pe", I_mm(b.t[:, :NW], wpa.t[:, c, ct * 128:(ct + 1) * 128], ya_.t[:, c, :NW], start=(c == 0), stop=(c == 3)),
                          r=[wpa, ya_], w=[b])
                ph.op("dve", I_tt(ta.t[:, :NW], a.t[:, :NW], gm_.t[:, ct, :NW], ALU.mult), r=[a, gm_], w=[ta])
                ph.op("dve", I_tt(tb.t[:, :NW], b.t[:, :NW], gm_.t[:, 8 + ct, :NW], ALU.mult), r=[b, gm_], w=[tb])
                ph.op("pool", I_tt(m_.t[:, ct, :NW], ta.t[:, :NW], tb.t[:, :NW], ALU.add), r=[ta, tb], w=[m_])
            for j in range(NW // TP):
                r0 = t0 + j * TP
                x = xt[nt % 2]; xo = xn[nt % 2]; y_ = yo[nt % 2]; sq = ss[nt % 2]; r_ = rs[nt % 2]
                ph.dma(I_dma(x.t[:TP, :], xsrc[r0:r0 + TP, :]), w=[x])
                for half in range(2):
                    po = pO[half]
                    hs = slice(half * 512, (half + 1) * 512)
                    for c in range(8):
                        ph.op("pe", I_mm(po.t[:TP, :], m_.t[:, c, j * TP:(j + 1) * TP], wo.t[:, c, hs], start=(c == 0), stop=(c == 7)),
                              r=[m_, wo], w=[po])
                    ph.op("dve", I_tt(xo.t[:TP, hs], po.t[:TP, :], gate.t[:TP, hs], ALU.mult), r=[po, gate], w=[xo])
                ph.op("pool", I_tt(xo.t[:TP, :], xo.t[:TP, :], x.t[:TP, :], ALU.add), r=[xo, x], w=[xo])
                nt += 1
                if not last:
                    ph.dma(I_dma(s.xres[r0:r0 + TP, :], xo.t[:TP, :]), r=[xo])
                else:
                    ph.op("act", I_act(junk.t[:TP, :], xo.t[:TP, :], AF.Square, accum_out=sq.t[:TP, 0:1]), r=[xo], w=[junk, sq])
                    ph.op("dve", I_ts(sq.t[:TP, :], sq.t[:TP, :], 1.0 / D, EPS, op0=ALU.mult, op1=ALU.add), r=[sq], w=[sq])
                    ph.op("act", I_act(sq.t[:TP, :], sq.t[:TP, :], AF.Sqrt), r=[sq], w=[sq])
                    ph.op("dve", I_recip(r_.t[:TP, :], sq.t[:TP, :]), r=[sq], w=[r_])
                    ph.op("dve", I_stt(y_.t[:TP, :], xo.t[:TP, :], r_.t[:TP, 0:1], gfin.t[:TP, :], ALU.mult, ALU.mult),
                          r=[xo, r_, gfin], w=[y_])
                    ph.dma(I_dma(s.yout[r0:r0 + TP, :], y_.t[:TP, :]), r=[y_])
    ph.finish()


NIT = 22
W0 = 64.0


def phase_attn(nc, A, l, s):
    ph = Phase(nc, f"A{l}{s.nm}")
    idf, idb = mk_ident(ph)
    T, S = s.T, s.S
    QP = min(128, T); QB = min(512, T)
    L = T if s.causal else S
    KSEL = float(min(TOPK, L // 4))
    ktiles = [(k0, min(128, S - k0)) for k0 in range(0, S, 128)]
    NKT = len(ktiles)
    kiTa = ph.sb([128, S], BF16, "kiTa"); kiTb = ph.sb([128, S], BF16, "kiTb")
    ph.op("pool", I_memset(kiTa.t[64:128, :], 0.0), w=[kiTa])
    ph.op("pool", I_memset(kiTb.t[0:64, :], 0.0), w=[kiTb])
    ph.dma(I_dma(kiTa.t[0:64, :], s.kiT[:, :]), w=[kiTa])
    ph.dma(I_dma(kiTb.t[64:128, :], s.kiT[:, :]), w=[kiTb])
    kiTz = (kiTa, kiTb)
    sc = ph.sb([128, S], F32, "sc")
    mk = ph.sb([128, S], BF16, "mk")
    maskT = ph.sb([128, NKT, QB], BF16, "maskT")
    qit = [ph.sb([128, 4, 128], BF16, f"qit{i}") for i in range(2)]
    wit = [ph.sb([128, 8], F32, f"wit{i}") for i in range(2)]
    dg = [ph.sb([128, 8, 128], BF16, f"dg{i}") for i in range(2)]
    rl = [ph.sb([128, 512], BF16, f"rl{i}") for i in range(2)]
    mid = ph.sb([128, 1], F32, "mid"); cnt = ph.sb([128, 1], F32, "cnt"); tq = ph.sb([128, 1], F32, "tq")
    thr = ph.sb([128, 1], F32, "thr")
    negmid = ph.sb([128, 1], F32, "negmid"); sact = ph.sb([128, 1], F32, "sact"); comb = ph.sb([128, 1], F32, "comb")
    junk2 = ph.sb([128, S], BF16, "junk2")
    qta = ph.sb([128, 4, 512], BF16, "qta"); qtb = ph.sb([128, 4, 512], BF16, "qtb")
    ph.op("pool", I_memset(qta.t[64:128, :, :], 0.0), w=[qta])
    ph.op("pool", I_memset(qtb.t[0:64, :, :], 0.0), w=[qtb])
    qtz = (qta, qtb)
    va = [ph.sb([128, 260], BF16, f"va{i}") for i in range(2)]
    ktl = [ph.sb([128, 2, 128], BF16, f"ktl{i}") for i in range(2)]
    pe_ = [ph.sb([128, 512], BF16, f"pe{i}") for i in range(2)]
    pm = [ph.sb([128, 512], BF16, f"pm{i}") for i in range(2)]
    oa = [ph.sb([128, 512], F32, f"oa{i}") for i in range(2)]
    for o__ in oa:
        ph.op("pool", I_memset(o__.t[:], 0.0), w=[o__])
    rden = [ph.sb([64, 512], F32, f"rden{i}") for i in range(2)]
    t1 = [ph.sb([64, 512], F32, f"t1{i}") for i in range(2)]
    zat = [ph.sb([64, 512], BF16, f"zat{i}") for i in range(2)]
    yat = [ph.sb([64, 512], BF16, f"yat{i}") for i in range(2)]
    iop = ph.sb([128, 64], F32, "iop")
    selden = ph.sb([128, 64], F32, "selden")
    ph.op("pool", I_iota(iop.t[:], [[0, 64]], 0, 1), w=[iop])
    ph.op("dve", I_ts(selden.t[:], iop.t[:], 64.0, None, op0=ALU.is_equal), r=[iop], w=[selden])
    lg = [ph.ps([128, 512], F32, f"lg{i}") for i in range(2)]
    acc = ph.ps([128, 512], F32, "acc")
    mtp = ph.ps([128, 8, 128], BF16, "mtp")
    oacc = [ph.ps([128, 512], F32, f"oacc{i}") for i in range(4)]

    nq = 0
    nr = 0
    nev = 0
    nv = 0
    nh = 0
    for jb in range(T // QB):
        qb0 = jb * QB
        Sblk = min(S, (jb + 1) * QB) if s.causal else S
        kts = [(i, k0, ksz) for i, (k0, ksz) in enumerate(ktiles) if k0 < Sblk]
        for jq in range(QB // QP):
            q0 = qb0 + jq * QP
            qoff = jq * QP
            Slim = (q0 + QP) if s.causal else S
            qi_ = qit[nq % 2]; wi_ = wit[nq % 2]; dg_ = dg[nq % 2]; nq += 1
            ph.dma(I_dma(qi_.t[:, :, :QP], s.qiT.rearrange("(hp p) t -> p hp t", p=128)[:, :, q0:q0 + QP]), w=[qi_])
            ph.dma(I_dma(wi_.t[:QP, :], s.wiS[q0:q0 + QP, :]), w=[wi_])
            for h in range(8):
                ph.op("dve", I_ts(dg_.t[:QP, h, :QP], idf.t[:QP, :QP], wi_.t[:QP, h:h + 1], None, op0=ALU.mult),
                      r=[idf, wi_], w=[dg_])
            for c0 in range(0, Slim, 512):
                csz = min(512, Slim - c0)
                bufs_ = []
                for h in range(8):
                    bufs_.append((lg[nr % 2], rl[nr % 2])); nr += 1

                def emit_lg1(h, c0=c0, csz=csz, bufs_=bufs_, qi_=qi_):
                    g, r_ = bufs_[h]
                    ph.op("pe", I_mm(g.t[:QP, :csz], qi_.t[:, h // 2, :QP], kiTz[h % 2].t[:, c0:c0 + csz]),
                          r=[qi_, kiTz[h % 2]], w=[g])
                emit_lg1(0)
                for h in range(8):
                    g, r_ = bufs_[h]
                    if h < 7:
                        emit_lg1(h + 1)
                    ph.op("act", I_act(r_.t[:QP, :csz], g.t[:QP, :csz], AF.Relu), r=[g], w=[r_])
                    ph.op("pe", I_mm(acc.t[:QP, :csz], dg_.t[:QP, h, :QP], r_.t[:QP, :csz], start=(h == 0), stop=(h == 7)),
                          r=[dg_, r_], w=[acc])
                ph.op("dve", I_copy(sc.t[:QP, c0:c0 + csz], acc.t[:QP, :csz]), r=[acc], w=[sc])
            if s.causal and QP == 128:
                ph.op("dve", I_memset(sc.t[0:64, Slim - 64:Slim], NEG), w=[sc])
            ph.op("dve", I_memset(mid.t[:QP, :], 0.0), w=[mid])
            ph.op("dve", I_memset(negmid.t[:QP, :], 0.0), w=[negmid])
            S1 = Slim
            if Slim >= 1024:
                S1 = int(Slim * 0.45) // 128 * 128
            w = W02
            for k in range(NIT2):
                w = w / 2.0
                ph.op("dve", I_ts(mk.t[:QP, :S1], sc.t[:QP, :S1], mid.t[:QP, 0:1], None, op0=ALU.is_ge, op1=ALU.add,
                                  accum_out=cnt.t[:QP, 0:1]), r=[sc, mid], w=[mk, cnt])
                if S1 < Slim:
                    ph.op("act", I_act(junk2.t[:QP, S1:Slim], sc.t[:QP, S1:Slim], AF.Sign, bias=negmid.t[:QP, 0:1],
                                       accum_out=sact.t[:QP, 0:1]), r=[sc, negmid], w=[junk2, sact])
                    ph.op("dve", I_stt(comb.t[:QP, :], sact.t[:QP, :], 0.5, cnt.t[:QP, :], ALU.mult, ALU.add), r=[sact, cnt], w=[comb])
                    src_, kadj = comb, KSEL - 0.5 * (Slim - S1)
                else:
                    src_, kadj = cnt, KSEL
                ph.op("dve", I_ts(tq.t[:QP, :], src_.t[:QP, :], kadj, 2.0 * w, op0=ALU.is_ge, op1=ALU.mult), r=[src_], w=[tq])
                ph.op("dve", I_stt(mid.t[:QP, :], mid.t[:QP, :], -w, tq.t[:QP, :], ALU.add, ALU.add), r=[mid, tq], w=[mid])
                if S1 < Slim:
                    ph.op("dve", I_stt(negmid.t[:QP, :], negmid.t[:QP, :], w, tq.t[:QP, :], ALU.add, ALU.subtract), r=[negmid, tq], w=[negmid])
            ph.op("dve", I_ts(thr.t[:QP, :], mid.t[:QP, :], -w, None, op0=ALU.add), r=[mid], w=[thr])
            ph.op("dve", I_ts(mk.t[:QP, :Slim], sc.t[:QP, :Slim], thr.t[:QP, 0:1], None, op0=ALU.is_ge), r=[sc, thr], w=[mk])
            mine = [(i, k0, ksz) for (i, k0, ksz) in kts if k0 < Slim]
            for g0 in range(0, len(mine), 8):
                grp = mine[g0:g0 + 8]
                for gi, (i, k0, ksz) in enumerate(grp):
                    ph.op("pe", I_tr(mtp.t[:ksz, gi, :QP], mk.t[:QP, k0:k0 + ksz], idb.t[:QP, :QP]), r=[mk, idb], w=[mtp])
                i0 = grp[0][0]
                ng = len(grp)
                eng = "act" if nev % 2 == 0 else "dve"; nev += 1
                if eng == "act":
                    ph.op("act", I_act(maskT.t[:, i0:i0 + ng, qoff:qoff + QP], mtp.t[:, 0:ng, :QP], AF.Copy), r=[mtp], w=[maskT])
                else:
                    ph.op("dve", I_copy(maskT.t[:, i0:i0 + ng, qoff:qoff + QP], mtp.t[:, 0:ng, :QP]), r=[mtp], w=[maskT])
            rest = [(i, k0, ksz) for (i, k0, ksz) in kts if k0 >= Slim]
            if rest:
                i0 = rest[0][0]; i1 = rest[-1][0] + 1
                ph.op("pool", I_memset(maskT.t[:, i0:i1, qoff:qoff + QP], 0.0), w=[maskT])
        import os as _os
        if _os.environ.get("ATTN_STOP") == "idx":
            continue
        qsrc = s.qT.rearrange("(hp p) t -> p hp t", p=128)
        ph.dma(I_dma(qta.t[0:64, :, :QB], qsrc[0:64, :, qb0:qb0 + QB]), w=[qta])
        ph.dma(I_dma(qtb.t[64:128, :, :QB], qsrc[64:128, :, qb0:qb0 + QB]), w=[qtb])
        for hg in range(2):
            units = [(idx, hh) for idx in range(len(kts)) for hh in range(4)]
            ub = {}

            def emit_st1(u, hg=hg, units=units, ub=ub):
                nonlocal nv
                idx, hh = units[u]
                i, k0, ksz = kts[idx]
                if hh == 0:
                    va_ = va[nv % 2]; kt_ = ktl[nv % 2]; nv += 1
                    ph.dma(I_dma(va_.t[:ksz, :], s.vaug[k0:k0 + ksz, hg * 260:(hg + 1) * 260]), w=[va_])
                    ph.dma(I_dma(kt_.t[:, :, :ksz], s.kT.rearrange("(hp p) s -> p hp s", p=128)[:, 2 * hg:2 * hg + 2, k0:k0 + ksz]), w=[kt_])
                    ub[idx] = (va_, kt_)
                va_, kt_ = ub[idx]
                h = hg * 4 + hh
                st_ = lg[u % 2]
                ph.op("pe", I_mm(st_.t[:ksz, :QB], kt_.t[:, hh // 2, :ksz], qtz[h % 2].t[:, h // 2, :QB]),
                      r=[kt_, qtz[h % 2]], w=[st_])
            emit_st1(0)
            for u, (idx, hh) in enumerate(units):
                i, k0, ksz = kts[idx]
                va_, kt_ = ub[idx]
                st_ = lg[u % 2]; p_ = pe_[u % 2]; m_ = pm[u % 2]
                if u + 1 < len(units):
                    emit_st1(u + 1)
                ph.op("act", I_act(p_.t[:ksz, :QB], st_.t[:ksz, :QB], AF.Exp), r=[st_], w=[p_])
                ph.op("dve", I_tt(m_.t[:ksz, :QB], p_.t[:ksz, :QB], maskT.t[:ksz, i, :QB], ALU.mult), r=[p_, maskT], w=[m_])
                ph.op("pe", I_mm(oacc[hh].t[:65, :QB], va_.t[:ksz, hh * 65:(hh + 1) * 65], m_.t[:ksz, :QB], start=(idx == 0), stop=(idx == len(kts) - 1)),
                      r=[va_, m_], w=[oacc[hh]])
            if _os.environ.get("ATTN_STOP") in ("pv", "mm", "exp", "qk", "dma"):
                continue
            for hh in range(4):
                h = hg * 4 + hh
                o_ = oa[hh % 2]; rd = rden[hh % 2]; t_ = t1[hh % 2]; z_ = zat[hh % 2]; y_ = yat[hh % 2]
                ph.op("act", I_act(o_.t[:65, :QB], oacc[hh].t[:65, :QB], AF.Copy), r=[oacc[hh]], w=[o_])
                ph.op("pe", I_mm(acc.t[:64, :QB], selden.t[:, :64], o_.t[:, :QB]), r=[selden, o_], w=[acc])
                ph.op("dve", I_recip(rd.t[:64, :QB], acc.t[:64, :QB]), r=[acc], w=[rd])
                ph.dma(I_dma(z_.t[:64, :QB], s.zaT[64 * h:64 * h + 64, qb0:qb0 + QB]), w=[z_])
                ph.op("dve", I_tt(t_.t[:64, :QB], o_.t[0:64, :QB], rd.t[:64, :QB], ALU.mult), r=[o_, rd], w=[t_])
                ph.op("pool", I_tt(y_.t[:64, :QB], t_.t[:64, :QB], z_.t[:64, :QB], ALU.mult), r=[t_, z_], w=[y_])
                ph.dma(I_dma(s.yaT[64 * h:64 * h + 64, qb0:qb0 + QB], y_.t[:64, :QB]), r=[y_])
    ph.finish()


PI = float(np.pi)


def phase_ssm(nc, A, l, seqs):
    ph = Phase(nc, f"S{l}")
    idf, idb = mk_ident(ph)
    LB = min(512, seqs[0].T)
    sm = lambda nm: ph.sb([128, 16], F32, nm)
    are, aim, ldt, dt, xr, ang, mag = sm("are"), sm("aim"), sm("ldt"), sm("dt"), sm("xr"), sm("ang"), sm("mag")
    kac, angr, angc, angcr, sn1, cs1, abr, abi = sm("kac"), sm("angr"), sm("angc"), sm("angcr"), sm("sn1"), sm("cs1"), sm("abr"), sm("abi")
    t1, t2, den, rdn, nr_, fre, fim, t3, t4 = sm("t1"), sm("t2"), sm("den"), sm("rdn"), sm("nr"), sm("fre"), sm("fim"), sm("t3"), sm("t4")
    wr, wi_, wt = sm("wr"), sm("wi"), sm("wt")
    ph.dma(I_dma(are.t[:], A["a_re"][l].rearrange("(pt q) -> q pt", q=128)), w=[are])
    ph.dma(I_dma(aim.t[:], A["a_im"][l].rearrange("(pt q) -> q pt", q=128)), w=[aim])
    for gl in range(2):
        src = bass.AP(A["log_dt"].tensor, l * 32 + gl, [[0, 64], [2, 16]])
        ph.dma(I_dma(ldt.t[gl * 64:(gl + 1) * 64, :], src), w=[ldt])
    V = "dve"
    pa, pb2, pc = sm("pa"), sm("pb2"), sm("pc")

    def horner(dst, y, coefs):
        ph.op(V, I_memset(dst.t[:], 1.0), w=[dst])
        for c in reversed(coefs):
            ph.op(V, I_tt(dst.t[:], dst.t[:], y.t[:], ALU.mult), r=[dst, y], w=[dst])
            ph.op(V, I_ts(dst.t[:], dst.t[:], float(c), 1.0, op0=ALU.mult, op1=ALU.add), r=[dst], w=[dst])

    def exp_acc(dst, src, nsq, deg):
        ph.op(V, I_ts(pa.t[:], src.t[:], 1.0 / (2 ** nsq), None, op0=ALU.mult), r=[src], w=[pa])
        horner(dst, pa, [1.0 / k for k in range(1, deg + 1)])
        for _ in range(nsq):
            ph.op(V, I_tt(dst.t[:], dst.t[:], dst.t[:], ALU.mult), r=[dst], w=[dst])

    def sincos_acc(sdst, cdst, x):
        ph.op(V, I_ts(pa.t[:], x.t[:], 0.125, None, op0=ALU.mult), r=[x], w=[pa])
        ph.op(V, I_tt(pb2.t[:], pa.t[:], pa.t[:], ALU.mult), r=[pa], w=[pb2])
        horner(sdst, pb2, [-1.0 / 6, -1.0 / 20, -1.0 / 42, -1.0 / 72, -1.0 / 110])
        ph.op(V, I_tt(sdst.t[:], sdst.t[:], pa.t[:], ALU.mult), r=[sdst, pa], w=[sdst])
        horner(cdst, pb2, [-1.0 / 2, -1.0 / 12, -1.0 / 30, -1.0 / 56, -1.0 / 90, -1.0 / 132])
        for _ in range(3):
            ph.op(V, I_tt(pc.t[:], sdst.t[:], sdst.t[:], ALU.mult), r=[sdst], w=[pc])
            ph.op(V, I_stt(sdst.t[:], sdst.t[:], 2.0, cdst.t[:], ALU.mult, ALU.mult), r=[sdst, cdst], w=[sdst])
            ph.op(V, I_ts(cdst.t[:], pc.t[:], -2.0, 1.0, op0=ALU.mult, op1=ALU.add), r=[pc], w=[cdst])

    exp_acc(dt, ldt, 3, 14)
    ph.op(V, I_tt(xr.t[:], are.t[:], dt.t[:], ALU.mult), r=[are, dt], w=[xr])
    ph.op(V, I_tt(ang.t[:], aim.t[:], dt.t[:], ALU.mult), r=[aim, dt], w=[ang])
    exp_acc(mag, xr, 0, 8)

    def reduce_angle(src, dst):
        ph.op(V, I_ts(kac.t[:], src.t[:], PI, None, op0=ALU.is_gt), r=[src], w=[kac])
        for j in range(1, 6):
            ph.op(V, I_stt(kac.t[:], src.t[:], (2 * j + 1) * PI, kac.t[:], ALU.is_gt, ALU.add), r=[src, kac], w=[kac])
        ph.op(V, I_stt(dst.t[:], kac.t[:], -2.0 * PI, src.t[:], ALU.mult, ALU.add), r=[kac, src], w=[dst])
    reduce_angle(ang, angr)
    sincos_acc(sn1, cs1, angr)
    ph.op(V, I_tt(t1.t[:], sn1.t[:], sn1.t[:], ALU.mult), r=[sn1], w=[t1])
    ph.op(V, I_tt(t2.t[:], cs1.t[:], cs1.t[:], ALU.mult), r=[cs1], w=[t2])
    ph.op(V, I_tt(t3.t[:], t1.t[:], t2.t[:], ALU.add), r=[t1, t2], w=[t3])
    ph.op(V, I_ts(t3.t[:], t3.t[:], -0.5, 1.5, op0=ALU.mult, op1=ALU.add), r=[t3], w=[t3])
    ph.op(V, I_tt(sn1.t[:], sn1.t[:], t3.t[:], ALU.mult), r=[sn1, t3], w=[sn1])
    ph.op(V, I_tt(cs1.t[:], cs1.t[:], t3.t[:], ALU.mult), r=[cs1, t3], w=[cs1])
    ph.op(V, I_tt(abr.t[:], mag.t[:], cs1.t[:], ALU.mult), r=[mag, cs1], w=[abr])
    ph.op(V, I_tt(abi.t[:], mag.t[:], sn1.t[:], ALU.mult), r=[mag, sn1], w=[abi])
    ph.op(V, I_tt(t1.t[:], are.t[:], are.t[:], ALU.mult), r=[are], w=[t1])
    ph.op(V, I_tt(t2.t[:], aim.t[:], aim.t[:], ALU.mult), r=[aim], w=[t2])
    ph.op(V, I_tt(den.t[:], t1.t[:], t2.t[:], ALU.add), r=[t1, t2], w=[den])
    ph.op(V, I_recip(rdn.t[:], den.t[:]), r=[den], w=[rdn])
    ph.op(V, I_ts(nr_.t[:], abr.t[:], -1.0, None, op0=ALU.add), r=[abr], w=[nr_])
    ph.op(V, I_tt(t1.t[:], nr_.t[:], are.t[:], ALU.mult), r=[nr_, are], w=[t1])
    ph.op(V, I_tt(t2.t[:], abi.t[:], aim.t[:], ALU.mult), r=[abi, aim], w=[t2])
    ph.op(V, I_tt(t3.t[:], t1.t[:], t2.t[:], ALU.add), r=[t1, t2], w=[t3])
    ph.op(V, I_tt(fre.t[:], t3.t[:], rdn.t[:], ALU.mult), r=[t3, rdn], w=[fre])
    ph.op(V, I_tt(t1.t[:], abi.t[:], are.t[:], ALU.mult), r=[abi, are], w=[t1])
    ph.op(V, I_tt(t2.t[:], nr_.t[:], aim.t[:], ALU.mult), r=[nr_, aim], w=[t2])
    ph.op(V, I_tt(t4.t[:], t1.t[:], t2.t[:], ALU.subtract), r=[t1, t2], w=[t4])
    ph.op(V, I_tt(fim.t[:], t4.t[:], rdn.t[:], ALU.mult), r=[t4, rdn], w=[fim])

    bre_t = ph.sb([128, 16, 16], F32, "bre_t"); bim_t = ph.sb([128, 16, 16], F32, "bim_t")
    cre_t = ph.sb([128, 16, 16], F32, "cre_t"); cim_t = ph.sb([128, 16, 16], F32, "cim_t")
    for t_, nm in ((bre_t, "b_re"), (bim_t, "b_im"), (cre_t, "c_reT"), (cim_t, "c_imT")):
        ph.dma(I_dma(t_.t[:], A[nm][l].rearrange("(pt q) n -> q pt n", q=128)), w=[t_])
    padall = ph.sb([128, 16, 2, 128], F32, "padall")
    Cpad = ph.sb([128, 16, 2, 128], BF16, "Cpad")
    ph.op("pool", I_memset(padall.t[:], 0.0), w=[padall])
    ph.op("pool", I_memset(Cpad.t[:], 0.0), w=[Cpad])
    tb16 = [ph.sb([128, 16], F32, f"tb16{i}") for i in range(2)]
    for pt in range(16):
        qq = pt % 4
        for half in range(2):
            rows = slice(half * 64, (half + 1) * 64)
            cols = slice(32 * qq + 16 * half, 32 * qq + 16 * half + 16)
            ta, tb = tb16
            ph.op(V, I_ts(ta.t[rows, :], bim_t.t[rows, pt, :], fim.t[rows, pt:pt + 1], None, op0=ALU.mult), r=[bim_t, fim], w=[ta])
            ph.op(V, I_stt(padall.t[rows, pt, 0, cols], bre_t.t[rows, pt, :], fre.t[rows, pt:pt + 1], ta.t[rows, :], ALU.mult, ALU.subtract),
                  r=[bre_t, fre, ta], w=[padall])
            ph.op(V, I_ts(tb.t[rows, :], bre_t.t[rows, pt, :], fim.t[rows, pt:pt + 1], None, op0=ALU.mult), r=[bre_t, fim], w=[tb])
            ph.op(V, I_stt(padall.t[rows, pt, 1, cols], bim_t.t[rows, pt, :], fre.t[rows, pt:pt + 1], tb.t[rows, :], ALU.mult, ALU.add),
                  r=[bim_t, fre, tb], w=[padall])
            ph.op("pool", I_copy(Cpad.t[rows, pt, 0, cols], cre_t.t[rows, pt, :]), r=[cre_t], w=[Cpad])
            ph.op("pool", I_ts(Cpad.t[rows, pt, 1, cols], cim_t.t[rows, pt, :], -1.0, None, op0=ALU.mult), r=[cim_t], w=[Cpad])
    pB = [ph.ps([128, 512], F32, f"pB{i}") for i in range(4)]
    pY = [ph.ps([128, 512], F32, f"pY{i}") for i in range(2)]
    pG = ph.ps([128, 512], F32, "pG")
    E = [[ph.sb([128, 128], BF16, f"E{part}_{pt}") for pt in range(16)] for part in range(2)]
    for part in range(2):
        for pt in range(16):
            ph.op("pe", I_mm(pG.t[:, 0:128], padall.t[:, pt, part, :], idf.t[:, :]), r=[padall, idf], w=[pG])
            ph.op("act", I_act(E[part][pt].t[:], pG.t[:, 0:128], AF.Copy), r=[pG], w=[E[part][pt]])
    dsk = ph.sb([128, 4], F32, "dsk")
    ph.dma(I_dma(dsk.t[:], A["d_skip"][l].rearrange("(ct q) -> q ct", q=128)), w=[dsk])
    bglu = ph.sb([128, 4], F32, "bglu")
    ph.dma(I_dma(bglu.t[:], A["b_glu"][l].rearrange("(co q) -> q co", q=128)), w=[bglu])
    wglu = ph.sb([128, 4, 512], BF16, "wglu")
    for c in range(4):
        ph.dma(I_dma(wglu.t[:, c, :], A["w_glu"][l, c * 128:(c + 1) * 128, :]), w=[wglu], q="pool")
    cs = ph.sb([128, 16, LB], F32, "cs"); sn = ph.sb([128, 16, LB], F32, "sn")
    ph.op("pool", I_memset(cs.t[:, :, 0:1], 1.0), w=[cs])
    ph.op("pool", I_memset(sn.t[:, :, 0:1], 0.0), w=[sn])
    ph.op(V, I_copy(wr.t[:], cs1.t[:]), r=[cs1], w=[wr])
    ph.op(V, I_copy(wi_.t[:], sn1.t[:]), r=[sn1], w=[wi_])
    tmpc = ph.sb([128, LB], F32, "tmpc"); tmps = ph.sb([128, LB], F32, "tmps")
    n = 1
    while n < LB:
        for pt in range(16):
            ph.op(V, I_ts(tmpc.t[:, 0:n], sn.t[:, pt, 0:n], wi_.t[:, pt:pt + 1], None, op0=ALU.mult), r=[sn, wi_], w=[tmpc])
            ph.op("pool", I_ts(tmps.t[:, 0:n], cs.t[:, pt, 0:n], wi_.t[:, pt:pt + 1], None, op0=ALU.mult), r=[cs, wi_], w=[tmps])
            ph.op(V, I_stt(cs.t[:, pt, n:2 * n], cs.t[:, pt, 0:n], wr.t[:, pt:pt + 1], tmpc.t[:, 0:n], ALU.mult, ALU.subtract),
                  r=[cs, wr, tmpc], w=[cs])
            ph.op(V, I_stt(sn.t[:, pt, n:2 * n], sn.t[:, pt, 0:n], wr.t[:, pt:pt + 1], tmps.t[:, 0:n], ALU.mult, ALU.add),
                  r=[sn, wr, tmps], w=[sn])
        ph.op(V, I_tt(t1.t[:], wi_.t[:], wi_.t[:], ALU.mult), r=[wi_], w=[t1])
        ph.op(V, I_tt(t2.t[:], wr.t[:], wr.t[:], ALU.mult), r=[wr], w=[t2])
        ph.op(V, I_stt(wt.t[:], wr.t[:], 2.0, wi_.t[:], ALU.mult, ALU.mult), r=[wr, wi_], w=[wt])
        ph.op(V, I_tt(wr.t[:], t2.t[:], t1.t[:], ALU.subtract), r=[t1, t2], w=[wr])
        ph.op(V, I_copy(wi_.t[:], wt.t[:]), r=[wt], w=[wi_])
        n *= 2

    wk = lambda nm: ph.sb([128, LB], F32, nm)
    WK = [[wk(n_ + str(k_)) for n_ in ("m1", "m2", "m3", "m4", "bre", "bim", "gre", "gim", "hre", "him")] for k_ in range(2)]
    HB = [(ph.sb([128, LB], BF16, f"hreb{k_}"), ph.sb([128, LB], BF16, f"himb{k_}")) for k_ in range(2)]
    yv, x2, inn, in2, sg = [wk(n_) for n_ in ("yv", "x2", "inn", "in2", "sg")]
    yg = [wk(f"yg{i}") for i in range(4)]
    ygb = [ph.sb([128, LB], BF16, f"ygb{i}") for i in range(4)]
    ut = [[ph.sb([128, LB], BF16, f"ut{k}{i}") for i in range(4)] for k in range(2)]
    zst = [ph.sb([128, LB], BF16, f"zst{i}") for i in range(2)]
    sgl = wk("sgl"); tg = wk("tg")
    ysf = [ph.sb([128, LB], BF16, f"ysf{i}") for i in range(2)]
    hpr = [sm(f"hpr{i}") for i in range(3)]; hpi = [sm(f"hpi{i}") for i in range(3)]
    a0 = ph.sb([128, 4], F32, "a0")
    nb = 0
    nz = 0
    for s in seqs:
        Lb = min(LB, s.T)
        hr_, hi_ = hpr[s.i], hpi[s.i]
        prev = s.i > 0
        if prev:
            ph.dma(I_dma(hr_.t[:], A["ssr"][l, s.i - 1].rearrange("(pt q) -> q pt", q=128)), w=[hr_])
            ph.dma(I_dma(hi_.t[:], A["ssi"][l, s.i - 1].rearrange("(pt q) -> q pt", q=128)), w=[hi_])
        for blk in range(s.T // Lb):
            t0 = blk * Lb
            u_ = ut[blk % 2]
            for ct in range(4):
                ph.dma(I_dma(u_[ct].t[:, :Lb], s.uT[ct * 128:(ct + 1) * 128, t0:t0 + Lb]), w=[u_[ct]])
            for ct in range(4):
                py = pY[ct % 2]
                for qq in range(4):
                    pt = ct * 4 + qq
                    pr = pB[(nb % 2) * 2]; pi_ = pB[(nb % 2) * 2 + 1]
                    m1, m2, m3, m4, bre_, bim_, gre, gim, hre, him = WK[nb % 2]
                    hreb, himb = HB[nb % 2]
                    nb += 1
                    ph.op("pe", I_mm(pr.t[:, :Lb], E[0][pt].t[:, :], u_[ct].t[:, :Lb]), r=[E[0][pt], u_[ct]], w=[pr])
                    ph.op("pe", I_mm(pi_.t[:, :Lb], E[1][pt].t[:, :], u_[ct].t[:, :Lb]), r=[E[1][pt], u_[ct]], w=[pi_])
                    c_ = cs.t[:, pt, :Lb]; s_ = sn.t[:, pt, :Lb]
                    ph.op(V, I_tt(m1.t[:, :Lb], pr.t[:, :Lb], c_, ALU.mult), r=[pr, cs], w=[m1])
                    ph.op(V, I_tt(m2.t[:, :Lb], pi_.t[:, :Lb], s_, ALU.mult), r=[pi_, sn], w=[m2])
                    ph.op(V, I_tt(m3.t[:, :Lb], pi_.t[:, :Lb], c_, ALU.mult), r=[pi_, cs], w=[m3])
                    ph.op(V, I_tt(m4.t[:, :Lb], pr.t[:, :Lb], s_, ALU.mult), r=[pr, sn], w=[m4])
                    ph.op("pool", I_tt(bre_.t[:, :Lb], m1.t[:, :Lb], m2.t[:, :Lb], ALU.add), r=[m1, m2], w=[bre_])
                    ph.op("pool", I_tt(bim_.t[:, :Lb], m3.t[:, :Lb], m4.t[:, :Lb], ALU.subtract), r=[m3, m4], w=[bim_])
                    if prev:
                        ph.op(V, I_tt(a0.t[:, 0:1], abi.t[:, pt:pt + 1], hi_.t[:, pt:pt + 1], ALU.mult), r=[abi, hi_], w=[a0])
                        ph.op(V, I_stt(a0.t[:, 1:2], abr.t[:, pt:pt + 1], hr_.t[:, pt:pt + 1], a0.t[:, 0:1], ALU.mult, ALU.subtract),
                              r=[abr, hr_, a0], w=[a0])
                        ph.op(V, I_tt(a0.t[:, 2:3], abi.t[:, pt:pt + 1], hr_.t[:, pt:pt + 1], ALU.mult), r=[abi, hr_], w=[a0])
                        ph.op(V, I_stt(a0.t[:, 3:4], abr.t[:, pt:pt + 1], hi_.t[:, pt:pt + 1], a0.t[:, 2:3], ALU.mult, ALU.add),
                              r=[abr, hi_, a0], w=[a0])
                        ph.op(V, I_tt(bre_.t[:, 0:1], bre_.t[:, 0:1], a0.t[:, 1:2], ALU.add), r=[bre_, a0], w=[bre_])
                        ph.op(V, I_tt(bim_.t[:, 0:1], bim_.t[:, 0:1], a0.t[:, 3:4], ALU.add), r=[bim_, a0], w=[bim_])
                    rb = mag.t[:, pt:pt + 1].to_broadcast([128, Lb])
                    ph.op(V, I_scan(gre.t[:, :Lb], rb, bre_.t[:, :Lb], 0.0), r=[mag, bre_], w=[gre])
                    ph.op(V, I_scan(gim.t[:, :Lb], rb, bim_.t[:, :Lb], 0.0), r=[mag, bim_], w=[gim])
                    ph.op(V, I_tt(m1.t[:, :Lb], gre.t[:, :Lb], c_, ALU.mult), r=[gre, cs], w=[m1])
                    ph.op(V, I_tt(m2.t[:, :Lb], gim.t[:, :Lb], s_, ALU.mult), r=[gim, sn], w=[m2])
                    ph.op("pool", I_tt(m3.t[:, :Lb], gre.t[:, :Lb], s_, ALU.mult), r=[gre, sn], w=[m3])
                    ph.op("pool", I_tt(m4.t[:, :Lb], gim.t[:, :Lb], c_, ALU.mult), r=[gim, cs], w=[m4])
                    ph.op(V, I_tt(hre.t[:, :Lb], m1.t[:, :Lb], m2.t[:, :Lb], ALU.subtract), r=[m1, m2], w=[hre])
                    ph.op("pool", I_tt(him.t[:, :Lb], m3.t[:, :Lb], m4.t[:, :Lb], ALU.add), r=[m3, m4], w=[him])
                    ph.op("act", I_act(hr_.t[:, pt:pt + 1], hre.t[:, Lb - 1:Lb], AF.Copy), r=[hre], w=[hr_])
                    ph.op("act", I_act(hi_.t[:, pt:pt + 1], him.t[:, Lb - 1:Lb], AF.Copy), r=[him], w=[hi_])
                    ph.op("act", I_act(hreb.t[:, :Lb], hre.t[:, :Lb], AF.Copy), r=[hre], w=[hreb])
                    ph.op("act", I_act(himb.t[:, :Lb], him.t[:, :Lb], AF.Copy), r=[him], w=[himb])
                    ph.op("pe", I_mm(py.t[:, :Lb], Cpad.t[:, pt, 0, :], hreb.t[:, :Lb], start=(qq == 0), stop=False), r=[Cpad, hreb], w=[py])
                    ph.op("pe", I_mm(py.t[:, :Lb], Cpad.t[:, pt, 1, :], himb.t[:, :Lb], start=False, stop=(qq == 3)), r=[Cpad, himb], w=[py])
                ph.op(V, I_stt(yv.t[:, :Lb], u_[ct].t[:, :Lb], dsk.t[:, ct:ct + 1], py.t[:, :Lb], ALU.mult, ALU.add), r=[u_[ct], dsk, py], w=[yv])
                ph.op("pool", I_tt(x2.t[:, :Lb], yv.t[:, :Lb], yv.t[:, :Lb], ALU.mult), r=[yv], w=[x2])
                ph.op("pool", I_ts(inn.t[:, :Lb], x2.t[:, :Lb], 0.044715, 1.0, op0=ALU.mult, op1=ALU.add), r=[x2], w=[inn])
                ph.op("pool", I_tt(in2.t[:, :Lb], inn.t[:, :Lb], yv.t[:, :Lb], ALU.mult), r=[inn, yv], w=[in2])
                ph.op("act", I_act(sg.t[:, :Lb], in2.t[:, :Lb], AF.Sigmoid, scale=1.5957691216057308), r=[in2], w=[sg])
                ph.op(V, I_tt(yg[ct].t[:, :Lb], yv.t[:, :Lb], sg.t[:, :Lb], ALU.mult), r=[yv, sg], w=[yg[ct]])
                ph.op("act", I_act(ygb[ct].t[:, :Lb], yg[ct].t[:, :Lb], AF.Copy), r=[yg[ct]], w=[ygb[ct]])
            prev = True
            for co in range(4):
                z_ = zst[nz % 2]; yf = ysf[nz % 2]; nz += 1
                for ct in range(4):
                    ph.op("pe", I_mm(pG.t[:, :Lb], wglu.t[:, ct, co * 128:(co + 1) * 128], ygb[ct].t[:, :Lb], start=(ct == 0), stop=(ct == 3)),
                          r=[wglu, ygb[ct]], w=[pG])
                ph.op("act", I_act(sgl.t[:, :Lb], pG.t[:, :Lb], AF.Sigmoid, bias=bglu.t[:, co:co + 1]), r=[pG, bglu], w=[sgl])
                ph.dma(I_dma(z_.t[:, :Lb], s.zsT[co * 128:(co + 1) * 128, t0:t0 + Lb]), w=[z_])
                ph.op(V, I_tt(tg.t[:, :Lb], yg[co].t[:, :Lb], sgl.t[:, :Lb], ALU.mult), r=[yg[co], sgl], w=[tg])
                ph.op("pool", I_tt(yf.t[:, :Lb], tg.t[:, :Lb], z_.t[:, :Lb], ALU.mult), r=[tg, z_], w=[yf])
                ph.dma(I_dma(s.ysT[co * 128:(co + 1) * 128, t0:t0 + Lb], yf.t[:, :Lb]), r=[yf])
        ph.dma(I_dma(s.srout[l].rearrange("(pt q) -> q pt", q=128), hr_.t[:]), r=[hr_])
        ph.dma(I_dma(s.siout[l].rearrange("(pt q) -> q pt", q=128), hi_.t[:]), r=[hi_])
    ph.finish()


def bc_mid(ap, n):
    return bass.AP(ap.tensor, ap.offset, [list(ap.ap[0]), [0, n]] + [list(x) for x in ap.ap[1:]])


NIT2 = 20
W02 = 16.0


def phase_attn2(nc, A, l, s):
    ph = Phase(nc, f"B{l}{s.nm}")
    idf, idb = mk_ident(ph)
    T, S = s.T, s.S
    QP = min(128, T); QB = min(256, T)
    TPB = QB // QP
    ntile = T // QP; nblk = T // QB
    L = T if s.causal else S
    KSEL = float(min(TOPK, L // 4))
    ktiles = [(k0, min(128, S - k0)) for k0 in range(0, S, 128)]
    NKT = len(ktiles)
    kiT2 = ph.sb([128, S], BF16, "kiT2")
    ph.dma(I_dma(kiT2.t[0:64, :], s.kiT[:, :]), w=[kiT2])
    ph.dma(I_dma(kiT2.t[64:128, :], s.kiT[:, :]), w=[kiT2])
    sc = [ph.sb([128, S], F32, f"sc{i}") for i in range(2)]
    junk = ph.sb([128, S], BF16, "junk")
    maskT = [ph.sb([128, NKT, QB], BF16, f"maskT{i}") for i in range(2)]
    qia = [ph.sb([128, 4, 128], BF16, f"qia{i}") for i in range(2)]
    qib = [ph.sb([128, 4, 128], BF16, f"qib{i}") for i in range(2)]
    qta = [ph.sb([128, 4, 256], BF16, f"qta{i}") for i in range(2)]
    qtb = [ph.sb([128, 4, 256], BF16, f"qtb{i}") for i in range(2)]
    for i in range(2):
        ph.op("pool", I_memset(qia[i].t[64:128, :, :], 0.0), w=[qia[i]])
        ph.op("pool", I_memset(qib[i].t[0:64, :, :], 0.0), w=[qib[i]])
        ph.op("pool", I_memset(qta[i].t[64:128, :, :], 0.0), w=[qta[i]])
        ph.op("pool", I_memset(qtb[i].t[0:64, :, :], 0.0), w=[qtb[i]])
    wit = [ph.sb([128, 8], F32, f"wit{i}") for i in range(2)]
    dg = [ph.sb([128, 8, 128], BF16, f"dg{i}") for i in range(2)]
    rl = [ph.sb([128, 512], BF16, f"rl{i}") for i in range(3)]
    sm1 = lambda nm: ph.sb([128, 1], F32, nm)
    mid, negmid, cnt, sact, comb, tq, thr = [sm1(n_) for n_ in ("mid", "negmid", "cnt", "sact", "comb", "tq", "thr")]
    dgthr = ph.sb([128, 128], F32, "dgthr")
    thrbc = ph.sb([128, 128], F32, "thrbc")
    onesf = ph.sb([128, 128], F32, "onesf")
    ph.op("pool", I_memset(onesf.t[:], 1.0), w=[onesf])
    va = [ph.sb([128, 130], BF16, f"va{i}") for i in range(3)]
    ktl = [ph.sb([128, 128], BF16, f"ktl{i}") for i in range(3)]
    pe_ = [ph.sb([128, 2, 256], BF16, f"pe{i}") for i in range(2)]
    pm = [ph.sb([128, 2, 256], BF16, f"pm{i}") for i in range(2)]
    oa = [ph.sb([128, 256], F32, f"oa{i}") for i in range(2)]
    for o__ in oa:
        ph.op("pool", I_memset(o__.t[:], 0.0), w=[o__])
    rden = [ph.sb([64, 256], F32, f"rden{i}") for i in range(2)]
    t1 = [ph.sb([64, 256], F32, f"t1{i}") for i in range(2)]
    zat = [ph.sb([64, 256], BF16, f"zat{i}") for i in range(2)]
    yat = [ph.sb([64, 256], BF16, f"yat{i}") for i in range(2)]
    iop = ph.sb([128, 64], F32, "iop")
    selden = ph.sb([128, 64], F32, "selden")
    ph.op("pool", I_iota(iop.t[:], [[0, 64]], 0, 1), w=[iop])
    ph.op("dve", I_ts(selden.t[:], iop.t[:], 64.0, None, op0=ALU.is_equal), r=[iop], w=[selden])
    lg = [ph.ps([128, 512], F32, f"lg{i}") for i in range(2)]
    acc = ph.ps([128, 512], F32, "acc")
    stb = [ph.ps([128, 2, 256], F32, f"st{i}") for i in range(2)]
    oacc = [ph.ps([128, 512], F32, f"oacc{i}") for i in range(2)]
    scTp = ph.ps([128, 4, 128], F32, "scTp")
    cnts = {"r": 0, "v": 0, "h": 0, "e": 0}

    def tile_info(i):
        q0 = i * QP
        Slim = (q0 + QP) if s.causal else S
        return q0, Slim

    def blk_kts(b):
        Sblk = min(S, (b + 1) * QB) if s.causal else S
        return [(i, k0, ksz) for i, (k0, ksz) in enumerate(ktiles) if k0 < Sblk]

    def gen_index(i):
        q0, Slim = tile_info(i)
        sl = i % 2
        qa, qb_, wi_, dg_, sc_ = qia[sl], qib[sl], wit[sl], dg[sl], sc[sl]
        qsrc = s.qiT.rearrange("(hp p) t -> p hp t", p=128)
        ph.dma(I_dma(qa.t[0:64, :, :QP], qsrc[0:64, :, q0:q0 + QP]), w=[qa])
        ph.dma(I_dma(qb_.t[64:128, :, :QP], qsrc[64:128, :, q0:q0 + QP]), w=[qb_])
        ph.dma(I_dma(wi_.t[:QP, :], s.wiS[q0:q0 + QP, :]), w=[wi_])
        for h in range(8):
            ph.op("dve", I_ts(dg_.t[:QP, h, :QP], idf.t[:QP, :QP], wi_.t[:QP, h:h + 1], None, op0=ALU.mult), r=[idf, wi_], w=[dg_])
        yield
        for c0 in range(0, Slim, 512):
            csz = min(512, Slim - c0)
            bufs = []
            for h in range(8):
                bufs.append((lg[cnts["r"] % 2], rl[cnts["r"] % 3])); cnts["r"] += 1

            def emit_lg(h):
                g, r_ = bufs[h]
                qz = qa if h % 2 == 0 else qb_
                ph.op("pe", I_mm(g.t[:QP, :csz], qz.t[:, h // 2, :QP], kiT2.t[:, c0:c0 + csz]), r=[qz, kiT2], w=[g])
            emit_lg(0)
            for h in range(8):
                g, r_ = bufs[h]
                if h < 7:
                    emit_lg(h + 1)
                if h % 4 == 3:
                    ph.op("dve", I_ts(r_.t[:QP, :csz], g.t[:QP, :csz], 0.0, None, op0=ALU.max), r=[g], w=[r_])
                else:
                    ph.op("act", I_act(r_.t[:QP, :csz], g.t[:QP, :csz], AF.Relu), r=[g], w=[r_])
                ph.op("pe", I_mm(acc.t[:QP, :csz], dg_.t[:QP, h, :QP], r_.t[:QP, :csz], start=(h == 0), stop=(h == 7)), r=[dg_, r_], w=[acc])
            ph.op("dve", I_copy(sc_.t[:QP, c0:c0 + csz], acc.t[:QP, :csz]), r=[acc], w=[sc_])
            yield
        if s.causal and QP == 128:
            ph.op("pool", I_memset(sc_.t[0:64, Slim - 64:Slim], NEG), w=[sc_])

    def n_index(i):
        q0, Slim = tile_info(i)
        return 1 + (Slim + 511) // 512

    def gen_bisect(i):
        q0, Slim = tile_info(i)
        sc_ = sc[i % 2]
        b = i // TPB
        qoff = (i % TPB) * QP
        mT = maskT[b % 2]
        S1 = Slim
        if Slim >= 1024:
            S1 = int(Slim * 0.45) // 128 * 128
        ph.op("dve", I_memset(mid.t[:QP, :], 0.0), w=[mid])
        w = W02
        kadj = KSEL - 0.5 * (Slim - S1)
        for k in range(NIT2):
            w = w / 2.0
            ph.op("pool", I_ts(negmid.t[:QP, :], mid.t[:QP, :], -w, None, op0=ALU.add), r=[mid], w=[negmid])
            ph.op("dve", I_ts(junk.t[:QP, :S1], sc_.t[:QP, :S1], mid.t[:QP, 0:1], -kadj, op0=ALU.is_ge, op1=ALU.add,
                              accum_out=cnt.t[:QP, 0:1]), r=[sc_, mid], w=[cnt])
            if S1 < Slim:
                ph.op("act", I_act(junk.t[:QP, S1:Slim], sc_.t[:QP, S1:Slim], AF.Sign, bias=mid.t[:QP, 0:1], scale=-1.0,
                                   accum_out=sact.t[:QP, 0:1]), r=[sc_, mid], w=[sact])
                ph.op("dve", I_stt(tq.t[:QP, :], sact.t[:QP, :], 0.5, cnt.t[:QP, :], ALU.mult, ALU.is_le), r=[sact, cnt], w=[tq])
            else:
                ph.op("dve", I_ts(tq.t[:QP, :], cnt.t[:QP, :], 0.0, None, op0=ALU.is_ge), r=[cnt], w=[tq])
            ph.op("dve", I_stt(mid.t[:QP, :], tq.t[:QP, :], 2.0 * w, negmid.t[:QP, :], ALU.mult, ALU.add), r=[tq, negmid], w=[mid])
            yield
        ph.op("dve", I_ts(thr.t[:QP, :], mid.t[:QP, :], -w, None, op0=ALU.add), r=[mid], w=[thr])
        ph.op("dve", I_ts(dgthr.t[:QP, :QP], idf.t[:QP, :QP], thr.t[:QP, 0:1], None, op0=ALU.mult), r=[idf, thr], w=[dgthr])
        ph.op("pe", I_mm(scTp.t[:, 0, 0:QP], onesf.t[:QP, :], dgthr.t[:QP, :QP]), r=[onesf, dgthr], w=[scTp])
        ph.op("act", I_act(thrbc.t[:, :QP], scTp.t[:, 0, 0:QP], AF.Copy), r=[scTp], w=[thrbc])
        yield
        kts = blk_kts(b)
        mine = [(i_, k0, ksz) for (i_, k0, ksz) in kts if k0 < Slim]
        for g0 in range(0, len(mine), 4):
            grp = mine[g0:g0 + 4]
            for gi, (i_, k0, ksz) in enumerate(grp):
                ph.op("pe", I_tr(scTp.t[:ksz, gi, :QP], sc_.t[:QP, k0:k0 + ksz], idf.t[:QP, :QP]), r=[sc_, idf], w=[scTp])
            i0 = grp[0][0]; ng = len(grp)
            ph.op("dve", I_tt(mT.t[:, i0:i0 + ng, qoff:qoff + QP], scTp.t[:, 0:ng, :QP], bc_mid(thrbc.t[:, 0:QP], ng), ALU.is_ge),
                  r=[scTp, thrbc], w=[mT])
            yield
        rest = [(i_, k0, ksz) for (i_, k0, ksz) in kts if k0 >= Slim]
        if rest:
            i0 = rest[0][0]; i1 = rest[-1][0] + 1
            ph.op("pool", I_memset(mT.t[:, i0:i1, qoff:qoff + QP], 0.0), w=[mT])

    def n_bisect(i):
        q0, Slim = tile_info(i)
        nm = len([1 for (k0, ksz) in ktiles if k0 < Slim])
        return NIT2 + 1 + (nm + 3) // 4

    import os as _os4
    _stop = _os4.environ.get("ATTN2_STOP", "")

    def gen_attn(b):
        qb0 = b * QB
        kts = blk_kts(b)
        mT = maskT[b % 2]
        qa, qb_ = qta[b % 2], qtb[b % 2]
        qsrc = s.qT.rearrange("(hp p) t -> p hp t", p=128)
        ph.dma(I_dma(qa.t[0:64, :, :QB], qsrc[0:64, :, qb0:qb0 + QB]), w=[qa])
        ph.dma(I_dma(qb_.t[64:128, :, :QB], qsrc[64:128, :, qb0:qb0 + QB]), w=[qb_])
        for hg in range(4):
            ubuf = []
            for idx in range(len(kts)):
                ubuf.append((va[cnts["v"] % 3], ktl[cnts["v"] % 3])); cnts["v"] += 1

            def emit_st(idx):
                i_, k0, ksz = kts[idx]
                va_, kt_ = ubuf[idx]
                st = stb[idx % 2]
                ph.dma(I_dma(va_.t[:ksz, :], s.vaug[k0:k0 + ksz, hg * 130:(hg + 1) * 130]), w=[va_])
                ph.dma(I_dma(kt_.t[:, :ksz], s.kT[hg * 128:(hg + 1) * 128, k0:k0 + ksz]), w=[kt_])
                ph.op("pe", I_mm(st.t[:ksz, 0, :QB], kt_.t[:, :ksz], qa.t[:, hg, :QB]), r=[kt_, qa], w=[st])
                ph.op("pe", I_mm(st.t[:ksz, 1, :QB], kt_.t[:, :ksz], qb_.t[:, hg, :QB]), r=[kt_, qb_], w=[st])
            emit_st(0)
            for idx, (i_, k0, ksz) in enumerate(kts):
                va_, kt_ = ubuf[idx]
                st = stb[idx % 2]
                p_ = pe_[cnts["h"] % 2]; m_ = pm[cnts["h"] % 2]; cnts["h"] += 1
                if idx + 1 < len(kts):
                    emit_st(idx + 1)
                ph.op("act", I_act(p_.t[:ksz, :, :QB], st.t[:ksz, :, :QB], AF.Exp), r=[st], w=[p_])
                ph.op("dve", I_tt(m_.t[:ksz, :, :QB], p_.t[:ksz, :, :QB], bc_mid(mT.t[:ksz, i_, :QB], 2), ALU.mult), r=[p_, mT], w=[m_])
                for hh in range(2):
                    ph.op("pe", I_mm(oacc[hh].t[:65, :QB], va_.t[:ksz, hh * 65:(hh + 1) * 65], m_.t[:ksz, hh, :QB],
                                     start=(idx == 0), stop=(idx == len(kts) - 1)), r=[va_, m_], w=[oacc[hh]])
                yield
            if _stop in ("st", "exp", "mul", "pv"):
                yield
                continue
            for hh in range(2):
                h = hg * 2 + hh
                e = cnts["e"] % 2; cnts["e"] += 1
                o_ = oa[e]; rd = rden[e]; t_ = t1[e]; z_ = zat[e]; y_ = yat[e]
                ph.op("act", I_act(o_.t[:65, :QB], oacc[hh].t[:65, :QB], AF.Copy), r=[oacc[hh]], w=[o_])
                if _stop == "ep1":
                    continue
                ph.op("pe", I_mm(acc.t[:64, 0:QB], selden.t[:, :64], o_.t[:, :QB]), r=[selden, o_], w=[acc])
                if _stop == "ep2":
                    continue
                ph.op("dve", I_recip(rd.t[:64, :QB], acc.t[:64, 0:QB]), r=[acc], w=[rd])
                if _stop == "ep3":
                    continue
                ph.dma(I_dma(z_.t[:64, :QB], s.zaT[64 * h:64 * h + 64, qb0:qb0 + QB]), w=[z_])
                ph.op("dve", I_tt(t_.t[:64, :QB], o_.t[0:64, :QB], rd.t[:64, :QB], ALU.mult), r=[o_, rd], w=[t_])
                if _stop == "ep4":
                    continue
                ph.op("pool", I_tt(y_.t[:64, :QB], t_.t[:64, :QB], z_.t[:64, :QB], ALU.mult), r=[t_, z_], w=[y_])
                if _stop == "ep5":
                    continue
                ph.dma(I_dma(s.yaT[64 * h:64 * h + 64, qb0:qb0 + QB], y_.t[:64, :QB]), r=[y_], q="act")
            yield

    def n_attn(b):
        return 4 * (len(blk_kts(b)) + 1)

    attn_state = {}
    nslot = ntile + 1 + TPB + 1
    for slot in range(nslot + 2 * TPB):
        work = []
        if slot < ntile:
            work.append([gen_index(slot), n_index(slot)])
        import os as _os3
        _stop = _os3.environ.get("ATTN2_STOP", "")
        if 1 <= slot <= ntile and _stop != "index":
            work.append([gen_bisect(slot - 1), n_bisect(slot - 1)])
        for b in range(nblk):
            ready = (b + 1) * TPB + 1
            if slot == ready and _stop not in ("index", "bisect"):
                attn_state[b] = [gen_attn(b), n_attn(b), 0]
        for b, stt_ in list(attn_state.items()):
            g, tot, done = stt_
            share = (tot + TPB - 1) // TPB
            work.append([g, min(share, tot - done) + 1])
            stt_[2] = done + share
            if stt_[2] >= tot:
                del attn_state[b]
        if not work:
            continue
        maxc = max(c for _, c in work)
        done_c = [0] * len(work)
        alive = [True] * len(work)
        for k in range(maxc):
            for wi_x, (g, c) in enumerate(work):
                while alive[wi_x] and done_c[wi_x] < c and (done_c[wi_x] + 1) * maxc <= (k + 1) * c:
                    try:
                        next(g)
                    except StopIteration:
                        alive[wi_x] = False
                    done_c[wi_x] += 1
        for wi_x, (g, c) in enumerate(work[: (1 if slot < ntile else 0) + (1 if (1 <= slot <= ntile and _stop != "index") else 0)]):
            if alive[wi_x]:
                for _ in g:
                    pass
    for b, stt_ in list(attn_state.items()):
        for _ in stt_[0]:
            pass
    ph.finish()
```

```python
import numpy as np
from contextlib import ExitStack
import concourse.bass as bass
import concourse.mybir as mybir
from concourse.bass_utils import run_bass_kernel_spmd

F32 = mybir.dt.float32
BF16 = mybir.dt.bfloat16
I32 = mybir.dt.int32
AF = mybir.ActivationFunctionType
ALU = mybir.AluOpType

D = 1024
DIN = 5704
NL = 2
EPS = 1e-6
OFF_U, OFF_ZS, OFF_Q, OFF_K, OFF_V, OFF_ZA, OFF_QI, OFF_KI, OFF_WI, OFF_GM = (
    0, 512, 1024, 1536, 2048, 2560, 3072, 3584, 3648, 3656)
NEG = -1.0e30
TOPK = 256


class TT:
    __slots__ = ("t", "name", "w", "r", "dsem", "dtot")

    def __init__(self, t, name):
        self.t = t
        self.name = name
        self.w = None
        self.r = []
        self.dsem = None
        self.dtot = 0

    def __getitem__(self, idx):
        return self.t[idx]


_SEMPOOL = {}
_SEMSTACK = []


class Phase:
    ENGS = ("sync", "pool", "act", "dve", "pe")

    def __init__(self, nc, name):
        self.nc = nc
        self.name = name
        self.es = ExitStack()
        self.ops = {e: [] for e in self.ENGS}
        self.cnt = {e: 0 for e in self.ENGS}
        self.known = {e: {} for e in self.ENGS}
        self.sems = {}
        self.semobj = {}
        self.nsem = 0
        for e in ("pool", "act", "dve", "pe"):
            s = self._newsem(f"{name}_{e}")
            self.sems[e] = s
            self.semobj[("c", e)] = s
        self.ntile = 0
        self.dma_tiles = []
        self.misc = None

    def _newsem(self, nm):
        pool = _SEMPOOL.get(id(self.nc))
        if pool is None:
            return self.es.enter_context(self.nc.semaphore(nm))
        sm_ = pool[self.nsem]
        self.nsem += 1
        return sm_

    def sb(self, shape, dtype, name=None):
        self.ntile += 1
        nm = f"{self.name}_{name or 't'}{self.ntile}"
        t = self.es.enter_context(self.nc.sbuf_tensor(nm, list(shape), dtype))
        return TT(t, nm)

    def ps(self, shape, dtype, name=None):
        self.ntile += 1
        nm = f"{self.name}_{name or 'p'}{self.ntile}"
        t = self.es.enter_context(self.nc.psum_tensor(nm, list(shape), dtype))
        return TT(t, nm)

    def _dsem(self, tt):
        if tt.dsem is None:
            tt.dsem = self._newsem(f"d_{tt.name}")
            self.semobj[("d", tt.name)] = tt.dsem
            self.dma_tiles.append(tt)
        return tt.dsem

    def _need(self, eng, dep, waits):
        if dep is None:
            return
        key, val = dep
        if key == ("c", "pe") and eng == "pe":
            return
        if key == ("c", eng) and eng in ("sync",):
            return
        k = self.known[eng]
        if k.get(key, 0) >= val:
            return
        k[key] = val
        waits.append((self.semobj[key], val))

    def _deps(self, eng, r, w):
        best = {}
        order = []

        def add(dep):
            if dep is None:
                return
            key, val = dep
            if key not in best:
                order.append(key)
                best[key] = val
            elif val > best[key]:
                best[key] = val
        for t in r:
            add(t.w)
        for t in w:
            add(t.w)
            for d in t.r:
                add(d)
        waits = []
        for key in order:
            self._need(eng, (key, best[key]), waits)
        return waits

    def op(self, eng, fn, r=(), w=()):
        waits = self._deps(eng, r, w)
        self.cnt[eng] += 1
        me = (("c", eng), self.cnt[eng])
        for t in r:
            t.r.append(me)
        for t in w:
            t.w = me
            t.r = []
        self.ops[eng].append((waits, fn, ("c", self.sems[eng])))

    def dma(self, fn, r=(), w=(), q="sync"):
        waits = self._deps(q, r, w)
        tiles = list(w) + list(r)
        if tiles:
            tt = tiles[0]
            sem = self._dsem(tt)
            tt.dtot += 16
            me = (("d", tt.name), tt.dtot)
        else:
            if self.misc is None:
                self.misc = TT(None, f"{self.name}_misc")
            tt = self.misc
            sem = self._dsem(tt)
            tt.dtot += 16
            me = (("d", tt.name), tt.dtot)
        for t in r:
            t.r.append(me)
        for t in w:
            t.w = me
            t.r = []
        self.ops[q].append((waits, fn, ("d", sem)))

    def finish(self):
        nc = self.nc
        finals = [(t.dsem, t.dtot) for t in self.dma_tiles if t.dtot > 0]

        def replay(e, name):
            for waits, fn, inc in self.ops[name]:
                for s, v in waits:
                    e.wait_ge(s, v)
                ins = fn(e)
                if inc[0] == "c":
                    ins.then_inc(inc[1], 1)
                else:
                    ins.then_inc(inc[1], 16)
            if name == "sync":
                for s, v in finals:
                    e.wait_ge(s, v)

        with nc.Block() as block:
            @block.sync
            def _(e):
                replay(e, "sync")

            @block.gpsimd
            def _(e):
                replay(e, "pool")

            @block.scalar
            def _(e):
                replay(e, "act")

            @block.vector
            def _(e):
                replay(e, "dve")

            @block.tensor
            def _(e):
                replay(e, "pe")
        allsems = list(self.semobj.values())
        with nc.Block() as block2:
            @block2.sync
            def _(e):
                for sm_ in allsems:
                    e.sem_clear(sm_)
        self.es.close()


def I_ts(out, in0, s1, s2=None, op0=ALU.mult, op1=None, accum_out=None):
    kw = {}
    if op1 is not None:
        kw["op1"] = op1
    if accum_out is not None:
        kw["accum_out"] = accum_out
    return lambda e: e.tensor_scalar(out=out, in0=in0, scalar1=s1, scalar2=s2, op0=op0, **kw)


def I_tt(out, in0, in1, op):
    return lambda e: e.tensor_tensor(out=out, in0=in0, in1=in1, op=op)


def I_stt(out, in0, scalar, in1, op0, op1):
    return lambda e: e.scalar_tensor_tensor(out=out, in0=in0, scalar=scalar, in1=in1, op0=op0, op1=op1)


def I_act(out, in_, func, bias=None, scale=None, accum_out=None):
    kw = {}
    if bias is not None:
        kw["bias"] = bias
    if scale is not None:
        kw["scale"] = scale
    if accum_out is not None:
        kw["accum_out"] = accum_out
    return lambda e: e.activation(out=out, in_=in_, func=func, **kw)


def I_mm(out, lhsT, rhs, start=True, stop=True):
    return lambda e: e.matmul(out, lhsT, rhs, start=start, stop=stop)


def I_tr(out, in_, ident):
    return lambda e: e.transpose(out, in_, ident)


def I_copy(out, in_):
    return lambda e: e.tensor_copy(out=out, in_=in_)


def I_memset(ap, v):
    return lambda e: e.memset(ap, v)


def I_recip(out, in_):
    return lambda e: e.reciprocal(out=out, in_=in_)


def I_scan(out, d0, d1, init):
    return lambda e: e.tensor_tensor_scan(out=out, data0=d0, data1=d1, initial=init, op0=ALU.mult, op1=ALU.add)


def I_dma(out, in_, **kw):
    return lambda e: e.dma_start(out=out, in_=in_, **kw)


def I_iota(out, pattern, base, cm):
    return lambda e: e.iota(out=out, pattern=pattern, base=base, channel_multiplier=cm,
                            allow_small_or_imprecise_dtypes=True)


def mk_ident(ph):
    iot = ph.sb([128, 128], F32, "iot")
    idf = ph.sb([128, 128], F32, "idf")
    idb = ph.sb([128, 128], BF16, "idb")
    ph.op("pool", I_iota(iot.t[:], [[1, 128]], 0, -1), w=[iot])
    ph.op("dve", I_ts(idf.t[:], iot.t[:], 0.0, None, op0=ALU.is_equal), r=[iot], w=[idf])
    ph.op("dve", I_copy(idb.t[:], idf.t[:]), r=[idf], w=[idb])
    return idf, idb


def bcast_rows(ap_row, n):
    return ap_row.to_broadcast([n, ap_row.shape[-1]])


class Seq:
    pass


def build(T=8192, TS=32, PAST=4096, stop_after=None, dbg=False):
    nc = bass.Bass("TRN2", target_bir_lowering=False)
    SS = PAST + TS
    din = lambda n, s, dt=F32: nc.dram_tensor(n, list(s), dt, kind="ExternalInput").ap()
    dout = lambda n, s, dt=F32: nc.dram_tensor(n, list(s), dt, kind="ExternalOutput").ap()
    scr_kind = "ExternalOutput" if dbg else "Internal"
    dscr = lambda n, s, dt=BF16: nc.dram_tensor(n, list(s), dt, kind=scr_kind).ap()

    A = {}
    A["xp"] = din("xp", [T, D]); A["xs"] = din("xs", [2, TS, D])
    A["ck"] = din("ck", [NL, 2, PAST, 512]); A["cv"] = din("cv", [NL, 2, PAST, 512])
    A["cki"] = din("cki", [NL, 2, PAST, 64])
    A["ssr"] = din("ssr", [NL, 2, 2048]); A["ssi"] = din("ssi", [NL, 2, 2048])
    A["call"] = din("call", [3, D])
    A["w_mod"] = din("w_mod", [NL, D, 3 * D]); A["b_mod"] = din("b_mod", [NL, 3 * D])
    A["g_norm"] = din("g_norm", [NL, D]); A["w_in"] = din("w_in", [NL, D, DIN])
    A["a_re"] = din("a_re", [NL, 2048]); A["a_im"] = din("a_im", [NL, 2048])
    A["log_dt"] = din("log_dt", [NL, 32])
    A["b_re"] = din("b_re", [NL, 2048, 16]); A["b_im"] = din("b_im", [NL, 2048, 16])
    A["c_reT"] = din("c_reT", [NL, 2048, 16]); A["c_imT"] = din("c_imT", [NL, 2048, 16])
    A["d_skip"] = din("d_skip", [NL, 512])
    A["w_glu"] = din("w_glu", [NL, 512, 512]); A["b_glu"] = din("b_glu", [NL, 512])
    A["w_ps"] = din("w_ps", [NL, 512, D]); A["w_pa"] = din("w_pa", [NL, 512, D])
    A["w_o"] = din("w_o", [NL, D, D]); A["g_final"] = din("g_final", [1, D])

    O = {}
    O["yp"] = dout("yp", [T, D]); O["ys"] = dout("ys", [2, TS, D])
    O["kp"] = dout("kp", [NL, T, 512]); O["vp"] = dout("vp", [NL, T, 512]); O["kip"] = dout("kip", [NL, T, 64])
    O["srp"] = dout("srp", [NL, 2048]); O["sip"] = dout("sip", [NL, 2048])
    O["ks"] = dout("ks", [NL, 2, TS, 512]); O["vs"] = dout("vs", [NL, 2, TS, 512])
    O["kis"] = dout("kis", [NL, 2, TS, 64])
    O["srs"] = dout("srs", [NL, 2, 2048]); O["sis"] = dout("sis", [NL, 2, 2048])

    modbc = dscr("modbc", [NL, 3, 3, 128, D], F32)

    seqs = []
    for i in range(3):
        s = Seq()
        s.i = i
        s.nm = "p" if i == 0 else f"s{i - 1}"
        s.T = T if i == 0 else TS
        s.S = T if i == 0 else SS
        s.koff = 0 if i == 0 else PAST
        s.causal = (i == 0)
        s.xin = A["xp"] if i == 0 else A["xs"][i - 1]
        s.xres = dscr(f"xres_{s.nm}", [s.T, D], F32)
        for nm, rows, cols in (("uT", 512, s.T), ("zsT", 512, s.T), ("qT", 512, s.T), ("kT", 512, s.S),
                               ("zaT", 512, s.T), ("qiT", 512, s.T), ("kiT", 64, s.S), ("gmT", 2048, s.T),
                               ("vaug", s.S, 520), ("ysT", 512, s.T), ("yaT", 512, s.T)):
            setattr(s, nm, dscr(f"{nm}_{s.nm}", [rows, cols], BF16))
        s.wiS = dscr(f"wiS_{s.nm}", [s.T, 8], F32)
        if i == 0:
            s.kout = [O["kp"][l] for l in range(NL)]; s.vout = [O["vp"][l] for l in range(NL)]
            s.kiout = [O["kip"][l] for l in range(NL)]
            s.srout = [O["srp"][l] for l in range(NL)]; s.siout = [O["sip"][l] for l in range(NL)]
            s.yout = O["yp"]
        else:
            j = i - 1
            s.kout = [O["ks"][l, j] for l in range(NL)]; s.vout = [O["vs"][l, j] for l in range(NL)]
            s.kiout = [O["kis"][l, j] for l in range(NL)]
            s.srout = [O["srs"][l, j] for l in range(NL)]; s.siout = [O["sis"][l, j] for l in range(NL)]
            s.yout = O["ys"][j]
        seqs.append(s)

    ncd = nc.allow_non_contiguous_dma(reason="small strided param loads")
    ncd.__enter__()
    es_ = ExitStack()
    _SEMSTACK.append(es_)

    phase_mod(nc, A, modbc)
    if stop_after == "mod":
        return nc
    for l in range(NL):
        phase_proj(nc, A, l, seqs, modbc)
        if stop_after == f"proj{l}":
            return nc
        phase_cache(nc, A, l, seqs, PAST)
        if stop_after == f"cache{l}":
            return nc
        phase_ssm(nc, A, l, seqs)
        if stop_after == f"ssm{l}":
            return nc
        import os as _os2
        for s in seqs:
            if _os2.environ.get("ATTN_SEQ") and s.nm not in _os2.environ.get("ATTN_SEQ").split(","):
                continue
            (phase_attn2 if (_os2.environ.get('ATTN_V','h')=='2' or (_os2.environ.get('ATTN_V','h')=='h' and s.i == 0)) else phase_attn)(nc, A, l, s)
        if stop_after == f"attn{l}":
            return nc
        phase_merge(nc, A, l, seqs, modbc)
        if stop_after == f"merge{l}":
            return nc
    return nc


def phase_mod(nc, A, modbc):
    ph = Phase(nc, "M")
    idf, idb = mk_ident(ph)
    cT = ph.sb([128, 8, 3], F32, "cT")
    for s_ in range(3):
        ph.dma(I_dma(cT.t[:, :, s_], A["call"][s_].rearrange("(c p) -> p c", p=128)), w=[cT])
    cTs = ph.sb([128, 8, 3], F32, "cTs")
    ph.op("act", I_act(cTs.t[:], cT.t[:], AF.Silu), r=[cT], w=[cTs])
    iop = ph.sb([3, 128], F32, "iop")
    ph.op("pool", I_iota(iop.t[:], [[0, 128]], 0, 1), w=[iop])
    sel = []
    for s in range(3):
        t = ph.sb([3, 128], F32, f"sel{s}")
        ph.op("dve", I_ts(t.t[:], iop.t[:], float(s), None, op0=ALU.is_equal), r=[iop], w=[t])
        sel.append(t)
    wst = [ph.sb([128, 8, 512], F32, f"wst{i}") for i in range(2)]
    pmm = [ph.ps([128, 512], F32, f"pm{i}") for i in range(2)]
    pbc = [ph.ps([128, 512], F32, f"pb{i}") for i in range(2)]
    stg = [ph.sb([128, D], F32, f"stg{i}") for i in range(2)]
    n = 0
    nb = 0
    for l in range(NL):
        bsb = ph.sb([3, 3 * D], F32, f"bsb{l}")
        ph.dma(I_dma(bsb.t[:], bcast_rows(A["b_mod"][l:l + 1, :], 3)), w=[bsb])
        gbc = ph.sb([128, D], F32, f"gbc{l}")
        ph.dma(I_dma(gbc.t[:], bcast_rows(A["g_norm"][l:l + 1, :], 128)), w=[gbc])
        mod = ph.sb([3, 3 * D], F32, f"mod{l}")
        for ct in range(6):
            w = wst[n % 2]; pm = pmm[n % 2]; n += 1
            ph.dma(I_dma(w.t[:], A["w_mod"][l].rearrange("(c p) n -> p c n", p=128)[:, :, ct * 512:(ct + 1) * 512]), w=[w])
            for c in range(8):
                ph.op("pe", I_mm(pm.t[0:3, :], cTs.t[:, c, :], w.t[:, c, :], start=(c == 0), stop=(c == 7)),
                      r=[cTs, w], w=[pm])
            ph.op("dve", I_tt(mod.t[:, ct * 512:(ct + 1) * 512], pm.t[0:3, :], bsb.t[:, ct * 512:(ct + 1) * 512], ALU.add),
                  r=[pm, bsb], w=[mod])
        ph.op("dve", I_ts(mod.t[:, D:2 * D], mod.t[:, D:2 * D], 1.0, None, op0=ALU.add), r=[mod], w=[mod])
        for s in range(3):
            for kind in range(3):
                st = stg[nb % 2]
                for half in range(2):
                    pb = pbc[nb % 2 if half == 0 else (nb + 1) % 2]
                    c0 = kind * D + half * 512
                    ph.op("pe", I_mm(pb.t[:, :], sel[s].t[:, :], mod.t[:, c0:c0 + 512]), r=[sel[s], mod], w=[pb])
                    if kind == 1:
                        ph.op("dve", I_tt(st.t[:, half * 512:(half + 1) * 512], pb.t[:, :], gbc.t[:, half * 512:(half + 1) * 512], ALU.mult),
                              r=[pb, gbc], w=[st])
                    else:
                        ph.op("act", I_act(st.t[:, half * 512:(half + 1) * 512], pb.t[:, :], AF.Copy), r=[pb], w=[st])
                nb += 1
                ph.dma(I_dma(modbc[l, s, kind], st.t[:]), r=[st])
    ph.finish()


def phase_proj(nc, A, l, seqs, modbc):
    ph = Phase(nc, f"P{l}")
    idf, idb = mk_ident(ph)
    wbf = ph.sb([128, 8, DIN], BF16, "wbf")
    win = A["w_in"][l].rearrange("(c p) n -> p c n", p=128)
    for c in range(8):
        for h0 in range(0, DIN, 1024):
            h1 = min(DIN, h0 + 1024)
            ph.dma(I_dma(wbf.t[:, c, h0:h1], win[:, c, h0:h1]), w=[wbf], q="pool")
    xt = [ph.sb([128, D], F32, f"xt{i}") for i in range(2)]
    junk = ph.sb([128, D], BF16, "junk")
    ss = [ph.sb([128, 1], F32, f"ss{i}") for i in range(2)]
    rstd = [ph.sb([128, 1], F32, f"rstd{i}") for i in range(2)]
    tmp = [ph.sb([128, D], F32, f"tmp{i}") for i in range(2)]
    hb = [ph.sb([128, D], BF16, f"hb{i}") for i in range(2)]
    hT = [ph.sb([128, 8, 512], BF16, f"hT{i}") for i in range(2)]
    gm_bc = ph.sb([128, D], F32, "gmbc")
    sh_bc = ph.sb([128, D], F32, "shbc")
    kvst = [ph.sb([128, 2 * 512], F32, f"kvst{i}") for i in range(2)]
    vst = [ph.sb([128, 8, 65], BF16, f"vst{i}") for i in range(2)]
    for v in vst:
        ph.op("pool", I_memset(v.t[:], 1.0), w=[v])
    kist = [ph.sb([128, 64], F32, f"kist{i}") for i in range(2)]
    wist = [ph.sb([128, 8], F32, f"wist{i}") for i in range(2)]
    fst = [ph.sb([128, 512], BF16, f"fst{i}") for i in range(4)]
    pT = ph.ps([128, 8, 128], BF16, "pT")
    pk = ph.ps([128, 512], F32, "pk")
    pv = ph.ps([128, 512], F32, "pv")
    pki = ph.ps([128, 72], F32, "pki")
    pf = [ph.ps([128, 512], F32, f"pf{i}") for i in range(3)]
    WISC = float(8 ** -0.5 * 64 ** -0.5)

    nt = 0
    nf = 0
    for s in seqs:
        ph.dma(I_dma(gm_bc.t[:], modbc[l, s.i, 1]), w=[gm_bc])
        ph.dma(I_dma(sh_bc.t[:], modbc[l, s.i, 0]), w=[sh_bc])
        xsrc = s.xin if l == 0 else s.xres
        TP = min(128, s.T)
        NW = min(512, s.T)
        for st in range(s.T // NW):
            ht = hT[st % 2]
            for j in range(NW // TP):
                r0 = st * NW + j * TP
                x = xt[nt % 2]; sq = ss[nt % 2]; rs = rstd[nt % 2]; tm = tmp[nt % 2]; h = hb[nt % 2]
                kv = kvst[nt % 2]; vs_ = vst[nt % 2]; kis_ = kist[nt % 2]; wis_ = wist[nt % 2]
                nt += 1
                ph.dma(I_dma(x.t[:TP, :], xsrc[r0:r0 + TP, :]), w=[x])
                ph.op("act", I_act(junk.t[:TP, :], x.t[:TP, :], AF.Square, accum_out=sq.t[:TP, 0:1]), r=[x], w=[junk, sq])
                ph.op("dve", I_ts(sq.t[:TP, :], sq.t[:TP, :], 1.0 / D, EPS, op0=ALU.mult, op1=ALU.add), r=[sq], w=[sq])
                ph.op("act", I_act(sq.t[:TP, :], sq.t[:TP, :], AF.Sqrt), r=[sq], w=[sq])
                ph.op("dve", I_recip(rs.t[:TP, :], sq.t[:TP, :]), r=[sq], w=[rs])
                ph.op("dve", I_stt(tm.t[:TP, :], x.t[:TP, :], rs.t[:TP, 0:1], gm_bc.t[:TP, :], ALU.mult, ALU.mult),
                      r=[x, rs, gm_bc], w=[tm])
                ph.op("pool", I_tt(h.t[:TP, :], tm.t[:TP, :], sh_bc.t[:TP, :], ALU.add), r=[tm, sh_bc], w=[h])
                for c in range(8):
                    ph.op("pe", I_tr(pT.t[:, c, :TP], h.t[:TP, c * 128:(c + 1) * 128], idb.t[:TP, :TP]), r=[h, idb], w=[pT])
                ph.op("act", I_act(ht.t[:, :, j * TP:(j + 1) * TP], pT.t[:, :, :TP], AF.Copy), r=[pT], w=[ht])
                for c in range(8):
                    ph.op("pe", I_mm(pk.t[:TP, :], ht.t[:, c, j * TP:(j + 1) * TP], wbf.t[:, c, OFF_K:OFF_K + 512],
                                     start=(c == 0), stop=(c == 7)), r=[ht, wbf], w=[pk])
                for c in range(8):
                    ph.op("pe", I_mm(pv.t[:TP, :], ht.t[:, c, j * TP:(j + 1) * TP], wbf.t[:, c, OFF_V:OFF_V + 512],
                                     start=(c == 0), stop=(c == 7)), r=[ht, wbf], w=[pv])
                for c in range(8):
                    ph.op("pe", I_mm(pki.t[:TP, :], ht.t[:, c, j * TP:(j + 1) * TP], wbf.t[:, c, OFF_KI:OFF_KI + 72],
                                     start=(c == 0), stop=(c == 7)), r=[ht, wbf], w=[pki])
                ph.op("act", I_act(kv.t[:TP, 0:512], pk.t[:TP, :], AF.Copy), r=[pk], w=[kv])
                ph.op("dve", I_copy(kv.t[:TP, 512:1024], pv.t[:TP, :]), r=[pv], w=[kv])
                ph.op("dve", I_copy(vs_.t[:TP, :, 0:64], pv.t[:TP, :].rearrange("p (h d) -> p h d", d=64)), r=[pv], w=[vs_])
                ph.op("act", I_act(kis_.t[:TP, :], pki.t[:TP, 0:64], AF.Copy), r=[pki], w=[kis_])
                ph.op("dve", I_ts(wis_.t[:TP, :], pki.t[:TP, 64:72], WISC, None, op0=ALU.mult), r=[pki], w=[wis_])
                ph.dma(I_dma(s.kout[l][r0:r0 + TP, :], kv.t[:TP, 0:512]), r=[kv])
                ph.dma(I_dma(s.vout[l][r0:r0 + TP, :], kv.t[:TP, 512:1024]), r=[kv])
                ph.dma(I_dma(s.vaug[s.koff + r0:s.koff + r0 + TP, :], vs_.t[:TP, :, :].rearrange("p h d -> p (h d)")), r=[vs_])
                ph.dma(I_dma(s.kiout[l][r0:r0 + TP, :], kis_.t[:TP, :]), r=[kis_])
                ph.dma(I_dma(s.wiS[r0:r0 + TP, :], wis_.t[:TP, :]), r=[wis_])
            t0 = st * NW
            fm = []
            for i in range(4):
                fm.append((OFF_U + i * 128, 128, "copy", s.uT[i * 128:(i + 1) * 128, t0:t0 + NW]))
                fm.append((OFF_Q + i * 128, 128, "q", s.qT[i * 128:(i + 1) * 128, t0:t0 + NW]))
                fm.append((OFF_K + i * 128, 128, "copy", s.kT[i * 128:(i + 1) * 128, s.koff + t0:s.koff + t0 + NW]))
                fm.append((OFF_QI + i * 128, 128, "copy", s.qiT[i * 128:(i + 1) * 128, t0:t0 + NW]))
            fm.append((OFF_KI, 64, "copy", s.kiT[0:64, s.koff + t0:s.koff + t0 + NW]))
            for i in range(4):
                fm.append((OFF_ZS + i * 128, 128, "silu", s.zsT[i * 128:(i + 1) * 128, t0:t0 + NW]))
            for i in range(4):
                fm.append((OFF_ZA + i * 128, 128, "silu", s.zaT[i * 128:(i + 1) * 128, t0:t0 + NW]))
            for i in range(16):
                fm.append((OFF_GM + i * 128, 128, "sig", s.gmT[i * 128:(i + 1) * 128, t0:t0 + NW]))
            for (off, M, kind, dst) in fm:
                p = pf[nf % 3]; f = fst[nf % 4]; nf += 1
                for c in range(8):
                    ph.op("pe", I_mm(p.t[:M, :NW], wbf.t[:, c, off:off + M], ht.t[:, c, :NW], start=(c == 0), stop=(c == 7)),
                          r=[wbf, ht], w=[p])
                if kind == "copy":
                    ph.op("dve", I_copy(f.t[:M, :NW], p.t[:M, :NW]), r=[p], w=[f])
                elif kind == "q":
                    ph.op("dve", I_ts(f.t[:M, :NW], p.t[:M, :NW], 0.125, None, op0=ALU.mult), r=[p], w=[f])
                elif kind == "silu":
                    ph.op("act", I_act(f.t[:M, :NW], p.t[:M, :NW], AF.Silu), r=[p], w=[f])
                else:
                    ph.op("act", I_act(f.t[:M, :NW], p.t[:M, :NW], AF.Sigmoid), r=[p], w=[f])
                ph.dma(I_dma(dst, f.t[:M, :NW]), r=[f])
    ph.finish()


def make_in_maps(inp, n_cores=8):
    f = lambda a: np.ascontiguousarray(np.asarray(a, dtype=np.float32))
    shared = {
        "w_mod": f(inp["w_mod"]), "b_mod": f(inp["b_mod"]), "g_norm": f(inp["g_norm"]), "w_in": f(inp["w_in"]),
        "a_re": f(inp["a_re"]).reshape(NL, 2048), "a_im": f(inp["a_im"]).reshape(NL, 2048),
        "log_dt": f(inp["log_dt"]),
        "b_re": f(inp["b_re"]).reshape(NL, 2048, 16), "b_im": f(inp["b_im"]).reshape(NL, 2048, 16),
        "c_reT": f(np.transpose(np.asarray(inp["c_re"]), (0, 1, 3, 2))).reshape(NL, 2048, 16),
        "c_imT": f(np.transpose(np.asarray(inp["c_im"]), (0, 1, 3, 2))).reshape(NL, 2048, 16),
        "d_skip": f(inp["d_skip"]).reshape(NL, 512),
        "w_glu": f(inp["w_glu"]), "b_glu": f(inp["b_glu"]), "w_ps": f(inp["w_ps"]), "w_pa": f(inp["w_pa"]),
        "w_o": f(inp["w_o"]), "g_final": f(inp["g_final"]).reshape(1, D),
    }
    xp = np.asarray(inp["x_prompt"]); xs = np.asarray(inp["x_sample"])
    ck = np.asarray(inp["cache_k"]); cv = np.asarray(inp["cache_v"]); cki = np.asarray(inp["cache_kidx"])
    sr = np.asarray(inp["state_ssm_re"]); si = np.asarray(inp["state_ssm_im"])
    cp = np.asarray(inp["c_prompt"]); cs = np.asarray(inp["c_sample"])
    PAST = ck.shape[2]
    maps = []
    for b in range(n_cores):
        m = dict(shared)
        m["xp"] = f(xp[b]); m["xs"] = f(xs[2 * b:2 * b + 2])
        m["ck"] = f(ck[:, 2 * b:2 * b + 2]).reshape(NL, 2, PAST, 512)
        m["cv"] = f(cv[:, 2 * b:2 * b + 2]).reshape(NL, 2, PAST, 512)
        m["cki"] = f(cki[:, 2 * b:2 * b + 2])
        m["ssr"] = f(sr[:, 2 * b:2 * b + 2]).reshape(NL, 2, 2048)
        m["ssi"] = f(si[:, 2 * b:2 * b + 2]).reshape(NL, 2, 2048)
        m["call"] = f(np.concatenate([cp[b:b + 1], cs[2 * b:2 * b + 2]], axis=0))
        maps.append(m)
    return maps


def assemble(results, T, TS, n_cores=8):
    R = results
    cat = lambda k: np.stack([np.asarray(r[k]) for r in R], axis=0)
    yp = cat("yp")
    ys = cat("ys").reshape(2 * n_cores, TS, D)
    kp = np.transpose(cat("kp"), (1, 0, 2, 3)).reshape(NL, n_cores, T, 8, 64)
    vp = np.transpose(cat("vp"), (1, 0, 2, 3)).reshape(NL, n_cores, T, 8, 64)
    kip = np.transpose(cat("kip"), (1, 0, 2, 3))
    srp = np.transpose(cat("srp"), (1, 0, 2)).reshape(NL, n_cores, 32, 64)
    sip = np.transpose(cat("sip"), (1, 0, 2)).reshape(NL, n_cores, 32, 64)
    ks = np.transpose(cat("ks"), (1, 0, 2, 3, 4)).reshape(NL, 2 * n_cores, TS, 8, 64)
    vs = np.transpose(cat("vs"), (1, 0, 2, 3, 4)).reshape(NL, 2 * n_cores, TS, 8, 64)
    kis = np.transpose(cat("kis"), (1, 0, 2, 3, 4)).reshape(NL, 2 * n_cores, TS, 64)
    srs = np.transpose(cat("srs"), (1, 0, 2, 3)).reshape(NL, 2 * n_cores, 32, 64)
    sis = np.transpose(cat("sis"), (1, 0, 2, 3)).reshape(NL, 2 * n_cores, 32, 64)
    outs = (yp, ys, kp, vp, kip, srp, sip, ks, vs, kis, srs, sis)
    return tuple(np.ascontiguousarray(o, dtype=np.float32) for o in outs)


_NC_CACHE = {}


def kernel(**inputs):
    T = int(np.asarray(inputs["x_prompt"]).shape[1])
    TS = int(np.asarray(inputs["x_sample"]).shape[1])
    PAST = int(np.asarray(inputs["cache_k"]).shape[2])
    n_cores = int(np.asarray(inputs["x_prompt"]).shape[0])
    key = (T, TS, PAST)
    nc = build(T=T, TS=TS, PAST=PAST)
    maps = make_in_maps(inputs, n_cores)
    res = run_bass_kernel_spmd(nc, maps, core_ids=list(range(n_cores)))
    return assemble(res.results, T, TS, n_cores)


def phase_cache(nc, A, l, seqs, PAST):
    ph = Phase(nc, f"C{l}")
    idf, idb = mk_ident(ph)
    ckt = [ph.sb([128, 512], F32, f"ckt{i}") for i in range(2)]
    cvt = [ph.sb([128, 512], F32, f"cvt{i}") for i in range(2)]
    cit = [ph.sb([128, 64], F32, f"cit{i}") for i in range(2)]
    ckb = [ph.sb([128, 512], BF16, f"ckb{i}") for i in range(2)]
    cib = [ph.sb([128, 64], BF16, f"cib{i}") for i in range(2)]
    kst = [ph.sb([128, 4, 128], BF16, f"kst{i}") for i in range(2)]
    kis = [ph.sb([64, 128], BF16, f"kis{i}") for i in range(2)]
    vst = [ph.sb([128, 8, 65], BF16, f"vst{i}") for i in range(2)]
    for v in vst:
        ph.op("pool", I_memset(v.t[:], 1.0), w=[v])
    pT = [ph.ps([128, 4, 128], BF16, f"pT{i}") for i in range(2)]
    pI = [ph.ps([64, 128], BF16, f"pI{i}") for i in range(2)]
    n = 0
    for s in seqs[1:]:
        j = s.i - 1
        for kt in range(PAST // 128):
            i = n % 2; n += 1
            r0 = kt * 128
            ph.dma(I_dma(ckt[i].t[:], A["ck"][l, j, r0:r0 + 128, :]), w=[ckt[i]])
            ph.dma(I_dma(cvt[i].t[:], A["cv"][l, j, r0:r0 + 128, :]), w=[cvt[i]])
            ph.dma(I_dma(cit[i].t[:], A["cki"][l, j, r0:r0 + 128, :]), w=[cit[i]])
            ph.op("act", I_act(ckb[i].t[:], ckt[i].t[:], AF.Copy), r=[ckt[i]], w=[ckb[i]])
            ph.op("act", I_act(cib[i].t[:], cit[i].t[:], AF.Copy), r=[cit[i]], w=[cib[i]])
            for hp in range(4):
                ph.op("pe", I_tr(pT[i].t[:, hp, :], ckb[i].t[:, hp * 128:(hp + 1) * 128], idb.t[:, :]), r=[ckb[i], idb], w=[pT[i]])
            ph.op("pe", I_tr(pI[i].t[:, :], cib[i].t[:, :], idb.t[:, :]), r=[cib[i], idb], w=[pI[i]])
            ph.op("dve", I_copy(kst[i].t[:], pT[i].t[:]), r=[pT[i]], w=[kst[i]])
            ph.op("dve", I_copy(kis[i].t[:], pI[i].t[:]), r=[pI[i]], w=[kis[i]])
            ph.op("pool", I_copy(vst[i].t[:, :, 0:64], cvt[i].t[:, :].rearrange("p (h d) -> p h d", d=64)), r=[cvt[i]], w=[vst[i]])
            ph.dma(I_dma(s.kT.rearrange("(hp p) s -> p hp s", p=128)[:, :, r0:r0 + 128], kst[i].t[:]), r=[kst[i]])
            ph.dma(I_dma(s.kiT[:, r0:r0 + 128], kis[i].t[:]), r=[kis[i]])
            ph.dma(I_dma(s.vaug[r0:r0 + 128, :], vst[i].t[:].rearrange("p h d -> p (h d)")), r=[vst[i]])
    ph.finish()


def phase_merge(nc, A, l, seqs, modbc):
    ph = Phase(nc, f"G{l}")
    last = (l == NL - 1)
    wps = ph.sb([128, 4, D], BF16, "wps"); wpa = ph.sb([128, 4, D], BF16, "wpa"); wo = ph.sb([128, 8, D], BF16, "wo")
    for c in range(4):
        ph.dma(I_dma(wps.t[:, c, :], A["w_ps"][l, c * 128:(c + 1) * 128, :]), w=[wps], q="pool")
        ph.dma(I_dma(wpa.t[:, c, :], A["w_pa"][l, c * 128:(c + 1) * 128, :]), w=[wpa], q="pool")
    for c in range(8):
        ph.dma(I_dma(wo.t[:, c, :], A["w_o"][l, c * 128:(c + 1) * 128, :]), w=[wo], q="pool")
    gate = ph.sb([128, D], F32, "gate")
    gfin = ph.sb([128, D], F32, "gfin")
    if last:
        ph.dma(I_dma(gfin.t[:], bcast_rows(A["g_final"][0:1, :], 128)), w=[gfin])
    yst = [ph.sb([128, 4, 512], BF16, f"yst{i}") for i in range(2)]
    yat = [ph.sb([128, 4, 512], BF16, f"yat{i}") for i in range(2)]
    gmt = [ph.sb([128, 16, 512], BF16, f"gmt{i}") for i in range(2)]
    mg = [ph.sb([128, 8, 512], BF16, f"mg{i}") for i in range(2)]
    tA = [ph.sb([128, 512], F32, f"tA{i}") for i in range(2)]
    tB = [ph.sb([128, 512], F32, f"tB{i}") for i in range(2)]
    xt = [ph.sb([128, D], F32, f"xt{i}") for i in range(2)]
    xn = [ph.sb([128, D], F32, f"xn{i}") for i in range(2)]
    yo = [ph.sb([128, D], F32, f"yo{i}") for i in range(2)]
    junk = ph.sb([128, D], BF16, "junk")
    ss = [ph.sb([128, 1], F32, f"ss{i}") for i in range(2)]
    rs = [ph.sb([128, 1], F32, f"rs{i}") for i in range(2)]
    pA = [ph.ps([128, 512], F32, f"pA{i}") for i in range(2)]
    pB = [ph.ps([128, 512], F32, f"pB{i}") for i in range(2)]
    pO = [ph.ps([128, 512], F32, f"pO{i}") for i in range(2)]
    nn = 0
    nt = 0
    for s in seqs:
        ph.dma(I_dma(gate.t[:], modbc[l, s.i, 2]), w=[gate])
        xsrc = s.xin if l == 0 else s.xres
        TP = min(128, s.T); NW = min(512, s.T)
        for st in range(s.T // NW):
            t0 = st * NW
            ys_ = yst[st % 2]; ya_ = yat[st % 2]; gm_ = gmt[st % 2]; m_ = mg[st % 2]
            ph.dma(I_dma(ys_.t[:, :, :NW], s.ysT.rearrange("(c p) t -> p c t", p=128)[:, :, t0:t0 + NW]), w=[ys_])
            ph.dma(I_dma(ya_.t[:, :, :NW], s.yaT.rearrange("(c p) t -> p c t", p=128)[:, :, t0:t0 + NW]), w=[ya_])
            ph.dma(I_dma(gm_.t[:, :, :NW], s.gmT.rearrange("(c p) t -> p c t", p=128)[:, :, t0:t0 + NW]), w=[gm_])
            for ct in range(8):
                a = pA[nn % 2]; b = pB[nn % 2]; ta = tA[nn % 2]; tb = tB[nn % 2]; nn += 1
                for c in range(4):
                    ph.op("pe", I_mm(a.t[:, :NW], wps.t[:, c, ct * 128:(ct + 1) * 128], ys_.t[:, c, :NW], start=(c == 0), stop=(c == 3)),
                          r=[wps, ys_], w=[a])
                for c in range(4):
                    ph.op("pe", I_mm(b.t[:, :NW], wpa.t[:, c, ct * 128:(ct + 1) * 128], ya_.t[:, c, :NW], start=(c == 0), stop=(c == 3)),
                          r=[wpa, ya_], w=[b])
                ph.op("dve", I_tt(ta.t[:, :NW], a.t[:, :NW], gm_.t[:, ct, :NW], ALU.mult), r=[a, gm_], w=[ta])
                ph.op("dve", I_tt(tb.t[:, :NW], b.t[:, :NW], gm_.t[:, 8 + ct, :NW], ALU.mult), r=[b, gm_], w=[tb])
                ph.op("pool", I_tt(m_.t[:, ct, :NW], ta.t[:, :NW], tb.t[:, :NW], ALU.add), r=[ta, tb], w=[m_])
            for j in range(NW // TP):
                r0 = t0 + j * TP
                x = xt[nt % 2]; xo = xn[nt % 2]; y_ = yo[nt % 2]; sq = ss[nt % 2]; r_ = rs[nt % 2]
                ph.dma(I_dma(x.t[:TP, :], xsrc[r0:r0 + TP, :]), w=[x])
                for half in range(2):
                    po = pO[half]
                    hs = slice(half * 512, (half + 1) * 512)
                    for c in range(8):
                        ph.op("pe", I_mm(po.t[:TP, :], m_.t[:, c, j * TP:(j + 1) * TP], wo.t[:, c, hs], start=(c == 0), stop=(c == 7)),
                              r=[m_, wo], w=[po])
                    ph.op("dve", I_tt(xo.t[:TP, hs], po.t[:TP, :], gate.t[:TP, hs], ALU.mult), r=[po, gate], w=[xo])
                ph.op("pool", I_tt(xo.t[:TP, :], xo.t[:TP, :], x.t[:TP, :], ALU.add), r=[xo, x], w=[xo])
                nt += 1
                if not last:
                    ph.dma(I_dma(s.xres[r0:r0 + TP, :], xo.t[:TP, :]), r=[xo])
                else:
                    ph.op("act", I_act(junk.t[:TP, :], xo.t[:TP, :], AF.Square, accum_out=sq.t[:TP, 0:1]), r=[xo], w=[junk, sq])
                    ph.op("dve", I_ts(sq.t[:TP, :], sq.t[:TP, :], 1.0 / D, EPS, op0=ALU.mult, op1=ALU.add), r=[sq], w=[sq])
                    ph.op("act", I_act(sq.t[:TP, :], sq.t[:TP, :], AF.Sqrt), r=[sq], w=[sq])
                    ph.op("dve", I_recip(r_.t[:TP, :], sq.t[:TP, :]), r=[sq], w=[r_])
                    ph.op("dve", I_stt(y_.t[:TP, :], xo.t[:TP, :], r_.t[:TP, 0:1], gfin.t[:TP, :], ALU.mult, ALU.mult),
                          r=[xo, r_, gfin], w=[y_])
                    ph.dma(I_dma(s.yout[r0:r0 + TP, :], y_.t[:TP, :]), r=[y_])
    ph.finish()


NIT = 22
W0 = 64.0


def phase_attn(nc, A, l, s):
    ph = Phase(nc, f"A{l}{s.nm}")
    idf, idb = mk_ident(ph)
    T, S = s.T, s.S
    QP = min(128, T); QB = min(512, T)
    L = T if s.causal else S
    KSEL = float(min(TOPK, L // 4))
    ktiles = [(k0, min(128, S - k0)) for k0 in range(0, S, 128)]
    NKT = len(ktiles)
    kiTa = ph.sb([128, S], BF16, "kiTa"); kiTb = ph.sb([128, S], BF16, "kiTb")
    ph.op("pool", I_memset(kiTa.t[64:128, :], 0.0), w=[kiTa])
    ph.op("pool", I_memset(kiTb.t[0:64, :], 0.0), w=[kiTb])
    ph.dma(I_dma(kiTa.t[0:64, :], s.kiT[:, :]), w=[kiTa])
    ph.dma(I_dma(kiTb.t[64:128, :], s.kiT[:, :]), w=[kiTb])
    kiTz = (kiTa, kiTb)
    sc = ph.sb([128, S], F32, "sc")
    mk = ph.sb([128, S], BF16, "mk")
    maskT = ph.sb([128, NKT, QB], BF16, "maskT")
    qit = [ph.sb([128, 4, 128], BF16, f"qit{i}") for i in range(2)]
    wit = [ph.sb([128, 8], F32, f"wit{i}") for i in range(2)]
    dg = [ph.sb([128, 8, 128], BF16, f"dg{i}") for i in range(2)]
    rl = [ph.sb([128, 512], BF16, f"rl{i}") for i in range(2)]
    mid = ph.sb([128, 1], F32, "mid"); cnt = ph.sb([128, 1], F32, "cnt"); tq = ph.sb([128, 1], F32, "tq")
    thr = ph.sb([128, 1], F32, "thr")
    negmid = ph.sb([128, 1], F32, "negmid"); sact = ph.sb([128, 1], F32, "sact"); comb = ph.sb([128, 1], F32, "comb")
    junk2 = ph.sb([128, S], BF16, "junk2")
    qta = ph.sb([128, 4, 512], BF16, "qta"); qtb = ph.sb([128, 4, 512], BF16, "qtb")
    ph.op("pool", I_memset(qta.t[64:128, :, :], 0.0), w=[qta])
    ph.op("pool", I_memset(qtb.t[0:64, :, :], 0.0), w=[qtb])
    qtz = (qta, qtb)
    va = [ph.sb([128, 260], BF16, f"va{i}") for i in range(2)]
    ktl = [ph.sb([128, 2, 128], BF16, f"ktl{i}") for i in range(2)]
    pe_ = [ph.sb([128, 512], BF16, f"pe{i}") for i in range(2)]
    pm = [ph.sb([128, 512], BF16, f"pm{i}") for i in range(2)]
    oa = [ph.sb([128, 512], F32, f"oa{i}") for i in range(2)]
    for o__ in oa:
        ph.op("pool", I_memset(o__.t[:], 0.0), w=[o__])
    rden = [ph.sb([64, 512], F32, f"rden{i}") for i in range(2)]
    t1 = [ph.sb([64, 512], F32, f"t1{i}") for i in range(2)]
    zat = [ph.sb([64, 512], BF16, f"zat{i}") for i in range(2)]
    yat = [ph.sb([64, 512], BF16, f"yat{i}") for i in range(2)]
    iop = ph.sb([128, 64], F32, "iop")
    selden = ph.sb([128, 64], F32, "selden")
    ph.op("pool", I_iota(iop.t[:], [[0, 64]], 0, 1), w=[iop])
    ph.op("dve", I_ts(selden.t[:], iop.t[:], 64.0, None, op0=ALU.is_equal), r=[iop], w=[selden])
    lg = [ph.ps([128, 512], F32, f"lg{i}") for i in range(2)]
    acc = ph.ps([128, 512], F32, "acc")
    mtp = ph.ps([128, 8, 128], BF16, "mtp")
    oacc = [ph.ps([128, 512], F32, f"oacc{i}") for i in range(4)]

    nq = 0
    nr = 0
    nev = 0
    nv = 0
    nh = 0
    for jb in range(T // QB):
        qb0 = jb * QB
        Sblk = min(S, (jb + 1) * QB) if s.causal else S
        kts = [(i, k0, ksz) for i, (k0, ksz) in enumerate(ktiles) if k0 < Sblk]
        for jq in range(QB // QP):
            q0 = qb0 + jq * QP
            qoff = jq * QP
            Slim = (q0 + QP) if s.causal else S
            qi_ = qit[nq % 2]; wi_ = wit[nq % 2]; dg_ = dg[nq % 2]; nq += 1
            ph.dma(I_dma(qi_.t[:, :, :QP], s.qiT.rearrange("(hp p) t -> p hp t", p=128)[:, :, q0:q0 + QP]), w=[qi_])
            ph.dma(I_dma(wi_.t[:QP, :], s.wiS[q0:q0 + QP, :]), w=[wi_])
            for h in range(8):
                ph.op("dve", I_ts(dg_.t[:QP, h, :QP], idf.t[:QP, :QP], wi_.t[:QP, h:h + 1], None, op0=ALU.mult),
                      r=[idf, wi_], w=[dg_])
            for c0 in range(0, Slim, 512):
                csz = min(512, Slim - c0)
                bufs_ = []
                for h in range(8):
                    bufs_.append((lg[nr % 2], rl[nr % 2])); nr += 1

                def emit_lg1(h, c0=c0, csz=csz, bufs_=bufs_, qi_=qi_):
                    g, r_ = bufs_[h]
                    ph.op("pe", I_mm(g.t[:QP, :csz], qi_.t[:, h // 2, :QP], kiTz[h % 2].t[:, c0:c0 + csz]),
                          r=[qi_, kiTz[h % 2]], w=[g])
                emit_lg1(0)
                for h in range(8):
                    g, r_ = bufs_[h]
                    if h < 7:
                        emit_lg1(h + 1)
                    ph.op("act", I_act(r_.t[:QP, :csz], g.t[:QP, :csz], AF.Relu), r=[g], w=[r_])
                    ph.op("pe", I_mm(acc.t[:QP, :csz], dg_.t[:QP, h, :QP], r_.t[:QP, :csz], start=(h == 0), stop=(h == 7)),
                          r=[dg_, r_], w=[acc])
                ph.op("dve", I_copy(sc.t[:QP, c0:c0 + csz], acc.t[:QP, :csz]), r=[acc], w=[sc])
            if s.causal and QP == 128:
                ph.op("dve", I_memset(sc.t[0:64, Slim - 64:Slim], NEG), w=[sc])
            ph.op("dve", I_memset(mid.t[:QP, :], 0.0), w=[mid])
            ph.op("dve", I_memset(negmid.t[:QP, :], 0.0), w=[negmid])
            S1 = Slim
            if Slim >= 1024:
                S1 = int(Slim * 0.45) // 128 * 128
            w = W02
            for k in range(NIT2):
                w = w / 2.0
                ph.op("dve", I_ts(mk.t[:QP, :S1], sc.t[:QP, :S1], mid.t[:QP, 0:1], None, op0=ALU.is_ge, op1=ALU.add,
                                  accum_out=cnt.t[:QP, 0:1]), r=[sc, mid], w=[mk, cnt])
                if S1 < Slim:
                    ph.op("act", I_act(junk2.t[:QP, S1:Slim], sc.t[:QP, S1:Slim], AF.Sign, bias=negmid.t[:QP, 0:1],
                                       accum_out=sact.t[:QP, 0:1]), r=[sc, negmid], w=[junk2, sact])
                    ph.op("dve", I_stt(comb.t[:QP, :], sact.t[:QP, :], 0.5, cnt.t[:QP, :], ALU.mult, ALU.add), r=[sact, cnt], w=[comb])
                    src_, kadj = comb, KSEL - 0.5 * (Slim - S1)
                else:
                    src_, kadj = cnt, KSEL
                ph.op("dve", I_ts(tq.t[:QP, :], src_.t[:QP, :], kadj, 2.0 * w, op0=ALU.is_ge, op1=ALU.mult), r=[src_], w=[tq])
                ph.op("dve", I_stt(mid.t[:QP, :], mid.t[:QP, :], -w, tq.t[:QP, :], ALU.add, ALU.add), r=[mid, tq], w=[mid])
                if S1 < Slim:
                    ph.op("dve", I_stt(negmid.t[:QP, :], negmid.t[:QP, :], w, tq.t[:QP, :], ALU.add, ALU.subtract), r=[negmid, tq], w=[negmid])
            ph.op("dve", I_ts(thr.t[:QP, :], mid.t[:QP, :], -w, None, op0=ALU.add), r=[mid], w=[thr])
            ph.op("dve", I_ts(mk.t[:QP, :Slim], sc.t[:QP, :Slim], thr.t[:QP, 0:1], None, op0=ALU.is_ge), r=[sc, thr], w=[mk])
            mine = [(i, k0, ksz) for (i, k0, ksz) in kts if k0 < Slim]
            for g0 in range(0, len(mine), 8):
                grp = mine[g0:g0 + 8]
                for gi, (i, k0, ksz) in enumerate(grp):
                    ph.op("pe", I_tr(mtp.t[:ksz, gi, :QP], mk.t[:QP, k0:k0 + ksz], idb.t[:QP, :QP]), r=[mk, idb], w=[mtp])
                i0 = grp[0][0]
                ng = len(grp)
                eng = "act" if nev % 2 == 0 else "dve"; nev += 1
                if eng == "act":
                    ph.op("act", I_act(maskT.t[:, i0:i0 + ng, qoff:qoff + QP], mtp.t[:, 0:ng, :QP], AF.Copy), r=[mtp], w=[maskT])
                else:
                    ph.op("dve", I_copy(maskT.t[:, i0:i0 + ng, qoff:qoff + QP], mtp.t[:, 0:ng, :QP]), r=[mtp], w=[maskT])
            rest = [(i, k0, ksz) for (i, k0, ksz) in kts if k0 >= Slim]
            if rest:
                i0 = rest[0][0]; i1 = rest[-1][0] + 1
                ph.op("pool", I_memset(maskT.t[:, i0:i1, qoff:qoff + QP], 0.0), w=[maskT])
        import os as _os
        if _os.environ.get("ATTN_STOP") == "idx":
            continue
        qsrc = s.qT.rearrange("(hp p) t -> p hp t", p=128)
        ph.dma(I_dma(qta.t[0:64, :, :QB], qsrc[0:64, :, qb0:qb0 + QB]), w=[qta])
        ph.dma(I_dma(qtb.t[64:128, :, :QB], qsrc[64:128, :, qb0:qb0 + QB]), w=[qtb])
        for hg in range(2):
            units = [(idx, hh) for idx in range(len(kts)) for hh in range(4)]
            ub = {}

            def emit_st1(u, hg=hg, units=units, ub=ub):
                nonlocal nv
                idx, hh = units[u]
                i, k0, ksz = kts[idx]
                if hh == 0:
                    va_ = va[nv % 2]; kt_ = ktl[nv % 2]; nv += 1
                    ph.dma(I_dma(va_.t[:ksz, :], s.vaug[k0:k0 + ksz, hg * 260:(hg + 1) * 260]), w=[va_])
                    ph.dma(I_dma(kt_.t[:, :, :ksz], s.kT.rearrange("(hp p) s -> p hp s", p=128)[:, 2 * hg:2 * hg + 2, k0:k0 + ksz]), w=[kt_])
                    ub[idx] = (va_, kt_)
                va_, kt_ = ub[idx]
                h = hg * 4 + hh
                st_ = lg[u % 2]
                ph.op("pe", I_mm(st_.t[:ksz, :QB], kt_.t[:, hh // 2, :ksz], qtz[h % 2].t[:, h // 2, :QB]),
                      r=[kt_, qtz[h % 2]], w=[st_])
            emit_st1(0)
            for u, (idx, hh) in enumerate(units):
                i, k0, ksz = kts[idx]
                va_, kt_ = ub[idx]
                st_ = lg[u % 2]; p_ = pe_[u % 2]; m_ = pm[u % 2]
                if u + 1 < len(units):
                    emit_st1(u + 1)
                ph.op("act", I_act(p_.t[:ksz, :QB], st_.t[:ksz, :QB], AF.Exp), r=[st_], w=[p_])
                ph.op("dve", I_tt(m_.t[:ksz, :QB], p_.t[:ksz, :QB], maskT.t[:ksz, i, :QB], ALU.mult), r=[p_, maskT], w=[m_])
                ph.op("pe", I_mm(oacc[hh].t[:65, :QB], va_.t[:ksz, hh * 65:(hh + 1) * 65], m_.t[:ksz, :QB], start=(idx == 0), stop=(idx == len(kts) - 1)),
                      r=[va_, m_], w=[oacc[hh]])
            if _os.environ.get("ATTN_STOP") in ("pv", "mm", "exp", "qk", "dma"):
                continue
            for hh in range(4):
                h = hg * 4 + hh
                o_ = oa[hh % 2]; rd = rden[hh % 2]; t_ = t1[hh % 2]; z_ = zat[hh % 2]; y_ = yat[hh % 2]
                ph.op("act", I_act(o_.t[:65, :QB], oacc[hh].t[:65, :QB], AF.Copy), r=[oacc[hh]], w=[o_])
                ph.op("pe", I_mm(acc.t[:64, :QB], selden.t[:, :64], o_.t[:, :QB]), r=[selden, o_], w=[acc])
                ph.op("dve", I_recip(rd.t[:64, :QB], acc.t[:64, :QB]), r=[acc], w=[rd])
                ph.dma(I_dma(z_.t[:64, :QB], s.zaT[64 * h:64 * h + 64, qb0:qb0 + QB]), w=[z_])
                ph.op("dve", I_tt(t_.t[:64, :QB], o_.t[0:64, :QB], rd.t[:64, :QB], ALU.mult), r=[o_, rd], w=[t_])
                ph.op("pool", I_tt(y_.t[:64, :QB], t_.t[:64, :QB], z_.t[:64, :QB], ALU.mult), r=[t_, z_], w=[y_])
                ph.dma(I_dma(s.yaT[64 * h:64 * h + 64, qb0:qb0 + QB], y_.t[:64, :QB]), r=[y_])
    ph.finish()


PI = float(np.pi)


def phase_ssm(nc, A, l, seqs):
    ph = Phase(nc, f"S{l}")
    idf, idb = mk_ident(ph)
    LB = min(512, seqs[0].T)
    sm = lambda nm: ph.sb([128, 16], F32, nm)
    are, aim, ldt, dt, xr, ang, mag = sm("are"), sm("aim"), sm("ldt"), sm("dt"), sm("xr"), sm("ang"), sm("mag")
    kac, angr, angc, angcr, sn1, cs1, abr, abi = sm("kac"), sm("angr"), sm("angc"), sm("angcr"), sm("sn1"), sm("cs1"), sm("abr"), sm("abi")
    t1, t2, den, rdn, nr_, fre, fim, t3, t4 = sm("t1"), sm("t2"), sm("den"), sm("rdn"), sm("nr"), sm("fre"), sm("fim"), sm("t3"), sm("t4")
    wr, wi_, wt = sm("wr"), sm("wi"), sm("wt")
    ph.dma(I_dma(are.t[:], A["a_re"][l].rearrange("(pt q) -> q pt", q=128)), w=[are])
    ph.dma(I_dma(aim.t[:], A["a_im"][l].rearrange("(pt q) -> q pt", q=128)), w=[aim])
    for gl in range(2):
        src = bass.AP(A["log_dt"].tensor, l * 32 + gl, [[0, 64], [2, 16]])
        ph.dma(I_dma(ldt.t[gl * 64:(gl + 1) * 64, :], src), w=[ldt])
    V = "dve"
    pa, pb2, pc = sm("pa"), sm("pb2"), sm("pc")

    def horner(dst, y, coefs):
        ph.op(V, I_memset(dst.t[:], 1.0), w=[dst])
        for c in reversed(coefs):
            ph.op(V, I_tt(dst.t[:], dst.t[:], y.t[:], ALU.mult), r=[dst, y], w=[dst])
            ph.op(V, I_ts(dst.t[:], dst.t[:], float(c), 1.0, op0=ALU.mult, op1=ALU.add), r=[dst], w=[dst])

    def exp_acc(dst, src, nsq, deg):
        ph.op(V, I_ts(pa.t[:], src.t[:], 1.0 / (2 ** nsq), None, op0=ALU.mult), r=[src], w=[pa])
        horner(dst, pa, [1.0 / k for k in range(1, deg + 1)])
        for _ in range(nsq):
            ph.op(V, I_tt(dst.t[:], dst.t[:], dst.t[:], ALU.mult), r=[dst], w=[dst])

    def sincos_acc(sdst, cdst, x):
        ph.op(V, I_ts(pa.t[:], x.t[:], 0.125, None, op0=ALU.mult), r=[x], w=[pa])
        ph.op(V, I_tt(pb2.t[:], pa.t[:], pa.t[:], ALU.mult), r=[pa], w=[pb2])
        horner(sdst, pb2, [-1.0 / 6, -1.0 / 20, -1.0 / 42, -1.0 / 72, -1.0 / 110])
        ph.op(V, I_tt(sdst.t[:], sdst.t[:], pa.t[:], ALU.mult), r=[sdst, pa], w=[sdst])
        horner(cdst, pb2, [-1.0 / 2, -1.0 / 12, -1.0 / 30, -1.0 / 56, -1.0 / 90, -1.0 / 132])
        for _ in range(3):
            ph.op(V, I_tt(pc.t[:], sdst.t[:], sdst.t[:], ALU.mult), r=[sdst], w=[pc])
            ph.op(V, I_stt(sdst.t[:], sdst.t[:], 2.0, cdst.t[:], ALU.mult, ALU.mult), r=[sdst, cdst], w=[sdst])
            ph.op(V, I_ts(cdst.t[:], pc.t[:], -2.0, 1.0, op0=ALU.mult, op1=ALU.add), r=[pc], w=[cdst])

    exp_acc(dt, ldt, 3, 14)
    ph.op(V, I_tt(xr.t[:], are.t[:], dt.t[:], ALU.mult), r=[are, dt], w=[xr])
    ph.op(V, I_tt(ang.t[:], aim.t[:], dt.t[:], ALU.mult), r=[aim, dt], w=[ang])
    exp_acc(mag, xr, 0, 8)

    def reduce_angle(src, dst):
        ph.op(V, I_ts(kac.t[:], src.t[:], PI, None, op0=ALU.is_gt), r=[src], w=[kac])
        for j in range(1, 6):
            ph.op(V, I_stt(kac.t[:], src.t[:], (2 * j + 1) * PI, kac.t[:], ALU.is_gt, ALU.add), r=[src, kac], w=[kac])
        ph.op(V, I_stt(dst.t[:], kac.t[:], -2.0 * PI, src.t[:], ALU.mult, ALU.add), r=[kac, src], w=[dst])
    reduce_angle(ang, angr)
    sincos_acc(sn1, cs1, angr)
    ph.op(V, I_tt(t1.t[:], sn1.t[:], sn1.t[:], ALU.mult), r=[sn1], w=[t1])
    ph.op(V, I_tt(t2.t[:], cs1.t[:], cs1.t[:], ALU.mult), r=[cs1], w=[t2])
    ph.op(V, I_tt(t3.t[:], t1.t[:], t2.t[:], ALU.add), r=[t1, t2], w=[t3])
    ph.op(V, I_ts(t3.t[:], t3.t[:], -0.5, 1.5, op0=ALU.mult, op1=ALU.add), r=[t3], w=[t3])
    ph.op(V, I_tt(sn1.t[:], sn1.t[:], t3.t[:], ALU.mult), r=[sn1, t3], w=[sn1])
    ph.op(V, I_tt(cs1.t[:], cs1.t[:], t3.t[:], ALU.mult), r=[cs1, t3], w=[cs1])
    ph.op(V, I_tt(abr.t[:], mag.t[:], cs1.t[:], ALU.mult), r=[mag, cs1], w=[abr])
    ph.op(V, I_tt(abi.t[:], mag.t[:], sn1.t[:], ALU.mult), r=[mag, sn1], w=[abi])
    ph.op(V, I_tt(t1.t[:], are.t[:], are.t[:], ALU.mult), r=[are], w=[t1])
    ph.op(V, I_tt(t2.t[:], aim.t[:], aim.t[:], ALU.mult), r=[aim], w=[t2])
    ph.op(V, I_tt(den.t[:], t1.t[:], t2.t[:], ALU.add), r=[t1, t2], w=[den])
    ph.op(V, I_recip(rdn.t[:], den.t[:]), r=[den], w=[rdn])
    ph.op(V, I_ts(nr_.t[:], abr.t[:], -1.0, None, op0=ALU.add), r=[abr], w=[nr_])
    ph.op(V, I_tt(t1.t[:], nr_.t[:], are.t[:], ALU.mult), r=[nr_, are], w=[t1])
    ph.op(V, I_tt(t2.t[:], abi.t[:], aim.t[:], ALU.mult), r=[abi, aim], w=[t2])
    ph.op(V, I_tt(t3.t[:], t1.t[:], t2.t[:], ALU.add), r=[t1, t2], w=[t3])
    ph.op(V, I_tt(fre.t[:], t3.t[:], rdn.t[:], ALU.mult), r=[t3, rdn], w=[fre])
    ph.op(V, I_tt(t1.t[:], abi.t[:], are.t[:], ALU.mult), r=[abi, are], w=[t1])
    ph.op(V, I_tt(t2.t[:], nr_.t[:], aim.t[:], ALU.mult), r=[nr_, aim], w=[t2])
    ph.op(V, I_tt(t4.t[:], t1.t[:], t2.t[:], ALU.subtract), r=[t1, t2], w=[t4])
    ph.op(V, I_tt(fim.t[:], t4.t[:], rdn.t[:], ALU.mult), r=[t4, rdn], w=[fim])

    bre_t = ph.sb([128, 16, 16], F32, "bre_t"); bim_t = ph.sb([128, 16, 16], F32, "bim_t")
    cre_t = ph.sb([128, 16, 16], F32, "cre_t"); cim_t = ph.sb([128, 16, 16], F32, "cim_t")
    for t_, nm in ((bre_t, "b_re"), (bim_t, "b_im"), (cre_t, "c_reT"), (cim_t, "c_imT")):
        ph.dma(I_dma(t_.t[:], A[nm][l].rearrange("(pt q) n -> q pt n", q=128)), w=[t_])
    padall = ph.sb([128, 16, 2, 128], F32, "padall")
    Cpad = ph.sb([128, 16, 2, 128], BF16, "Cpad")
    ph.op("pool", I_memset(padall.t[:], 0.0), w=[padall])
    ph.op("pool", I_memset(Cpad.t[:], 0.0), w=[Cpad])
    tb16 = [ph.sb([128, 16], F32, f"tb16{i}") for i in range(2)]
    for pt in range(16):
        qq = pt % 4
        for half in range(2):
            rows = slice(half * 64, (half + 1) * 64)
            cols = slice(32 * qq + 16 * half, 32 * qq + 16 * half + 16)
            ta, tb = tb16
            ph.op(V, I_ts(ta.t[rows, :], bim_t.t[rows, pt, :], fim.t[rows, pt:pt + 1], None, op0=ALU.mult), r=[bim_t, fim], w=[ta])
            ph.op(V, I_stt(padall.t[rows, pt, 0, cols], bre_t.t[rows, pt, :], fre.t[rows, pt:pt + 1], ta.t[rows, :], ALU.mult, ALU.subtract),
                  r=[bre_t, fre, ta], w=[padall])
            ph.op(V, I_ts(tb.t[rows, :], bre_t.t[rows, pt, :], fim.t[rows, pt:pt + 1], None, op0=ALU.mult), r=[bre_t, fim], w=[tb])
            ph.op(V, I_stt(padall.t[rows, pt, 1, cols], bim_t.t[rows, pt, :], fre.t[rows, pt:pt + 1], tb.t[rows, :], ALU.mult, ALU.add),
                  r=[bim_t, fre, tb], w=[padall])
            ph.op("pool", I_copy(Cpad.t[rows, pt, 0, cols], cre_t.t[rows, pt, :]), r=[cre_t], w=[Cpad])
            ph.op("pool", I_ts(Cpad.t[rows, pt, 1, cols], cim_t.t[rows, pt, :], -1.0, None, op0=ALU.mult), r=[cim_t], w=[Cpad])
    pB = [ph.ps([128, 512], F32, f"pB{i}") for i in range(4)]
    pY = [ph.ps([128, 512], F32, f"pY{i}") for i in range(2)]
    pG = ph.ps([128, 512], F32, "pG")
    E = [[ph.sb([128, 128], BF16, f"E{part}_{pt}") for pt in range(16)] for part in range(2)]
    for part in range(2):
        for pt in range(16):
            ph.op("pe", I_mm(pG.t[:, 0:128], padall.t[:, pt, part, :], idf.t[:, :]), r=[padall, idf], w=[pG])
            ph.op("act", I_act(E[part][pt].t[:], pG.t[:, 0:128], AF.Copy), r=[pG], w=[E[part][pt]])
    dsk = ph.sb([128, 4], F32, "dsk")
    ph.dma(I_dma(dsk.t[:], A["d_skip"][l].rearrange("(ct q) -> q ct", q=128)), w=[dsk])
    bglu = ph.sb([128, 4], F32, "bglu")
    ph.dma(I_dma(bglu.t[:], A["b_glu"][l].rearrange("(co q) -> q co", q=128)), w=[bglu])
    wglu = ph.sb([128, 4, 512], BF16, "wglu")
    for c in range(4):
        ph.dma(I_dma(wglu.t[:, c, :], A["w_glu"][l, c * 128:(c + 1) * 128, :]), w=[wglu], q="pool")
    cs = ph.sb([128, 16, LB], F32, "cs"); sn = ph.sb([128, 16, LB], F32, "sn")
    ph.op("pool", I_memset(cs.t[:, :, 0:1], 1.0), w=[cs])
    ph.op("pool", I_memset(sn.t[:, :, 0:1], 0.0), w=[sn])
    ph.op(V, I_copy(wr.t[:], cs1.t[:]), r=[cs1], w=[wr])
    ph.op(V, I_copy(wi_.t[:], sn1.t[:]), r=[sn1], w=[wi_])
    tmpc = ph.sb([128, LB], F32, "tmpc"); tmps = ph.sb([128, LB], F32, "tmps")
    n = 1
    while n < LB:
        for pt in range(16):
            ph.op(V, I_ts(tmpc.t[:, 0:n], sn.t[:, pt, 0:n], wi_.t[:, pt:pt + 1], None, op0=ALU.mult), r=[sn, wi_], w=[tmpc])
            ph.op("pool", I_ts(tmps.t[:, 0:n], cs.t[:, pt, 0:n], wi_.t[:, pt:pt + 1], None, op0=ALU.mult), r=[cs, wi_], w=[tmps])
            ph.op(V, I_stt(cs.t[:, pt, n:2 * n], cs.t[:, pt, 0:n], wr.t[:, pt:pt + 1], tmpc.t[:, 0:n], ALU.mult, ALU.subtract),
                  r=[cs, wr, tmpc], w=[cs])
            ph.op(V, I_stt(sn.t[:, pt, n:2 * n], sn.t[:, pt, 0:n], wr.t[:, pt:pt + 1], tmps.t[:, 0:n], ALU.mult, ALU.add),
                  r=[sn, wr, tmps], w=[sn])
        ph.op(V, I_tt(t1.t[:], wi_.t[:], wi_.t[:], ALU.mult), r=[wi_], w=[t1])
        ph.op(V, I_tt(t2.t[:], wr.t[:], wr.t[:], ALU.mult), r=[wr], w=[t2])
        ph.op(V, I_stt(wt.t[:], wr.t[:], 2.0, wi_.t[:], ALU.mult, ALU.mult), r=[wr, wi_], w=[wt])
        ph.op(V, I_tt(wr.t[:], t2.t[:], t1.t[:], ALU.subtract), r=[t1, t2], w=[wr])
        ph.op(V, I_copy(wi_.t[:], wt.t[:]), r=[wt], w=[wi_])
        n *= 2

    wk = lambda nm: ph.sb([128, LB], F32, nm)
    WK = [[wk(n_ + str(k_)) for n_ in ("m1", "m2", "m3", "m4", "bre", "bim", "gre", "gim", "hre", "him")] for k_ in range(2)]
    HB = [(ph.sb([128, LB], BF16, f"hreb{k_}"), ph.sb([128, LB], BF16, f"himb{k_}")) for k_ in range(2)]
    yv, x2, inn, in2, sg = [wk(n_) for n_ in ("yv", "x2", "inn", "in2", "sg")]
    yg = [wk(f"yg{i}") for i in range(4)]
    ygb = [ph.sb([128, LB], BF16, f"ygb{i}") for i in range(4)]
    ut = [[ph.sb([128, LB], BF16, f"ut{k}{i}") for i in range(4)] for k in range(2)]
    zst = [ph.sb([128, LB], BF16, f"zst{i}") for i in range(2)]
    sgl = wk("sgl"); tg = wk("tg")
    ysf = [ph.sb([128, LB], BF16, f"ysf{i}") for i in range(2)]
    hpr = [sm(f"hpr{i}") for i in range(3)]; hpi = [sm(f"hpi{i}") for i in range(3)]
    a0 = ph.sb([128, 4], F32, "a0")
    nb = 0
    nz = 0
    for s in seqs:
        Lb = min(LB, s.T)
        hr_, hi_ = hpr[s.i], hpi[s.i]
        prev = s.i > 0
        if prev:
            ph.dma(I_dma(hr_.t[:], A["ssr"][l, s.i - 1].rearrange("(pt q) -> q pt", q=128)), w=[hr_])
            ph.dma(I_dma(hi_.t[:], A["ssi"][l, s.i - 1].rearrange("(pt q) -> q pt", q=128)), w=[hi_])
        for blk in range(s.T // Lb):
            t0 = blk * Lb
            u_ = ut[blk % 2]
            for ct in range(4):
                ph.dma(I_dma(u_[ct].t[:, :Lb], s.uT[ct * 128:(ct + 1) * 128, t0:t0 + Lb]), w=[u_[ct]])
            for ct in range(4):
                py = pY[ct % 2]
                for qq in range(4):
                    pt = ct * 4 + qq
                    pr = pB[(nb % 2) * 2]; pi_ = pB[(nb % 2) * 2 + 1]
                    m1, m2, m3, m4, bre_, bim_, gre, gim, hre, him = WK[nb % 2]
                    hreb, himb = HB[nb % 2]
                    nb += 1
                    ph.op("pe", I_mm(pr.t[:, :Lb], E[0][pt].t[:, :], u_[ct].t[:, :Lb]), r=[E[0][pt], u_[ct]], w=[pr])
                    ph.op("pe", I_mm(pi_.t[:, :Lb], E[1][pt].t[:, :], u_[ct].t[:, :Lb]), r=[E[1][pt], u_[ct]], w=[pi_])
                    c_ = cs.t[:, pt, :Lb]; s_ = sn.t[:, pt, :Lb]
                    ph.op(V, I_tt(m1.t[:, :Lb], pr.t[:, :Lb], c_, ALU.mult), r=[pr, cs], w=[m1])
                    ph.op(V, I_tt(m2.t[:, :Lb], pi_.t[:, :Lb], s_, ALU.mult), r=[pi_, sn], w=[m2])
                    ph.op(V, I_tt(m3.t[:, :Lb], pi_.t[:, :Lb], c_, ALU.mult), r=[pi_, cs], w=[m3])
                    ph.op(V, I_tt(m4.t[:, :Lb], pr.t[:, :Lb], s_, ALU.mult), r=[pr, sn], w=[m4])
                    ph.op("pool", I_tt(bre_.t[:, :Lb], m1.t[:, :Lb], m2.t[:, :Lb], ALU.add), r=[m1, m2], w=[bre_])
                    ph.op("pool", I_tt(bim_.t[:, :Lb], m3.t[:, :Lb], m4.t[:, :Lb], ALU.subtract), r=[m3, m4], w=[bim_])
                    if prev:
                        ph.op(V, I_tt(a0.t[:, 0:1], abi.t[:, pt:pt + 1], hi_.t[:, pt:pt + 1], ALU.mult), r=[abi, hi_], w=[a0])
                        ph.op(V, I_stt(a0.t[:, 1:2], abr.t[:, pt:pt + 1], hr_.t[:, pt:pt + 1], a0.t[:, 0:1], ALU.mult, ALU.subtract),
                              r=[abr, hr_, a0], w=[a0])
                        ph.op(V, I_tt(a0.t[:, 2:3], abi.t[:, pt:pt + 1], hr_.t[:, pt:pt + 1], ALU.mult), r=[abi, hr_], w=[a0])
                        ph.op(V, I_stt(a0.t[:, 3:4], abr.t[:, pt:pt + 1], hi_.t[:, pt:pt + 1], a0.t[:, 2:3], ALU.mult, ALU.add),
                              r=[abr, hi_, a0], w=[a0])
                        ph.op(V, I_tt(bre_.t[:, 0:1], bre_.t[:, 0:1], a0.t[:, 1:2], ALU.add), r=[bre_, a0], w=[bre_])
                        ph.op(V, I_tt(bim_.t[:, 0:1], bim_.t[:, 0:1], a0.t[:, 3:4], ALU.add), r=[bim_, a0], w=[bim_])
                    rb = mag.t[:, pt:pt + 1].to_broadcast([128, Lb])
                    ph.op(V, I_scan(gre.t[:, :Lb], rb, bre_.t[:, :Lb], 0.0), r=[mag, bre_], w=[gre])
                    ph.op(V, I_scan(gim.t[:, :Lb], rb, bim_.t[:, :Lb], 0.0), r=[mag, bim_], w=[gim])
                    ph.op(V, I_tt(m1.t[:, :Lb], gre.t[:, :Lb], c_, ALU.mult), r=[gre, cs], w=[m1])
                    ph.op(V, I_tt(m2.t[:, :Lb], gim.t[:, :Lb], s_, ALU.mult), r=[gim, sn], w=[m2])
                    ph.op("pool", I_tt(m3.t[:, :Lb], gre.t[:, :Lb], s_, ALU.mult), r=[gre, sn], w=[m3])
                    ph.op("pool", I_tt(m4.t[:, :Lb], gim.t[:, :Lb], c_, ALU.mult), r=[gim, cs], w=[m4])
                    ph.op(V, I_tt(hre.t[:, :Lb], m1.t[:, :Lb], m2.t[:, :Lb], ALU.subtract), r=[m1, m2], w=[hre])
                    ph.op("pool", I_tt(him.t[:, :Lb], m3.t[:, :Lb], m4.t[:, :Lb], ALU.add), r=[m3, m4], w=[him])
                    ph.op("act", I_act(hr_.t[:, pt:pt + 1], hre.t[:, Lb - 1:Lb], AF.Copy), r=[hre], w=[hr_])
                    ph.op("act", I_act(hi_.t[:, pt:pt + 1], him.t[:, Lb - 1:Lb], AF.Copy), r=[him], w=[hi_])
                    ph.op("act", I_act(hreb.t[:, :Lb], hre.t[:, :Lb], AF.Copy), r=[hre], w=[hreb])
                    ph.op("act", I_act(himb.t[:, :Lb], him.t[:, :Lb], AF.Copy), r=[him], w=[himb])
                    ph.op("pe", I_mm(py.t[:, :Lb], Cpad.t[:, pt, 0, :], hreb.t[:, :Lb], start=(qq == 0), stop=False), r=[Cpad, hreb], w=[py])
                    ph.op("pe", I_mm(py.t[:, :Lb], Cpad.t[:, pt, 1, :], himb.t[:, :Lb], start=False, stop=(qq == 3)), r=[Cpad, himb], w=[py])
                ph.op(V, I_stt(yv.t[:, :Lb], u_[ct].t[:, :Lb], dsk.t[:, ct:ct + 1], py.t[:, :Lb], ALU.mult, ALU.add), r=[u_[ct], dsk, py], w=[yv])
                ph.op("pool", I_tt(x2.t[:, :Lb], yv.t[:, :Lb], yv.t[:, :Lb], ALU.mult), r=[yv], w=[x2])
                ph.op("pool", I_ts(inn.t[:, :Lb], x2.t[:, :Lb], 0.044715, 1.0, op0=ALU.mult, op1=ALU.add), r=[x2], w=[inn])
                ph.op("pool", I_tt(in2.t[:, :Lb], inn.t[:, :Lb], yv.t[:, :Lb], ALU.mult), r=[inn, yv], w=[in2])
                ph.op("act", I_act(sg.t[:, :Lb], in2.t[:, :Lb], AF.Sigmoid, scale=1.5957691216057308), r=[in2], w=[sg])
                ph.op(V, I_tt(yg[ct].t[:, :Lb], yv.t[:, :Lb], sg.t[:, :Lb], ALU.mult), r=[yv, sg], w=[yg[ct]])
                ph.op("act", I_act(ygb[ct].t[:, :Lb], yg[ct].t[:, :Lb], AF.Copy), r=[yg[ct]], w=[ygb[ct]])
            prev = True
            for co in range(4):
                z_ = zst[nz % 2]; yf = ysf[nz % 2]; nz += 1
                for ct in range(4):
                    ph.op("pe", I_mm(pG.t[:, :Lb], wglu.t[:, ct, co * 128:(co + 1) * 128], ygb[ct].t[:, :Lb], start=(ct == 0), stop=(ct == 3)),
                          r=[wglu, ygb[ct]], w=[pG])
                ph.op("act", I_act(sgl.t[:, :Lb], pG.t[:, :Lb], AF.Sigmoid, bias=bglu.t[:, co:co + 1]), r=[pG, bglu], w=[sgl])
                ph.dma(I_dma(z_.t[:, :Lb], s.zsT[co * 128:(co + 1) * 128, t0:t0 + Lb]), w=[z_])
                ph.op(V, I_tt(tg.t[:, :Lb], yg[co].t[:, :Lb], sgl.t[:, :Lb], ALU.mult), r=[yg[co], sgl], w=[tg])
                ph.op("pool", I_tt(yf.t[:, :Lb], tg.t[:, :Lb], z_.t[:, :Lb], ALU.mult), r=[tg, z_], w=[yf])
                ph.dma(I_dma(s.ysT[co * 128:(co + 1) * 128, t0:t0 + Lb], yf.t[:, :Lb]), r=[yf])
        ph.dma(I_dma(s.srout[l].rearrange("(pt q) -> q pt", q=128), hr_.t[:]), r=[hr_])
        ph.dma(I_dma(s.siout[l].rearrange("(pt q) -> q pt", q=128), hi_.t[:]), r=[hi_])
    ph.finish()


def bc_mid(ap, n):
    return bass.AP(ap.tensor, ap.offset, [list(ap.ap[0]), [0, n]] + [list(x) for x in ap.ap[1:]])


NIT2 = 20
W02 = 16.0


def phase_attn2(nc, A, l, s):
    ph = Phase(nc, f"B{l}{s.nm}")
    idf, idb = mk_ident(ph)
    T, S = s.T, s.S
    QP = min(128, T); QB = min(256, T)
    TPB = QB // QP
    ntile = T // QP; nblk = T // QB
    L = T if s.causal else S
    KSEL = float(min(TOPK, L // 4))
    ktiles = [(k0, min(128, S - k0)) for k0 in range(0, S, 128)]
    NKT = len(ktiles)
    kiT2 = ph.sb([128, S], BF16, "kiT2")
    ph.dma(I_dma(kiT2.t[0:64, :], s.kiT[:, :]), w=[kiT2])
    ph.dma(I_dma(kiT2.t[64:128, :], s.kiT[:, :]), w=[kiT2])
    sc = [ph.sb([128, S], F32, f"sc{i}") for i in range(2)]
    junk = ph.sb([128, S], BF16, "junk")
    maskT = [ph.sb([128, NKT, QB], BF16, f"maskT{i}") for i in range(2)]
    qia = [ph.sb([128, 4, 128], BF16, f"qia{i}") for i in range(2)]
    qib = [ph.sb([128, 4, 128], BF16, f"qib{i}") for i in range(2)]
    qta = [ph.sb([128, 4, 256], BF16, f"qta{i}") for i in range(2)]
    qtb = [ph.sb([128, 4, 256], BF16, f"qtb{i}") for i in range(2)]
    for i in range(2):
        ph.op("pool", I_memset(qia[i].t[64:128, :, :], 0.0), w=[qia[i]])
        ph.op("pool", I_memset(qib[i].t[0:64, :, :], 0.0), w=[qib[i]])
        ph.op("pool", I_memset(qta[i].t[64:128, :, :], 0.0), w=[qta[i]])
        ph.op("pool", I_memset(qtb[i].t[0:64, :, :], 0.0), w=[qtb[i]])
    wit = [ph.sb([128, 8], F32, f"wit{i}") for i in range(2)]
    dg = [ph.sb([128, 8, 128], BF16, f"dg{i}") for i in range(2)]
    rl = [ph.sb([128, 512], BF16, f"rl{i}") for i in range(3)]
    sm1 = lambda nm: ph.sb([128, 1], F32, nm)
    mid, negmid, cnt, sact, comb, tq, thr = [sm1(n_) for n_ in ("mid", "negmid", "cnt", "sact", "comb", "tq", "thr")]
    dgthr = ph.sb([128, 128], F32, "dgthr")
    thrbc = ph.sb([128, 128], F32, "thrbc")
    onesf = ph.sb([128, 128], F32, "onesf")
    ph.op("pool", I_memset(onesf.t[:], 1.0), w=[onesf])
    va = [ph.sb([128, 130], BF16, f"va{i}") for i in range(3)]
    ktl = [ph.sb([128, 128], BF16, f"ktl{i}") for i in range(3)]
    pe_ = [ph.sb([128, 2, 256], BF16, f"pe{i}") for i in range(2)]
    pm = [ph.sb([128, 2, 256], BF16, f"pm{i}") for i in range(2)]
    oa = [ph.sb([128, 256], F32, f"oa{i}") for i in range(2)]
    for o__ in oa:
        ph.op("pool", I_memset(o__.t[:], 0.0), w=[o__])
    rden = [ph.sb([64, 256], F32, f"rden{i}") for i in range(2)]
    t1 = [ph.sb([64, 256], F32, f"t1{i}") for i in range(2)]
    zat = [ph.sb([64, 256], BF16, f"zat{i}") for i in range(2)]
    yat = [ph.sb([64, 256], BF16, f"yat{i}") for i in range(2)]
    iop = ph.sb([128, 64], F32, "iop")
    selden = ph.sb([128, 64], F32, "selden")
    ph.op("pool", I_iota(iop.t[:], [[0, 64]], 0, 1), w=[iop])
    ph.op("dve", I_ts(selden.t[:], iop.t[:], 64.0, None, op0=ALU.is_equal), r=[iop], w=[selden])
    lg = [ph.ps([128, 512], F32, f"lg{i}") for i in range(2)]
    acc = ph.ps([128, 512], F32, "acc")
    stb = [ph.ps([128, 2, 256], F32, f"st{i}") for i in range(2)]
    oacc = [ph.ps([128, 512], F32, f"oacc{i}") for i in range(2)]
    scTp = ph.ps([128, 4, 128], F32, "scTp")
    cnts = {"r": 0, "v": 0, "h": 0, "e": 0}

    def tile_info(i):
        q0 = i * QP
        Slim = (q0 + QP) if s.causal else S
        return q0, Slim

    def blk_kts(b):
        Sblk = min(S, (b + 1) * QB) if s.causal else S
        return [(i, k0, ksz) for i, (k0, ksz) in enumerate(ktiles) if k0 < Sblk]

    def gen_index(i):
        q0, Slim = tile_info(i)
        sl = i % 2
        qa, qb_, wi_, dg_, sc_ = qia[sl], qib[sl], wit[sl], dg[sl], sc[sl]
        qsrc = s.qiT.rearrange("(hp p) t -> p hp t", p=128)
        ph.dma(I_dma(qa.t[0:64, :, :QP], qsrc[0:64, :, q0:q0 + QP]), w=[qa])
        ph.dma(I_dma(qb_.t[64:128, :, :QP], qsrc[64:128, :, q0:q0 + QP]), w=[qb_])
        ph.dma(I_dma(wi_.t[:QP, :], s.wiS[q0:q0 + QP, :]), w=[wi_])
        for h in range(8):
            ph.op("dve", I_ts(dg_.t[:QP, h, :QP], idf.t[:QP, :QP], wi_.t[:QP, h:h + 1], None, op0=ALU.mult), r=[idf, wi_], w=[dg_])
        yield
        for c0 in range(0, Slim, 512):
            csz = min(512, Slim - c0)
            bufs = []
            for h in range(8):
                bufs.append((lg[cnts["r"] % 2], rl[cnts["r"] % 3])); cnts["r"] += 1

            def emit_lg(h):
                g, r_ = bufs[h]
                qz = qa if h % 2 == 0 else qb_
                ph.op("pe", I_mm(g.t[:QP, :csz], qz.t[:, h // 2, :QP], kiT2.t[:, c0:c0 + csz]), r=[qz, kiT2], w=[g])
            emit_lg(0)
            for h in range(8):
                g, r_ = bufs[h]
                if h < 7:
                    emit_lg(h + 1)
                if h % 4 == 3:
                    ph.op("dve", I_ts(r_.t[:QP, :csz], g.t[:QP, :csz], 0.0, None, op0=ALU.max), r=[g], w=[r_])
                else:
                    ph.op("act", I_act(r_.t[:QP, :csz], g.t[:QP, :csz], AF.Relu), r=[g], w=[r_])
                ph.op("pe", I_mm(acc.t[:QP, :csz], dg_.t[:QP, h, :QP], r_.t[:QP, :csz], start=(h == 0), stop=(h == 7)), r=[dg_, r_], w=[acc])
            ph.op("dve", I_copy(sc_.t[:QP, c0:c0 + csz], acc.t[:QP, :csz]), r=[acc], w=[sc_])
            yield
        if s.causal and QP == 128:
            ph.op("pool", I_memset(sc_.t[0:64, Slim - 64:Slim], NEG), w=[sc_])

    def n_index(i):
        q0, Slim = tile_info(i)
        return 1 + (Slim + 511) // 512

    def gen_bisect(i):
        q0, Slim = tile_info(i)
        sc_ = sc[i % 2]
        b = i // TPB
        qoff = (i % TPB) * QP
        mT = maskT[b % 2]
        S1 = Slim
        if Slim >= 1024:
            S1 = int(Slim * 0.45) // 128 * 128
        ph.op("dve", I_memset(mid.t[:QP, :], 0.0), w=[mid])
        w = W02
        kadj = KSEL - 0.5 * (Slim - S1)
        for k in range(NIT2):
            w = w / 2.0
            ph.op("pool", I_ts(negmid.t[:QP, :], mid.t[:QP, :], -w, None, op0=ALU.add), r=[mid], w=[negmid])
            ph.op("dve", I_ts(junk.t[:QP, :S1], sc_.t[:QP, :S1], mid.t[:QP, 0:1], -kadj, op0=ALU.is_ge, op1=ALU.add,
                              accum_out=cnt.t[:QP, 0:1]), r=[sc_, mid], w=[cnt])
            if S1 < Slim:
                ph.op("act", I_act(junk.t[:QP, S1:Slim], sc_.t[:QP, S1:Slim], AF.Sign, bias=mid.t[:QP, 0:1], scale=-1.0,
                                   accum_out=sact.t[:QP, 0:1]), r=[sc_, mid], w=[sact])
                ph.op("dve", I_stt(tq.t[:QP, :], sact.t[:QP, :], 0.5, cnt.t[:QP, :], ALU.mult, ALU.is_le), r=[sact, cnt], w=[tq])
            else:
                ph.op("dve", I_ts(tq.t[:QP, :], cnt.t[:QP, :], 0.0, None, op0=ALU.is_ge), r=[cnt], w=[tq])
            ph.op("dve", I_stt(mid.t[:QP, :], tq.t[:QP, :], 2.0 * w, negmid.t[:QP, :], ALU.mult, ALU.add), r=[tq, negmid], w=[mid])
            yield
        ph.op("dve", I_ts(thr.t[:QP, :], mid.t[:QP, :], -w, None, op0=ALU.add), r=[mid], w=[thr])
        ph.op("dve", I_ts(dgthr.t[:QP, :QP], idf.t[:QP, :QP], thr.t[:QP, 0:1], None, op0=ALU.mult), r=[idf, thr], w=[dgthr])
        ph.op("pe", I_mm(scTp.t[:, 0, 0:QP], onesf.t[:QP, :], dgthr.t[:QP, :QP]), r=[onesf, dgthr], w=[scTp])
        ph.op("act", I_act(thrbc.t[:, :QP], scTp.t[:, 0, 0:QP], AF.Copy), r=[scTp], w=[thrbc])
        yield
        kts = blk_kts(b)
        mine = [(i_, k0, ksz) for (i_, k0, ksz) in kts if k0 < Slim]
        for g0 in range(0, len(mine), 4):
            grp = mine[g0:g0 + 4]
            for gi, (i_, k0, ksz) in enumerate(grp):
                ph.op("pe", I_tr(scTp.t[:ksz, gi, :QP], sc_.t[:QP, k0:k0 + ksz], idf.t[:QP, :QP]), r=[sc_, idf], w=[scTp])
            i0 = grp[0][0]; ng = len(grp)
            ph.op("dve", I_tt(mT.t[:, i0:i0 + ng, qoff:qoff + QP], scTp.t[:, 0:ng, :QP], bc_mid(thrbc.t[:, 0:QP], ng), ALU.is_ge),
                  r=[scTp, thrbc], w=[mT])
            yield
        rest = [(i_, k0, ksz) for (i_, k0, ksz) in kts if k0 >= Slim]
        if rest:
            i0 = rest[0][0]; i1 = rest[-1][0] + 1
            ph.op("pool", I_memset(mT.t[:, i0:i1, qoff:qoff + QP], 0.0), w=[mT])

    def n_bisect(i):
        q0, Slim = tile_info(i)
        nm = len([1 for (k0, ksz) in ktiles if k0 < Slim])
        return NIT2 + 1 + (nm + 3) // 4

    import os as _os4
    _stop = _os4.environ.get("ATTN2_STOP", "")

    def gen_attn(b):
        qb0 = b * QB
        kts = blk_kts(b)
        mT = maskT[b % 2]
        qa, qb_ = qta[b % 2], qtb[b % 2]
        qsrc = s.qT.rearrange("(hp p) t -> p hp t", p=128)
        ph.dma(I_dma(qa.t[0:64, :, :QB], qsrc[0:64, :, qb0:qb0 + QB]), w=[qa])
        ph.dma(I_dma(qb_.t[64:128, :, :QB], qsrc[64:128, :, qb0:qb0 + QB]), w=[qb_])
        for hg in range(4):
            ubuf = []
            for idx in range(len(kts)):
                ubuf.append((va[cnts["v"] % 3], ktl[cnts["v"] % 3])); cnts["v"] += 1

            def emit_st(idx):
                i_, k0, ksz = kts[idx]
                va_, kt_ = ubuf[idx]
                st = stb[idx % 2]
                ph.dma(I_dma(va_.t[:ksz, :], s.vaug[k0:k0 + ksz, hg * 130:(hg + 1) * 130]), w=[va_])
                ph.dma(I_dma(kt_.t[:, :ksz], s.kT[hg * 128:(hg + 1) * 128, k0:k0 + ksz]), w=[kt_])
                ph.op("pe", I_mm(st.t[:ksz, 0, :QB], kt_.t[:, :ksz], qa.t[:, hg, :QB]), r=[kt_, qa], w=[st])
                ph.op("pe", I_mm(st.t[:ksz, 1, :QB], kt_.t[:, :ksz], qb_.t[:, hg, :QB]), r=[kt_, qb_], w=[st])
            emit_st(0)
            for idx, (i_, k0, ksz) in enumerate(kts):
                va_, kt_ = ubuf[idx]
                st = stb[idx % 2]
                p_ = pe_[cnts["h"] % 2]; m_ = pm[cnts["h"] % 2]; cnts["h"] += 1
                if idx + 1 < len(kts):
                    emit_st(idx + 1)
                ph.op("act", I_act(p_.t[:ksz, :, :QB], st.t[:ksz, :, :QB], AF.Exp), r=[st], w=[p_])
                ph.op("dve", I_tt(m_.t[:ksz, :, :QB], p_.t[:ksz, :, :QB], bc_mid(mT.t[:ksz, i_, :QB], 2), ALU.mult), r=[p_, mT], w=[m_])
                for hh in range(2):
                    ph.op("pe", I_mm(oacc[hh].t[:65, :QB], va_.t[:ksz, hh * 65:(hh + 1) * 65], m_.t[:ksz, hh, :QB],
                                     start=(idx == 0), stop=(idx == len(kts) - 1)), r=[va_, m_], w=[oacc[hh]])
                yield
            if _stop in ("st", "exp", "mul", "pv"):
                yield
                continue
            for hh in range(2):
                h = hg * 2 + hh
                e = cnts["e"] % 2; cnts["e"] += 1
                o_ = oa[e]; rd = rden[e]; t_ = t1[e]; z_ = zat[e]; y_ = yat[e]
                ph.op("act", I_act(o_.t[:65, :QB], oacc[hh].t[:65, :QB], AF.Copy), r=[oacc[hh]], w=[o_])
                if _stop == "ep1":
                    continue
                ph.op("pe", I_mm(acc.t[:64, 0:QB], selden.t[:, :64], o_.t[:, :QB]), r=[selden, o_], w=[acc])
                if _stop == "ep2":
                    continue
                ph.op("dve", I_recip(rd.t[:64, :QB], acc.t[:64, 0:QB]), r=[acc], w=[rd])
                if _stop == "ep3":
                    continue
                ph.dma(I_dma(z_.t[:64, :QB], s.zaT[64 * h:64 * h + 64, qb0:qb0 + QB]), w=[z_])
                ph.op("dve", I_tt(t_.t[:64, :QB], o_.t[0:64, :QB], rd.t[:64, :QB], ALU.mult), r=[o_, rd], w=[t_])
                if _stop == "ep4":
                    continue
                ph.op("pool", I_tt(y_.t[:64, :QB], t_.t[:64, :QB], z_.t[:64, :QB], ALU.mult), r=[t_, z_], w=[y_])
                if _stop == "ep5":
                    continue
                ph.dma(I_dma(s.yaT[64 * h:64 * h + 64, qb0:qb0 + QB], y_.t[:64, :QB]), r=[y_], q="act")
            yield

    def n_attn(b):
        return 4 * (len(blk_kts(b)) + 1)

    attn_state = {}
    nslot = ntile + 1 + TPB + 1
    for slot in range(nslot + 2 * TPB):
        work = []
        if slot < ntile:
            work.append([gen_index(slot), n_index(slot)])
        import os as _os3
        _stop = _os3.environ.get("ATTN2_STOP", "")
        if 1 <= slot <= ntile and _stop != "index":
            work.append([gen_bisect(slot - 1), n_bisect(slot - 1)])
        for b in range(nblk):
            ready = (b + 1) * TPB + 1
            if slot == ready and _stop not in ("index", "bisect"):
                attn_state[b] = [gen_attn(b), n_attn(b), 0]
        for b, stt_ in list(attn_state.items()):
            g, tot, done = stt_
            share = (tot + TPB - 1) // TPB
            work.append([g, min(share, tot - done) + 1])
            stt_[2] = done + share
            if stt_[2] >= tot:
                del attn_state[b]
        if not work:
            continue
        maxc = max(c for _, c in work)
        done_c = [0] * len(work)
        alive = [True] * len(work)
        for k in range(maxc):
            for wi_x, (g, c) in enumerate(work):
                while alive[wi_x] and done_c[wi_x] < c and (done_c[wi_x] + 1) * maxc <= (k + 1) * c:
                    try:
                        next(g)
                    except StopIteration:
                        alive[wi_x] = False
                    done_c[wi_x] += 1
        for wi_x, (g, c) in enumerate(work[: (1 if slot < ntile else 0) + (1 if (1 <= slot <= ntile and _stop != "index") else 0)]):
            if alive[wi_x]:
                for _ in g:
                    pass
    for b, stt_ in list(attn_state.items()):
        for _ in stt_[0]:
            pass
    ph.finish()
```
